# Optimizing a Trainium2 kernel written in Bass

```python
import jax
import jax.numpy as jnp
from jax import lax
import numpy as np

D_MODEL = 2048
BATCH = 4
SEQ = 4096
DEPTH = 4

HEAD_DIM = 128
A_GROUPS = 4
A_WIDTH = A_GROUPS * HEAD_DIM
CHUNK = 128
B_HEADS = 4
DILATED_PAIRS = ((128, 1), (512, 4), (2048, 16))
C_HEADS = 8
C_KV_HEADS = 2
C_GROUP = C_HEADS // C_KV_HEADS
CMP_LEN = 32
CMP_STRIDE = 16
SEL_LEN = 64
SEL_TOP = 16
WIN_LEN = 512
NSA_QBLOCK = 64
BAND_BLOCK = 128
MIX_WIDTH = A_WIDTH + (B_HEADS + C_HEADS) * HEAD_DIM
IN_SPLITS = (2 * A_WIDTH,
             3 * B_HEADS * HEAD_DIM,
             C_HEADS * HEAD_DIM,
             2 * C_KV_HEADS * HEAD_DIM,
             2 * C_KV_HEADS * HEAD_DIM,
             2 * C_KV_HEADS * HEAD_DIM,
             3 * C_HEADS)
IN_WIDTH = sum(IN_SPLITS)
N_MEM = 256
X_HEADS = 4
X_WIDTH = X_HEADS * HEAD_DIM
D_FF = 4 * D_MODEL
EPS = 1e-6
NEG_INF = -1e30

kernel_name = 'hybrid_gmlp_dilated_nsa_decoder'


def rms_norm(x, g):
    xf = x.astype(jnp.float32)
    y = xf * lax.rsqrt(jnp.mean(xf * xf, axis=-1, keepdims=True) + EPS)
    return (y * g.astype(jnp.float32)).astype(x.dtype)


def layer_norm(x, g, b):
    xf = x.astype(jnp.float32)
    xc = xf - jnp.mean(xf, axis=-1, keepdims=True)
    y = xc * lax.rsqrt(jnp.mean(xc * xc, axis=-1, keepdims=True) + EPS)
    return (y * g.astype(jnp.float32) + b.astype(jnp.float32)).astype(x.dtype)


def banded_attention(q, k, v, max_dist):
    bsz, ng, nr, L, hd = q.shape
    blk = BAND_BLOCK
    nprev = -(-max_dist // blk)
    nb = -(-L // blk)
    pad = nb * blk - L
    qb = jnp.pad(q, ((0, 0), (0, 0), (0, 0), (0, pad), (0, 0))).reshape(bsz, ng, nr, nb, blk, hd)

    def windows(t):
        tb = jnp.pad(t, ((0, 0), (0, 0), (nprev * blk, pad), (0, 0))).reshape(bsz, ng, nb + nprev, blk, hd)
        return jnp.concatenate([tb[:, :, i:i + nb] for i in range(nprev + 1)], axis=3)

    kw, vw = windows(k), windows(v)
    s = jnp.einsum('bgrnqd,bgnkd->bgrnqk', qb, kw).astype(jnp.float32) * (hd ** -0.5)
    qpos = jnp.arange(nb)[:, None] * blk + jnp.arange(blk)[None, :]
    kpos = (jnp.arange(nb)[:, None] - nprev) * blk + jnp.arange((nprev + 1) * blk)[None, :]
    dist = qpos[:, :, None] - kpos[:, None, :]
    mask = (dist >= 0) & (dist <= max_dist) & (kpos[:, None, :] >= 0)
    s = jnp.where(mask, s, NEG_INF)
    m = jnp.max(s, axis=-1, keepdims=True)
    e = jnp.exp(s - m)
    den = jnp.sum(e, axis=-1, keepdims=True)
    o = jnp.einsum('bgrnqk,bgnkd->bgrnqd', (e / den).astype(v.dtype), vw)
    lse = (m + jnp.log(den))[..., 0]
    o = o.reshape(bsz, ng, nr, nb * blk, hd)[:, :, :, :L]
    lse = lse.reshape(bsz, ng, nr, nb * blk)[:, :, :, :L]
    return o, lse


def spatial_gating(uv, ln_g, ln_b, w_s, b_s):
    bsz, S, _ = uv.shape
    u, v = jnp.split(jax.nn.gelu(uv), 2, axis=-1)
    v = layer_norm(v, ln_g, ln_b).reshape(bsz, S // CHUNK, CHUNK, A_GROUPS, HEAD_DIM)
    w = w_s * jnp.tril(jnp.ones((CHUNK, CHUNK), w_s.dtype))
    sv = jnp.einsum('gts,bcsgh->bctgh', w, v) + b_s.T[:, :, None]
    return u * sv.reshape(bsz, S, A_WIDTH)


def dilated_attention(q, k, v):
    bsz, nh, S, hd = q.shape
    outs, lses = [], []
    for window, d in DILATED_PAIRS:
        def fold(t):
            return t.reshape(bsz, nh, S // d, d, hd).transpose(0, 1, 3, 2, 4).reshape(bsz, nh * d, S // d, hd)
        o, l = banded_attention(fold(q)[:, :, None], fold(k), fold(v), window // d)
        outs.append(o[:, :, 0].reshape(bsz, nh, d, S // d, hd).transpose(0, 1, 3, 2, 4).reshape(bsz, nh, S, hd))
        lses.append(l[:, :, 0].reshape(bsz, nh, d, S // d).transpose(0, 1, 3, 2).reshape(bsz, nh, S))
    w = jax.nn.softmax(jnp.stack(lses), axis=0)
    o = jnp.einsum('pbhs,pbhsd->bhsd', w.astype(q.dtype), jnp.stack(outs))
    return o.transpose(0, 2, 1, 3).reshape(bsz, S, nh * hd)


def compress_blocks(kraw, pos, w1, w2):
    bsz, ng, S, hd = kraw.shape
    sub = kraw.reshape(bsz, ng, S // CMP_STRIDE, CMP_STRIDE, hd)
    blocks = jnp.concatenate([sub[:, :, :-1], sub[:, :, 1:]], axis=3) + pos
    n_c = blocks.shape[2]
    h = jax.nn.gelu(blocks.reshape(bsz, ng, n_c, CMP_LEN * hd) @ w1)
    return h @ w2


def nsa_attention(q, kv_cmp, kv_slc, kv_win, gates, cmp_pos, ck_w1, ck_w2, cv_w1, cv_w2):
    bsz, ng, nr, S, hd = q.shape
    scale = hd ** -0.5
    k_c = compress_blocks(kv_cmp[0], cmp_pos, ck_w1, ck_w2)
    v_c = compress_blocks(kv_cmp[1], cmp_pos, cv_w1, cv_w2)
    n_c = k_c.shape[2]
    n_sel = S // SEL_LEN
    n_top = min(SEL_TOP, n_sel)
    k_s = kv_slc[0].reshape(bsz, ng, n_sel, SEL_LEN, hd)
    v_s = kv_slc[1].reshape(bsz, ng, n_sel, SEL_LEN, hd)
    c_start = np.arange(n_c) * CMP_STRIDE
    s_start = np.arange(n_sel) * SEL_LEN
    overlap = jnp.asarray(((c_start[:, None] <= s_start[None, :] + SEL_LEN - 1)
                           & (c_start[:, None] + CMP_LEN - 1 >= s_start[None, :])).astype(np.float32))
    c_end = jnp.arange(n_c) * CMP_STRIDE + CMP_LEN - 1
    sel_idx = jnp.arange(n_sel)
    b_idx = jnp.arange(bsz)[:, None, None, None]
    g_idx = jnp.arange(ng)[None, :, None, None]
    nqb = S // NSA_QBLOCK
    q_blocks = q.reshape(bsz, ng, nr, nqb, NSA_QBLOCK, hd).transpose(3, 0, 1, 2, 4, 5)

    def block_step(args):
        qb, bi = args
        t = bi * NSA_QBLOCK + jnp.arange(NSA_QBLOCK)
        valid_c = c_end[None, :] <= t[:, None]
        s_c = jnp.einsum('bgrqd,bgnd->bgrqn', qb, k_c).astype(jnp.float32) * scale
        p_c = jax.nn.softmax(jnp.where(valid_c, s_c, NEG_INF), axis=-1) * valid_c
        o_c = jnp.einsum('bgrqn,bgnd->bgrqd', p_c.astype(v_c.dtype), v_c)
        imp = jnp.einsum('bgrqn,nj->bgqj', p_c, overlap)
        jt = t // SEL_LEN
        forced = (sel_idx[None] == 0) | (sel_idx[None] == jt[:, None]) | (sel_idx[None] == jt[:, None] - 1)
        valid_s = sel_idx[None] * SEL_LEN <= t[:, None]
        score = jnp.where(forced, 1e4, jnp.where(valid_s, imp, -1.0))
        _, idx = lax.top_k(score, n_top)
        ks = k_s[b_idx, g_idx, idx]
        vs = v_s[b_idx, g_idx, idx]
        kpos = idx[..., None] * SEL_LEN + jnp.arange(SEL_LEN)
        mask_s = (kpos <= t[None, None, :, None, None])[:, :, None]
        s_s = jnp.einsum('bgrqd,bgqnkd->bgrqnk', qb, ks).astype(jnp.float32) * scale
        s_s = jnp.where(mask_s, s_s, NEG_INF).reshape(bsz, ng, nr, NSA_QBLOCK, n_top * SEL_LEN)
        p_s = jax.nn.softmax(s_s, axis=-1).reshape(bsz, ng, nr, NSA_QBLOCK, n_top, SEL_LEN)
        o_s = jnp.einsum('bgrqnk,bgqnkd->bgrqd', p_s.astype(vs.dtype), vs)
        return o_c, o_s

    o_c, o_s = lax.map(block_step, (q_blocks, jnp.arange(nqb)))
    o_c = o_c.transpose(1, 2, 3, 0, 4, 5).reshape(bsz, ng, nr, S, hd)
    o_s = o_s.transpose(1, 2, 3, 0, 4, 5).reshape(bsz, ng, nr, S, hd)
    o_w, _ = banded_attention(q, kv_win[0], kv_win[1], WIN_LEN - 1)
    g = jax.nn.sigmoid(gates.astype(jnp.float32)).reshape(bsz, S, ng, nr, 3)
    g = g.transpose(4, 0, 2, 3, 1)[..., None].astype(q.dtype)
    o = g[0] * o_c + g[1] * o_s + g[2] * o_w
    return o.transpose(0, 3, 1, 2, 4).reshape(bsz, S, ng * nr * hd)


def setup_inputs(seed: int = 0) -> dict:
    key = jax.random.key(seed)
    ks = jax.random.split(key, 24)
    f32 = jnp.float32

    def normal(k, shape, scale):
        return jax.random.normal(k, shape, f32) * scale

    def gain(k, shape):
        return 1.0 + normal(k, shape, 0.02)

    L = DEPTH
    return {
        'x': normal(ks[0], (BATCH, SEQ, D_MODEL), 1.0),
        'mem': normal(ks[1], (BATCH, N_MEM, D_MODEL), 1.0),
        'norm_mix': gain(ks[2], (L, D_MODEL)),
        'w_in': normal(ks[3], (L, D_MODEL, IN_WIDTH), D_MODEL ** -0.5),
        'gmlp_ln_g': gain(ks[4], (L, A_WIDTH)),
        'gmlp_ln_b': normal(ks[5], (L, A_WIDTH), 0.02),
        'gmlp_w_s': normal(ks[6], (L, A_GROUPS, CHUNK, CHUNK), CHUNK ** -0.5),
        'gmlp_b_s': gain(ks[7], (L, A_GROUPS, CHUNK)),
        'cmp_pos': normal(ks[8], (L, CMP_LEN, HEAD_DIM), 0.02),
        'cmp_k_w1': normal(ks[9], (L, CMP_LEN * HEAD_DIM, HEAD_DIM), (CMP_LEN * HEAD_DIM) ** -0.5),
        'cmp_k_w2': normal(ks[10], (L, HEAD_DIM, HEAD_DIM), HEAD_DIM ** -0.5),
        'cmp_v_w1': normal(ks[11], (L, CMP_LEN * HEAD_DIM, HEAD_DIM), (CMP_LEN * HEAD_DIM) ** -0.5),
        'cmp_v_w2': normal(ks[12], (L, HEAD_DIM, HEAD_DIM), HEAD_DIM ** -0.5),
        'w_out': normal(ks[13], (L, MIX_WIDTH, D_MODEL), 0.5 * MIX_WIDTH ** -0.5),
        'norm_xattn': gain(ks[14], (L, D_MODEL)),
        'norm_mem': gain(ks[15], (L, D_MODEL)),
        'xattn_wq': normal(ks[16], (L, D_MODEL, X_WIDTH), D_MODEL ** -0.5),
        'xattn_wkv': normal(ks[17], (L, D_MODEL, 2 * X_WIDTH), D_MODEL ** -0.5),
        'xattn_wo': normal(ks[18], (L, X_WIDTH, D_MODEL), 0.5 * X_WIDTH ** -0.5),
        'norm_mlp': gain(ks[19], (L, D_MODEL)),
        'w_up': normal(ks[20], (L, D_MODEL, D_FF), D_MODEL ** -0.5),
        'w_down': normal(ks[21], (L, D_FF, D_MODEL), 0.5 * D_FF ** -0.5),
        'final_norm': gain(ks[22], (D_MODEL,)),
    }


def reference(x, mem, norm_mix, w_in, gmlp_ln_g, gmlp_ln_b, gmlp_w_s, gmlp_b_s, cmp_pos,
              cmp_k_w1, cmp_k_w2, cmp_v_w1, cmp_v_w2, w_out, norm_xattn, norm_mem,
              xattn_wq, xattn_wkv, xattn_wo, norm_mlp, w_up, w_down, final_norm):
    bsz, S, _ = x.shape
    n_mem = mem.shape[1]
    split_at = [int(i) for i in np.cumsum(IN_SPLITS)[:-1]]
    x_scale = HEAD_DIM ** -0.5
    for l in range(DEPTH):
        h = rms_norm(x, norm_mix[l])
        z = h @ w_in[l]
        z_a, z_b, z_cq, z_ckv_c, z_ckv_s, z_ckv_w, z_cg = jnp.split(z, split_at, axis=-1)
        out_a = spatial_gating(z_a, gmlp_ln_g[l], gmlp_ln_b[l], gmlp_w_s[l], gmlp_b_s[l])
        qkv_b = z_b.reshape(bsz, S, 3, B_HEADS, HEAD_DIM).transpose(2, 0, 3, 1, 4)
        out_b = dilated_attention(qkv_b[0], qkv_b[1], qkv_b[2])
        q_c = z_cq.reshape(bsz, S, C_KV_HEADS, C_GROUP, HEAD_DIM).transpose(0, 2, 3, 1, 4)

        def kv_heads(t):
            return t.reshape(bsz, S, 2, C_KV_HEADS, HEAD_DIM).transpose(2, 0, 3, 1, 4)

        out_c = nsa_attention(q_c, kv_heads(z_ckv_c), kv_heads(z_ckv_s), kv_heads(z_ckv_w), z_cg,
                              cmp_pos[l], cmp_k_w1[l], cmp_k_w2[l], cmp_v_w1[l], cmp_v_w2[l])
        x = x + jnp.concatenate([out_a, out_b, out_c], axis=-1) @ w_out[l]
        h = rms_norm(x, norm_xattn[l])
        m = rms_norm(mem, norm_mem[l])
        q = (h @ xattn_wq[l]).reshape(bsz, S, X_HEADS, HEAD_DIM)
        kv = (m @ xattn_wkv[l]).reshape(bsz, n_mem, 2, X_HEADS, HEAD_DIM)
        s = jnp.einsum('bshd,bmhd->bhsm', q, kv[:, :, 0]).astype(jnp.float32) * x_scale
        p = jax.nn.softmax(s, axis=-1).astype(x.dtype)
        o = jnp.einsum('bhsm,bmhd->bshd', p, kv[:, :, 1]).reshape(bsz, S, X_WIDTH)
        x = x + o @ xattn_wo[l]
        h = rms_norm(x, norm_mlp[l])
        x = x + jnp.square(jax.nn.relu(h @ w_up[l])) @ w_down[l]
    return rms_norm(x, final_norm)
```

```python
from concourse.bass_utils import run_bass_kernel_spmd
import numpy as np
import concourse.bass as bass
import concourse.mybir as mybir
from contextlib import ExitStack

F32 = mybir.dt.float32
BF16 = mybir.dt.bfloat16
AF = mybir.ActivationFunctionType
ALU = mybir.AluOpType
AX = mybir.AxisListType

STRICT_SAME_ENGINE = True


class Buf:
    __slots__ = ("name", "w", "r")

    def __init__(self, name):
        self.name = name
        self.w = None
        self.r = []


class Op:
    __slots__ = ("eng", "idx", "fn", "waits", "needed", "val", "kind", "semh", "key")

    def __init__(self, eng, idx, fn, kind="c"):
        self.eng = eng
        self.idx = idx
        self.fn = fn
        self.waits = []
        self.needed = False
        self.val = None
        self.kind = kind
        self.semh = None
        self.key = None


class Sched:
    ENGS = ("pe", "act", "dve", "pool", "sp")
    NDMA = {"sp": 20, "pool": 12}

    def __init__(self, nc, es, arena_elems=None):
        self.nc = nc
        self.es = es
        self.ops = {e: [] for e in self.ENGS}
        self.seen = {e: {} for e in self.ENGS}
        self.csem = {e: es.enter_context(nc.semaphore("cs_" + e)) for e in self.ENGS}
        self.dsem = {q: [es.enter_context(nc.semaphore("ds_%s%d" % (q, i))) for i in range(n)]
                     for q, n in self.NDMA.items()}
        self.ccsem = es.enter_context(nc.semaphore("ccs"))
        self.cccount = 0
        self.lastcc = None
        self.dcount = {q: 0 for q in self.NDMA}
        self.dlast = {q: [None] * n for q, n in self.NDMA.items()}
        self.lastc = {e: None for e in self.ENGS}
        self.nbuf = 0
        self.arena = None
        self.aoff = 0
        if arena_elems:
            self.arena = es.enter_context(nc.sbuf_tensor("arena", [128, arena_elems], BF16))
            self.arena_elems = arena_elems
        self.fence = []
        self.fence_id = 0
        self.passed = {e: 0 for e in self.ENGS}

    def sb(self, name, shape, dt):
        if self.arena is None:
            return self.es.enter_context(self.nc.sbuf_tensor(name, list(shape), dt))
        n = 1
        for s in shape[1:]:
            n *= s
        esz = 4 if dt == F32 else 2
        nb = (n * esz + 31) // 32 * 32
        o = self.aoff
        self.aoff += nb // 2
        assert self.aoff <= self.arena_elems, ("arena overflow", name, self.aoff)
        v = self.arena[0:shape[0], o:o + n * esz // 2]
        if dt != BF16:
            v = v.bitcast(dt)
        if len(shape) == 3:
            v = v.rearrange("p (a b) -> p a b", a=shape[1])
        elif len(shape) == 4:
            v = v.rearrange("p (a b c) -> p a b c", a=shape[1], b=shape[2])
        return v

    def phase_reset(self):
        self.aoff = 0

    def ps(self, name, shape, dt):
        return self.es.enter_context(self.nc.psum_tensor(name, list(shape), dt))

    def buf(self, name=None):
        self.nbuf += 1
        return Buf(name or "b%d" % self.nbuf)

    def barrier(self):
        f = []
        for e in self.ENGS:
            if self.lastc[e] is not None:
                f.append(self.lastc[e])
        for q in self.NDMA:
            for o in self.dlast[q]:
                if o is not None:
                    f.append(o)
        if self.lastcc is not None:
            f.append(self.lastcc)
        self.fence = f
        self.fence_id += 1

    def _dep(self, op, dep, force=False):
        if dep is None or dep is op:
            return
        if dep.kind == "c" and dep.eng == op.eng:
            if op.eng in ("pe", "sp") or not (STRICT_SAME_ENGINE or force):
                return
        seen = self.seen[op.eng]
        if seen.get(dep.key, -1) >= dep.idx:
            return
        seen[dep.key] = dep.idx
        dep.needed = True
        op.waits.append(dep)

    def op(self, eng, fn, reads=(), writes=(), kind="c"):
        lst = self.ops[eng]
        o = Op(eng, len(lst), fn, kind=kind)
        if kind == "d":
            q = eng
            n = self.NDMA[q]
            j = self.dcount[q] % n
            o.semh = self.dsem[q][j]
            o.key = ("d", q, j)
            o.idx = self.dcount[q] // n
            o.val = 16 * (o.idx + 1)
            prev = self.dlast[q][j]
            if prev is not None:
                self._dep(o, prev)
            self.dlast[q][j] = o
            self.dcount[q] += 1
        elif kind == "cc":
            o.semh = self.ccsem
            o.key = ("cc",)
            o.idx = self.cccount
            self.cccount += 1
            o.val = self.cccount
            self.lastcc = o
        else:
            o.semh = self.csem[eng]
            o.key = ("c", eng)
            self.lastc[eng] = o
        if self.passed[eng] < self.fence_id:
            for d in self.fence:
                self._dep(o, d, force=True)
            self.passed[eng] = self.fence_id
        for b in reads:
            self._dep(o, b.w)
        for b in writes:
            self._dep(o, b.w)
            for r in b.r:
                self._dep(o, r)
        for b in reads:
            b.r.append(o)
        for b in writes:
            b.w = o
            b.r = []
        lst.append(o)
        return o

    def dma(self, q, out, in_, reads=(), writes=()):
        return self.op(q, lambda e: e.dma_start(out=out, in_=in_), reads, writes, kind="d")

    def allgather(self, pairs, groups):
        self.barrier()
        o = None
        for src, dst in pairs:
            o = self.op("pool", lambda e, src=src, dst=dst: e.collective_compute(
                "AllGather", ALU.bypass, replica_groups=groups, ins=[src], outs=[dst]), kind="cc")
        self.barrier()
        return o

    def emit(self, final_waits=()):
        nc = self.nc
        for e in self.ENGS:
            c = 0
            for o in self.ops[e]:
                if o.kind == "c" and o.needed:
                    c += 1
                    o.val = c
        engmap = {"pe": "tensor", "act": "scalar", "dve": "vector", "pool": "gpsimd", "sp": "sync"}

        def run(e, engine):
            for o in self.ops[e]:
                for d in o.waits:
                    engine.wait_ge(d.semh, d.val)
                ins = o.fn(engine)
                if o.kind == "d":
                    ins.then_inc(o.semh, 16)
                elif o.kind == "cc":
                    ins.then_inc(o.semh, 1)
                elif o.needed:
                    ins.then_inc(o.semh, 1)
            if e == "sp":
                best = {}
                for d in final_waits:
                    if d.key not in best or best[d.key].val < d.val:
                        best[d.key] = d
                for d in best.values():
                    engine.wait_ge(d.semh, d.val)

        with nc.Block() as block:
            for e in self.ENGS:
                getattr(block, engmap[e])(lambda engine, e=e: run(e, engine))


D = 2048
HD = 128
INW = 5144
EPS = 1e-6


def make_psum(S):
    PS = [S.ps("ps%d" % i, [128, 512], F32) for i in range(8)]
    PS_b = [S.buf() for _ in range(8)]
    return PS, PS_b


def emit_A(S, PSP, io, NT):
    NTT = NT // 128
    NSB = NT // 512
    KC = D // 128
    x, gmix, w_in, ln_g, ln_b, wsT, bsT, tril, ident = (io[k] for k in ("x", "gmix", "w_in", "ln_g", "ln_b", "wsT", "bsT", "tril", "ident"))
    o_mixA, o_fm, o_tm, o_gate = (io[k] for k in ("o_mixA", "o_fm", "o_tm", "o_gate"))
    hT = S.sb("hT", [128, KC, NT], BF16)
    hT_b = [S.buf() for _ in range(NTT)]
    gB = S.sb("gB", [128, D], F32); gB_b = S.buf()
    identb = S.sb("identb", [128, 128], BF16); ident_b = S.buf()
    xt = [S.sb("xt%d" % i, [128, D], F32) for i in range(2)]
    xt_b = [S.buf() for _ in range(2)]
    hb = [S.sb("hb%d" % i, [128, D], BF16) for i in range(2)]
    hb_b = [S.buf() for _ in range(2)]
    junk = S.sb("junk", [128, D], BF16); junk_b = S.buf()
    st = [S.sb("st%d" % i, [128, 8], F32) for i in range(2)]
    st_b = [S.buf() for _ in range(2)]
    W = [S.sb("W%d" % i, [128, KC, 512], BF16) for i in range(3)]
    W_b = [S.buf() for _ in range(3)]
    PS, PS_b = PSP
    psn = [0]

    def next_ps():
        i = psn[0] % 8
        psn[0] += 1
        return PS[i], PS_b[i]

    outs = []
    S.dma("sp", gB[:], gmix.partition_broadcast(128), writes=[gB_b])
    S.dma("pool", identb[:], ident[:, :], writes=[ident_b])

    for tt in range(NTT):
        i = tt % 2
        S.dma("sp", xt[i][:], x[tt * 128:(tt + 1) * 128, :], writes=[xt_b[i]])
        S.op("act", lambda e, i=i: e.activation(out=junk[:], in_=xt[i][:], func=AF.Square,
                                                accum_out=st[i][:, 0:1]),
             reads=[xt_b[i]], writes=[junk_b, st_b[i]])
        S.op("dve", lambda e, i=i: e.tensor_scalar(out=st[i][:, 1:2], in0=st[i][:, 0:1], scalar1=1.0 / D,
                                                   scalar2=EPS, op0=ALU.mult, op1=ALU.add),
             reads=[st_b[i]], writes=[st_b[i]])
        S.op("act", lambda e, i=i: e.activation(out=st[i][:, 3:4], in_=st[i][:, 1:2], func=AF.Sqrt),
             reads=[st_b[i]], writes=[st_b[i]])
        S.op("dve", lambda e, i=i: e.reciprocal(out=st[i][:, 2:3], in_=st[i][:, 3:4]),
             reads=[st_b[i]], writes=[st_b[i]])
        S.op("dve", lambda e, i=i: e.scalar_tensor_tensor(out=hb[i][:], in0=xt[i][:], scalar=st[i][:, 2:3],
                                                          in1=gB[:], op0=ALU.mult, op1=ALU.mult),
             reads=[xt_b[i], st_b[i], gB_b], writes=[hb_b[i]])
        for q in range(KC // 4):
            p, pb = next_ps()
            pv = p[:].bitcast(BF16)
            for j in range(4):
                kc = q * 4 + j
                S.op("pe", lambda e, i=i, kc=kc, j=j, pv=pv: e.transpose(
                    out=pv[:, j * 128:(j + 1) * 128], in_=hb[i][:, kc * 128:(kc + 1) * 128], identity=identb[:]),
                    reads=[hb_b[i], ident_b], writes=[pb])
            eng = "act" if q % 2 == 0 else "dve"
            if eng == "act":
                S.op("act", lambda e, q=q, tt=tt, pv=pv: e.activation(
                    out=hT[:, q * 4:(q + 1) * 4, tt * 128:(tt + 1) * 128],
                    in_=pv[:, 0:512].rearrange("p (a b) -> p a b", a=4), func=AF.Copy),
                    reads=[pb], writes=[hT_b[tt]])
            else:
                S.op("dve", lambda e, q=q, tt=tt, pv=pv: e.tensor_copy(
                    out=hT[:, q * 4:(q + 1) * 4, tt * 128:(tt + 1) * 128],
                    in_=pv[:, 0:512].rearrange("p (a b) -> p a b", a=4)),
                    reads=[pb], writes=[hT_b[tt]])

    wn = [0]

    def load_W(c0, width):
        i = wn[0] % 3
        wn[0] += 1
        S.dma("pool", W[i][:, :, 0:width], w_in[:, c0:c0 + width].rearrange("(k p) c -> p k c", p=128),
              writes=[W_b[i]])
        return W[i], W_b[i]

    ostage = [S.sb("os%d" % i, [128, 512], BF16) for i in range(4)]
    ostage_b = [S.buf() for _ in range(4)]
    osn = [0]

    def next_os():
        i = osn[0] % 4
        osn[0] += 1
        return ostage[i], ostage_b[i]

    evn = [0]

    def evac(dst, dst_b, src, src_b, extra_reads=()):
        e = "act" if evn[0] % 2 == 0 else "dve"
        evn[0] += 1
        if e == "act":
            S.op("act", lambda en: en.activation(out=dst, in_=src, func=AF.Copy),
                 reads=[src_b] + list(extra_reads), writes=[dst_b])
        else:
            S.op("dve", lambda en: en.tensor_copy(out=dst, in_=src),
                 reads=[src_b] + list(extra_reads), writes=[dst_b])

    def fm_group(c0, head0, nheads):
        Wt, Wb = load_W(c0, nheads * 128)
        for j in range(nheads):
            for sb_ in range(NSB):
                p, pb = next_ps()
                for kc in range(KC):
                    S.op("pe", lambda e, p=p, Wt=Wt, kc=kc, j=j, sb_=sb_: e.matmul(
                        p[:], lhsT=Wt[:, kc, j * 128:(j + 1) * 128], rhs=hT[:, kc, sb_ * 512:(sb_ + 1) * 512],
                        start=(kc == 0), stop=(kc == KC - 1)),
                        reads=[Wb] + hT_b[sb_ * 4:(sb_ + 1) * 4], writes=[pb])
                o, ob = next_os()
                evac(o[:], ob, p[:], pb)
                outs.append(S.dma("sp", o_fm[head0 + j, :, sb_ * 512:(sb_ + 1) * 512], o[:], reads=[ob]))

    def tm_group(c0, width, oc0):
        Wt, Wb = load_W(c0, width)
        for tt in range(NTT):
            p, pb = next_ps()
            for kc in range(KC):
                S.op("pe", lambda e, p=p, Wt=Wt, kc=kc, tt=tt: e.matmul(
                    p[:, 0:width], lhsT=hT[:, kc, tt * 128:(tt + 1) * 128], rhs=Wt[:, kc, 0:width],
                    start=(kc == 0), stop=(kc == KC - 1)),
                    reads=[Wb, hT_b[tt]], writes=[pb])
            o, ob = next_os()
            evac(o[:, 0:width], ob, p[:, 0:width], pb)
            outs.append(S.dma("sp", o_tm[tt * 128:(tt + 1) * 128, oc0:oc0 + width], o[:, 0:width], reads=[ob]))

    lngB = S.sb("lngB", [128, 512], F32); lnbB = S.sb("lnbB", [128, 512], F32); ln_bb = S.buf()
    S.dma("sp", lngB[:], ln_g.partition_broadcast(128), writes=[ln_bb])
    S.dma("sp", lnbB[:], ln_b.partition_broadcast(128), writes=[ln_bb])
    wsf = S.sb("wsf", [128, 4, 128], F32); trilf = S.sb("trilf", [128, 128], F32)
    wsb = S.sb("wsb", [128, 4, 128], BF16); ws_b = S.buf(); wsf_b = S.buf()
    bsb = S.sb("bsb", [128, 4], F32); bs_b = S.buf()
    S.dma("sp", wsf[:], wsT[:, :, :], writes=[wsf_b])
    S.dma("sp", trilf[:], tril[:, :], writes=[wsf_b])
    S.dma("sp", bsb[:], bsT[:, :], writes=[bs_b])
    for g in range(4):
        S.op("dve", lambda e, g=g: e.tensor_tensor(out=wsb[:, g, :], in0=wsf[:, g, :], in1=trilf[:], op=ALU.mult),
             reads=[wsf_b], writes=[ws_b])
    Wu, Wub = load_W(0, 512)
    Wv, Wvb = load_W(512, 512)
    ug = [S.sb("ug%d" % i, [128, 512], F32) for i in range(2)]; ug_b = [S.buf() for _ in range(2)]
    vg = [S.sb("vg%d" % i, [128, 512], F32) for i in range(2)]; vg_b = [S.buf() for _ in range(2)]
    vn = [S.sb("vn%d" % i, [128, 512], BF16) for i in range(2)]; vn_b = [S.buf() for _ in range(2)]
    oa = [S.sb("oa%d" % i, [128, 512], BF16) for i in range(2)]; oa_b = [S.buf() for _ in range(2)]
    bst = [S.sb("bst%d" % i, [128, 8], F32) for i in range(2)]; bst_b = [S.buf() for _ in range(2)]
    for tt in range(NTT):
        i = tt % 2
        pu, pub = next_ps()
        pvv, pvb = next_ps()
        for (p, pb, Wt, Wb) in ((pu, pub, Wu, Wub), (pvv, pvb, Wv, Wvb)):
            for kc in range(KC):
                S.op("pe", lambda e, p=p, Wt=Wt, kc=kc, tt=tt: e.matmul(
                    p[:], lhsT=hT[:, kc, tt * 128:(tt + 1) * 128], rhs=Wt[:, kc, :],
                    start=(kc == 0), stop=(kc == KC - 1)),
                    reads=[Wb, hT_b[tt]], writes=[pb])
        S.op("act", lambda e, i=i, pu=pu: e.activation(out=ug[i][:], in_=pu[:], func=AF.Gelu_apprx_tanh),
             reads=[pub], writes=[ug_b[i]])
        S.op("act", lambda e, i=i, pvv=pvv: e.activation(out=vg[i][:], in_=pvv[:], func=AF.Gelu_apprx_tanh),
             reads=[pvb], writes=[vg_b[i]])
        S.op("dve", lambda e, i=i: e.bn_stats(out=bst[i][:, 0:6], in_=vg[i][:]), reads=[vg_b[i]], writes=[bst_b[i]])
        S.op("dve", lambda e, i=i: e.bn_aggr(out=bst[i][:, 6:8], in_=bst[i][:, 0:6]), reads=[bst_b[i]], writes=[bst_b[i]])
        S.op("dve", lambda e, i=i: e.tensor_scalar(out=bst[i][:, 1:2], in0=bst[i][:, 7:8], scalar1=EPS, scalar2=None,
                                                   op0=ALU.add), reads=[bst_b[i]], writes=[bst_b[i]])
        S.op("act", lambda e, i=i: e.activation(out=bst[i][:, 2:3], in_=bst[i][:, 1:2], func=AF.Sqrt),
             reads=[bst_b[i]], writes=[bst_b[i]])
        S.op("dve", lambda e, i=i: e.reciprocal(out=bst[i][:, 0:1], in_=bst[i][:, 2:3]),
             reads=[bst_b[i]], writes=[bst_b[i]])
        S.op("dve", lambda e, i=i: e.tensor_scalar(out=vg[i][:], in0=vg[i][:], scalar1=bst[i][:, 6:7],
                                                   scalar2=bst[i][:, 0:1], op0=ALU.subtract, op1=ALU.mult),
             reads=[vg_b[i], bst_b[i]], writes=[vg_b[i]])
        S.op("dve", lambda e, i=i: e.tensor_tensor(out=vg[i][:], in0=vg[i][:], in1=lngB[:], op=ALU.mult),
             reads=[vg_b[i], ln_bb], writes=[vg_b[i]])
        S.op("dve", lambda e, i=i: e.tensor_tensor(out=vn[i][:], in0=vg[i][:], in1=lnbB[:], op=ALU.add),
             reads=[vg_b[i], ln_bb], writes=[vn_b[i]])
        psv, psvb = next_ps()
        for g in range(4):
            S.op("pe", lambda e, g=g, i=i, psv=psv: e.matmul(
                psv[:, g * 128:(g + 1) * 128], lhsT=wsb[:, g, :], rhs=vn[i][:, g * 128:(g + 1) * 128],
                start=True, stop=True), reads=[ws_b, vn_b[i]], writes=[psvb])
        for g in range(4):
            S.op("dve", lambda e, g=g, i=i, psv=psv: e.scalar_tensor_tensor(
                out=oa[i][:, g * 128:(g + 1) * 128], in0=psv[:, g * 128:(g + 1) * 128], scalar=bsb[:, g:g + 1],
                in1=ug[i][:, g * 128:(g + 1) * 128], op0=ALU.add, op1=ALU.mult),
                reads=[psvb, bs_b, ug_b[i]], writes=[oa_b[i]])
        pt, ptb = next_ps()
        ptv = pt[:].bitcast(BF16)
        for g in range(4):
            S.op("pe", lambda e, g=g, i=i, ptv=ptv: e.transpose(
                out=ptv[:, g * 128:(g + 1) * 128], in_=oa[i][:, g * 128:(g + 1) * 128], identity=identb[:]),
                reads=[oa_b[i], ident_b], writes=[ptb])
        o, ob = next_os()
        evac(o[:], ob, ptv[:, 0:512], ptb)
        outs.append(S.dma("sp", o_mixA[:, :, tt * 128:(tt + 1) * 128].rearrange("g p t -> p g t"),
                          o[:].rearrange("p (g t) -> p g t", g=4), reads=[ob]))

    fm_group(1024, 0, 4)
    fm_group(1536, 4, 4)
    tm_group(2048, 512, 0)
    fm_group(2560, 8, 4)
    fm_group(3072, 12, 4)
    fm_group(3584, 16, 4)
    fm_group(4096, 20, 2)
    fm_group(4608, 22, 2)
    tm_group(4352, 256, 512)
    tm_group(4864, 256, 768)
    Wt, Wb = load_W(5120, 24)
    gst = [S.sb("gst%d" % i, [128, 24], F32) for i in range(2)]; gst_b = [S.buf() for _ in range(2)]
    for tt in range(NTT):
        i = tt % 2
        p, pb = next_ps()
        for kc in range(KC):
            S.op("pe", lambda e, p=p, Wt=Wt, kc=kc, tt=tt: e.matmul(
                p[:, 0:24], lhsT=hT[:, kc, tt * 128:(tt + 1) * 128], rhs=Wt[:, kc, 0:24],
                start=(kc == 0), stop=(kc == KC - 1)),
                reads=[Wb, hT_b[tt]], writes=[pb])
        S.op("act", lambda e, i=i, p=p: e.activation(out=gst[i][:], in_=p[:, 0:24], func=AF.Sigmoid),
             reads=[pb], writes=[gst_b[i]])
        outs.append(S.dma("sp", o_gate[tt * 128:(tt + 1) * 128, :], gst[i][:], reads=[gst_b[i]]))
    return outs


def build_A(NT=2048):
    nc = bass.Bass("TRN2", target_bir_lowering=False)
    NTT = NT // 128
    NSB = NT // 512
    KC = D // 128
    dr = lambda n, s, dt, k: nc.dram_tensor(n, list(s), dt, kind=k).ap()
    x = dr("x", [NT, D], F32, "ExternalInput")
    gmix = dr("gmix", [1, D], F32, "ExternalInput")
    w_in = dr("w_in", [D, INW], F32, "ExternalInput")
    ln_g = dr("ln_g", [1, 512], F32, "ExternalInput")
    ln_b = dr("ln_b", [1, 512], F32, "ExternalInput")
    wsT = dr("wsT", [128, 4, 128], F32, "ExternalInput")
    bsT = dr("bsT", [128, 4], F32, "ExternalInput")
    tril = dr("tril", [128, 128], F32, "ExternalInput")
    ident = dr("ident", [128, 128], F32, "ExternalInput")
    o_mixA = dr("o_mixA", [4, 128, NT], BF16, "ExternalOutput")
    o_fm = dr("o_fm", [24, 128, NT], BF16, "ExternalOutput")
    o_tm = dr("o_tm", [NT, 1024], BF16, "ExternalOutput")
    o_gate = dr("o_gate", [NT, 24], F32, "ExternalOutput")

    io = dict(x=x, gmix=gmix, w_in=w_in, ln_g=ln_g, ln_b=ln_b, wsT=wsT, bsT=bsT, tril=tril, ident=ident,
              o_mixA=o_mixA, o_fm=o_fm, o_tm=o_tm, o_gate=o_gate)
    with ExitStack() as es:
        S = Sched(nc, es)
        outs = emit_A(S, make_psum(S), io, NT)
        S.emit(final_waits=outs)
    return nc


HD = 128
BIG = 30000.0
SCALE = 128 ** -0.5
OFF0 = 384
DEBUG = False


def strip_const(mmin, fn):
    width = 384 - 128 * mmin + 512
    kl = np.arange(128)[:, None]
    v = np.arange(width)[None, :]
    return fn(v - OFF0 - kl).astype(np.float32)


def dil_fn(d):
    c = ((d >= 0) & (d <= 128)).astype(np.float64) + ((d >= 0) & (d % 4 == 0) & (d <= 512)) + ((d >= 0) & (d % 16 == 0) & (d <= 2048))
    out = np.full(d.shape, -BIG)
    m = c > 0
    out[m] = np.log(c[m]) / SCALE
    return out


def win_fn(d):
    return np.where((d >= 0) & (d <= 511), 0.0, -BIG)


def caus_fn(d):
    return np.where(d >= 0, 0.0, -BIG)


def consts_B(S):
    nqt = S // 128
    c = {}
    c["ident"] = np.eye(128, dtype=np.float32)
    c["dstrip"] = strip_const(-16, dil_fn)
    c["wstrip"] = strip_const(-4, win_fn)
    c["cstrip"] = strip_const(0, caus_fn)
    n = np.arange(256)
    t = np.arange(S)
    valid = (16 * n[:, None] + 31) <= t[None, :]
    cb = np.where(valid, 0.0, -BIG).astype(np.float32).reshape(2, 128, S // 512, 512).transpose(2, 1, 0, 3)
    c["cbias"] = np.ascontiguousarray(cb)
    n_sel = S // 64
    cs = n * 16
    ss = np.arange(64) * 64
    ov = ((cs[:, None] <= ss[None, :] + 63) & (cs[:, None] + 31 >= ss[None, :])).astype(np.float32)
    ov[255] = 0
    ov[:, n_sel:] = 0
    c["ov"] = np.ascontiguousarray(ov.reshape(2, 128, 64).transpose(1, 0, 2))
    sm = np.zeros((nqt, 128, 2, 64), np.float32)
    for gq in range(nqt):
        tt = 128 * gq + np.arange(128)
        jt = tt // 64
        s = np.arange(64)[None, :]
        valid_s = s <= jt[:, None]
        add = np.where(valid_s, 0.0, -1.0)
        vm = valid_s.astype(np.float32)
        for val, cond in ((1e4, s == 0), (2e4, s == jt[:, None]), (3e4, s == jt[:, None] - 1)):
            cond = np.broadcast_to(cond, add.shape)
            add = np.where(cond, val, add)
            vm = np.where(cond, 0.0, vm)
        sm[gq, :, 0] = vm
        sm[gq, :, 1] = add
    c["selmask"] = sm
    E = np.zeros((64, S // 128, 128), np.float32)
    for kt in range(S // 128):
        E[2 * kt, kt, :64] = 1
        E[2 * kt + 1, kt, 64:] = 1
    c["esel"] = E
    return c


class DirectLoaderB:
    def __init__(self, S_, io):
        self.S = S_
        self.io = io

    def fm(self, tile, b, kind, i):
        src = self.io[kind]
        ap = src[i, :, :] if kind in ("Bq", "Bk", "Cq") else src[:, :]
        self.S.dma("sp", tile[:], ap, writes=[b])

    def tm(self, tile, b, kind, i):
        src = self.io[kind]
        ap = src[:, i * 128:(i + 1) * 128] if kind == "Bv" else src
        self.S.dma("sp", tile[:, :, 0:128], ap.rearrange("(k p) d -> p k d", p=128), writes=[b])

    def gates(self, gt, b, qs):
        self.S.dma("sp", gt[:], self.io["gates"][qs * 512:(qs + 1) * 512, :].rearrange("(j p) c -> p j c", p=128), writes=[b])


def emit_B(S_, PSP, io, S, ld=None):
    NQS = S // 512
    NKT = S // 128
    WD = 384 + 128 * 16 + 512
    WW = 384 + 128 * 4 + 512
    WC = 384 + 512
    if ld is None:
        ld = DirectLoaderB(S_, io)
    (posT, ck_w1, ck_w2, cv_w1, cv_w2, ident, dstrip, wstrip, cstrip, cbias, ovd, selmask, esel, mixT) = (io[k] for k in (
        "posT", "ck_w1", "ck_w2", "cv_w1", "cv_w2", "ident", "dstrip", "wstrip", "cstrip", "cbias", "ov", "selmask", "esel", "mixT"))
    dbg = io.get("dbg")
    sb, buf, op, dma = S_.sb, S_.buf, S_.op, S_.dma
    BB = [sb("BB%d" % i, [128, S], BF16) for i in range(6)]; BB_b = [buf() for _ in range(6)]
    VA = [sb("VA%d" % i, [128, NKT, 129], BF16) for i in range(2)]; VA_b = [buf() for _ in range(2)]
    identb = sb("identb", [128, 128], BF16); identf = sb("identf", [128, 128], F32); id_b = buf()
    dst = sb("dst", [128, WD], BF16); wst = sb("wst", [128, WW], BF16); cst = sb("cst", [128, WC], BF16); strip_b = buf()
    eselb = sb("eselb", [64, NKT, 128], BF16); esel_b = buf()
    PT = [sb("PT%d" % i, [128, 512], BF16) for i in range(3)]; PT_b = [buf() for _ in range(3)]
    accS = sb("accS", [128, 4, 4, 128], F32); acc_b = [[buf() for _ in range(4)] for _ in range(4)]
    accB = sb("accB", [128, 4, 128], BF16); accB_b = buf()
    osb = [sb("osb%d" % i, [128, 512], BF16) for i in range(2)]; osb_b = [buf() for _ in range(2)]
    sm = [sb("sm%d" % i, [128, 16], F32) for i in range(4)]; sm_b = [buf() for _ in range(4)]
    gt = sb("gt", [128, 4, 12], F32); gt_b = buf()
    PS, PS_b = PSP
    cnt = {"st": 0, "acc": 0, "pt": 0, "sm": 0, "os": 0}

    def st_ps():
        i = cnt["st"] % 3; cnt["st"] += 1
        return PS[i], PS_b[i]

    def acc_ps():
        i = 3 + cnt["acc"] % 4; cnt["acc"] += 1
        return PS[i], PS_b[i]

    MISC, MISC_b = PS[7], PS_b[7]

    def next_pt():
        i = cnt["pt"] % 3; cnt["pt"] += 1
        return PT[i], PT_b[i]

    def next_sm():
        i = cnt["sm"] % 4; cnt["sm"] += 1
        return sm[i], sm_b[i]

    CQ = "sp" if io.get("bf16_consts") else "pool"
    dma("sp", identf[:], ident[:, :], writes=[id_b])
    op("dve", lambda e: e.tensor_copy(out=identb[:], in_=identf[:]), reads=[id_b], writes=[id_b])
    dma(CQ, dst[:], dstrip[:, :], writes=[strip_b])
    dma(CQ, wst[:], wstrip[:, :], writes=[strip_b])
    dma(CQ, cst[:], cstrip[:, :], writes=[strip_b])
    dma(CQ, eselb[:], esel[:, :, :], writes=[esel_b])
    for i in range(2):
        op("dve", lambda e, i=i: e.memset(VA[i][:, :, 128:129], 1.0), writes=[VA_b[i]])

    outs = []

    def attend(qT, q_b, qs, kT, k_b, Vt, V_b, ktiles, strip, extra_bias=None, ncols=129, rhs_fn=None, band=None):
        banks = [acc_ps(), acc_ps()]
        accs = [(banks[j // 2][0][:, (j % 2) * 256:(j % 2) * 256 + ncols], banks[j // 2][1]) for j in range(4)]
        first = [True] * 4
        last_kt = {}
        def live(kt, j):
            d = 4 * qs + j - kt
            return d >= 0 and (band is None or d <= band)

        for kt in ktiles:
            for j in range(4):
                if live(kt, j):
                    last_kt[j] = kt
        def stage1(kt):
            m = kt - 4 * qs
            p, pb = st_ps()
            nb = (1 if strip is not None and strip(m) is not None else 0) + (1 if extra_bias else 0)
            op("pe", lambda e, p=p, kt=kt: e.matmul(p[:], lhsT=kT[:, kt * 128:(kt + 1) * 128],
                                                   rhs=qT[:, qs * 512:(qs + 1) * 512], start=True, stop=(nb == 0)),
               reads=[k_b, q_b], writes=[pb])
            k = 0
            if extra_bias:
                k += 1
                l_ap, r_ap, rb = extra_bias(kt)
                op("pe", lambda e, p=p, l_ap=l_ap, r_ap=r_ap, k=k: e.matmul(p[:], lhsT=l_ap, rhs=r_ap, start=False, stop=(k == nb)),
                   reads=rb, writes=[pb])
            if strip is not None and strip(m) is not None:
                k += 1
                s_ap = strip(m)
                op("pe", lambda e, p=p, s_ap=s_ap: e.matmul(p[:], lhsT=identb[:], rhs=s_ap, start=False, stop=True),
                   reads=[id_b, strip_b], writes=[pb])
            pt, ptb = next_pt()
            op("act", lambda e, p=p, pt=pt: e.activation(out=pt[:], in_=p[:], func=AF.Exp, scale=SCALE),
               reads=[pb], writes=[ptb])
            return pt, ptb

        def stage2(kt, pt, ptb):
            for j in range(4):
                if not live(kt, j):
                    continue
                a, ab = accs[j]
                rhs = Vt[:, kt, 0:ncols] if rhs_fn is None else rhs_fn(kt)
                op("pe", lambda e, a=a, pt=pt, j=j, rhs=rhs, st=(first[j] and j % 2 == 0), sp=(last_kt[j] == kt): e.matmul(
                    a, lhsT=pt[:, j * 128:(j + 1) * 128], rhs=rhs, start=st, stop=sp, skip_group_check=True),
                    reads=[ptb, V_b], writes=[ab])
                first[j] = False

        kts = list(ktiles)
        cur = stage1(kts[0])
        for i, kt in enumerate(kts):
            nxt = stage1(kts[i + 1]) if i + 1 < len(kts) else None
            stage2(kt, *cur)
            cur = nxt
        return accs

    def coef_of(a, ab, ncols, gate_ap=None, gate_b=None):
        s, sbb = next_sm()
        op("dve", lambda e: e.tensor_scalar(out=s[:, 0:1], in0=a[:, ncols - 1:ncols], scalar1=1e-30, scalar2=None, op0=ALU.max),
           reads=[ab], writes=[sbb])
        op("dve", lambda e: e.reciprocal(out=s[:, 1:2], in_=s[:, 0:1]), reads=[sbb], writes=[sbb])
        if gate_ap is not None:
            op("dve", lambda e: e.tensor_tensor(out=s[:, 1:2], in0=s[:, 1:2], in1=gate_ap, op=ALU.mult),
               reads=[sbb, gate_b], writes=[sbb])
        return s[:, 1:2], sbb

    def flush_head(head_out, qs, r):
        for j in range(4):
            op("act", lambda e, j=j: e.activation(out=accB[:, j, :], in_=accS[:, r, j, :], func=AF.Copy),
               reads=[acc_b[r][j]], writes=[accB_b])
        pv = MISC[:].bitcast(BF16)
        for j in range(4):
            op("pe", lambda e, j=j: e.transpose(out=pv[:, j * 128:(j + 1) * 128], in_=accB[:, j, :], identity=identb[:]),
               reads=[accB_b, id_b], writes=[MISC_b])
        i = cnt["os"] % 2; cnt["os"] += 1
        op("dve", lambda e, i=i: e.tensor_copy(out=osb[i][:], in_=pv[:, 0:512]), reads=[MISC_b], writes=[osb_b[i]])
        outs.append(dma("sp", mixT[head_out, :, qs * 512:(qs + 1) * 512], osb[i][:], reads=[osb_b[i]]))

    def dstrip_of(m):
        return dst[:, OFF0 - 128 * m:OFF0 - 128 * m + 512]

    for h in range(2):
        qT, q_b = BB[2 * h], BB_b[2 * h]
        kT, k_b = BB[2 * h + 1], BB_b[2 * h + 1]
        Vt, V_b = VA[h], VA_b[h]
        ld.fm(qT, q_b, "Bq", h)
        ld.fm(kT, k_b, "Bk", h)
        ld.tm(Vt, V_b, "Bv", h)
        for qs in range(NQS):
            ktiles = [kt for kt in range(4 * qs - 16, 4 * qs + 4) if kt >= 0]
            accs = attend(qT, q_b, qs, kT, k_b, Vt, V_b, ktiles, dstrip_of, band=16)
            for j in range(4):
                a, ab = accs[j]
                cf, cfb = coef_of(a, ab, 129)
                op("dve", lambda e, a=a, cf=cf, j=j: e.tensor_scalar(out=accS[:, 0, j, :], in0=a[:, 0:128], scalar1=cf,
                                                                   scalar2=None, op0=ALU.mult),
                   reads=[ab, cfb], writes=[acc_b[0][j]])
            flush_head(h, qs, 0)

    w1 = sb("w1", [128, 32, 128], BF16); w1_b = buf()
    w2 = sb("w2", [128, 128], BF16); w2_b = buf()
    posb = sb("posb", [128, 32], BF16); pos_b = buf()
    cvec = sb("cvec", [128, 1], F32); cvec_b = buf()
    hc = sb("hc", [128, 256], BF16); hc_b = buf()
    kcT = sb("kcT", [128, 256], BF16); kcT_b = buf()
    Rc = sb("Rc", [128, 2, 193], BF16); Rc_b = buf()
    ovf = sb("ovf", [128, 2, 64], F32); ovf_b = buf()
    posf = sb("posf", [128, 32], F32); w1f = sb("w1f", [128, 32, 128], F32); w2f = sb("w2f", [128, 128], F32); wf_b = buf()
    dma("sp", posf[:], posT[:, :], writes=[wf_b])
    op("dve", lambda e: e.tensor_copy(out=posb[:], in_=posf[:]), reads=[wf_b], writes=[pos_b])
    dma("sp", ovf[:], ovd[:, :, :], writes=[ovf_b])
    op("dve", lambda e: e.memset(Rc[:, :, 192:193], 1.0), writes=[Rc_b])
    op("dve", lambda e: e.tensor_copy(out=Rc[:, :, 128:192], in_=ovf[:]), reads=[ovf_b], writes=[Rc_b])
    ld.fm(BB[4], BB_b[4], "cmpk", 0)
    ld.fm(BB[5], BB_b[5], "cmpv", 0)
    for which in range(2):
        raw, raw_b = BB[4 + which], BB_b[4 + which]
        w1d, w2d = (ck_w1, ck_w2) if which == 0 else (cv_w1, cv_w2)
        dma("sp", w1f[:], w1d.rearrange("(i d) f -> d i f", d=128), writes=[wf_b])
        dma("sp", w2f[:], w2d[:, :], writes=[wf_b])
        op("act", lambda e: e.activation(out=w1[:], in_=w1f[:], func=AF.Copy), reads=[wf_b], writes=[w1_b])
        op("dve", lambda e: e.tensor_copy(out=w2[:], in_=w2f[:]), reads=[wf_b], writes=[w2_b])
        rv = raw[:].rearrange("p (n s) -> p n s", s=16)
        nblk = S // 16 - 1
        for i in range(32):
            op("pe", lambda e, i=i: e.matmul(MISC[:, 0:1], lhsT=w1[:, i, :], rhs=posb[:, i:i + 1], start=(i == 0), stop=(i == 31)),
               reads=[w1_b, pos_b], writes=[MISC_b])
        op("dve", lambda e: e.tensor_copy(out=cvec[:], in_=MISC[:, 0:1]), reads=[MISC_b], writes=[cvec_b])
        p, pb = st_ps()
        for i in range(32):
            rhs = rv[:, 0:nblk, i] if i < 16 else rv[:, 1:nblk + 1, i - 16]
            op("pe", lambda e, p=p, i=i, rhs=rhs: e.matmul(p[:, 0:nblk], lhsT=w1[:, i, :], rhs=rhs, start=(i == 0), stop=(i == 31)),
               reads=[w1_b, raw_b], writes=[pb])
        op("dve", lambda e: e.memset(hc[:], 0.0), writes=[hc_b])
        op("act", lambda e, p=p: e.activation(out=hc[:, 0:nblk], in_=p[:, 0:nblk], func=AF.Gelu_apprx_tanh, bias=cvec[:]),
           reads=[pb, cvec_b], writes=[hc_b])
        if which == 0:
            p2, p2b = st_ps()
            op("pe", lambda e, p2=p2: e.matmul(p2[:, 0:256], lhsT=w2[:], rhs=hc[:], start=True, stop=True),
               reads=[w2_b, hc_b], writes=[p2b])
            op("dve", lambda e, p2=p2: e.tensor_copy(out=kcT[:], in_=p2[:, 0:256]), reads=[p2b], writes=[kcT_b])
        else:
            for c in range(2):
                p2, p2b = st_ps()
                op("pe", lambda e, p2=p2, c=c: e.matmul(p2[:, 0:128], lhsT=hc[:, c * 128:(c + 1) * 128], rhs=w2[:], start=True, stop=True),
                   reads=[w2_b, hc_b], writes=[p2b])
                op("dve", lambda e, p2=p2, c=c: e.tensor_copy(out=Rc[:, c, 0:128], in_=p2[:, 0:128]), reads=[p2b], writes=[Rc_b])

    if DEBUG:
        outs.append(dma("sp", dbg[:, 0:256], kcT[:], reads=[kcT_b]))
        outs.append(dma("sp", dbg[:, 256:512], hc[:], reads=[hc_b]))
        outs.append(dma("sp", dbg[:, 512:768], Rc[:, 0, 0:128], reads=[Rc_b])) if False else None
    for r in range(4):
        ld.fm(BB[r], BB_b[r], "Cq", r)
    ld.fm(BB[4], BB_b[4], "slck", 0)
    ld.fm(BB[5], BB_b[5], "wink", 0)
    ld.tm(VA[0], VA_b[0], "slcv", 0)
    ld.tm(VA[1], VA_b[1], "winv", 0)
    if io.get("after_loads"):
        io["after_loads"](BB_b + VA_b)
    cbs = sb("cbs", [128, 2, 512], BF16); cbs_b = buf()
    smk = sb("smk", [128, 4, 2, 64], F32); smk_b = buf()
    imp = sb("imp", [128, 4, 64], F32); imp_b = [buf() for _ in range(4)]
    sc = sb("sc", [128, 64], F32); sc2 = sb("sc2", [128, 64], F32); sc_b = buf()
    mx8 = sb("mx8", [128, 16], F32); mx8_b = buf()
    selb = sb("selb", [128, 64], F32); selb_b = buf()
    selbT = sb("selbT", [64, 512], BF16); selbT_b = buf()

    def wstrip_of(m):
        return wst[:, OFF0 - 128 * m:OFF0 - 128 * m + 512]

    def cstrip_of(m):
        if m < 0:
            return None
        return cst[:, OFF0 - 128 * m:OFF0 - 128 * m + 512]

    for qs in range(NQS):
        ld.gates(gt, gt_b, qs)
        dma(CQ, cbs[:], cbias[qs, :, :, :], writes=[cbs_b])
        dma("sp", smk[:], selmask[4 * qs:4 * qs + 4, :, :, :].rearrange("j p a s -> p j a s"), writes=[smk_b])
        nchunks = [c for c in range(2) if (c * 2048 + 31) <= (qs * 512 + 511)]
        for r in range(4):
            qT, q_b = BB[r], BB_b[r]
            banks = [acc_ps(), acc_ps()]
            accs = [(banks[j // 2][0][:, (j % 2) * 256:(j % 2) * 256 + 193], banks[j // 2][1]) for j in range(4)]
            for ci, c in enumerate(nchunks):
                p, pb = st_ps()
                op("pe", lambda e, p=p, c=c, qT=qT, qs=qs: e.matmul(p[:], lhsT=kcT[:, c * 128:(c + 1) * 128], rhs=qT[:, qs * 512:(qs + 1) * 512],
                                                            start=True, stop=False), reads=[kcT_b, q_b], writes=[pb])
                op("pe", lambda e, p=p, c=c, cbs=cbs: e.matmul(p[:], lhsT=identb[:], rhs=cbs[:, c, :], start=False, stop=True),
                   reads=[id_b, cbs_b], writes=[pb])
                pt, ptb = next_pt()
                op("act", lambda e, p=p, pt=pt: e.activation(out=pt[:], in_=p[:], func=AF.Exp, scale=SCALE), reads=[pb], writes=[ptb])
                for j in range(4):
                    a, ab = accs[j]
                    op("pe", lambda e, a=a, pt=pt, j=j, c=c, ci=ci: e.matmul(a, lhsT=pt[:, j * 128:(j + 1) * 128], rhs=Rc[:, c, :],
                                                                           start=(ci == 0 and j % 2 == 0), stop=(ci == len(nchunks) - 1),
                                                                           skip_group_check=True),
                       reads=[ptb, Rc_b], writes=[ab])
            for j in range(4):
                a, ab = accs[j]
                s, sbb = next_sm()
                op("dve", lambda e, a=a, s=s: e.tensor_scalar(out=s[:, 0:1], in0=a[:, 192:193], scalar1=1e-30, scalar2=None, op0=ALU.max),
                   reads=[ab], writes=[sbb])
                op("dve", lambda e, s=s: e.reciprocal(out=s[:, 1:2], in_=s[:, 0:1]), reads=[sbb], writes=[sbb])
                op("dve", lambda e, s=s, j=j, r=r, gt=gt: e.tensor_tensor(out=s[:, 2:3], in0=s[:, 1:2], in1=gt[:, j, r * 3:r * 3 + 1], op=ALU.mult),
                   reads=[sbb, gt_b], writes=[sbb])
                op("dve", lambda e, a=a, s=s, j=j, r=r: e.tensor_scalar(out=accS[:, r, j, :], in0=a[:, 0:128], scalar1=s[:, 2:3],
                                                                      scalar2=None, op0=ALU.mult),
                   reads=[ab, sbb], writes=[acc_b[r][j]])
                if r == 0:
                    op("dve", lambda e, a=a, s=s, j=j: e.tensor_scalar(out=imp[:, j, :], in0=a[:, 128:192], scalar1=s[:, 1:2],
                                                                     scalar2=None, op0=ALU.mult),
                       reads=[ab, sbb], writes=[imp_b[j]])
                else:
                    op("dve", lambda e, a=a, s=s, j=j: e.scalar_tensor_tensor(out=imp[:, j, :], in0=a[:, 128:192], scalar=s[:, 1:2],
                                                                            in1=imp[:, j, :], op0=ALU.mult, op1=ALU.add),
                       reads=[ab, sbb, imp_b[j]], writes=[imp_b[j]])
        need_sel = (qs * 512 + 511) >= 1024
        for j in (range(4) if need_sel else ()):
            op("dve", lambda e, j=j, smk=smk: e.tensor_tensor(out=sc[:], in0=imp[:, j, :], in1=smk[:, j, 0, :], op=ALU.mult),
               reads=[imp_b[j], smk_b], writes=[sc_b])
            op("dve", lambda e, j=j, smk=smk: e.tensor_tensor(out=sc[:], in0=sc[:], in1=smk[:, j, 1, :], op=ALU.add),
               reads=[sc_b, smk_b], writes=[sc_b])
            op("dve", lambda e: e.max(out=mx8[:, 0:8], in_=sc[:]), reads=[sc_b], writes=[mx8_b])
            op("dve", lambda e: e.match_replace(out=sc2[:], in_to_replace=mx8[:, 0:8], in_values=sc[:], imm_value=-1e9),
               reads=[sc_b, mx8_b], writes=[sc_b])
            op("dve", lambda e: e.max(out=mx8[:, 8:16], in_=sc2[:]), reads=[sc_b], writes=[mx8_b])
            op("dve", lambda e: e.tensor_scalar(out=selb[:], in0=sc[:], scalar1=mx8[:, 15:16], scalar2=1.0, op0=ALU.is_ge, op1=ALU.subtract),
               reads=[sc_b, mx8_b], writes=[selb_b])
            op("pe", lambda e: e.transpose(out=MISC[0:64, 0:128], in_=selb[:], identity=identf[:]),
               reads=[selb_b, id_b], writes=[MISC_b])
            op("act", lambda e, j=j: e.activation(out=selbT[:, j * 128:(j + 1) * 128], in_=MISC[0:64, 0:128], func=AF.Copy, scale=BIG),
               reads=[MISC_b], writes=[selbT_b])
        for r in range(4):
            qT, q_b = BB[r], BB_b[r]
            accs = attend(qT, q_b, qs, BB[4], BB_b[4], VA[0], VA_b[0], list(range(0, 4 * qs + 4)), cstrip_of,
                          extra_bias=(lambda kt: (eselb[:, kt, :], selbT[:], [esel_b, selbT_b])) if need_sel else None)
            for j in range(4):
                a, ab = accs[j]
                cf, cfb = coef_of(a, ab, 129, gt[:, j, r * 3 + 1:r * 3 + 2], gt_b)
                op("dve", lambda e, a=a, cf=cf, j=j, r=r: e.scalar_tensor_tensor(out=accS[:, r, j, :], in0=a[:, 0:128], scalar=cf,
                                                                               in1=accS[:, r, j, :], op0=ALU.mult, op1=ALU.add),
                   reads=[ab, cfb, acc_b[r][j]], writes=[acc_b[r][j]])
            ktiles = [kt for kt in range(4 * qs - 4, 4 * qs + 4) if kt >= 0]
            accs = attend(qT, q_b, qs, BB[5], BB_b[5], VA[1], VA_b[1], ktiles, wstrip_of, band=4)
            for j in range(4):
                a, ab = accs[j]
                cf, cfb = coef_of(a, ab, 129, gt[:, j, r * 3 + 2:r * 3 + 3], gt_b)
                op("dve", lambda e, a=a, cf=cf, j=j, r=r: e.scalar_tensor_tensor(out=accS[:, r, j, :], in0=a[:, 0:128], scalar=cf,
                                                                               in1=accS[:, r, j, :], op0=ALU.mult, op1=ALU.add),
                   reads=[ab, cfb, acc_b[r][j]], writes=[acc_b[r][j]])
            flush_head(2 + r, qs, r)
    return outs


def build_B(S=4096):
    nc = bass.Bass("TRN2", target_bir_lowering=False)
    NQS = S // 512
    NKT = S // 128
    dr = lambda n, s, dt, k: nc.dram_tensor(n, list(s), dt, kind=k).ap()
    Bq = dr("Bq", [2, 128, S], BF16, "ExternalInput")
    Bk = dr("Bk", [2, 128, S], BF16, "ExternalInput")
    Bv = dr("Bv", [S, 256], BF16, "ExternalInput")
    Cq = dr("Cq", [4, 128, S], BF16, "ExternalInput")
    cmpk = dr("cmpk", [128, S], BF16, "ExternalInput")
    cmpv = dr("cmpv", [128, S], BF16, "ExternalInput")
    slck = dr("slck", [128, S], BF16, "ExternalInput")
    slcv = dr("slcv", [S, 128], BF16, "ExternalInput")
    wink = dr("wink", [128, S], BF16, "ExternalInput")
    winv = dr("winv", [S, 128], BF16, "ExternalInput")
    gates = dr("gates", [S, 12], F32, "ExternalInput")
    posT = dr("posT", [128, 32], F32, "ExternalInput")
    ck_w1 = dr("ck_w1", [4096, 128], F32, "ExternalInput")
    ck_w2 = dr("ck_w2", [128, 128], F32, "ExternalInput")
    cv_w1 = dr("cv_w1", [4096, 128], F32, "ExternalInput")
    cv_w2 = dr("cv_w2", [128, 128], F32, "ExternalInput")
    WD = 384 + 128 * 16 + 512
    WW = 384 + 128 * 4 + 512
    WC = 384 + 512
    ident = dr("ident", [128, 128], F32, "ExternalInput")
    dstrip = dr("dstrip", [128, WD], F32, "ExternalInput")
    wstrip = dr("wstrip", [128, WW], F32, "ExternalInput")
    cstrip = dr("cstrip", [128, WC], F32, "ExternalInput")
    cbias = dr("cbias", [NQS, 128, 2, 512], F32, "ExternalInput")
    ovd = dr("ov", [128, 2, 64], F32, "ExternalInput")
    selmask = dr("selmask", [NKT, 128, 2, 64], F32, "ExternalInput")
    esel = dr("esel", [64, NKT, 128], F32, "ExternalInput")
    mixT = dr("mixT", [6, 128, S], BF16, "ExternalOutput")
    dbg = dr("dbg", [128, 1024], BF16, "ExternalOutput") if DEBUG else None

    io = dict(Bq=Bq, Bk=Bk, Bv=Bv, Cq=Cq, cmpk=cmpk, cmpv=cmpv, slck=slck, slcv=slcv, wink=wink, winv=winv, gates=gates,
              posT=posT, ck_w1=ck_w1, ck_w2=ck_w2, cv_w1=cv_w1, cv_w2=cv_w2, ident=ident, dstrip=dstrip, wstrip=wstrip,
              cstrip=cstrip, cbias=cbias, ov=ovd, selmask=selmask, esel=esel, mixT=mixT, dbg=dbg)
    with ExitStack() as es:
        S_ = Sched(nc, es)
        outs = emit_B(S_, make_psum(S_), io, S)
        S_.emit(final_waits=outs)
    return nc


D = 2048
EPS = 1e-6
DFF = 8192


def emit_C(S, PSP, io, NT, FINAL, mx_loader=None, perm=None):
    NSB = NT // 512
    KC = D // 128
    SCALE = 128 ** -0.5
    if perm is None:
        perm = list(range(16))
    (x, mixT, w_out, g_x, g_mem, mem, wq, wkv, wo, g_mlp, w_up, w_down, g_fin, ident, y) = (io.get(k) for k in (
        "x", "mixT", "w_out", "g_x", "g_mem", "mem", "wq", "wkv", "wo", "g_mlp", "w_up", "w_down", "g_fin", "ident", "y"))
    xs = S.sb("xs", [128, 4, D], F32); xs_b = [S.buf() for _ in range(4)]
    mx = S.sb("mx", [128, KC, 512], BF16); mx_b = S.buf()
    hT = S.sb("hT", [128, KC, 512], BF16); hT_b = [S.buf() for _ in range(4)]
    aT = S.sb("aT", [128, 32, 512], BF16); aT_b = [S.buf() for _ in range(32)]
    W = [S.sb("W%d" % i, [128, KC, 512], BF16) for i in range(2)]; W_b = [S.buf() for _ in range(2)]
    gB = S.sb("gB", [128, D], F32); gB_b = S.buf()
    hb = [S.sb("hb%d" % i, [128, D], BF16) for i in range(2)]; hb_b = [S.buf() for _ in range(2)]
    st = [S.sb("st%d" % i, [128, 8], F32) for i in range(2)]; st_b = [S.buf() for _ in range(2)]
    identb = S.sb("identb", [128, 128], BF16); ident_b = S.buf()
    ones = S.sb("ones", [128, 128], BF16); ones_b = S.buf()
    qT = S.sb("qT", [128, 4, 512], BF16); qT_b = [S.buf() for _ in range(4)]
    kT = S.sb("kT", [128, 4, 256], BF16); kT_b = S.buf()
    Vm = S.sb("Vm", [128, 2, 512], BF16); Vm_b = S.buf()
    PT = [S.sb("PT%d" % i, [128, 2, 512], BF16) for i in range(2)]; PT_b = [S.buf() for _ in range(2)]
    oT = S.sb("oT", [128, 4, 512], BF16); oT_b = [S.buf() for _ in range(4)]
    rden0 = S.sb("rden0", [128, 512], F32); rden = [rden0, rden0]; rden0_b = S.buf(); rden_b = [rden0_b, rden0_b]
    rtmp = [S.sb("rtmp%d" % i, [128, 512], F32) for i in range(2)]; rtmp_b = [S.buf() for _ in range(2)]
    PS, PS_b = PSP
    psn = [0]

    def next_ps():
        i = psn[0] % 8
        psn[0] += 1
        return PS[i], PS_b[i]

    wn = [0]

    def load_W(view, K, width):
        i = wn[0] % 2
        wn[0] += 1
        S.dma(WQ, W[i][:, 0:K, 0:width], view, writes=[W_b[i]])
        return W[i], W_b[i]

    WQ = io.get("wqueue", "pool")
    wsrc = dict(w_out=w_out, wq=wq, wkv=wkv, wo=wo, w_up=w_up, w_down=w_down)

    def wview(name, tr, tc):
        if "wtile" in io:
            return io["wtile"](name, tr, tc)
        w = wsrc[name]
        kr = min(16, w.shape[0] // 128)
        return w[tr * 2048:tr * 2048 + kr * 128, tc * 512:(tc + 1) * 512].rearrange("(k p) c -> p k c", p=128)

    evn = [0]

    def copy_any(dst, src, reads, writes):
        e = "act" if evn[0] % 2 == 0 else "dve"
        evn[0] += 1
        if e == "act":
            S.op("act", lambda en: en.activation(out=dst, in_=src, func=AF.Copy), reads=reads, writes=writes)
        else:
            S.op("dve", lambda en: en.tensor_copy(out=dst, in_=src), reads=reads, writes=writes)

    rn = [0]

    def rms_T(src, src_b, g_ap, dstT, dst_b, c0, to_out=None):
        i = rn[0] % 2
        rn[0] += 1
        S.op("act", lambda e: e.activation(out=hb[i][:], in_=src, func=AF.Square, accum_out=st[i][:, 0:1]),
             reads=[src_b], writes=[hb_b[i], st_b[i]])
        S.op("dve", lambda e: e.tensor_scalar(out=st[i][:, 1:2], in0=st[i][:, 0:1], scalar1=1.0 / D, scalar2=EPS,
                                              op0=ALU.mult, op1=ALU.add), reads=[st_b[i]], writes=[st_b[i]])
        S.op("act", lambda e: e.activation(out=st[i][:, 3:4], in_=st[i][:, 1:2], func=AF.Sqrt),
             reads=[st_b[i]], writes=[st_b[i]])
        S.op("dve", lambda e: e.reciprocal(out=st[i][:, 2:3], in_=st[i][:, 3:4]), reads=[st_b[i]], writes=[st_b[i]])
        if to_out is not None:
            S.op("dve", lambda e: e.scalar_tensor_tensor(out=to_out, in0=src, scalar=st[i][:, 2:3], in1=gB[:],
                                                         op0=ALU.mult, op1=ALU.mult),
                 reads=[src_b, st_b[i], gB_b], writes=[src_b])
            return
        S.op("dve", lambda e: e.scalar_tensor_tensor(out=hb[i][:], in0=src, scalar=st[i][:, 2:3], in1=gB[:],
                                                     op0=ALU.mult, op1=ALU.mult),
             reads=[src_b, st_b[i], gB_b], writes=[hb_b[i]])
        for q in range(KC // 4):
            p, pb = next_ps()
            pv = p[:].bitcast(BF16)
            for j in range(4):
                kc = q * 4 + j
                S.op("pe", lambda e, kc=kc, j=j, pv=pv: e.transpose(
                    out=pv[:, j * 128:(j + 1) * 128], in_=hb[i][:, kc * 128:(kc + 1) * 128], identity=identb[:]),
                    reads=[hb_b[i], ident_b], writes=[pb])
            copy_any(dstT[:, q * 4:(q + 1) * 4, c0:c0 + 128], pv[:, 0:512].rearrange("p (a b) -> p a b", a=4),
                     [pb], [dst_b])

    S.dma("pool", identb[:], ident[:, :], writes=[ident_b])
    if io.get("after_setup"):
        io["after_setup"]()
    SQ = "dve" if io.get("no_pool") else "pool"
    S.op("dve", lambda e: e.memset(ones[:], 1.0), writes=[ones_b])

    S.dma("sp", gB[:], g_mem.partition_broadcast(128), writes=[gB_b])
    for mt in range(2):
        S.dma("sp", xs[:, mt, :], mem[mt * 128:(mt + 1) * 128, :], writes=[xs_b[mt]])
        rms_T(xs[:, mt, :], xs_b[mt], None, hT, hT_b[mt], mt * 128)
    Wt, Wb = load_W(wview("wkv", 0, 0), KC, 512)
    for h in range(4):
        p, pb = next_ps()
        for kc in range(KC):
            S.op("pe", lambda e, p=p, Wt=Wt, kc=kc, h=h: e.matmul(
                p[:, 0:256], lhsT=Wt[:, kc, h * 128:(h + 1) * 128], rhs=hT[:, kc, 0:256],
                start=(kc == 0), stop=(kc == KC - 1)), reads=[Wb, hT_b[0], hT_b[1]], writes=[pb])
        copy_any(kT[:, h, :], p[:, 0:256], [pb], [kT_b])
    Wt, Wb = load_W(wview("wkv", 0, 1), KC, 512)
    for mt in range(2):
        p, pb = next_ps()
        for kc in range(KC):
            S.op("pe", lambda e, p=p, Wt=Wt, kc=kc, mt=mt: e.matmul(
                p[:], lhsT=hT[:, kc, mt * 128:(mt + 1) * 128], rhs=Wt[:, kc, :],
                start=(kc == 0), stop=(kc == KC - 1)), reads=[Wb, hT_b[mt]], writes=[pb])
        copy_any(Vm[:, mt, :], p[:], [pb], [Vm_b])

    outs = []
    for blk in range(NSB):
        t0 = blk * 512
        for t in range(4):
            S.dma("sp", xs[:, t, :], x[t0 + t * 128:t0 + (t + 1) * 128, :], writes=[xs_b[t]])
        if mx_loader is None:
            S.dma("sp", mx[:], mixT[:, :, t0:t0 + 512].rearrange("k p t -> p k t"), writes=[mx_b])
        else:
            mx_loader(S, mx, mx_b, t0)
        for cg in range(4):
            Wt, Wb = load_W(wview("w_out", 0, cg), KC, 512)
            for t in range(4):
                p, pb = next_ps()
                for kc in range(KC):
                    S.op("pe", lambda e, p=p, Wt=Wt, kc=kc, t=t: e.matmul(
                        p[:], lhsT=mx[:, kc, t * 128:(t + 1) * 128], rhs=Wt[:, perm[kc], :],
                        start=(kc == 0), stop=(kc == KC - 1)), reads=[Wb, mx_b], writes=[pb])
                S.op("dve", lambda e, p=p, t=t, cg=cg: e.tensor_tensor(
                    out=xs[:, t, cg * 512:(cg + 1) * 512], in0=p[:], in1=xs[:, t, cg * 512:(cg + 1) * 512],
                    op=ALU.add), reads=[pb, xs_b[t]], writes=[xs_b[t]])
        S.dma("sp", gB[:], g_x.partition_broadcast(128), writes=[gB_b])
        for t in range(4):
            rms_T(xs[:, t, :], xs_b[t], None, hT, hT_b[t], t * 128)
        Wt, Wb = load_W(wview("wq", 0, 0), KC, 512)
        for h in range(4):
            p, pb = next_ps()
            for kc in range(KC):
                S.op("pe", lambda e, p=p, Wt=Wt, kc=kc, h=h: e.matmul(
                    p[:], lhsT=Wt[:, kc, h * 128:(h + 1) * 128], rhs=hT[:, kc, :],
                    start=(kc == 0), stop=(kc == KC - 1)), reads=[Wb] + hT_b, writes=[pb])
            copy_any(qT[:, h, :], p[:], [pb], [qT_b[h]])
        for h in range(4):
            i = h % 2
            for mc in range(2):
                p, pb = next_ps()
                S.op("pe", lambda e, p=p, h=h, mc=mc: e.matmul(
                    p[:], lhsT=kT[:, h, mc * 128:(mc + 1) * 128], rhs=qT[:, h, :], start=True, stop=True),
                    reads=[kT_b, qT_b[h]], writes=[pb])
                S.op("act", lambda e, p=p, i=i, mc=mc: e.activation(out=PT[i][:, mc, :], in_=p[:], func=AF.Exp,
                                                                  scale=SCALE), reads=[pb], writes=[PT_b[i]])
            pd, pdb = next_ps()
            po, pob = next_ps()
            for mc in range(2):
                S.op("pe", lambda e, pd=pd, i=i, mc=mc: e.matmul(
                    pd[:], lhsT=ones[:], rhs=PT[i][:, mc, :], start=(mc == 0), stop=(mc == 1)),
                    reads=[ones_b, PT_b[i]], writes=[pdb])
            for mc in range(2):
                S.op("pe", lambda e, po=po, i=i, mc=mc, h=h: e.matmul(
                    po[:], lhsT=Vm[:, mc, h * 128:(h + 1) * 128], rhs=PT[i][:, mc, :], start=(mc == 0), stop=(mc == 1)),
                    reads=[Vm_b, PT_b[i]], writes=[pob])
            S.op("dve", lambda e, pd=pd, i=i: e.reciprocal(out=rden[i][:], in_=pd[:]), reads=[pdb], writes=[rden_b[i]])
            S.op("dve", lambda e, po=po, i=i, h=h: e.tensor_tensor(out=oT[:, h, :], in0=po[:], in1=rden[i][:], op=ALU.mult),
                 reads=[pob, rden_b[i]], writes=[oT_b[h]])
        for cg in range(4):
            Wt, Wb = load_W(wview("wo", 0, cg), 4, 512)
            for t in range(4):
                p, pb = next_ps()
                for h in range(4):
                    S.op("pe", lambda e, p=p, Wt=Wt, h=h, t=t: e.matmul(
                        p[:], lhsT=oT[:, h, t * 128:(t + 1) * 128], rhs=Wt[:, h, :],
                        start=(h == 0), stop=(h == 3)), reads=[Wb] + oT_b, writes=[pb])
                S.op("dve", lambda e, p=p, t=t, cg=cg: e.tensor_tensor(
                    out=xs[:, t, cg * 512:(cg + 1) * 512], in0=p[:], in1=xs[:, t, cg * 512:(cg + 1) * 512],
                    op=ALU.add), reads=[pb, xs_b[t]], writes=[xs_b[t]])
        S.dma("sp", gB[:], g_mlp.partition_broadcast(128), writes=[gB_b])
        for t in range(4):
            rms_T(xs[:, t, :], xs_b[t], None, hT, hT_b[t], t * 128)
        for half in range(2):
            for ug in range(8):
                c0 = half * 4096 + ug * 512
                Wt, Wb = load_W(wview("w_up", 0, c0 // 512), KC, 512)
                for j in range(4):
                    hc = ug * 4 + j
                    p, pb = next_ps()
                    for kc in range(KC):
                        S.op("pe", lambda e, p=p, Wt=Wt, kc=kc, j=j: e.matmul(
                            p[:], lhsT=Wt[:, kc, j * 128:(j + 1) * 128], rhs=hT[:, kc, :],
                            start=(kc == 0), stop=(kc == KC - 1)), reads=[Wb] + hT_b, writes=[pb])
                    i = hc % 2
                    S.op("act", lambda e, p=p, i=i: e.activation(out=rtmp[i][:], in_=p[:], func=AF.Relu),
                         reads=[pb], writes=[rtmp_b[i]])
                    S.op(SQ, lambda e, i=i, hc=hc: e.tensor_tensor(out=aT[:, hc, :], in0=rtmp[i][:], in1=rtmp[i][:],
                                                                   op=ALU.mult), reads=[rtmp_b[i]], writes=[aT_b[hc]])
            for cg in range(4):
                acc = [next_ps() for _ in range(4)]
                for hq in range(2):
                    r0 = half * 4096 + hq * 2048
                    Wt, Wb = load_W(wview("w_down", r0 // 2048, cg), KC, 512)
                    for t in range(4):
                        p, pb = acc[t]
                        for kc in range(KC):
                            hc = hq * 16 + kc
                            S.op("pe", lambda e, p=p, Wt=Wt, kc=kc, hc=hc, t=t: e.matmul(
                                p[:], lhsT=aT[:, hc, t * 128:(t + 1) * 128], rhs=Wt[:, kc, :],
                                start=(hc == 0), stop=(hc == 31)), reads=[Wb, aT_b[hc]], writes=[pb])
                for t in range(4):
                    p, pb = acc[t]
                    S.op("dve", lambda e, p=p, t=t, cg=cg: e.tensor_tensor(
                        out=xs[:, t, cg * 512:(cg + 1) * 512], in0=p[:], in1=xs[:, t, cg * 512:(cg + 1) * 512],
                        op=ALU.add), reads=[pb, xs_b[t]], writes=[xs_b[t]])
        if FINAL:
            S.dma("sp", gB[:], g_fin.partition_broadcast(128), writes=[gB_b])
            for t in range(4):
                rms_T(xs[:, t, :], xs_b[t], None, None, None, 0, to_out=xs[:, t, :])
        for t in range(4):
            outs.append(S.dma("sp", y[t0 + t * 128:t0 + (t + 1) * 128, :], xs[:, t, :], reads=[xs_b[t]]))
    return outs


def build_C(NT=2048, FINAL=False):
    nc = bass.Bass("TRN2", target_bir_lowering=False)
    NSB = NT // 512
    KC = D // 128
    dr = lambda n, s, dt, k: nc.dram_tensor(n, list(s), dt, kind=k).ap()
    x = dr("x", [NT, D], F32, "ExternalInput")
    mixT = dr("mixT", [16, 128, NT], BF16, "ExternalInput")
    w_out = dr("w_out", [D, D], F32, "ExternalInput")
    g_x = dr("g_x", [1, D], F32, "ExternalInput")
    g_mem = dr("g_mem", [1, D], F32, "ExternalInput")
    mem = dr("mem", [256, D], F32, "ExternalInput")
    wq = dr("wq", [D, 512], F32, "ExternalInput")
    wkv = dr("wkv", [D, 1024], F32, "ExternalInput")
    wo = dr("wo", [512, D], F32, "ExternalInput")
    g_mlp = dr("g_mlp", [1, D], F32, "ExternalInput")
    w_up = dr("w_up", [D, DFF], F32, "ExternalInput")
    w_down = dr("w_down", [DFF, D], F32, "ExternalInput")
    g_fin = dr("g_fin", [1, D], F32, "ExternalInput")
    ident = dr("ident", [128, 128], F32, "ExternalInput")
    y = dr("y", [NT, D], F32, "ExternalOutput")
    SCALE = 128 ** -0.5

    io = dict(x=x, mixT=mixT, w_out=w_out, g_x=g_x, g_mem=g_mem, mem=mem, wq=wq, wkv=wkv, wo=wo, g_mlp=g_mlp,
              w_up=w_up, w_down=w_down, g_fin=g_fin, ident=ident, y=y)
    with ExitStack() as es:
        S = Sched(nc, es)
        outs = emit_C(S, make_psum(S), io, NT, FINAL)
        S.emit(final_waits=outs)
    return nc


NLAYER = 4
ARENA = 92160
GROUPS = [[0, 1], [2, 3], [4, 5], [6, 7]]


class FusedLoaderB:
    FM_IDX = {"Bq": (0, 2), "Bk": (4, 2), "Cq": (8, 4), "cmpk": (16, 1), "cmpv": (18, 1), "slck": (20, 1), "wink": (22, 1)}
    TM_COL = {"Bv": (0, 256), "slcv": (512, 128), "winv": (768, 128)}

    def __init__(self, S_, exA_dst, gate_dst, sel, S):
        self.S = S_
        self.dst = exA_dst
        self.gdst = gate_dst
        self.selt = S_.sb("selt", [128, 2], F32); self.sel_b = S_.buf()
        S_.dma("sp", self.selt[:], sel[:, :], writes=[self.sel_b])
        self.T = [S_.sb("Tfm%d" % i, [128, S], BF16) for i in range(2)]; self.T_b = [S_.buf() for _ in range(2)]
        self.Tg = S_.sb("Tg", [128, 4, 12], F32); self.Tg_b = S_.buf()
        self.n = 0
        self.half = S // 2

    def _blend(self, dst_ap, dst_b, t_ap, t_b):
        S_ = self.S
        S_.op("dve", lambda e: e.tensor_scalar(out=t_ap, in0=t_ap, scalar1=self.selt[:, 1:2], scalar2=None, op0=ALU.mult),
              reads=[t_b, self.sel_b], writes=[t_b])
        S_.op("dve", lambda e: e.scalar_tensor_tensor(out=dst_ap, in0=dst_ap, scalar=self.selt[:, 0:1], in1=t_ap,
                                                      op0=ALU.mult, op1=ALU.add),
              reads=[dst_b, t_b, self.sel_b], writes=[dst_b])

    def fm(self, tile, b, kind, i):
        base, stride = self.FM_IDX[kind]
        ii = self.n % 2
        self.n += 1
        T, Tb = self.T[ii], self.T_b[ii]
        H = self.half
        for r in range(2):
            for g, (dstt, dstb) in enumerate(((tile, b), (T, Tb))):
                R = (base + g * stride + i) * 128
                row = ((R // 512) * 2 + r) * 512 + (R % 512)
                self.S.dma("sp", dstt[:, r * H:(r + 1) * H], self.dst[row:row + 128, :], writes=[dstb])
        self._blend(tile[:], b, T[:], Tb)

    def tm(self, tile, b, kind, i):
        base, gstride = self.TM_COL[kind]
        ii = self.n % 2
        self.n += 1
        T, Tb = self.T[ii], self.T_b[ii]
        Tv = T[:].rearrange("p (k d) -> p k d", d=128)
        nk = self.half // 128
        hk = nk // 2
        for r in range(2):
            for piece in range(2):
                row0 = ((6 + piece) * 2 + r) * 512
                v = self.dst[row0:row0 + 512, :].rearrange("r (two c) -> (r two) c", two=2)
                k0 = r * nk + piece * hk
                for g in range(2):
                    c0 = base + g * gstride + i * 128
                    src = v[:, c0:c0 + 128].rearrange("(k p) d -> p k d", p=128)
                    if g == 0:
                        self.S.dma("sp", tile[:, k0:k0 + hk, 0:128], src, writes=[b])
                    else:
                        self.S.dma("sp", Tv[:, k0:k0 + hk, :], src, writes=[Tb])
        self._blend(tile[:, :, 0:128], b, Tv, Tb)

    def gates(self, gt, b, qs):
        nq = self.half // 512
        r, ql = qs // nq, qs % nq
        for g, (dstt, dstb) in enumerate(((gt, b), (self.Tg, self.Tg_b))):
            src = self.gdst[r * self.half + ql * 512:r * self.half + (ql + 1) * 512, 12 * g:12 * g + 12]
            self.S.dma("sp", dstt[:], src.rearrange("(j p) c -> p j c", p=128), writes=[dstb])
        self._blend(gt[:], b, self.Tg[:], self.Tg_b)


def build_all(NL=NLAYER):
    nc = bass.Bass("TRN2", target_bir_lowering=False)
    NT, S, D_ = 2048, 4096, 2048
    ein = lambda n, s, dt=F32: nc.dram_tensor(n, list(s), dt, kind="ExternalInput").ap()
    x = ein("x", [NT, D_]); mem = ein("mem", [256, D_]); sel = ein("sel", [128, 2])
    norm_mix = ein("norm_mix", [NL, 1, D_]); w_in = ein("w_in", [NL, D_, INW])
    ln_g = ein("ln_g", [NL, 1, 512]); ln_b = ein("ln_b", [NL, 1, 512])
    wsT = ein("wsT", [NL, 128, 4, 128]); bsT = ein("bsT", [NL, 128, 4]); posT = ein("posT", [NL, 128, 32])
    ck_w1 = ein("ck_w1", [NL, 4096, 128]); ck_w2 = ein("ck_w2", [NL, 128, 128])
    cv_w1 = ein("cv_w1", [NL, 4096, 128]); cv_w2 = ein("cv_w2", [NL, 128, 128])
    w_out = ein("w_out", [NL, D_, D_]); g_x = ein("g_x", [NL, 1, D_]); g_mem = ein("g_mem", [NL, 1, D_])
    wq = ein("wq", [NL, D_, 512]); wkv = ein("wkv", [NL, D_, 1024]); wo = ein("wo", [NL, 512, D_])
    g_mlp = ein("g_mlp", [NL, 1, D_]); w_up = ein("w_up", [NL, D_, DFF]); w_down = ein("w_down", [NL, DFF, D_])
    g_fin = ein("g_fin", [1, D_])
    tril = ein("tril", [128, 128]); ident = ein("ident", [128, 128])
    WD = 384 + 128 * 16 + 512; WW = 384 + 128 * 4 + 512; WC = 384 + 512
    dstrip = ein("dstrip", [128, WD], BF16); wstrip = ein("wstrip", [128, WW], BF16); cstrip = ein("cstrip", [128, WC], BF16)
    cbias = ein("cbias", [S // 512, 128, 2, 512], BF16); ovd = ein("ov", [128, 2, 64])
    selmask = ein("selmask", [S // 128, 128, 2, 64]); esel = ein("esel", [64, S // 128, 128], BF16)
    y = nc.dram_tensor("y", [NT, D_], F32, kind="ExternalOutput").ap()
    xs_t = nc.dram_tensor("xs_s", [NT, D_], F32)
    mixA_t = nc.dram_tensor("mixA_s", [4, 128, NT], BF16)
    exA_src_t = nc.dram_tensor("exA_src", [4096, NT], BF16)
    exA_dst_t = nc.dram_tensor("exA_dst", [8192, NT], BF16)
    gate_src_t = nc.dram_tensor("gate_src", [NT, 24], F32)
    gate_dst_t = nc.dram_tensor("gate_dst", [2 * NT, 24], F32)
    exB_src_t = nc.dram_tensor("exB_src", [768, S], BF16)
    exB_dst_t = nc.dram_tensor("exB_dst", [1536, S], BF16)
    WSPEC = {"w_out": (1, 4, 16), "wq": (1, 1, 16), "wkv": (1, 2, 16), "wo": (1, 4, 4), "w_up": (1, 16, 16), "w_down": (4, 4, 16)}
    wbf2 = [{k: nc.dram_tensor("wbf%d_%s" % (par, k), [ntr, ntc, 128, kr, 512], BF16).ap() for k, (ntr, ntc, kr) in WSPEC.items()}
            for par in range(2)]
    xs_s, mixA_s, exA_src, exA_dst = xs_t.ap(), mixA_t.ap(), exA_src_t.ap(), exA_dst_t.ap()
    gate_src, gate_dst, exB_src, exB_dst = gate_src_t.ap(), gate_dst_t.ap(), exB_src_t.ap(), exB_dst_t.ap()

    perm = [0, 1, 2, 3]
    for k in range(12):
        c, g, e = k // 4, (k // 2) % 2, k % 2
        i = 2 * c + e
        perm.append(4 + 2 * g + i if i < 2 else 8 + 4 * g + (i - 2))

    def precast(l, bufs):
        wl = {"w_out": w_out[l], "wq": wq[l], "wkv": wkv[l], "wo": wo[l], "w_up": w_up[l], "w_down": w_down[l]}
        for k, (ntr, ntc, kr) in WSPEC.items():
            for tr in range(ntr):
                for tc in range(ntc):
                    srcv = wl[k][tr * 2048:tr * 2048 + kr * 128, tc * 512:(tc + 1) * 512].rearrange("(k p) c -> p k c", p=128)
                    Sc.dma("pool", wbf2[l % 2][k][tr, tc], srcv, reads=bufs)

    with ExitStack() as es:
        Sc = Sched(nc, es, arena_elems=ARENA)
        PSP = make_psum(Sc)
        outs = []
        for l in range(NL):
            x_src = x if l == 0 else xs_s
            x_dst = y if l == NL - 1 else xs_s
            Sc.phase_reset()
            ioA = dict(x=x_src, gmix=norm_mix[l], w_in=w_in[l], ln_g=ln_g[l], ln_b=ln_b[l], wsT=wsT[l], bsT=bsT[l],
                       tril=tril, ident=ident, o_mixA=mixA_s,
                       o_fm=exA_src[0:3072, :].rearrange("(h p) t -> h p t", p=128),
                       o_tm=exA_src[3072:4096, :].rearrange("r (two c) -> (r two) c", two=2),
                       o_gate=gate_src)
            emit_A(Sc, PSP, ioA, NT)
            Sc.allgather([(exA_src[i * 512:(i + 1) * 512, :], exA_dst[i * 1024:(i + 1) * 1024, :]) for i in range(8)]
                         + [(gate_src, gate_dst)], GROUPS)
            Sc.phase_reset()
            ld = FusedLoaderB(Sc, exA_dst, gate_dst, sel, S)
            ioB = dict(posT=posT[l], ck_w1=ck_w1[l], ck_w2=ck_w2[l], cv_w1=cv_w1[l], cv_w2=cv_w2[l], ident=ident,
                       dstrip=dstrip, wstrip=wstrip, cstrip=cstrip, cbias=cbias, ov=ovd, selmask=selmask, esel=esel,
                       mixT=exB_src.rearrange("(h p) t -> h p t", p=128), bf16_consts=True)

            if l == 0:
                ioB["after_loads"] = lambda bufs: precast(0, bufs)
            emit_B(Sc, PSP, ioB, S, ld=ld)
            Sc.allgather([(exB_src[i * 256:(i + 1) * 256, :], exB_dst[i * 512:(i + 1) * 512, :]) for i in range(3)], GROUPS)
            Sc.phase_reset()
            selt = Sc.sb("seltC", [128, 2], F32); sel_b = Sc.buf()
            Sc.dma("sp", selt[:], sel[:, :], writes=[sel_b])
            Tmx = Sc.sb("Tmx", [128, 12, 512], BF16); Tmx_b = Sc.buf()
            dv = exB_dst.rearrange("(k p) t -> p k t", p=128)

            def mx_loader(S_, mx, mx_b, t0, selt=selt, sel_b=sel_b, Tmx=Tmx, Tmx_b=Tmx_b, dv=dv):
                S_.dma("sp", mx[:, 0:4, :], mixA_s[:, :, t0:t0 + 512].rearrange("k p t -> p k t"), writes=[mx_b])
                S_.dma("sp", mx[:, 4:16, :], dv[:, :, t0:t0 + 512], writes=[mx_b])
                S_.dma("sp", Tmx[:], dv[:, :, NT + t0:NT + t0 + 512], writes=[Tmx_b])
                S_.op("dve", lambda e: e.tensor_scalar(out=Tmx[:], in0=Tmx[:], scalar1=selt[:, 1:2], scalar2=None, op0=ALU.mult),
                      reads=[Tmx_b, sel_b], writes=[Tmx_b])
                S_.op("dve", lambda e: e.scalar_tensor_tensor(out=mx[:, 4:16, :], in0=mx[:, 4:16, :], scalar=selt[:, 0:1], in1=Tmx[:],
                                                              op0=ALU.mult, op1=ALU.add),
                      reads=[mx_b, Tmx_b, sel_b], writes=[mx_b])

            ioC = dict(x=x_src, w_out=w_out[l], g_x=g_x[l], g_mem=g_mem[l], mem=mem, wq=wq[l], wkv=wkv[l], wo=wo[l],
                       g_mlp=g_mlp[l], w_up=w_up[l], w_down=w_down[l], g_fin=g_fin, ident=ident, y=x_dst,
                       wqueue="sp", wtile=lambda name, tr, tc, wbf=wbf2[l % 2]: wbf[name][tr, tc], no_pool=True)
            if l + 1 < NL:
                ioC["after_setup"] = lambda l=l: precast(l + 1, [])
            outs = emit_C(Sc, PSP, ioC, NT, FINAL=(l == NL - 1), mx_loader=mx_loader, perm=perm)
            Sc.barrier()
        Sc.emit(final_waits=outs)
    return nc

_NC = {}


def kernel(x, mem, norm_mix, w_in, gmlp_ln_g, gmlp_ln_b, gmlp_w_s, gmlp_b_s, cmp_pos,
           cmp_k_w1, cmp_k_w2, cmp_v_w1, cmp_v_w2, w_out, norm_xattn, norm_mem,
           xattn_wq, xattn_wkv, xattn_wo, norm_mlp, w_up, w_down, final_norm):
    f32 = np.float32
    A = lambda a: np.ascontiguousarray(np.asarray(a), dtype=f32)
    if "nc" not in _NC:
        _NC["nc"] = build_all(NLAYER)
    nc = _NC["nc"]
    L = NLAYER
    shared = dict(
        norm_mix=A(norm_mix).reshape(L, 1, -1), w_in=A(w_in), ln_g=A(gmlp_ln_g).reshape(L, 1, -1),
        ln_b=A(gmlp_ln_b).reshape(L, 1, -1), wsT=A(np.asarray(gmlp_w_s).transpose(0, 3, 1, 2)),
        bsT=A(np.asarray(gmlp_b_s).transpose(0, 2, 1)), posT=A(np.asarray(cmp_pos).transpose(0, 2, 1)),
        ck_w1=A(cmp_k_w1), ck_w2=A(cmp_k_w2), cv_w1=A(cmp_v_w1), cv_w2=A(cmp_v_w2),
        w_out=A(w_out), g_x=A(norm_xattn).reshape(L, 1, -1), g_mem=A(norm_mem).reshape(L, 1, -1),
        wq=A(xattn_wq), wkv=A(xattn_wkv), wo=A(xattn_wo), g_mlp=A(norm_mlp).reshape(L, 1, -1),
        w_up=A(w_up), w_down=A(w_down), g_fin=A(final_norm).reshape(1, -1),
        tril=np.triu(np.ones((128, 128), f32)))
    import ml_dtypes
    for k, v in consts_B(4096).items():
        if k in ("dstrip", "wstrip", "cstrip", "cbias", "esel"):
            v = v.astype(ml_dtypes.bfloat16)
        shared[k] = np.ascontiguousarray(v)
    x = np.asarray(x)
    mem = np.asarray(mem)
    in_maps = []
    for c in range(8):
        b, hh = c // 2, c % 2
        sel = np.zeros((128, 2), f32)
        sel[:, hh] = 1.0
        d = dict(x=A(x[b, hh * 2048:(hh + 1) * 2048]), mem=A(mem[b]), sel=sel)
        d.update(shared)
        in_maps.append(d)
    res = run_bass_kernel_spmd(nc, in_maps, core_ids=list(range(8)))
    out = np.empty((4, 4096, 2048), f32)
    for c in range(8):
        out[c // 2, (c % 2) * 2048:(c % 2 + 1) * 2048] = res.results[c]["y"]
    return out
```

```python
from concourse.bass_utils import run_bass_kernel_spmd
import numpy as np
import concourse.bass as bass
import concourse.mybir as mybir
from contextlib import ExitStack

F32 = mybir.dt.float32
BF16 = mybir.dt.bfloat16
AF = mybir.ActivationFunctionType
ALU = mybir.AluOpType
AX = mybir.AxisListType

STRICT_SAME_ENGINE = True


class Buf:
    __slots__ = ("name", "w", "r")

    def __init__(self, name):
        self.name = name
        self.w = None
        self.r = []


class Op:
    __slots__ = ("eng", "idx", "fn", "waits", "needed", "val", "kind", "semh", "key")

    def __init__(self, eng, idx, fn, kind="c"):
        self.eng = eng
        self.idx = idx
        self.fn = fn
        self.waits = []
        self.needed = False
        self.val = None
        self.kind = kind
        self.semh = None
        self.key = None


class Sched:
    ENGS = ("pe", "act", "dve", "pool", "sp")
    NDMA = {"sp": 20, "pool": 12}

    def __init__(self, nc, es, arena_elems=None):
        self.nc = nc
        self.es = es
        self.ops = {e: [] for e in self.ENGS}
        self.seen = {e: {} for e in self.ENGS}
        self.csem = {e: es.enter_context(nc.semaphore("cs_" + e)) for e in self.ENGS}
        self.dsem = {q: [es.enter_context(nc.semaphore("ds_%s%d" % (q, i))) for i in range(n)]
                     for q, n in self.NDMA.items()}
        self.ccsem = es.enter_context(nc.semaphore("ccs"))
        self.cccount = 0
        self.lastcc = None
        self.dcount = {q: 0 for q in self.NDMA}
        self.dlast = {q: [None] * n for q, n in self.NDMA.items()}
        self.lastc = {e: None for e in self.ENGS}
        self.nbuf = 0
        self.arena = None
        self.aoff = 0
        if arena_elems:
            self.arena = es.enter_context(nc.sbuf_tensor("arena", [128, arena_elems], BF16))
            self.arena_elems = arena_elems
        self.fence = []
        self.fence_id = 0
        self.passed = {e: 0 for e in self.ENGS}

    def sb(self, name, shape, dt):
        if self.arena is None:
            return self.es.enter_context(self.nc.sbuf_tensor(name, list(shape), dt))
        n = 1
        for s in shape[1:]:
            n *= s
        esz = 4 if dt == F32 else 2
        nb = (n * esz + 31) // 32 * 32
        o = self.aoff
        self.aoff += nb // 2
        assert self.aoff <= self.arena_elems, ("arena overflow", name, self.aoff)
        v = self.arena[0:shape[0], o:o + n * esz // 2]
        if dt != BF16:
            v = v.bitcast(dt)
        if len(shape) == 3:
            v = v.rearrange("p (a b) -> p a b", a=shape[1])
        elif len(shape) == 4:
            v = v.rearrange("p (a b c) -> p a b c", a=shape[1], b=shape[2])
        return v

    def phase_reset(self):
        self.aoff = 0

    def ps(self, name, shape, dt):
        return self.es.enter_context(self.nc.psum_tensor(name, list(shape), dt))

    def buf(self, name=None):
        self.nbuf += 1
        return Buf(name or "b%d" % self.nbuf)

    def barrier(self):
        f = []
        for e in self.ENGS:
            if self.lastc[e] is not None:
                f.append(self.lastc[e])
        for q in self.NDMA:
            for o in self.dlast[q]:
                if o is not None:
                    f.append(o)
        if self.lastcc is not None:
            f.append(self.lastcc)
        self.fence = f
        self.fence_id += 1

    def _dep(self, op, dep, force=False):
        if dep is None or dep is op:
            return
        if dep.kind == "c" and dep.eng == op.eng:
            if op.eng in ("pe", "sp") or not (STRICT_SAME_ENGINE or force):
                return
        seen = self.seen[op.eng]
        if seen.get(dep.key, -1) >= dep.idx:
            return
        seen[dep.key] = dep.idx
        dep.needed = True
        op.waits.append(dep)

    def op(self, eng, fn, reads=(), writes=(), kind="c"):
        lst = self.ops[eng]
        o = Op(eng, len(lst), fn, kind=kind)
        if kind == "d":
            q = eng
            n = self.NDMA[q]
            j = self.dcount[q] % n
            o.semh = self.dsem[q][j]
            o.key = ("d", q, j)
            o.idx = self.dcount[q] // n
            o.val = 16 * (o.idx + 1)
            prev = self.dlast[q][j]
            if prev is not None:
                self._dep(o, prev)
            self.dlast[q][j] = o
            self.dcount[q] += 1
        elif kind == "cc":
            o.semh = self.ccsem
            o.key = ("cc",)
            o.idx = self.cccount
            self.cccount += 1
            o.val = self.cccount
            self.lastcc = o
        else:
            o.semh = self.csem[eng]
            o.key = ("c", eng)
            self.lastc[eng] = o
        if self.passed[eng] < self.fence_id:
            for d in self.fence:
                self._dep(o, d, force=True)
            self.passed[eng] = self.fence_id
        for b in reads:
            self._dep(o, b.w)
        for b in writes:
            self._dep(o, b.w)
            for r in b.r:
                self._dep(o, r)
        for b in reads:
            b.r.append(o)
        for b in writes:
            b.w = o
            b.r = []
        lst.append(o)
        return o

    def dma(self, q, out, in_, reads=(), writes=()):
        return self.op(q, lambda e: e.dma_start(out=out, in_=in_), reads, writes, kind="d")

    def allgather(self, pairs, groups):
        self.barrier()
        o = None
        for src, dst in pairs:
            o = self.op("pool", lambda e, src=src, dst=dst: e.collective_compute(
                "AllGather", ALU.bypass, replica_groups=groups, ins=[src], outs=[dst]), kind="cc")
        self.barrier()
        return o

    def emit(self, final_waits=()):
        nc = self.nc
        for e in self.ENGS:
            c = 0
            for o in self.ops[e]:
                if o.kind == "c" and o.needed:
                    c += 1
                    o.val = c
        engmap = {"pe": "tensor", "act": "scalar", "dve": "vector", "pool": "gpsimd", "sp": "sync"}

        def run(e, engine):
            for o in self.ops[e]:
                for d in o.waits:
                    engine.wait_ge(d.semh, d.val)
                ins = o.fn(engine)
                if o.kind == "d":
                    ins.then_inc(o.semh, 16)
                elif o.kind == "cc":
                    ins.then_inc(o.semh, 1)
                elif o.needed:
                    ins.then_inc(o.semh, 1)
            if e == "sp":
                best = {}
                for d in final_waits:
                    if d.key not in best or best[d.key].val < d.val:
                        best[d.key] = d
                for d in best.values():
                    engine.wait_ge(d.semh, d.val)

        with nc.Block() as block:
            for e in self.ENGS:
                getattr(block, engmap[e])(lambda engine, e=e: run(e, engine))


D = 2048
HD = 128
INW = 5144
EPS = 1e-6


def make_psum(S):
    PS = [S.ps("ps%d" % i, [128, 512], F32) for i in range(8)]
    PS_b = [S.buf() for _ in range(8)]
    return PS, PS_b


def emit_A(S, PSP, io, NT):
    NTT = NT // 128
    NSB = NT // 512
    KC = D // 128
    x, gmix, w_in, ln_g, ln_b, wsT, bsT, tril, ident = (io[k] for k in ("x", "gmix", "w_in", "ln_g", "ln_b", "wsT", "bsT", "tril", "ident"))
    o_mixA, o_fm, o_tm, o_gate = (io[k] for k in ("o_mixA", "o_fm", "o_tm", "o_gate"))
    hT = S.sb("hT", [128, KC, NT], BF16)
    hT_b = [S.buf() for _ in range(NTT)]
    gB = S.sb("gB", [128, D], F32); gB_b = S.buf()
    identb = S.sb("identb", [128, 128], BF16); ident_b = S.buf()
    xt = [S.sb("xt%d" % i, [128, D], F32) for i in range(2)]
    xt_b = [S.buf() for _ in range(2)]
    hb = [S.sb("hb%d" % i, [128, D], BF16) for i in range(2)]
    hb_b = [S.buf() for _ in range(2)]
    junk = S.sb("junk", [128, D], BF16); junk_b = S.buf()
    st = [S.sb("st%d" % i, [128, 8], F32) for i in range(2)]
    st_b = [S.buf() for _ in range(2)]
    W = [S.sb("W%d" % i, [128, KC, 512], BF16) for i in range(3)]
    W_b = [S.buf() for _ in range(3)]
    PS, PS_b = PSP
    psn = [0]

    def next_ps():
        i = psn[0] % 8
        psn[0] += 1
        return PS[i], PS_b[i]

    outs = []
    S.dma("sp", gB[:], gmix.partition_broadcast(128), writes=[gB_b])
    S.dma("pool", identb[:], ident[:, :], writes=[ident_b])

    for tt in range(NTT):
        i = tt % 2
        S.dma("sp", xt[i][:], x[tt * 128:(tt + 1) * 128, :], writes=[xt_b[i]])
        S.op("act", lambda e, i=i: e.activation(out=junk[:], in_=xt[i][:], func=AF.Square,
                                                accum_out=st[i][:, 0:1]),
             reads=[xt_b[i]], writes=[junk_b, st_b[i]])
        S.op("dve", lambda e, i=i: e.tensor_scalar(out=st[i][:, 1:2], in0=st[i][:, 0:1], scalar1=1.0 / D,
                                                   scalar2=EPS, op0=ALU.mult, op1=ALU.add),
             reads=[st_b[i]], writes=[st_b[i]])
        S.op("act", lambda e, i=i: e.activation(out=st[i][:, 3:4], in_=st[i][:, 1:2], func=AF.Sqrt),
             reads=[st_b[i]], writes=[st_b[i]])
        S.op("dve", lambda e, i=i: e.reciprocal(out=st[i][:, 2:3], in_=st[i][:, 3:4]),
             reads=[st_b[i]], writes=[st_b[i]])
        S.op("dve", lambda e, i=i: e.scalar_tensor_tensor(out=hb[i][:], in0=xt[i][:], scalar=st[i][:, 2:3],
                                                          in1=gB[:], op0=ALU.mult, op1=ALU.mult),
             reads=[xt_b[i], st_b[i], gB_b], writes=[hb_b[i]])
        for q in range(KC // 4):
            p, pb = next_ps()
            pv = p[:].bitcast(BF16)
            for j in range(4):
                kc = q * 4 + j
                S.op("pe", lambda e, i=i, kc=kc, j=j, pv=pv: e.transpose(
                    out=pv[:, j * 128:(j + 1) * 128], in_=hb[i][:, kc * 128:(kc + 1) * 128], identity=identb[:]),
                    reads=[hb_b[i], ident_b], writes=[pb])
            eng = "act" if q % 2 == 0 else "dve"
            if eng == "act":
                S.op("act", lambda e, q=q, tt=tt, pv=pv: e.activation(
                    out=hT[:, q * 4:(q + 1) * 4, tt * 128:(tt + 1) * 128],
                    in_=pv[:, 0:512].rearrange("p (a b) -> p a b", a=4), func=AF.Copy),
                    reads=[pb], writes=[hT_b[tt]])
            else:
                S.op("dve", lambda e, q=q, tt=tt, pv=pv: e.tensor_copy(
                    out=hT[:, q * 4:(q + 1) * 4, tt * 128:(tt + 1) * 128],
                    in_=pv[:, 0:512].rearrange("p (a b) -> p a b", a=4)),
                    reads=[pb], writes=[hT_b[tt]])

    wn = [0]

    def load_W(c0, width):
        i = wn[0] % 3
        wn[0] += 1
        S.dma("pool", W[i][:, :, 0:width], w_in[:, c0:c0 + width].rearrange("(k p) c -> p k c", p=128),
              writes=[W_b[i]])
        return W[i], W_b[i]

    ostage = [S.sb("os%d" % i, [128, 512], BF16) for i in range(4)]
    ostage_b = [S.buf() for _ in range(4)]
    osn = [0]

    def next_os():
        i = osn[0] % 4
        osn[0] += 1
        return ostage[i], ostage_b[i]

    evn = [0]

    def evac(dst, dst_b, src, src_b, extra_reads=()):
        e = "act" if evn[0] % 2 == 0 else "dve"
        evn[0] += 1
        if e == "act":
            S.op("act", lambda en: en.activation(out=dst, in_=src, func=AF.Copy),
                 reads=[src_b] + list(extra_reads), writes=[dst_b])
        else:
            S.op("dve", lambda en: en.tensor_copy(out=dst, in_=src),
                 reads=[src_b] + list(extra_reads), writes=[dst_b])

    def fm_group(c0, head0, nheads):
        Wt, Wb = load_W(c0, nheads * 128)
        for j in range(nheads):
            for sb_ in range(NSB):
                p, pb = next_ps()
                for kc in range(KC):
                    S.op("pe", lambda e, p=p, Wt=Wt, kc=kc, j=j, sb_=sb_: e.matmul(
                        p[:], lhsT=Wt[:, kc, j * 128:(j + 1) * 128], rhs=hT[:, kc, sb_ * 512:(sb_ + 1) * 512],
                        start=(kc == 0), stop=(kc == KC - 1)),
                        reads=[Wb] + hT_b[sb_ * 4:(sb_ + 1) * 4], writes=[pb])
                o, ob = next_os()
                evac(o[:], ob, p[:], pb)
                outs.append(S.dma("sp", o_fm[head0 + j, :, sb_ * 512:(sb_ + 1) * 512], o[:], reads=[ob]))

    def tm_group(c0, width, oc0):
        Wt, Wb = load_W(c0, width)
        for tt in range(NTT):
            p, pb = next_ps()
            for kc in range(KC):
                S.op("pe", lambda e, p=p, Wt=Wt, kc=kc, tt=tt: e.matmul(
                    p[:, 0:width], lhsT=hT[:, kc, tt * 128:(tt + 1) * 128], rhs=Wt[:, kc, 0:width],
                    start=(kc == 0), stop=(kc == KC - 1)),
                    reads=[Wb, hT_b[tt]], writes=[pb])
            o, ob = next_os()
            evac(o[:, 0:width], ob, p[:, 0:width], pb)
            outs.append(S.dma("sp", o_tm[tt * 128:(tt + 1) * 128, oc0:oc0 + width], o[:, 0:width], reads=[ob]))

    lngB = S.sb("lngB", [128, 512], F32); lnbB = S.sb("lnbB", [128, 512], F32); ln_bb = S.buf()
    S.dma("sp", lngB[:], ln_g.partition_broadcast(128), writes=[ln_bb])
    S.dma("sp", lnbB[:], ln_b.partition_broadcast(128), writes=[ln_bb])
    wsf = S.sb("wsf", [128, 4, 128], F32); trilf = S.sb("trilf", [128, 128], F32)
    wsb = S.sb("wsb", [128, 4, 128], BF16); ws_b = S.buf(); wsf_b = S.buf()
    bsb = S.sb("bsb", [128, 4], F32); bs_b = S.buf()
    S.dma("sp", wsf[:], wsT[:, :, :], writes=[wsf_b])
    S.dma("sp", trilf[:], tril[:, :], writes=[wsf_b])
    S.dma("sp", bsb[:], bsT[:, :], writes=[bs_b])
    for g in range(4):
        S.op("dve", lambda e, g=g: e.tensor_tensor(out=wsb[:, g, :], in0=wsf[:, g, :], in1=trilf[:], op=ALU.mult),
             reads=[wsf_b], writes=[ws_b])
    Wu, Wub = load_W(0, 512)
    Wv, Wvb = load_W(512, 512)
    ug = [S.sb("ug%d" % i, [128, 512], F32) for i in range(2)]; ug_b = [S.buf() for _ in range(2)]
    vg = [S.sb("vg%d" % i, [128, 512], F32) for i in range(2)]; vg_b = [S.buf() for _ in range(2)]
    vn = [S.sb("vn%d" % i, [128, 512], BF16) for i in range(2)]; vn_b = [S.buf() for _ in range(2)]
    oa = [S.sb("oa%d" % i, [128, 512], BF16) for i in range(2)]; oa_b = [S.buf() for _ in range(2)]
    bst = [S.sb("bst%d" % i, [128, 8], F32) for i in range(2)]; bst_b = [S.buf() for _ in range(2)]
    for tt in range(NTT):
        i = tt % 2
        pu, pub = next_ps()
        pvv, pvb = next_ps()
        for (p, pb, Wt, Wb) in ((pu, pub, Wu, Wub), (pvv, pvb, Wv, Wvb)):
            for kc in range(KC):
                S.op("pe", lambda e, p=p, Wt=Wt, kc=kc, tt=tt: e.matmul(
                    p[:], lhsT=hT[:, kc, tt * 128:(tt + 1) * 128], rhs=Wt[:, kc, :],
                    start=(kc == 0), stop=(kc == KC - 1)),
                    reads=[Wb, hT_b[tt]], writes=[pb])
        S.op("act", lambda e, i=i, pu=pu: e.activation(out=ug[i][:], in_=pu[:], func=AF.Gelu_apprx_tanh),
             reads=[pub], writes=[ug_b[i]])
        S.op("act", lambda e, i=i, pvv=pvv: e.activation(out=vg[i][:], in_=pvv[:], func=AF.Gelu_apprx_tanh),
             reads=[pvb], writes=[vg_b[i]])
        S.op("dve", lambda e, i=i: e.bn_stats(out=bst[i][:, 0:6], in_=vg[i][:]), reads=[vg_b[i]], writes=[bst_b[i]])
        S.op("dve", lambda e, i=i: e.bn_aggr(out=bst[i][:, 6:8], in_=bst[i][:, 0:6]), reads=[bst_b[i]], writes=[bst_b[i]])
        S.op("dve", lambda e, i=i: e.tensor_scalar(out=bst[i][:, 1:2], in0=bst[i][:, 7:8], scalar1=EPS, scalar2=None,
                                                   op0=ALU.add), reads=[bst_b[i]], writes=[bst_b[i]])
        S.op("act", lambda e, i=i: e.activation(out=bst[i][:, 2:3], in_=bst[i][:, 1:2], func=AF.Sqrt),
             reads=[bst_b[i]], writes=[bst_b[i]])
        S.op("dve", lambda e, i=i: e.reciprocal(out=bst[i][:, 0:1], in_=bst[i][:, 2:3]),
             reads=[bst_b[i]], writes=[bst_b[i]])
        S.op("dve", lambda e, i=i: e.tensor_scalar(out=vg[i][:], in0=vg[i][:], scalar1=bst[i][:, 6:7],
                                                   scalar2=bst[i][:, 0:1], op0=ALU.subtract, op1=ALU.mult),
             reads=[vg_b[i], bst_b[i]], writes=[vg_b[i]])
        S.op("dve", lambda e, i=i: e.tensor_tensor(out=vg[i][:], in0=vg[i][:], in1=lngB[:], op=ALU.mult),
             reads=[vg_b[i], ln_bb], writes=[vg_b[i]])
        S.op("dve", lambda e, i=i: e.tensor_tensor(out=vn[i][:], in0=vg[i][:], in1=lnbB[:], op=ALU.add),
             reads=[vg_b[i], ln_bb], writes=[vn_b[i]])
        psv, psvb = next_ps()
        for g in range(4):
            S.op("pe", lambda e, g=g, i=i, psv=psv: e.matmul(
                psv[:, g * 128:(g + 1) * 128], lhsT=wsb[:, g, :], rhs=vn[i][:, g * 128:(g + 1) * 128],
                start=True, stop=True), reads=[ws_b, vn_b[i]], writes=[psvb])
        for g in range(4):
            S.op("dve", lambda e, g=g, i=i, psv=psv: e.scalar_tensor_tensor(
                out=oa[i][:, g * 128:(g + 1) * 128], in0=psv[:, g * 128:(g + 1) * 128], scalar=bsb[:, g:g + 1],
                in1=ug[i][:, g * 128:(g + 1) * 128], op0=ALU.add, op1=ALU.mult),
                reads=[psvb, bs_b, ug_b[i]], writes=[oa_b[i]])
        pt, ptb = next_ps()
        ptv = pt[:].bitcast(BF16)
        for g in range(4):
            S.op("pe", lambda e, g=g, i=i, ptv=ptv: e.transpose(
                out=ptv[:, g * 128:(g + 1) * 128], in_=oa[i][:, g * 128:(g + 1) * 128], identity=identb[:]),
                reads=[oa_b[i], ident_b], writes=[ptb])
        o, ob = next_os()
        evac(o[:], ob, ptv[:, 0:512], ptb)
        outs.append(S.dma("sp", o_mixA[:, :, tt * 128:(tt + 1) * 128].rearrange("g p t -> p g t"),
                          o[:].rearrange("p (g t) -> p g t", g=4), reads=[ob]))

    fm_group(1024, 0, 4)
    fm_group(1536, 4, 4)
    tm_group(2048, 512, 0)
    fm_group(2560, 8, 4)
    fm_group(3072, 12, 4)
    fm_group(3584, 16, 4)
    fm_group(4096, 20, 2)
    fm_group(4608, 22, 2)
    tm_group(4352, 256, 512)
    tm_group(4864, 256, 768)
    Wt, Wb = load_W(5120, 24)
    gst = [S.sb("gst%d" % i, [128, 24], F32) for i in range(2)]; gst_b = [S.buf() for _ in range(2)]
    for tt in range(NTT):
        i = tt % 2
        p, pb = next_ps()
        for kc in range(KC):
            S.op("pe", lambda e, p=p, Wt=Wt, kc=kc, tt=tt: e.matmul(
                p[:, 0:24], lhsT=hT[:, kc, tt * 128:(tt + 1) * 128], rhs=Wt[:, kc, 0:24],
                start=(kc == 0), stop=(kc == KC - 1)),
                reads=[Wb, hT_b[tt]], writes=[pb])
        S.op("act", lambda e, i=i, p=p: e.activation(out=gst[i][:], in_=p[:, 0:24], func=AF.Sigmoid),
             reads=[pb], writes=[gst_b[i]])
        outs.append(S.dma("sp", o_gate[tt * 128:(tt + 1) * 128, :], gst[i][:], reads=[gst_b[i]]))
    return outs


def build_A(NT=2048):
    nc = bass.Bass("TRN2", target_bir_lowering=False)
    NTT = NT // 128
    NSB = NT // 512
    KC = D // 128
    dr = lambda n, s, dt, k: nc.dram_tensor(n, list(s), dt, kind=k).ap()
    x = dr("x", [NT, D], F32, "ExternalInput")
    gmix = dr("gmix", [1, D], F32, "ExternalInput")
    w_in = dr("w_in", [D, INW], F32, "ExternalInput")
    ln_g = dr("ln_g", [1, 512], F32, "ExternalInput")
    ln_b = dr("ln_b", [1, 512], F32, "ExternalInput")
    wsT = dr("wsT", [128, 4, 128], F32, "ExternalInput")
    bsT = dr("bsT", [128, 4], F32, "ExternalInput")
    tril = dr("tril", [128, 128], F32, "ExternalInput")
    ident = dr("ident", [128, 128], F32, "ExternalInput")
    o_mixA = dr("o_mixA", [4, 128, NT], BF16, "ExternalOutput")
    o_fm = dr("o_fm", [24, 128, NT], BF16, "ExternalOutput")
    o_tm = dr("o_tm", [NT, 1024], BF16, "ExternalOutput")
    o_gate = dr("o_gate", [NT, 24], F32, "ExternalOutput")

    io = dict(x=x, gmix=gmix, w_in=w_in, ln_g=ln_g, ln_b=ln_b, wsT=wsT, bsT=bsT, tril=tril, ident=ident,
              o_mixA=o_mixA, o_fm=o_fm, o_tm=o_tm, o_gate=o_gate)
    with ExitStack() as es:
        S = Sched(nc, es)
        outs = emit_A(S, make_psum(S), io, NT)
        S.emit(final_waits=outs)
    return nc


HD = 128
BIG = 30000.0
SCALE = 128 ** -0.5
OFF0 = 384
DEBUG = False


def strip_const(mmin, fn):
    width = 384 - 128 * mmin + 512
    kl = np.arange(128)[:, None]
    v = np.arange(width)[None, :]
    return fn(v - OFF0 - kl).astype(np.float32)


def dil_fn(d):
    c = ((d >= 0) & (d <= 128)).astype(np.float64) + ((d >= 0) & (d % 4 == 0) & (d <= 512)) + ((d >= 0) & (d % 16 == 0) & (d <= 2048))
    out = np.full(d.shape, -BIG)
    m = c > 0
    out[m] = np.log(c[m]) / SCALE
    return out


def win_fn(d):
    return np.where((d >= 0) & (d <= 511), 0.0, -BIG)


def caus_fn(d):
    return np.where(d >= 0, 0.0, -BIG)


def consts_B(S):
    nqt = S // 128
    c = {}
    c["ident"] = np.eye(128, dtype=np.float32)
    c["dstrip"] = strip_const(-16, dil_fn)
    c["wstrip"] = strip_const(-4, win_fn)
    c["cstrip"] = strip_const(0, caus_fn)
    n = np.arange(256)
    t = np.arange(S)
    valid = (16 * n[:, None] + 31) <= t[None, :]
    cb = np.where(valid, 0.0, -BIG).astype(np.float32).reshape(2, 128, S // 512, 512).transpose(2, 1, 0, 3)
    c["cbias"] = np.ascontiguousarray(cb)
    n_sel = S // 64
    cs = n * 16
    ss = np.arange(64) * 64
    ov = ((cs[:, None] <= ss[None, :] + 63) & (cs[:, None] + 31 >= ss[None, :])).astype(np.float32)
    ov[255] = 0
    ov[:, n_sel:] = 0
    c["ov"] = np.ascontiguousarray(ov.reshape(2, 128, 64).transpose(1, 0, 2))
    sm = np.zeros((nqt, 128, 2, 64), np.float32)
    for gq in range(nqt):
        tt = 128 * gq + np.arange(128)
        jt = tt // 64
        s = np.arange(64)[None, :]
        valid_s = s <= jt[:, None]
        add = np.where(valid_s, 0.0, -1.0)
        vm = valid_s.astype(np.float32)
        for val, cond in ((1e4, s == 0), (2e4, s == jt[:, None]), (3e4, s == jt[:, None] - 1)):
            cond = np.broadcast_to(cond, add.shape)
            add = np.where(cond, val, add)
            vm = np.where(cond, 0.0, vm)
        sm[gq, :, 0] = vm
        sm[gq, :, 1] = add
    c["selmask"] = sm
    E = np.zeros((64, S // 128, 128), np.float32)
    for kt in range(S // 128):
        E[2 * kt, kt, :64] = 1
        E[2 * kt + 1, kt, 64:] = 1
    c["esel"] = E
    return c


class DirectLoaderB:
    def __init__(self, S_, io):
        self.S = S_
        self.io = io

    def fm(self, tile, b, kind, i):
        src = self.io[kind]
        ap = src[i, :, :] if kind in ("Bq", "Bk", "Cq") else src[:, :]
        self.S.dma("sp", tile[:], ap, writes=[b])

    def tm(self, tile, b, kind, i):
        src = self.io[kind]
        ap = src[:, i * 128:(i + 1) * 128] if kind == "Bv" else src
        self.S.dma("sp", tile[:, :, 0:128], ap.rearrange("(k p) d -> p k d", p=128), writes=[b])

    def gates(self, gt, b, qs):
        self.S.dma("sp", gt[:], self.io["gates"][qs * 512:(qs + 1) * 512, :].rearrange("(j p) c -> p j c", p=128), writes=[b])


def emit_B(S_, PSP, io, S, ld=None):
    NQS = S // 512
    NKT = S // 128
    WD = 384 + 128 * 16 + 512
    WW = 384 + 128 * 4 + 512
    WC = 384 + 512
    if ld is None:
        ld = DirectLoaderB(S_, io)
    (posT, ck_w1, ck_w2, cv_w1, cv_w2, ident, dstrip, wstrip, cstrip, cbias, ovd, selmask, esel, mixT) = (io[k] for k in (
        "posT", "ck_w1", "ck_w2", "cv_w1", "cv_w2", "ident", "dstrip", "wstrip", "cstrip", "cbias", "ov", "selmask", "esel", "mixT"))
    dbg = io.get("dbg")
    sb, buf, op, dma = S_.sb, S_.buf, S_.op, S_.dma
    BB = [sb("BB%d" % i, [128, S], BF16) for i in range(6)]; BB_b = [buf() for _ in range(6)]
    VA = [sb("VA%d" % i, [128, NKT, 129], BF16) for i in range(2)]; VA_b = [buf() for _ in range(2)]
    identb = sb("identb", [128, 128], BF16); identf = sb("identf", [128, 128], F32); id_b = buf()
    dst = sb("dst", [128, WD], BF16); wst = sb("wst", [128, WW], BF16); cst = sb("cst", [128, WC], BF16); strip_b = buf()
    eselb = sb("eselb", [64, NKT, 128], BF16); esel_b = buf()
    PT = [sb("PT%d" % i, [128, 512], BF16) for i in range(3)]; PT_b = [buf() for _ in range(3)]
    accS = sb("accS", [128, 4, 4, 128], F32); acc_b = [[buf() for _ in range(4)] for _ in range(4)]
    accB = sb("accB", [128, 4, 128], BF16); accB_b = buf()
    osb = [sb("osb%d" % i, [128, 512], BF16) for i in range(2)]; osb_b = [buf() for _ in range(2)]
    sm = [sb("sm%d" % i, [128, 16], F32) for i in range(4)]; sm_b = [buf() for _ in range(4)]
    gt = sb("gt", [128, 4, 12], F32); gt_b = buf()
    PS, PS_b = PSP
    cnt = {"st": 0, "acc": 0, "pt": 0, "sm": 0, "os": 0}

    def st_ps():
        i = cnt["st"] % 3; cnt["st"] += 1
        return PS[i], PS_b[i]

    def acc_ps():
        i = 3 + cnt["acc"] % 4; cnt["acc"] += 1
        return PS[i], PS_b[i]

    MISC, MISC_b = PS[7], PS_b[7]

    def next_pt():
        i = cnt["pt"] % 3; cnt["pt"] += 1
        return PT[i], PT_b[i]

    def next_sm():
        i = cnt["sm"] % 4; cnt["sm"] += 1
        return sm[i], sm_b[i]

    CQ = "sp" if io.get("bf16_consts") else "pool"
    dma("sp", identf[:], ident[:, :], writes=[id_b])
    op("dve", lambda e: e.tensor_copy(out=identb[:], in_=identf[:]), reads=[id_b], writes=[id_b])
    dma(CQ, dst[:], dstrip[:, :], writes=[strip_b])
    dma(CQ, wst[:], wstrip[:, :], writes=[strip_b])
    dma(CQ, cst[:], cstrip[:, :], writes=[strip_b])
    dma(CQ, eselb[:], esel[:, :, :], writes=[esel_b])
    for i in range(2):
        op("dve", lambda e, i=i: e.memset(VA[i][:, :, 128:129], 1.0), writes=[VA_b[i]])

    outs = []

    def attend(qT, q_b, qs, kT, k_b, Vt, V_b, ktiles, strip, extra_bias=None, ncols=129, rhs_fn=None, band=None):
        banks = [acc_ps(), acc_ps()]
        accs = [(banks[j // 2][0][:, (j % 2) * 256:(j % 2) * 256 + ncols], banks[j // 2][1]) for j in range(4)]
        first = [True] * 4
        last_kt = {}
        def live(kt, j):
            d = 4 * qs + j - kt
            return d >= 0 and (band is None or d <= band)

        for kt in ktiles:
            for j in range(4):
                if live(kt, j):
                    last_kt[j] = kt
        def stage1(kt):
            m = kt - 4 * qs
            p, pb = st_ps()
            nb = (1 if strip is not None and strip(m) is not None else 0) + (1 if extra_bias else 0)
            op("pe", lambda e, p=p, kt=kt: e.matmul(p[:], lhsT=kT[:, kt * 128:(kt + 1) * 128],
                                                   rhs=qT[:, qs * 512:(qs + 1) * 512], start=True, stop=(nb == 0)),
               reads=[k_b, q_b], writes=[pb])
            k = 0
            if extra_bias:
                k += 1
                l_ap, r_ap, rb = extra_bias(kt)
                op("pe", lambda e, p=p, l_ap=l_ap, r_ap=r_ap, k=k: e.matmul(p[:], lhsT=l_ap, rhs=r_ap, start=False, stop=(k == nb)),
                   reads=rb, writes=[pb])
            if strip is not None and strip(m) is not None:
                k += 1
                s_ap = strip(m)
                op("pe", lambda e, p=p, s_ap=s_ap: e.matmul(p[:], lhsT=identb[:], rhs=s_ap, start=False, stop=True),
                   reads=[id_b, strip_b], writes=[pb])
            pt, ptb = next_pt()
            op("act", lambda e, p=p, pt=pt: e.activation(out=pt[:], in_=p[:], func=AF.Exp, scale=SCALE),
               reads=[pb], writes=[ptb])
            return pt, ptb

        def stage2(kt, pt, ptb):
            for j in range(4):
                if not live(kt, j):
                    continue
                a, ab = accs[j]
                rhs = Vt[:, kt, 0:ncols] if rhs_fn is None else rhs_fn(kt)
                op("pe", lambda e, a=a, pt=pt, j=j, rhs=rhs, st=(first[j] and j % 2 == 0), sp=(last_kt[j] == kt): e.matmul(
                    a, lhsT=pt[:, j * 128:(j + 1) * 128], rhs=rhs, start=st, stop=sp, skip_group_check=True),
                    reads=[ptb, V_b], writes=[ab])
                first[j] = False

        kts = list(ktiles)
        cur = stage1(kts[0])
        for i, kt in enumerate(kts):
            nxt = stage1(kts[i + 1]) if i + 1 < len(kts) else None
            stage2(kt, *cur)
            cur = nxt
        return accs

    def coef_of(a, ab, ncols, gate_ap=None, gate_b=None):
        s, sbb = next_sm()
        op("dve", lambda e: e.tensor_scalar(out=s[:, 0:1], in0=a[:, ncols - 1:ncols], scalar1=1e-30, scalar2=None, op0=ALU.max),
           reads=[ab], writes=[sbb])
        op("dve", lambda e: e.reciprocal(out=s[:, 1:2], in_=s[:, 0:1]), reads=[sbb], writes=[sbb])
        if gate_ap is not None:
            op("dve", lambda e: e.tensor_tensor(out=s[:, 1:2], in0=s[:, 1:2], in1=gate_ap, op=ALU.mult),
               reads=[sbb, gate_b], writes=[sbb])
        return s[:, 1:2], sbb

    def flush_head(head_out, qs, r):
        for j in range(4):
            op("act", lambda e, j=j: e.activation(out=accB[:, j, :], in_=accS[:, r, j, :], func=AF.Copy),
               reads=[acc_b[r][j]], writes=[accB_b])
        pv = MISC[:].bitcast(BF16)
        for j in range(4):
            op("pe", lambda e, j=j: e.transpose(out=pv[:, j * 128:(j + 1) * 128], in_=accB[:, j, :], identity=identb[:]),
               reads=[accB_b, id_b], writes=[MISC_b])
        i = cnt["os"] % 2; cnt["os"] += 1
        op("dve", lambda e, i=i: e.tensor_copy(out=osb[i][:], in_=pv[:, 0:512]), reads=[MISC_b], writes=[osb_b[i]])
        outs.append(dma("sp", mixT[head_out, :, qs * 512:(qs + 1) * 512], osb[i][:], reads=[osb_b[i]]))

    def dstrip_of(m):
        return dst[:, OFF0 - 128 * m:OFF0 - 128 * m + 512]

    for h in range(2):
        qT, q_b = BB[2 * h], BB_b[2 * h]
        kT, k_b = BB[2 * h + 1], BB_b[2 * h + 1]
        Vt, V_b = VA[h], VA_b[h]
        ld.fm(qT, q_b, "Bq", h)
        ld.fm(kT, k_b, "Bk", h)
        ld.tm(Vt, V_b, "Bv", h)
        for qs in range(NQS):
            ktiles = [kt for kt in range(4 * qs - 16, 4 * qs + 4) if kt >= 0]
            accs = attend(qT, q_b, qs, kT, k_b, Vt, V_b, ktiles, dstrip_of, band=16)
            for j in range(4):
                a, ab = accs[j]
                cf, cfb = coef_of(a, ab, 129)
                op("dve", lambda e, a=a, cf=cf, j=j: e.tensor_scalar(out=accS[:, 0, j, :], in0=a[:, 0:128], scalar1=cf,
                                                                   scalar2=None, op0=ALU.mult),
                   reads=[ab, cfb], writes=[acc_b[0][j]])
            flush_head(h, qs, 0)

    w1 = sb("w1", [128, 32, 128], BF16); w1_b = buf()
    w2 = sb("w2", [128, 128], BF16); w2_b = buf()
    posb = sb("posb", [128, 32], BF16); pos_b = buf()
    cvec = sb("cvec", [128, 1], F32); cvec_b = buf()
    hc = sb("hc", [128, 256], BF16); hc_b = buf()
    kcT = sb("kcT", [128, 256], BF16); kcT_b = buf()
    Rc = sb("Rc", [128, 2, 193], BF16); Rc_b = buf()
    ovf = sb("ovf", [128, 2, 64], F32); ovf_b = buf()
    posf = sb("posf", [128, 32], F32); w1f = sb("w1f", [128, 32, 128], F32); w2f = sb("w2f", [128, 128], F32); wf_b = buf()
    dma("sp", posf[:], posT[:, :], writes=[wf_b])
    op("dve", lambda e: e.tensor_copy(out=posb[:], in_=posf[:]), reads=[wf_b], writes=[pos_b])
    dma("sp", ovf[:], ovd[:, :, :], writes=[ovf_b])
    op("dve", lambda e: e.memset(Rc[:, :, 192:193], 1.0), writes=[Rc_b])
    op("dve", lambda e: e.tensor_copy(out=Rc[:, :, 128:192], in_=ovf[:]), reads=[ovf_b], writes=[Rc_b])
    ld.fm(BB[4], BB_b[4], "cmpk", 0)
    ld.fm(BB[5], BB_b[5], "cmpv", 0)
    for which in range(2):
        raw, raw_b = BB[4 + which], BB_b[4 + which]
        w1d, w2d = (ck_w1, ck_w2) if which == 0 else (cv_w1, cv_w2)
        dma("sp", w1f[:], w1d.rearrange("(i d) f -> d i f", d=128), writes=[wf_b])
        dma("sp", w2f[:], w2d[:, :], writes=[wf_b])
        op("act", lambda e: e.activation(out=w1[:], in_=w1f[:], func=AF.Copy), reads=[wf_b], writes=[w1_b])
        op("dve", lambda e: e.tensor_copy(out=w2[:], in_=w2f[:]), reads=[wf_b], writes=[w2_b])
        rv = raw[:].rearrange("p (n s) -> p n s", s=16)
        nblk = S // 16 - 1
        for i in range(32):
            op("pe", lambda e, i=i: e.matmul(MISC[:, 0:1], lhsT=w1[:, i, :], rhs=posb[:, i:i + 1], start=(i == 0), stop=(i == 31)),
               reads=[w1_b, pos_b], writes=[MISC_b])
        op("dve", lambda e: e.tensor_copy(out=cvec[:], in_=MISC[:, 0:1]), reads=[MISC_b], writes=[cvec_b])
        p, pb = st_ps()
        for i in range(32):
            rhs = rv[:, 0:nblk, i] if i < 16 else rv[:, 1:nblk + 1, i - 16]
            op("pe", lambda e, p=p, i=i, rhs=rhs: e.matmul(p[:, 0:nblk], lhsT=w1[:, i, :], rhs=rhs, start=(i == 0), stop=(i == 31)),
               reads=[w1_b, raw_b], writes=[pb])
        op("dve", lambda e: e.memset(hc[:], 0.0), writes=[hc_b])
        op("act", lambda e, p=p: e.activation(out=hc[:, 0:nblk], in_=p[:, 0:nblk], func=AF.Gelu_apprx_tanh, bias=cvec[:]),
           reads=[pb, cvec_b], writes=[hc_b])
        if which == 0:
            p2, p2b = st_ps()
            op("pe", lambda e, p2=p2: e.matmul(p2[:, 0:256], lhsT=w2[:], rhs=hc[:], start=True, stop=True),
               reads=[w2_b, hc_b], writes=[p2b])
            op("dve", lambda e, p2=p2: e.tensor_copy(out=kcT[:], in_=p2[:, 0:256]), reads=[p2b], writes=[kcT_b])
        else:
            for c in range(2):
                p2, p2b = st_ps()
                op("pe", lambda e, p2=p2, c=c: e.matmul(p2[:, 0:128], lhsT=hc[:, c * 128:(c + 1) * 128], rhs=w2[:], start=True, stop=True),
                   reads=[w2_b, hc_b], writes=[p2b])
                op("dve", lambda e, p2=p2, c=c: e.tensor_copy(out=Rc[:, c, 0:128], in_=p2[:, 0:128]), reads=[p2b], writes=[Rc_b])

    if DEBUG:
        outs.append(dma("sp", dbg[:, 0:256], kcT[:], reads=[kcT_b]))
        outs.append(dma("sp", dbg[:, 256:512], hc[:], reads=[hc_b]))
        outs.append(dma("sp", dbg[:, 512:768], Rc[:, 0, 0:128], reads=[Rc_b])) if False else None
    for r in range(4):
        ld.fm(BB[r], BB_b[r], "Cq", r)
    ld.fm(BB[4], BB_b[4], "slck", 0)
    ld.fm(BB[5], BB_b[5], "wink", 0)
    ld.tm(VA[0], VA_b[0], "slcv", 0)
    ld.tm(VA[1], VA_b[1], "winv", 0)
    if io.get("after_loads"):
        io["after_loads"](BB_b + VA_b)
    cbs = sb("cbs", [128, 2, 512], BF16); cbs_b = buf()
    smk = sb("smk", [128, 4, 2, 64], F32); smk_b = buf()
    imp = sb("imp", [128, 4, 64], F32); imp_b = [buf() for _ in range(4)]
    sc = sb("sc", [128, 64], F32); sc2 = sb("sc2", [128, 64], F32); sc_b = buf()
    mx8 = sb("mx8", [128, 16], F32); mx8_b = buf()
    selb = sb("selb", [128, 64], F32); selb_b = buf()
    selbT = sb("selbT", [64, 512], BF16); selbT_b = buf()

    def wstrip_of(m):
        return wst[:, OFF0 - 128 * m:OFF0 - 128 * m + 512]

    def cstrip_of(m):
        if m < 0:
            return None
        return cst[:, OFF0 - 128 * m:OFF0 - 128 * m + 512]

    for qs in range(NQS):
        ld.gates(gt, gt_b, qs)
        dma(CQ, cbs[:], cbias[qs, :, :, :], writes=[cbs_b])
        dma("sp", smk[:], selmask[4 * qs:4 * qs + 4, :, :, :].rearrange("j p a s -> p j a s"), writes=[smk_b])
        nchunks = [c for c in range(2) if (c * 2048 + 31) <= (qs * 512 + 511)]
        for r in range(4):
            qT, q_b = BB[r], BB_b[r]
            banks = [acc_ps(), acc_ps()]
            accs = [(banks[j // 2][0][:, (j % 2) * 256:(j % 2) * 256 + 193], banks[j // 2][1]) for j in range(4)]
            for ci, c in enumerate(nchunks):
                p, pb = st_ps()
                op("pe", lambda e, p=p, c=c, qT=qT, qs=qs: e.matmul(p[:], lhsT=kcT[:, c * 128:(c + 1) * 128], rhs=qT[:, qs * 512:(qs + 1) * 512],
                                                            start=True, stop=False), reads=[kcT_b, q_b], writes=[pb])
                op("pe", lambda e, p=p, c=c, cbs=cbs: e.matmul(p[:], lhsT=identb[:], rhs=cbs[:, c, :], start=False, stop=True),
                   reads=[id_b, cbs_b], writes=[pb])
                pt, ptb = next_pt()
                op("act", lambda e, p=p, pt=pt: e.activation(out=pt[:], in_=p[:], func=AF.Exp, scale=SCALE), reads=[pb], writes=[ptb])
                for j in range(4):
                    a, ab = accs[j]
                    op("pe", lambda e, a=a, pt=pt, j=j, c=c, ci=ci: e.matmul(a, lhsT=pt[:, j * 128:(j + 1) * 128], rhs=Rc[:, c, :],
                                                                           start=(ci == 0 and j % 2 == 0), stop=(ci == len(nchunks) - 1),
                                                                           skip_group_check=True),
                       reads=[ptb, Rc_b], writes=[ab])
            for j in range(4):
                a, ab = accs[j]
                s, sbb = next_sm()
                op("dve", lambda e, a=a, s=s: e.tensor_scalar(out=s[:, 0:1], in0=a[:, 192:193], scalar1=1e-30, scalar2=None, op0=ALU.max),
                   reads=[ab], writes=[sbb])
                op("dve", lambda e, s=s: e.reciprocal(out=s[:, 1:2], in_=s[:, 0:1]), reads=[sbb], writes=[sbb])
                op("dve", lambda e, s=s, j=j, r=r, gt=gt: e.tensor_tensor(out=s[:, 2:3], in0=s[:, 1:2], in1=gt[:, j, r * 3:r * 3 + 1], op=ALU.mult),
                   reads=[sbb, gt_b], writes=[sbb])
                op("dve", lambda e, a=a, s=s, j=j, r=r: e.tensor_scalar(out=accS[:, r, j, :], in0=a[:, 0:128], scalar1=s[:, 2:3],
                                                                      scalar2=None, op0=ALU.mult),
                   reads=[ab, sbb], writes=[acc_b[r][j]])
                if r == 0:
                    op("dve", lambda e, a=a, s=s, j=j: e.tensor_scalar(out=imp[:, j, :], in0=a[:, 128:192], scalar1=s[:, 1:2],
                                                                     scalar2=None, op0=ALU.mult),
                       reads=[ab, sbb], writes=[imp_b[j]])
                else:
                    op("dve", lambda e, a=a, s=s, j=j: e.scalar_tensor_tensor(out=imp[:, j, :], in0=a[:, 128:192], scalar=s[:, 1:2],
                                                                            in1=imp[:, j, :], op0=ALU.mult, op1=ALU.add),
                       reads=[ab, sbb, imp_b[j]], writes=[imp_b[j]])
        need_sel = (qs * 512 + 511) >= 1024
        for j in (range(4) if need_sel else ()):
            op("dve", lambda e, j=j, smk=smk: e.tensor_tensor(out=sc[:], in0=imp[:, j, :], in1=smk[:, j, 0, :], op=ALU.mult),
               reads=[imp_b[j], smk_b], writes=[sc_b])
            op("dve", lambda e, j=j, smk=smk: e.tensor_tensor(out=sc[:], in0=sc[:], in1=smk[:, j, 1, :], op=ALU.add),
               reads=[sc_b, smk_b], writes=[sc_b])
            op("dve", lambda e: e.max(out=mx8[:, 0:8], in_=sc[:]), reads=[sc_b], writes=[mx8_b])
            op("dve", lambda e: e.match_replace(out=sc2[:], in_to_replace=mx8[:, 0:8], in_values=sc[:], imm_value=-1e9),
               reads=[sc_b, mx8_b], writes=[sc_b])
            op("dve", lambda e: e.max(out=mx8[:, 8:16], in_=sc2[:]), reads=[sc_b], writes=[mx8_b])
            op("dve", lambda e: e.tensor_scalar(out=selb[:], in0=sc[:], scalar1=mx8[:, 15:16], scalar2=1.0, op0=ALU.is_ge, op1=ALU.subtract),
               reads=[sc_b, mx8_b], writes=[selb_b])
            op("pe", lambda e: e.transpose(out=MISC[0:64, 0:128], in_=selb[:], identity=identf[:]),
               reads=[selb_b, id_b], writes=[MISC_b])
            op("act", lambda e, j=j: e.activation(out=selbT[:, j * 128:(j + 1) * 128], in_=MISC[0:64, 0:128], func=AF.Copy, scale=BIG),
               reads=[MISC_b], writes=[selbT_b])
        for r in range(4):
            qT, q_b = BB[r], BB_b[r]
            accs = attend(qT, q_b, qs, BB[4], BB_b[4], VA[0], VA_b[0], list(range(0, 4 * qs + 4)), cstrip_of,
                          extra_bias=(lambda kt: (eselb[:, kt, :], selbT[:], [esel_b, selbT_b])) if need_sel else None)
            for j in range(4):
                a, ab = accs[j]
                cf, cfb = coef_of(a, ab, 129, gt[:, j, r * 3 + 1:r * 3 + 2], gt_b)
                op("dve", lambda e, a=a, cf=cf, j=j, r=r: e.scalar_tensor_tensor(out=accS[:, r, j, :], in0=a[:, 0:128], scalar=cf,
                                                                               in1=accS[:, r, j, :], op0=ALU.mult, op1=ALU.add),
                   reads=[ab, cfb, acc_b[r][j]], writes=[acc_b[r][j]])
            ktiles = [kt for kt in range(4 * qs - 4, 4 * qs + 4) if kt >= 0]
            accs = attend(qT, q_b, qs, BB[5], BB_b[5], VA[1], VA_b[1], ktiles, wstrip_of, band=4)
            for j in range(4):
                a, ab = accs[j]
                cf, cfb = coef_of(a, ab, 129, gt[:, j, r * 3 + 2:r * 3 + 3], gt_b)
                op("dve", lambda e, a=a, cf=cf, j=j, r=r: e.scalar_tensor_tensor(out=accS[:, r, j, :], in0=a[:, 0:128], scalar=cf,
                                                                               in1=accS[:, r, j, :], op0=ALU.mult, op1=ALU.add),
                   reads=[ab, cfb, acc_b[r][j]], writes=[acc_b[r][j]])
            flush_head(2 + r, qs, r)
    return outs


def build_B(S=4096):
    nc = bass.Bass("TRN2", target_bir_lowering=False)
    NQS = S // 512
    NKT = S // 128
    dr = lambda n, s, dt, k: nc.dram_tensor(n, list(s), dt, kind=k).ap()
    Bq = dr("Bq", [2, 128, S], BF16, "ExternalInput")
    Bk = dr("Bk", [2, 128, S], BF16, "ExternalInput")
    Bv = dr("Bv", [S, 256], BF16, "ExternalInput")
    Cq = dr("Cq", [4, 128, S], BF16, "ExternalInput")
    cmpk = dr("cmpk", [128, S], BF16, "ExternalInput")
    cmpv = dr("cmpv", [128, S], BF16, "ExternalInput")
    slck = dr("slck", [128, S], BF16, "ExternalInput")
    slcv = dr("slcv", [S, 128], BF16, "ExternalInput")
    wink = dr("wink", [128, S], BF16, "ExternalInput")
    winv = dr("winv", [S, 128], BF16, "ExternalInput")
    gates = dr("gates", [S, 12], F32, "ExternalInput")
    posT = dr("posT", [128, 32], F32, "ExternalInput")
    ck_w1 = dr("ck_w1", [4096, 128], F32, "ExternalInput")
    ck_w2 = dr("ck_w2", [128, 128], F32, "ExternalInput")
    cv_w1 = dr("cv_w1", [4096, 128], F32, "ExternalInput")
    cv_w2 = dr("cv_w2", [128, 128], F32, "ExternalInput")
    WD = 384 + 128 * 16 + 512
    WW = 384 + 128 * 4 + 512
    WC = 384 + 512
    ident = dr("ident", [128, 128], F32, "ExternalInput")
    dstrip = dr("dstrip", [128, WD], F32, "ExternalInput")
    wstrip = dr("wstrip", [128, WW], F32, "ExternalInput")
    cstrip = dr("cstrip", [128, WC], F32, "ExternalInput")
    cbias = dr("cbias", [NQS, 128, 2, 512], F32, "ExternalInput")
    ovd = dr("ov", [128, 2, 64], F32, "ExternalInput")
    selmask = dr("selmask", [NKT, 128, 2, 64], F32, "ExternalInput")
    esel = dr("esel", [64, NKT, 128], F32, "ExternalInput")
    mixT = dr("mixT", [6, 128, S], BF16, "ExternalOutput")
    dbg = dr("dbg", [128, 1024], BF16, "ExternalOutput") if DEBUG else None

    io = dict(Bq=Bq, Bk=Bk, Bv=Bv, Cq=Cq, cmpk=cmpk, cmpv=cmpv, slck=slck, slcv=slcv, wink=wink, winv=winv, gates=gates,
              posT=posT, ck_w1=ck_w1, ck_w2=ck_w2, cv_w1=cv_w1, cv_w2=cv_w2, ident=ident, dstrip=dstrip, wstrip=wstrip,
              cstrip=cstrip, cbias=cbias, ov=ovd, selmask=selmask, esel=esel, mixT=mixT, dbg=dbg)
    with ExitStack() as es:
        S_ = Sched(nc, es)
        outs = emit_B(S_, make_psum(S_), io, S)
        S_.emit(final_waits=outs)
    return nc


D = 2048
EPS = 1e-6
DFF = 8192


def emit_C(S, PSP, io, NT, FINAL, mx_loader=None, perm=None):
    NSB = NT // 512
    KC = D // 128
    SCALE = 128 ** -0.5
    if perm is None:
        perm = list(range(16))
    (x, mixT, w_out, g_x, g_mem, mem, wq, wkv, wo, g_mlp, w_up, w_down, g_fin, ident, y) = (io.get(k) for k in (
        "x", "mixT", "w_out", "g_x", "g_mem", "mem", "wq", "wkv", "wo", "g_mlp", "w_up", "w_down", "g_fin", "ident", "y"))
    xs = S.sb("xs", [128, 4, D], F32); xs_b = [S.buf() for _ in range(4)]
    mx = S.sb("mx", [128, KC, 512], BF16); mx_b = S.buf()
    hT = S.sb("hT", [128, KC, 512], BF16); hT_b = [S.buf() for _ in range(4)]
    aT = S.sb("aT", [128, 32, 512], BF16); aT_b = [S.buf() for _ in range(32)]
    W = [S.sb("W%d" % i, [128, KC, 512], BF16) for i in range(2)]; W_b = [S.buf() for _ in range(2)]
    gB = S.sb("gB", [128, D], F32); gB_b = S.buf()
    hb = [S.sb("hb%d" % i, [128, D], BF16) for i in range(2)]; hb_b = [S.buf() for _ in range(2)]
    st = [S.sb("st%d" % i, [128, 8], F32) for i in range(2)]; st_b = [S.buf() for _ in range(2)]
    identb = S.sb("identb", [128, 128], BF16); ident_b = S.buf()
    ones = S.sb("ones", [128, 128], BF16); ones_b = S.buf()
    qT = S.sb("qT", [128, 4, 512], BF16); qT_b = [S.buf() for _ in range(4)]
    kT = S.sb("kT", [128, 4, 256], BF16); kT_b = S.buf()
    Vm = S.sb("Vm", [128, 2, 512], BF16); Vm_b = S.buf()
    PT = [S.sb("PT%d" % i, [128, 2, 512], BF16) for i in range(2)]; PT_b = [S.buf() for _ in range(2)]
    oT = S.sb("oT", [128, 4, 512], BF16); oT_b = [S.buf() for _ in range(4)]
    rden0 = S.sb("rden0", [128, 512], F32); rden = [rden0, rden0]; rden0_b = S.buf(); rden_b = [rden0_b, rden0_b]
    rtmp = [S.sb("rtmp%d" % i, [128, 512], F32) for i in range(2)]; rtmp_b = [S.buf() for _ in range(2)]
    PS, PS_b = PSP
    psn = [0]

    def next_ps():
        i = psn[0] % 8
        psn[0] += 1
        return PS[i], PS_b[i]

    wn = [0]

    def load_W(view, K, width):
        i = wn[0] % 2
        wn[0] += 1
        S.dma(WQ, W[i][:, 0:K, 0:width], view, writes=[W_b[i]])
        if io.get("tick"):
            io["tick"]()
        return W[i], W_b[i]

    WQ = io.get("wqueue", "pool")
    wsrc = dict(w_out=w_out, wq=wq, wkv=wkv, wo=wo, w_up=w_up, w_down=w_down)

    def wview(name, tr, tc):
        if "wtile" in io:
            return io["wtile"](name, tr, tc)
        w = wsrc[name]
        kr = min(16, w.shape[0] // 128)
        return w[tr * 2048:tr * 2048 + kr * 128, tc * 512:(tc + 1) * 512].rearrange("(k p) c -> p k c", p=128)

    evn = [0]

    def copy_any(dst, src, reads, writes):
        e = "act" if evn[0] % 2 == 0 else "dve"
        evn[0] += 1
        if e == "act":
            S.op("act", lambda en: en.activation(out=dst, in_=src, func=AF.Copy), reads=reads, writes=writes)
        else:
            S.op("dve", lambda en: en.tensor_copy(out=dst, in_=src), reads=reads, writes=writes)

    rn = [0]

    def rms_T(src, src_b, g_ap, dstT, dst_b, c0, to_out=None):
        i = rn[0] % 2
        rn[0] += 1
        S.op("act", lambda e: e.activation(out=hb[i][:], in_=src, func=AF.Square, accum_out=st[i][:, 0:1]),
             reads=[src_b], writes=[hb_b[i], st_b[i]])
        S.op("dve", lambda e: e.tensor_scalar(out=st[i][:, 1:2], in0=st[i][:, 0:1], scalar1=1.0 / D, scalar2=EPS,
                                              op0=ALU.mult, op1=ALU.add), reads=[st_b[i]], writes=[st_b[i]])
        S.op("act", lambda e: e.activation(out=st[i][:, 3:4], in_=st[i][:, 1:2], func=AF.Sqrt),
             reads=[st_b[i]], writes=[st_b[i]])
        S.op("dve", lambda e: e.reciprocal(out=st[i][:, 2:3], in_=st[i][:, 3:4]), reads=[st_b[i]], writes=[st_b[i]])
        if to_out is not None:
            S.op("dve", lambda e: e.scalar_tensor_tensor(out=to_out, in0=src, scalar=st[i][:, 2:3], in1=gB[:],
                                                         op0=ALU.mult, op1=ALU.mult),
                 reads=[src_b, st_b[i], gB_b], writes=[src_b])
            return
        S.op("dve", lambda e: e.scalar_tensor_tensor(out=hb[i][:], in0=src, scalar=st[i][:, 2:3], in1=gB[:],
                                                     op0=ALU.mult, op1=ALU.mult),
             reads=[src_b, st_b[i], gB_b], writes=[hb_b[i]])
        for q in range(KC // 4):
            p, pb = next_ps()
            pv = p[:].bitcast(BF16)
            for j in range(4):
                kc = q * 4 + j
                S.op("pe", lambda e, kc=kc, j=j, pv=pv: e.transpose(
                    out=pv[:, j * 128:(j + 1) * 128], in_=hb[i][:, kc * 128:(kc + 1) * 128], identity=identb[:]),
                    reads=[hb_b[i], ident_b], writes=[pb])
            copy_any(dstT[:, q * 4:(q + 1) * 4, c0:c0 + 128], pv[:, 0:512].rearrange("p (a b) -> p a b", a=4),
                     [pb], [dst_b])

    S.dma("pool", identb[:], ident[:, :], writes=[ident_b])
    if io.get("after_setup"):
        io["after_setup"]()
    SQ = "dve" if io.get("no_pool") else "pool"
    S.op("dve", lambda e: e.memset(ones[:], 1.0), writes=[ones_b])

    S.dma("sp", gB[:], g_mem.partition_broadcast(128), writes=[gB_b])
    for mt in range(2):
        S.dma("sp", xs[:, mt, :], mem[mt * 128:(mt + 1) * 128, :], writes=[xs_b[mt]])
        rms_T(xs[:, mt, :], xs_b[mt], None, hT, hT_b[mt], mt * 128)
    Wt, Wb = load_W(wview("wkv", 0, 0), KC, 512)
    for h in range(4):
        p, pb = next_ps()
        for kc in range(KC):
            S.op("pe", lambda e, p=p, Wt=Wt, kc=kc, h=h: e.matmul(
                p[:, 0:256], lhsT=Wt[:, kc, h * 128:(h + 1) * 128], rhs=hT[:, kc, 0:256],
                start=(kc == 0), stop=(kc == KC - 1)), reads=[Wb, hT_b[0], hT_b[1]], writes=[pb])
        copy_any(kT[:, h, :], p[:, 0:256], [pb], [kT_b])
    Wt, Wb = load_W(wview("wkv", 0, 1), KC, 512)
    for mt in range(2):
        p, pb = next_ps()
        for kc in range(KC):
            S.op("pe", lambda e, p=p, Wt=Wt, kc=kc, mt=mt: e.matmul(
                p[:], lhsT=hT[:, kc, mt * 128:(mt + 1) * 128], rhs=Wt[:, kc, :],
                start=(kc == 0), stop=(kc == KC - 1)), reads=[Wb, hT_b[mt]], writes=[pb])
        copy_any(Vm[:, mt, :], p[:], [pb], [Vm_b])

    outs = []
    for blk in range(NSB):
        t0 = blk * 512
        for t in range(4):
            S.dma("sp", xs[:, t, :], x[t0 + t * 128:t0 + (t + 1) * 128, :], writes=[xs_b[t]])
        if mx_loader is None:
            S.dma("sp", mx[:], mixT[:, :, t0:t0 + 512].rearrange("k p t -> p k t"), writes=[mx_b])
        else:
            mx_loader(S, mx, mx_b, t0)
        for cg in range(4):
            Wt, Wb = load_W(wview("w_out", 0, cg), KC, 512)
            for t in range(4):
                p, pb = next_ps()
                for kc in range(KC):
                    S.op("pe", lambda e, p=p, Wt=Wt, kc=kc, t=t: e.matmul(
                        p[:], lhsT=mx[:, kc, t * 128:(t + 1) * 128], rhs=Wt[:, perm[kc], :],
                        start=(kc == 0), stop=(kc == KC - 1)), reads=[Wb, mx_b], writes=[pb])
                S.op("dve", lambda e, p=p, t=t, cg=cg: e.tensor_tensor(
                    out=xs[:, t, cg * 512:(cg + 1) * 512], in0=p[:], in1=xs[:, t, cg * 512:(cg + 1) * 512],
                    op=ALU.add), reads=[pb, xs_b[t]], writes=[xs_b[t]])
        S.dma("sp", gB[:], g_x.partition_broadcast(128), writes=[gB_b])
        for t in range(4):
            rms_T(xs[:, t, :], xs_b[t], None, hT, hT_b[t], t * 128)
        Wt, Wb = load_W(wview("wq", 0, 0), KC, 512)
        for h in range(4):
            p, pb = next_ps()
            for kc in range(KC):
                S.op("pe", lambda e, p=p, Wt=Wt, kc=kc, h=h: e.matmul(
                    p[:], lhsT=Wt[:, kc, h * 128:(h + 1) * 128], rhs=hT[:, kc, :],
                    start=(kc == 0), stop=(kc == KC - 1)), reads=[Wb] + hT_b, writes=[pb])
            copy_any(qT[:, h, :], p[:], [pb], [qT_b[h]])
        for h in range(4):
            i = h % 2
            for mc in range(2):
                p, pb = next_ps()
                S.op("pe", lambda e, p=p, h=h, mc=mc: e.matmul(
                    p[:], lhsT=kT[:, h, mc * 128:(mc + 1) * 128], rhs=qT[:, h, :], start=True, stop=True),
                    reads=[kT_b, qT_b[h]], writes=[pb])
                S.op("act", lambda e, p=p, i=i, mc=mc: e.activation(out=PT[i][:, mc, :], in_=p[:], func=AF.Exp,
                                                                  scale=SCALE), reads=[pb], writes=[PT_b[i]])
            pd, pdb = next_ps()
            po, pob = next_ps()
            for mc in range(2):
                S.op("pe", lambda e, pd=pd, i=i, mc=mc: e.matmul(
                    pd[:], lhsT=ones[:], rhs=PT[i][:, mc, :], start=(mc == 0), stop=(mc == 1)),
                    reads=[ones_b, PT_b[i]], writes=[pdb])
            for mc in range(2):
                S.op("pe", lambda e, po=po, i=i, mc=mc, h=h: e.matmul(
                    po[:], lhsT=Vm[:, mc, h * 128:(h + 1) * 128], rhs=PT[i][:, mc, :], start=(mc == 0), stop=(mc == 1)),
                    reads=[Vm_b, PT_b[i]], writes=[pob])
            S.op("dve", lambda e, pd=pd, i=i: e.reciprocal(out=rden[i][:], in_=pd[:]), reads=[pdb], writes=[rden_b[i]])
            S.op("dve", lambda e, po=po, i=i, h=h: e.tensor_tensor(out=oT[:, h, :], in0=po[:], in1=rden[i][:], op=ALU.mult),
                 reads=[pob, rden_b[i]], writes=[oT_b[h]])
        for cg in range(4):
            Wt, Wb = load_W(wview("wo", 0, cg), 4, 512)
            for t in range(4):
                p, pb = next_ps()
                for h in range(4):
                    S.op("pe", lambda e, p=p, Wt=Wt, h=h, t=t: e.matmul(
                        p[:], lhsT=oT[:, h, t * 128:(t + 1) * 128], rhs=Wt[:, h, :],
                        start=(h == 0), stop=(h == 3)), reads=[Wb] + oT_b, writes=[pb])
                S.op("dve", lambda e, p=p, t=t, cg=cg: e.tensor_tensor(
                    out=xs[:, t, cg * 512:(cg + 1) * 512], in0=p[:], in1=xs[:, t, cg * 512:(cg + 1) * 512],
                    op=ALU.add), reads=[pb, xs_b[t]], writes=[xs_b[t]])
        S.dma("sp", gB[:], g_mlp.partition_broadcast(128), writes=[gB_b])
        for t in range(4):
            rms_T(xs[:, t, :], xs_b[t], None, hT, hT_b[t], t * 128)
        for half in range(2):
            for ug in range(8):
                c0 = half * 4096 + ug * 512
                Wt, Wb = load_W(wview("w_up", 0, c0 // 512), KC, 512)
                for j in range(4):
                    hc = ug * 4 + j
                    p, pb = next_ps()
                    for kc in range(KC):
                        S.op("pe", lambda e, p=p, Wt=Wt, kc=kc, j=j: e.matmul(
                            p[:], lhsT=Wt[:, kc, j * 128:(j + 1) * 128], rhs=hT[:, kc, :],
                            start=(kc == 0), stop=(kc == KC - 1)), reads=[Wb] + hT_b, writes=[pb])
                    i = hc % 2
                    S.op("act", lambda e, p=p, i=i: e.activation(out=rtmp[i][:], in_=p[:], func=AF.Relu),
                         reads=[pb], writes=[rtmp_b[i]])
                    S.op(SQ, lambda e, i=i, hc=hc: e.tensor_tensor(out=aT[:, hc, :], in0=rtmp[i][:], in1=rtmp[i][:],
                                                                   op=ALU.mult), reads=[rtmp_b[i]], writes=[aT_b[hc]])
            for cg in range(4):
                acc = [next_ps() for _ in range(4)]
                for hq in range(2):
                    r0 = half * 4096 + hq * 2048
                    Wt, Wb = load_W(wview("w_down", r0 // 2048, cg), KC, 512)
                    for t in range(4):
                        p, pb = acc[t]
                        for kc in range(KC):
                            hc = hq * 16 + kc
                            S.op("pe", lambda e, p=p, Wt=Wt, kc=kc, hc=hc, t=t: e.matmul(
                                p[:], lhsT=aT[:, hc, t * 128:(t + 1) * 128], rhs=Wt[:, kc, :],
                                start=(hc == 0), stop=(hc == 31)), reads=[Wb, aT_b[hc]], writes=[pb])
                for t in range(4):
                    p, pb = acc[t]
                    S.op("dve", lambda e, p=p, t=t, cg=cg: e.tensor_tensor(
                        out=xs[:, t, cg * 512:(cg + 1) * 512], in0=p[:], in1=xs[:, t, cg * 512:(cg + 1) * 512],
                        op=ALU.add), reads=[pb, xs_b[t]], writes=[xs_b[t]])
        if FINAL:
            S.dma("sp", gB[:], g_fin.partition_broadcast(128), writes=[gB_b])
            for t in range(4):
                rms_T(xs[:, t, :], xs_b[t], None, None, None, 0, to_out=xs[:, t, :])
        for t in range(4):
            outs.append(S.dma("sp", y[t0 + t * 128:t0 + (t + 1) * 128, :], xs[:, t, :], reads=[xs_b[t]]))
    return outs


def build_C(NT=2048, FINAL=False):
    nc = bass.Bass("TRN2", target_bir_lowering=False)
    NSB = NT // 512
    KC = D // 128
    dr = lambda n, s, dt, k: nc.dram_tensor(n, list(s), dt, kind=k).ap()
    x = dr("x", [NT, D], F32, "ExternalInput")
    mixT = dr("mixT", [16, 128, NT], BF16, "ExternalInput")
    w_out = dr("w_out", [D, D], F32, "ExternalInput")
    g_x = dr("g_x", [1, D], F32, "ExternalInput")
    g_mem = dr("g_mem", [1, D], F32, "ExternalInput")
    mem = dr("mem", [256, D], F32, "ExternalInput")
    wq = dr("wq", [D, 512], F32, "ExternalInput")
    wkv = dr("wkv", [D, 1024], F32, "ExternalInput")
    wo = dr("wo", [512, D], F32, "ExternalInput")
    g_mlp = dr("g_mlp", [1, D], F32, "ExternalInput")
    w_up = dr("w_up", [D, DFF], F32, "ExternalInput")
    w_down = dr("w_down", [DFF, D], F32, "ExternalInput")
    g_fin = dr("g_fin", [1, D], F32, "ExternalInput")
    ident = dr("ident", [128, 128], F32, "ExternalInput")
    y = dr("y", [NT, D], F32, "ExternalOutput")
    SCALE = 128 ** -0.5

    io = dict(x=x, mixT=mixT, w_out=w_out, g_x=g_x, g_mem=g_mem, mem=mem, wq=wq, wkv=wkv, wo=wo, g_mlp=g_mlp,
              w_up=w_up, w_down=w_down, g_fin=g_fin, ident=ident, y=y)
    with ExitStack() as es:
        S = Sched(nc, es)
        outs = emit_C(S, make_psum(S), io, NT, FINAL)
        S.emit(final_waits=outs)
    return nc


NLAYER = 4
ARENA = 92160
GROUPS = [[0, 1], [2, 3], [4, 5], [6, 7]]


class FusedLoaderB:
    FM_IDX = {"Bq": (0, 2), "Bk": (4, 2), "Cq": (8, 4), "cmpk": (16, 1), "cmpv": (18, 1), "slck": (20, 1), "wink": (22, 1)}
    TM_COL = {"Bv": (0, 256), "slcv": (512, 128), "winv": (768, 128)}

    def __init__(self, S_, exA_dst, gate_dst, sel, S):
        self.S = S_
        self.dst = exA_dst
        self.gdst = gate_dst
        self.selt = S_.sb("selt", [128, 2], F32); self.sel_b = S_.buf()
        S_.dma("sp", self.selt[:], sel[:, :], writes=[self.sel_b])
        self.T = [S_.sb("Tfm%d" % i, [128, S], BF16) for i in range(2)]; self.T_b = [S_.buf() for _ in range(2)]
        self.Tg = S_.sb("Tg", [128, 4, 12], F32); self.Tg_b = S_.buf()
        self.n = 0
        self.half = S // 2

    def _blend(self, dst_ap, dst_b, t_ap, t_b):
        S_ = self.S
        S_.op("dve", lambda e: e.tensor_scalar(out=t_ap, in0=t_ap, scalar1=self.selt[:, 1:2], scalar2=None, op0=ALU.mult),
              reads=[t_b, self.sel_b], writes=[t_b])
        S_.op("dve", lambda e: e.scalar_tensor_tensor(out=dst_ap, in0=dst_ap, scalar=self.selt[:, 0:1], in1=t_ap,
                                                      op0=ALU.mult, op1=ALU.add),
              reads=[dst_b, t_b, self.sel_b], writes=[dst_b])

    def fm(self, tile, b, kind, i):
        base, stride = self.FM_IDX[kind]
        ii = self.n % 2
        self.n += 1
        T, Tb = self.T[ii], self.T_b[ii]
        H = self.half
        for r in range(2):
            for g, (dstt, dstb) in enumerate(((tile, b), (T, Tb))):
                R = (base + g * stride + i) * 128
                row = ((R // 512) * 2 + r) * 512 + (R % 512)
                self.S.dma("sp", dstt[:, r * H:(r + 1) * H], self.dst[row:row + 128, :], writes=[dstb])
        self._blend(tile[:], b, T[:], Tb)

    def tm(self, tile, b, kind, i):
        base, gstride = self.TM_COL[kind]
        ii = self.n % 2
        self.n += 1
        T, Tb = self.T[ii], self.T_b[ii]
        Tv = T[:].rearrange("p (k d) -> p k d", d=128)
        nk = self.half // 128
        hk = nk // 2
        for r in range(2):
            for piece in range(2):
                row0 = ((6 + piece) * 2 + r) * 512
                v = self.dst[row0:row0 + 512, :].rearrange("r (two c) -> (r two) c", two=2)
                k0 = r * nk + piece * hk
                for g in range(2):
                    c0 = base + g * gstride + i * 128
                    src = v[:, c0:c0 + 128].rearrange("(k p) d -> p k d", p=128)
                    if g == 0:
                        self.S.dma("sp", tile[:, k0:k0 + hk, 0:128], src, writes=[b])
                    else:
                        self.S.dma("sp", Tv[:, k0:k0 + hk, :], src, writes=[Tb])
        self._blend(tile[:, :, 0:128], b, Tv, Tb)

    def gates(self, gt, b, qs):
        nq = self.half // 512
        r, ql = qs // nq, qs % nq
        for g, (dstt, dstb) in enumerate(((gt, b), (self.Tg, self.Tg_b))):
            src = self.gdst[r * self.half + ql * 512:r * self.half + (ql + 1) * 512, 12 * g:12 * g + 12]
            self.S.dma("sp", dstt[:], src.rearrange("(j p) c -> p j c", p=128), writes=[dstb])
        self._blend(gt[:], b, self.Tg[:], self.Tg_b)


def build_all(NL=NLAYER):
    nc = bass.Bass("TRN2", target_bir_lowering=False)
    NT, S, D_ = 2048, 4096, 2048
    ein = lambda n, s, dt=F32: nc.dram_tensor(n, list(s), dt, kind="ExternalInput").ap()
    x = ein("x", [NT, D_]); mem = ein("mem", [256, D_]); sel = ein("sel", [128, 2])
    norm_mix = ein("norm_mix", [NL, 1, D_]); w_in = ein("w_in", [NL, D_, INW])
    ln_g = ein("ln_g", [NL, 1, 512]); ln_b = ein("ln_b", [NL, 1, 512])
    wsT = ein("wsT", [NL, 128, 4, 128]); bsT = ein("bsT", [NL, 128, 4]); posT = ein("posT", [NL, 128, 32])
    ck_w1 = ein("ck_w1", [NL, 4096, 128]); ck_w2 = ein("ck_w2", [NL, 128, 128])
    cv_w1 = ein("cv_w1", [NL, 4096, 128]); cv_w2 = ein("cv_w2", [NL, 128, 128])
    w_out = ein("w_out", [NL, D_, D_]); g_x = ein("g_x", [NL, 1, D_]); g_mem = ein("g_mem", [NL, 1, D_])
    wq = ein("wq", [NL, D_, 512]); wkv = ein("wkv", [NL, D_, 1024]); wo = ein("wo", [NL, 512, D_])
    g_mlp = ein("g_mlp", [NL, 1, D_]); w_up = ein("w_up", [NL, D_, DFF]); w_down = ein("w_down", [NL, DFF, D_])
    g_fin = ein("g_fin", [1, D_])
    tril = ein("tril", [128, 128]); ident = ein("ident", [128, 128])
    WD = 384 + 128 * 16 + 512; WW = 384 + 128 * 4 + 512; WC = 384 + 512
    dstrip = ein("dstrip", [128, WD], BF16); wstrip = ein("wstrip", [128, WW], BF16); cstrip = ein("cstrip", [128, WC], BF16)
    cbias = ein("cbias", [S // 512, 128, 2, 512], BF16); ovd = ein("ov", [128, 2, 64])
    selmask = ein("selmask", [S // 128, 128, 2, 64]); esel = ein("esel", [64, S // 128, 128], BF16)
    y = nc.dram_tensor("y", [NT, D_], F32, kind="ExternalOutput").ap()
    xs_t = nc.dram_tensor("xs_s", [NT, D_], F32)
    mixA_t = nc.dram_tensor("mixA_s", [4, 128, NT], BF16)
    exA_src_t = nc.dram_tensor("exA_src", [4096, NT], BF16)
    exA_dst_t = nc.dram_tensor("exA_dst", [8192, NT], BF16)
    gate_src_t = nc.dram_tensor("gate_src", [NT, 24], F32)
    gate_dst_t = nc.dram_tensor("gate_dst", [2 * NT, 24], F32)
    exB_src_t = nc.dram_tensor("exB_src", [768, S], BF16)
    exB_dst_t = nc.dram_tensor("exB_dst", [1536, S], BF16)
    WSPEC = {"w_out": (1, 4, 16), "wq": (1, 1, 16), "wkv": (1, 2, 16), "wo": (1, 4, 4), "w_up": (1, 16, 16), "w_down": (4, 4, 16)}
    wbf2 = [{k: nc.dram_tensor("wbf%d_%s" % (par, k), [ntr, ntc, 128, kr, 512], BF16).ap() for k, (ntr, ntc, kr) in WSPEC.items()}
            for par in range(2)]
    xs_s, mixA_s, exA_src, exA_dst = xs_t.ap(), mixA_t.ap(), exA_src_t.ap(), exA_dst_t.ap()
    gate_src, gate_dst, exB_src, exB_dst = gate_src_t.ap(), gate_dst_t.ap(), exB_src_t.ap(), exB_dst_t.ap()

    perm = [0, 1, 2, 3]
    for k in range(12):
        c, g, e = k // 4, (k // 2) % 2, k % 2
        i = 2 * c + e
        perm.append(4 + 2 * g + i if i < 2 else 8 + 4 * g + (i - 2))

    def precast_jobs(l):
        wl = {"w_out": w_out[l], "wq": wq[l], "wkv": wkv[l], "wo": wo[l], "w_up": w_up[l], "w_down": w_down[l]}
        jobs = []
        for k, (ntr, ntc, kr) in WSPEC.items():
            for tr in range(ntr):
                for tc in range(ntc):
                    srcv = wl[k][tr * 2048:tr * 2048 + kr * 128, tc * 512:(tc + 1) * 512].rearrange("(k p) c -> p k c", p=128)
                    jobs.append((wbf2[l % 2][k][tr, tc], srcv))
        return jobs

    def precast(l, bufs):
        wl = {"w_out": w_out[l], "wq": wq[l], "wkv": wkv[l], "wo": wo[l], "w_up": w_up[l], "w_down": w_down[l]}
        for k, (ntr, ntc, kr) in WSPEC.items():
            for tr in range(ntr):
                for tc in range(ntc):
                    srcv = wl[k][tr * 2048:tr * 2048 + kr * 128, tc * 512:(tc + 1) * 512].rearrange("(k p) c -> p k c", p=128)
                    Sc.dma("pool", wbf2[l % 2][k][tr, tc], srcv, reads=bufs)

    with ExitStack() as es:
        Sc = Sched(nc, es, arena_elems=ARENA)
        PSP = make_psum(Sc)
        outs = []
        for l in range(NL):
            x_src = x if l == 0 else xs_s
            x_dst = y if l == NL - 1 else xs_s
            Sc.phase_reset()
            ioA = dict(x=x_src, gmix=norm_mix[l], w_in=w_in[l], ln_g=ln_g[l], ln_b=ln_b[l], wsT=wsT[l], bsT=bsT[l],
                       tril=tril, ident=ident, o_mixA=mixA_s,
                       o_fm=exA_src[0:3072, :].rearrange("(h p) t -> h p t", p=128),
                       o_tm=exA_src[3072:4096, :].rearrange("r (two c) -> (r two) c", two=2),
                       o_gate=gate_src)
            emit_A(Sc, PSP, ioA, NT)
            Sc.allgather([(exA_src[i * 512:(i + 1) * 512, :], exA_dst[i * 1024:(i + 1) * 1024, :]) for i in range(8)]
                         + [(gate_src, gate_dst)], GROUPS)
            Sc.phase_reset()
            ld = FusedLoaderB(Sc, exA_dst, gate_dst, sel, S)
            ioB = dict(posT=posT[l], ck_w1=ck_w1[l], ck_w2=ck_w2[l], cv_w1=cv_w1[l], cv_w2=cv_w2[l], ident=ident,
                       dstrip=dstrip, wstrip=wstrip, cstrip=cstrip, cbias=cbias, ov=ovd, selmask=selmask, esel=esel,
                       mixT=exB_src.rearrange("(h p) t -> h p t", p=128), bf16_consts=True)

            if l == 0:
                ioB["after_loads"] = lambda bufs: precast(0, bufs)
            emit_B(Sc, PSP, ioB, S, ld=ld)
            Sc.allgather([(exB_src[i * 256:(i + 1) * 256, :], exB_dst[i * 512:(i + 1) * 512, :]) for i in range(3)], GROUPS)
            Sc.phase_reset()
            selt = Sc.sb("seltC", [128, 2], F32); sel_b = Sc.buf()
            Sc.dma("sp", selt[:], sel[:, :], writes=[sel_b])
            Tmx = Sc.sb("Tmx", [128, 12, 512], BF16); Tmx_b = Sc.buf()
            dv = exB_dst.rearrange("(k p) t -> p k t", p=128)

            def mx_loader(S_, mx, mx_b, t0, selt=selt, sel_b=sel_b, Tmx=Tmx, Tmx_b=Tmx_b, dv=dv):
                S_.dma("sp", mx[:, 0:4, :], mixA_s[:, :, t0:t0 + 512].rearrange("k p t -> p k t"), writes=[mx_b])
                S_.dma("sp", mx[:, 4:16, :], dv[:, :, t0:t0 + 512], writes=[mx_b])
                S_.dma("sp", Tmx[:], dv[:, :, NT + t0:NT + t0 + 512], writes=[Tmx_b])
                S_.op("dve", lambda e: e.tensor_scalar(out=Tmx[:], in0=Tmx[:], scalar1=selt[:, 1:2], scalar2=None, op0=ALU.mult),
                      reads=[Tmx_b, sel_b], writes=[Tmx_b])
                S_.op("dve", lambda e: e.scalar_tensor_tensor(out=mx[:, 4:16, :], in0=mx[:, 4:16, :], scalar=selt[:, 0:1], in1=Tmx[:],
                                                              op0=ALU.mult, op1=ALU.add),
                      reads=[mx_b, Tmx_b, sel_b], writes=[mx_b])

            ioC = dict(x=x_src, w_out=w_out[l], g_x=g_x[l], g_mem=g_mem[l], mem=mem, wq=wq[l], wkv=wkv[l], wo=wo[l],
                       g_mlp=g_mlp[l], w_up=w_up[l], w_down=w_down[l], g_fin=g_fin, ident=ident, y=x_dst,
                       wqueue="sp", wtile=lambda name, tr, tc, wbf=wbf2[l % 2]: wbf[name][tr, tc], no_pool=True)
            if l + 1 < NL:
                jobs = precast_jobs(l + 1)
                mark = Sc.sb("mark", [128, 8], F32)
                state = {"n": 0}

                def tick(jobs=jobs, mark=mark, state=state):
                    n = state["n"]
                    state["n"] += 1
                    if n % 3 == 0 and n // 3 < len(jobs):
                        mb = Sc.buf()
                        Sc.op("dve", lambda e: e.memset(mark[:, 0:1], 0.0), writes=[mb])
                        dstv, srcv = jobs[n // 3]
                        Sc.dma("pool", dstv, srcv, reads=[mb])

                ioC["tick"] = tick
            outs = emit_C(Sc, PSP, ioC, NT, FINAL=(l == NL - 1), mx_loader=mx_loader, perm=perm)
            Sc.barrier()
        Sc.emit(final_waits=outs)
    return nc

_NC = {}


def kernel(x, mem, norm_mix, w_in, gmlp_ln_g, gmlp_ln_b, gmlp_w_s, gmlp_b_s, cmp_pos,
           cmp_k_w1, cmp_k_w2, cmp_v_w1, cmp_v_w2, w_out, norm_xattn, norm_mem,
           xattn_wq, xattn_wkv, xattn_wo, norm_mlp, w_up, w_down, final_norm):
    f32 = np.float32
    A = lambda a: np.ascontiguousarray(np.asarray(a), dtype=f32)
    if "nc" not in _NC:
        _NC["nc"] = build_all(NLAYER)
    nc = _NC["nc"]
    L = NLAYER
    shared = dict(
        norm_mix=A(norm_mix).reshape(L, 1, -1), w_in=A(w_in), ln_g=A(gmlp_ln_g).reshape(L, 1, -1),
        ln_b=A(gmlp_ln_b).reshape(L, 1, -1), wsT=A(np.asarray(gmlp_w_s).transpose(0, 3, 1, 2)),
        bsT=A(np.asarray(gmlp_b_s).transpose(0, 2, 1)), posT=A(np.asarray(cmp_pos).transpose(0, 2, 1)),
        ck_w1=A(cmp_k_w1), ck_w2=A(cmp_k_w2), cv_w1=A(cmp_v_w1), cv_w2=A(cmp_v_w2),
        w_out=A(w_out), g_x=A(norm_xattn).reshape(L, 1, -1), g_mem=A(norm_mem).reshape(L, 1, -1),
        wq=A(xattn_wq), wkv=A(xattn_wkv), wo=A(xattn_wo), g_mlp=A(norm_mlp).reshape(L, 1, -1),
        w_up=A(w_up), w_down=A(w_down), g_fin=A(final_norm).reshape(1, -1),
        tril=np.triu(np.ones((128, 128), f32)))
    import ml_dtypes
    for k, v in consts_B(4096).items():
        if k in ("dstrip", "wstrip", "cstrip", "cbias", "esel"):
            v = v.astype(ml_dtypes.bfloat16)
        shared[k] = np.ascontiguousarray(v)
    x = np.asarray(x)
    mem = np.asarray(mem)
    in_maps = []
    for c in range(8):
        b, hh = c // 2, c % 2
        sel = np.zeros((128, 2), f32)
        sel[:, hh] = 1.0
        d = dict(x=A(x[b, hh * 2048:(hh + 1) * 2048]), mem=A(mem[b]), sel=sel)
        d.update(shared)
        in_maps.append(d)
    res = run_bass_kernel_spmd(nc, in_maps, core_ids=list(range(8)))
    out = np.empty((4, 4096, 2048), f32)
    for c in range(8):
        out[c // 2, (c % 2) * 2048:(c % 2 + 1) * 2048] = res.results[c]["y"]
    return out
```

```python
from concourse.bass_utils import run_bass_kernel_spmd
import numpy as np
import concourse.bass as bass
import concourse.mybir as mybir
from contextlib import ExitStack

F32 = mybir.dt.float32
BF16 = mybir.dt.bfloat16
AF = mybir.ActivationFunctionType
ALU = mybir.AluOpType
AX = mybir.AxisListType

STRICT_SAME_ENGINE = True


class Buf:
    __slots__ = ("name", "w", "r")

    def __init__(self, name):
        self.name = name
        self.w = None
        self.r = []


class Op:
    __slots__ = ("eng", "idx", "fn", "waits", "needed", "val", "kind", "semh", "key")

    def __init__(self, eng, idx, fn, kind="c"):
        self.eng = eng
        self.idx = idx
        self.fn = fn
        self.waits = []
        self.needed = False
        self.val = None
        self.kind = kind
        self.semh = None
        self.key = None


class Sched:
    ENGS = ("pe", "act", "dve", "pool", "sp")
    NDMA = {"sp": 20, "pool": 12}

    def __init__(self, nc, es, arena_elems=None):
        self.nc = nc
        self.es = es
        self.ops = {e: [] for e in self.ENGS}
        self.seen = {e: {} for e in self.ENGS}
        self.csem = {e: es.enter_context(nc.semaphore("cs_" + e)) for e in self.ENGS}
        self.dsem = {q: [es.enter_context(nc.semaphore("ds_%s%d" % (q, i))) for i in range(n)]
                     for q, n in self.NDMA.items()}
        self.ccsem = es.enter_context(nc.semaphore("ccs"))
        self.cccount = 0
        self.lastcc = None
        self.dcount = {q: 0 for q in self.NDMA}
        self.dlast = {q: [None] * n for q, n in self.NDMA.items()}
        self.lastc = {e: None for e in self.ENGS}
        self.nbuf = 0
        self.arena = None
        self.aoff = 0
        if arena_elems:
            self.arena = es.enter_context(nc.sbuf_tensor("arena", [128, arena_elems], BF16))
            self.arena_elems = arena_elems
        self.fence = []
        self.fence_id = 0
        self.passed = {e: 0 for e in self.ENGS}

    def sb(self, name, shape, dt):
        if self.arena is None:
            return self.es.enter_context(self.nc.sbuf_tensor(name, list(shape), dt))
        n = 1
        for s in shape[1:]:
            n *= s
        esz = 4 if dt == F32 else 2
        nb = (n * esz + 31) // 32 * 32
        o = self.aoff
        self.aoff += nb // 2
        assert self.aoff <= self.arena_elems, ("arena overflow", name, self.aoff)
        v = self.arena[0:shape[0], o:o + n * esz // 2]
        if dt != BF16:
            v = v.bitcast(dt)
        if len(shape) == 3:
            v = v.rearrange("p (a b) -> p a b", a=shape[1])
        elif len(shape) == 4:
            v = v.rearrange("p (a b c) -> p a b c", a=shape[1], b=shape[2])
        return v

    def phase_reset(self):
        self.aoff = 0

    def ps(self, name, shape, dt):
        return self.es.enter_context(self.nc.psum_tensor(name, list(shape), dt))

    def buf(self, name=None):
        self.nbuf += 1
        return Buf(name or "b%d" % self.nbuf)

    def barrier(self):
        f = []
        for e in self.ENGS:
            if self.lastc[e] is not None:
                f.append(self.lastc[e])
        for q in self.NDMA:
            for o in self.dlast[q]:
                if o is not None:
                    f.append(o)
        if self.lastcc is not None:
            f.append(self.lastcc)
        self.fence = f
        self.fence_id += 1

    def _dep(self, op, dep, force=False):
        if dep is None or dep is op:
            return
        if dep.kind == "c" and dep.eng == op.eng:
            if op.eng in ("pe", "sp") or not (STRICT_SAME_ENGINE or force):
                return
        seen = self.seen[op.eng]
        if seen.get(dep.key, -1) >= dep.idx:
            return
        seen[dep.key] = dep.idx
        dep.needed = True
        op.waits.append(dep)

    def op(self, eng, fn, reads=(), writes=(), kind="c"):
        lst = self.ops[eng]
        o = Op(eng, len(lst), fn, kind=kind)
        if kind == "d":
            q = eng
            n = self.NDMA[q]
            j = self.dcount[q] % n
            o.semh = self.dsem[q][j]
            o.key = ("d", q, j)
            o.idx = self.dcount[q] // n
            o.val = 16 * (o.idx + 1)
            prev = self.dlast[q][j]
            if prev is not None:
                self._dep(o, prev)
            self.dlast[q][j] = o
            self.dcount[q] += 1
        elif kind == "cc":
            o.semh = self.ccsem
            o.key = ("cc",)
            o.idx = self.cccount
            self.cccount += 1
            o.val = self.cccount
            self.lastcc = o
        else:
            o.semh = self.csem[eng]
            o.key = ("c", eng)
            self.lastc[eng] = o
        if self.passed[eng] < self.fence_id:
            for d in self.fence:
                self._dep(o, d, force=True)
            self.passed[eng] = self.fence_id
        for b in reads:
            self._dep(o, b.w)
        for b in writes:
            self._dep(o, b.w)
            for r in b.r:
                self._dep(o, r)
        for b in reads:
            b.r.append(o)
        for b in writes:
            b.w = o
            b.r = []
        lst.append(o)
        return o

    def dma(self, q, out, in_, reads=(), writes=()):
        return self.op(q, lambda e: e.dma_start(out=out, in_=in_), reads, writes, kind="d")

    def allgather(self, pairs, groups, post_barrier=True):
        self.barrier()
        o = None
        for src, dst in pairs:
            o = self.op("pool", lambda e, src=src, dst=dst: e.collective_compute(
                "AllGather", ALU.bypass, replica_groups=groups, ins=[src], outs=[dst]), kind="cc")
        if post_barrier:
            self.barrier()
        return o

    def emit(self, final_waits=()):
        nc = self.nc
        for e in self.ENGS:
            c = 0
            for o in self.ops[e]:
                if o.kind == "c" and o.needed:
                    c += 1
                    o.val = c
        engmap = {"pe": "tensor", "act": "scalar", "dve": "vector", "pool": "gpsimd", "sp": "sync"}

        def run(e, engine):
            for o in self.ops[e]:
                for d in o.waits:
                    engine.wait_ge(d.semh, d.val)
                ins = o.fn(engine)
                if o.kind == "d":
                    ins.then_inc(o.semh, 16)
                elif o.kind == "cc":
                    ins.then_inc(o.semh, 1)
                elif o.needed:
                    ins.then_inc(o.semh, 1)
            if e == "sp":
                best = {}
                for d in final_waits:
                    if d.key not in best or best[d.key].val < d.val:
                        best[d.key] = d
                for d in best.values():
                    engine.wait_ge(d.semh, d.val)

        with nc.Block() as block:
            for e in self.ENGS:
                getattr(block, engmap[e])(lambda engine, e=e: run(e, engine))


D = 2048
HD = 128
INW = 5144
EPS = 1e-6


def make_psum(S):
    PS = [S.ps("ps%d" % i, [128, 512], F32) for i in range(8)]
    PS_b = [S.buf() for _ in range(8)]
    return PS, PS_b


def emit_A(S, PSP, io, NT):
    NTT = NT // 128
    NSB = NT // 512
    KC = D // 128
    x, gmix, w_in, ln_g, ln_b, wsT, bsT, tril, ident = (io[k] for k in ("x", "gmix", "w_in", "ln_g", "ln_b", "wsT", "bsT", "tril", "ident"))
    o_mixA, o_fm, o_tm, o_gate = (io[k] for k in ("o_mixA", "o_fm", "o_tm", "o_gate"))
    hT = S.sb("hT", [128, KC, NT], BF16)
    hT_b = [S.buf() for _ in range(NTT)]
    gB = S.sb("gB", [128, D], F32); gB_b = S.buf()
    identb = S.sb("identb", [128, 128], BF16); ident_b = S.buf()
    xt = [S.sb("xt%d" % i, [128, D], F32) for i in range(2)]
    xt_b = [S.buf() for _ in range(2)]
    hb = [S.sb("hb%d" % i, [128, D], BF16) for i in range(2)]
    hb_b = [S.buf() for _ in range(2)]
    junk = S.sb("junk", [128, D], BF16); junk_b = S.buf()
    st = [S.sb("st%d" % i, [128, 8], F32) for i in range(2)]
    st_b = [S.buf() for _ in range(2)]
    W = [S.sb("W%d" % i, [128, KC, 512], BF16) for i in range(3)]
    W_b = [S.buf() for _ in range(3)]
    PS, PS_b = PSP
    psn = [0]

    def next_ps():
        i = psn[0] % 8
        psn[0] += 1
        return PS[i], PS_b[i]

    outs = []
    S.dma("sp", gB[:], gmix.partition_broadcast(128), writes=[gB_b])
    S.dma("pool", identb[:], ident[:, :], writes=[ident_b])

    for tt in range(NTT):
        i = tt % 2
        S.dma("sp", xt[i][:], x[tt * 128:(tt + 1) * 128, :], writes=[xt_b[i]])
        S.op("act", lambda e, i=i: e.activation(out=junk[:], in_=xt[i][:], func=AF.Square,
                                                accum_out=st[i][:, 0:1]),
             reads=[xt_b[i]], writes=[junk_b, st_b[i]])
        S.op("dve", lambda e, i=i: e.tensor_scalar(out=st[i][:, 1:2], in0=st[i][:, 0:1], scalar1=1.0 / D,
                                                   scalar2=EPS, op0=ALU.mult, op1=ALU.add),
             reads=[st_b[i]], writes=[st_b[i]])
        S.op("act", lambda e, i=i: e.activation(out=st[i][:, 3:4], in_=st[i][:, 1:2], func=AF.Sqrt),
             reads=[st_b[i]], writes=[st_b[i]])
        S.op("dve", lambda e, i=i: e.reciprocal(out=st[i][:, 2:3], in_=st[i][:, 3:4]),
             reads=[st_b[i]], writes=[st_b[i]])
        S.op("dve", lambda e, i=i: e.scalar_tensor_tensor(out=hb[i][:], in0=xt[i][:], scalar=st[i][:, 2:3],
                                                          in1=gB[:], op0=ALU.mult, op1=ALU.mult),
             reads=[xt_b[i], st_b[i], gB_b], writes=[hb_b[i]])
        for q in range(KC // 4):
            p, pb = next_ps()
            pv = p[:].bitcast(BF16)
            for j in range(4):
                kc = q * 4 + j
                S.op("pe", lambda e, i=i, kc=kc, j=j, pv=pv: e.transpose(
                    out=pv[:, j * 128:(j + 1) * 128], in_=hb[i][:, kc * 128:(kc + 1) * 128], identity=identb[:]),
                    reads=[hb_b[i], ident_b], writes=[pb])
            eng = "act" if q % 2 == 0 else "dve"
            if eng == "act":
                S.op("act", lambda e, q=q, tt=tt, pv=pv: e.activation(
                    out=hT[:, q * 4:(q + 1) * 4, tt * 128:(tt + 1) * 128],
                    in_=pv[:, 0:512].rearrange("p (a b) -> p a b", a=4), func=AF.Copy),
                    reads=[pb], writes=[hT_b[tt]])
            else:
                S.op("dve", lambda e, q=q, tt=tt, pv=pv: e.tensor_copy(
                    out=hT[:, q * 4:(q + 1) * 4, tt * 128:(tt + 1) * 128],
                    in_=pv[:, 0:512].rearrange("p (a b) -> p a b", a=4)),
                    reads=[pb], writes=[hT_b[tt]])

    wn = [0]

    def load_W(c0, width):
        i = wn[0] % 3
        wn[0] += 1
        S.dma("pool", W[i][:, :, 0:width], w_in[:, c0:c0 + width].rearrange("(k p) c -> p k c", p=128),
              writes=[W_b[i]])
        return W[i], W_b[i]

    ostage = [S.sb("os%d" % i, [128, 512], BF16) for i in range(4)]
    ostage_b = [S.buf() for _ in range(4)]
    osn = [0]

    def next_os():
        i = osn[0] % 4
        osn[0] += 1
        return ostage[i], ostage_b[i]

    evn = [0]

    def evac(dst, dst_b, src, src_b, extra_reads=()):
        e = "act" if evn[0] % 2 == 0 else "dve"
        evn[0] += 1
        if e == "act":
            S.op("act", lambda en: en.activation(out=dst, in_=src, func=AF.Copy),
                 reads=[src_b] + list(extra_reads), writes=[dst_b])
        else:
            S.op("dve", lambda en: en.tensor_copy(out=dst, in_=src),
                 reads=[src_b] + list(extra_reads), writes=[dst_b])

    def fm_group(c0, head0, nheads):
        Wt, Wb = load_W(c0, nheads * 128)
        for j in range(nheads):
            for sb_ in range(NSB):
                p, pb = next_ps()
                for kc in range(KC):
                    S.op("pe", lambda e, p=p, Wt=Wt, kc=kc, j=j, sb_=sb_: e.matmul(
                        p[:], lhsT=Wt[:, kc, j * 128:(j + 1) * 128], rhs=hT[:, kc, sb_ * 512:(sb_ + 1) * 512],
                        start=(kc == 0), stop=(kc == KC - 1)),
                        reads=[Wb] + hT_b[sb_ * 4:(sb_ + 1) * 4], writes=[pb])
                o, ob = next_os()
                evac(o[:], ob, p[:], pb)
                outs.append(S.dma("sp", o_fm[head0 + j, :, sb_ * 512:(sb_ + 1) * 512], o[:], reads=[ob]))

    def tm_group(c0, width, oc0):
        Wt, Wb = load_W(c0, width)
        for tt in range(NTT):
            p, pb = next_ps()
            for kc in range(KC):
                S.op("pe", lambda e, p=p, Wt=Wt, kc=kc, tt=tt: e.matmul(
                    p[:, 0:width], lhsT=hT[:, kc, tt * 128:(tt + 1) * 128], rhs=Wt[:, kc, 0:width],
                    start=(kc == 0), stop=(kc == KC - 1)),
                    reads=[Wb, hT_b[tt]], writes=[pb])
            o, ob = next_os()
            evac(o[:, 0:width], ob, p[:, 0:width], pb)
            outs.append(S.dma("sp", o_tm[tt * 128:(tt + 1) * 128, oc0:oc0 + width], o[:, 0:width], reads=[ob]))

    fm_group(1024, 0, 4)
    fm_group(1536, 4, 4)
    tm_group(2048, 512, 0)
    fm_group(2560, 8, 4)
    fm_group(3072, 12, 4)
    fm_group(3584, 16, 4)
    fm_group(4096, 20, 2)
    fm_group(4608, 22, 2)
    tm_group(4352, 256, 512)
    tm_group(4864, 256, 768)
    Wt, Wb = load_W(5120, 24)
    gst = [S.sb("gst%d" % i, [128, 24], F32) for i in range(2)]; gst_b = [S.buf() for _ in range(2)]
    for tt in range(NTT):
        i = tt % 2
        p, pb = next_ps()
        for kc in range(KC):
            S.op("pe", lambda e, p=p, Wt=Wt, kc=kc, tt=tt: e.matmul(
                p[:, 0:24], lhsT=hT[:, kc, tt * 128:(tt + 1) * 128], rhs=Wt[:, kc, 0:24],
                start=(kc == 0), stop=(kc == KC - 1)),
                reads=[Wb, hT_b[tt]], writes=[pb])
        S.op("act", lambda e, i=i, p=p: e.activation(out=gst[i][:], in_=p[:, 0:24], func=AF.Sigmoid),
             reads=[pb], writes=[gst_b[i]])
        outs.append(S.dma("sp", o_gate[tt * 128:(tt + 1) * 128, :], gst[i][:], reads=[gst_b[i]]))
    def do_gmlp():
        lngB = S.sb("lngB", [128, 512], F32); lnbB = S.sb("lnbB", [128, 512], F32); ln_bb = S.buf()
        S.dma("sp", lngB[:], ln_g.partition_broadcast(128), writes=[ln_bb])
        S.dma("sp", lnbB[:], ln_b.partition_broadcast(128), writes=[ln_bb])
        wsf = S.sb("wsf", [128, 4, 128], F32); trilf = S.sb("trilf", [128, 128], F32)
        wsb = S.sb("wsb", [128, 4, 128], BF16); ws_b = S.buf(); wsf_b = S.buf()
        bsb = S.sb("bsb", [128, 4], F32); bs_b = S.buf()
        S.dma("sp", wsf[:], wsT[:, :, :], writes=[wsf_b])
        S.dma("sp", trilf[:], tril[:, :], writes=[wsf_b])
        S.dma("sp", bsb[:], bsT[:, :], writes=[bs_b])
        for g in range(4):
            S.op("dve", lambda e, g=g: e.tensor_tensor(out=wsb[:, g, :], in0=wsf[:, g, :], in1=trilf[:], op=ALU.mult),
                 reads=[wsf_b], writes=[ws_b])
        Wu, Wub = load_W(0, 512)
        Wv, Wvb = load_W(512, 512)
        ug = [S.sb("ug%d" % i, [128, 512], F32) for i in range(2)]; ug_b = [S.buf() for _ in range(2)]
        vg = [S.sb("vg%d" % i, [128, 512], F32) for i in range(2)]; vg_b = [S.buf() for _ in range(2)]
        vn = [S.sb("vn%d" % i, [128, 512], BF16) for i in range(2)]; vn_b = [S.buf() for _ in range(2)]
        oa = [S.sb("oa%d" % i, [128, 512], BF16) for i in range(2)]; oa_b = [S.buf() for _ in range(2)]
        bst = [S.sb("bst%d" % i, [128, 8], F32) for i in range(2)]; bst_b = [S.buf() for _ in range(2)]
        for tt in range(NTT):
            i = tt % 2
            pu, pub = next_ps()
            pvv, pvb = next_ps()
            for (p, pb, Wt, Wb) in ((pu, pub, Wu, Wub), (pvv, pvb, Wv, Wvb)):
                for kc in range(KC):
                    S.op("pe", lambda e, p=p, Wt=Wt, kc=kc, tt=tt: e.matmul(
                        p[:], lhsT=hT[:, kc, tt * 128:(tt + 1) * 128], rhs=Wt[:, kc, :],
                        start=(kc == 0), stop=(kc == KC - 1)),
                        reads=[Wb, hT_b[tt]], writes=[pb])
            S.op("act", lambda e, i=i, pu=pu: e.activation(out=ug[i][:], in_=pu[:], func=AF.Gelu_apprx_tanh),
                 reads=[pub], writes=[ug_b[i]])
            S.op("act", lambda e, i=i, pvv=pvv: e.activation(out=vg[i][:], in_=pvv[:], func=AF.Gelu_apprx_tanh),
                 reads=[pvb], writes=[vg_b[i]])
            S.op("dve", lambda e, i=i: e.bn_stats(out=bst[i][:, 0:6], in_=vg[i][:]), reads=[vg_b[i]], writes=[bst_b[i]])
            S.op("dve", lambda e, i=i: e.bn_aggr(out=bst[i][:, 6:8], in_=bst[i][:, 0:6]), reads=[bst_b[i]], writes=[bst_b[i]])
            S.op("dve", lambda e, i=i: e.tensor_scalar(out=bst[i][:, 1:2], in0=bst[i][:, 7:8], scalar1=EPS, scalar2=None,
                                                       op0=ALU.add), reads=[bst_b[i]], writes=[bst_b[i]])
            S.op("act", lambda e, i=i: e.activation(out=bst[i][:, 2:3], in_=bst[i][:, 1:2], func=AF.Sqrt),
                 reads=[bst_b[i]], writes=[bst_b[i]])
            S.op("dve", lambda e, i=i: e.reciprocal(out=bst[i][:, 0:1], in_=bst[i][:, 2:3]),
                 reads=[bst_b[i]], writes=[bst_b[i]])
            S.op("dve", lambda e, i=i: e.tensor_scalar(out=vg[i][:], in0=vg[i][:], scalar1=bst[i][:, 6:7],
                                                       scalar2=bst[i][:, 0:1], op0=ALU.subtract, op1=ALU.mult),
                 reads=[vg_b[i], bst_b[i]], writes=[vg_b[i]])
            S.op("dve", lambda e, i=i: e.tensor_tensor(out=vg[i][:], in0=vg[i][:], in1=lngB[:], op=ALU.mult),
                 reads=[vg_b[i], ln_bb], writes=[vg_b[i]])
            S.op("dve", lambda e, i=i: e.tensor_tensor(out=vn[i][:], in0=vg[i][:], in1=lnbB[:], op=ALU.add),
                 reads=[vg_b[i], ln_bb], writes=[vn_b[i]])
            psv, psvb = next_ps()
            for g in range(4):
                S.op("pe", lambda e, g=g, i=i, psv=psv: e.matmul(
                    psv[:, g * 128:(g + 1) * 128], lhsT=wsb[:, g, :], rhs=vn[i][:, g * 128:(g + 1) * 128],
                    start=True, stop=True), reads=[ws_b, vn_b[i]], writes=[psvb])
            for g in range(4):
                S.op("dve", lambda e, g=g, i=i, psv=psv: e.scalar_tensor_tensor(
                    out=oa[i][:, g * 128:(g + 1) * 128], in0=psv[:, g * 128:(g + 1) * 128], scalar=bsb[:, g:g + 1],
                    in1=ug[i][:, g * 128:(g + 1) * 128], op0=ALU.add, op1=ALU.mult),
                    reads=[psvb, bs_b, ug_b[i]], writes=[oa_b[i]])
            pt, ptb = next_ps()
            ptv = pt[:].bitcast(BF16)
            for g in range(4):
                S.op("pe", lambda e, g=g, i=i, ptv=ptv: e.transpose(
                    out=ptv[:, g * 128:(g + 1) * 128], in_=oa[i][:, g * 128:(g + 1) * 128], identity=identb[:]),
                    reads=[oa_b[i], ident_b], writes=[ptb])
            o, ob = next_os()
            evac(o[:], ob, ptv[:, 0:512], ptb)
            outs.append(S.dma("sp", o_mixA[:, :, tt * 128:(tt + 1) * 128].rearrange("g p t -> p g t"),
                              o[:].rearrange("p (g t) -> p g t", g=4), reads=[ob]))


    if io.get("mid_hook"):
        io["mid_hook"]()
    do_gmlp()
    return outs


def build_A(NT=2048):
    nc = bass.Bass("TRN2", target_bir_lowering=False)
    NTT = NT // 128
    NSB = NT // 512
    KC = D // 128
    dr = lambda n, s, dt, k: nc.dram_tensor(n, list(s), dt, kind=k).ap()
    x = dr("x", [NT, D], F32, "ExternalInput")
    gmix = dr("gmix", [1, D], F32, "ExternalInput")
    w_in = dr("w_in", [D, INW], F32, "ExternalInput")
    ln_g = dr("ln_g", [1, 512], F32, "ExternalInput")
    ln_b = dr("ln_b", [1, 512], F32, "ExternalInput")
    wsT = dr("wsT", [128, 4, 128], F32, "ExternalInput")
    bsT = dr("bsT", [128, 4], F32, "ExternalInput")
    tril = dr("tril", [128, 128], F32, "ExternalInput")
    ident = dr("ident", [128, 128], F32, "ExternalInput")
    o_mixA = dr("o_mixA", [4, 128, NT], BF16, "ExternalOutput")
    o_fm = dr("o_fm", [24, 128, NT], BF16, "ExternalOutput")
    o_tm = dr("o_tm", [NT, 1024], BF16, "ExternalOutput")
    o_gate = dr("o_gate", [NT, 24], F32, "ExternalOutput")

    io = dict(x=x, gmix=gmix, w_in=w_in, ln_g=ln_g, ln_b=ln_b, wsT=wsT, bsT=bsT, tril=tril, ident=ident,
              o_mixA=o_mixA, o_fm=o_fm, o_tm=o_tm, o_gate=o_gate)
    with ExitStack() as es:
        S = Sched(nc, es)
        outs = emit_A(S, make_psum(S), io, NT)
        S.emit(final_waits=outs)
    return nc


HD = 128
BIG = 30000.0
SCALE = 128 ** -0.5
OFF0 = 384
DEBUG = False


def strip_const(mmin, fn):
    width = 384 - 128 * mmin + 512
    kl = np.arange(128)[:, None]
    v = np.arange(width)[None, :]
    return fn(v - OFF0 - kl).astype(np.float32)


def dil_fn(d):
    c = ((d >= 0) & (d <= 128)).astype(np.float64) + ((d >= 0) & (d % 4 == 0) & (d <= 512)) + ((d >= 0) & (d % 16 == 0) & (d <= 2048))
    out = np.full(d.shape, -BIG)
    m = c > 0
    out[m] = np.log(c[m]) / SCALE
    return out


def win_fn(d):
    return np.where((d >= 0) & (d <= 511), 0.0, -BIG)


def caus_fn(d):
    return np.where(d >= 0, 0.0, -BIG)


def consts_B(S):
    nqt = S // 128
    c = {}
    c["ident"] = np.eye(128, dtype=np.float32)
    c["dstrip"] = strip_const(-16, dil_fn)
    c["wstrip"] = strip_const(-4, win_fn)
    c["cstrip"] = strip_const(0, caus_fn)
    n = np.arange(256)
    t = np.arange(S)
    valid = (16 * n[:, None] + 31) <= t[None, :]
    cb = np.where(valid, 0.0, -BIG).astype(np.float32).reshape(2, 128, S // 512, 512).transpose(2, 1, 0, 3)
    c["cbias"] = np.ascontiguousarray(cb)
    n_sel = S // 64
    cs = n * 16
    ss = np.arange(64) * 64
    ov = ((cs[:, None] <= ss[None, :] + 63) & (cs[:, None] + 31 >= ss[None, :])).astype(np.float32)
    ov[255] = 0
    ov[:, n_sel:] = 0
    c["ov"] = np.ascontiguousarray(ov.reshape(2, 128, 64).transpose(1, 0, 2))
    sm = np.zeros((nqt, 128, 2, 64), np.float32)
    for gq in range(nqt):
        tt = 128 * gq + np.arange(128)
        jt = tt // 64
        s = np.arange(64)[None, :]
        valid_s = s <= jt[:, None]
        add = np.where(valid_s, 0.0, -1.0)
        vm = valid_s.astype(np.float32)
        for val, cond in ((1e4, s == 0), (2e4, s == jt[:, None]), (3e4, s == jt[:, None] - 1)):
            cond = np.broadcast_to(cond, add.shape)
            add = np.where(cond, val, add)
            vm = np.where(cond, 0.0, vm)
        sm[gq, :, 0] = vm
        sm[gq, :, 1] = add
    c["selmask"] = sm
    E = np.zeros((64, S // 128, 128), np.float32)
    for kt in range(S // 128):
        E[2 * kt, kt, :64] = 1
        E[2 * kt + 1, kt, 64:] = 1
    c["esel"] = E
    return c


class DirectLoaderB:
    def __init__(self, S_, io):
        self.S = S_
        self.io = io

    def fm(self, tile, b, kind, i):
        src = self.io[kind]
        ap = src[i, :, :] if kind in ("Bq", "Bk", "Cq") else src[:, :]
        self.S.dma("sp", tile[:], ap, writes=[b])

    def tm(self, tile, b, kind, i):
        src = self.io[kind]
        ap = src[:, i * 128:(i + 1) * 128] if kind == "Bv" else src
        self.S.dma("sp", tile[:, :, 0:128], ap.rearrange("(k p) d -> p k d", p=128), writes=[b])

    def gates(self, gt, b, qs):
        self.S.dma("sp", gt[:], self.io["gates"][qs * 512:(qs + 1) * 512, :].rearrange("(j p) c -> p j c", p=128), writes=[b])


def emit_B(S_, PSP, io, S, ld=None):
    NQS = S // 512
    NKT = S // 128
    WD = 384 + 128 * 16 + 512
    WW = 384 + 128 * 4 + 512
    WC = 384 + 512
    if ld is None:
        ld = DirectLoaderB(S_, io)
    (posT, ck_w1, ck_w2, cv_w1, cv_w2, ident, dstrip, wstrip, cstrip, cbias, ovd, selmask, esel, mixT) = (io[k] for k in (
        "posT", "ck_w1", "ck_w2", "cv_w1", "cv_w2", "ident", "dstrip", "wstrip", "cstrip", "cbias", "ov", "selmask", "esel", "mixT"))
    dbg = io.get("dbg")
    sb, buf, op, dma = S_.sb, S_.buf, S_.op, S_.dma
    BB = [sb("BB%d" % i, [128, S], BF16) for i in range(6)]; BB_b = [buf() for _ in range(6)]
    VA = [sb("VA%d" % i, [128, NKT, 129], BF16) for i in range(2)]; VA_b = [buf() for _ in range(2)]
    identb = sb("identb", [128, 128], BF16); identf = sb("identf", [128, 128], F32); id_b = buf()
    dst = sb("dst", [128, WD], BF16); wst = sb("wst", [128, WW], BF16); cst = sb("cst", [128, WC], BF16); strip_b = buf()
    eselb = sb("eselb", [64, NKT, 128], BF16); esel_b = buf()
    PT = [sb("PT%d" % i, [128, 512], BF16) for i in range(3)]; PT_b = [buf() for _ in range(3)]
    accS = sb("accS", [128, 4, 4, 128], F32); acc_b = [[buf() for _ in range(4)] for _ in range(4)]
    accB = sb("accB", [128, 4, 128], BF16); accB_b = buf()
    osb = [sb("osb%d" % i, [128, 512], BF16) for i in range(2)]; osb_b = [buf() for _ in range(2)]
    sm = [sb("sm%d" % i, [128, 16], F32) for i in range(4)]; sm_b = [buf() for _ in range(4)]
    gt = sb("gt", [128, 4, 12], F32); gt_b = buf()
    PS, PS_b = PSP
    cnt = {"st": 0, "acc": 0, "pt": 0, "sm": 0, "os": 0}

    def st_ps():
        i = cnt["st"] % 3; cnt["st"] += 1
        return PS[i], PS_b[i]

    def acc_ps():
        i = 3 + cnt["acc"] % 4; cnt["acc"] += 1
        return PS[i], PS_b[i]

    MISC, MISC_b = PS[7], PS_b[7]

    def next_pt():
        i = cnt["pt"] % 3; cnt["pt"] += 1
        return PT[i], PT_b[i]

    def next_sm():
        i = cnt["sm"] % 4; cnt["sm"] += 1
        return sm[i], sm_b[i]

    CQ = "sp" if io.get("bf16_consts") else "pool"
    dma("sp", identf[:], ident[:, :], writes=[id_b])
    op("dve", lambda e: e.tensor_copy(out=identb[:], in_=identf[:]), reads=[id_b], writes=[id_b])
    dma(CQ, dst[:], dstrip[:, :], writes=[strip_b])
    dma(CQ, wst[:], wstrip[:, :], writes=[strip_b])
    dma(CQ, cst[:], cstrip[:, :], writes=[strip_b])
    dma(CQ, eselb[:], esel[:, :, :], writes=[esel_b])
    for i in range(2):
        op("dve", lambda e, i=i: e.memset(VA[i][:, :, 128:129], 1.0), writes=[VA_b[i]])

    outs = []

    def attend(qT, q_b, qs, kT, k_b, Vt, V_b, ktiles, strip, extra_bias=None, ncols=129, rhs_fn=None, band=None):
        banks = [acc_ps(), acc_ps()]
        accs = [(banks[j // 2][0][:, (j % 2) * 256:(j % 2) * 256 + ncols], banks[j // 2][1]) for j in range(4)]
        first = [True] * 4
        last_kt = {}
        def live(kt, j):
            d = 4 * qs + j - kt
            return d >= 0 and (band is None or d <= band)

        for kt in ktiles:
            for j in range(4):
                if live(kt, j):
                    last_kt[j] = kt
        def stage1(kt):
            m = kt - 4 * qs
            p, pb = st_ps()
            nb = (1 if strip is not None and strip(m) is not None else 0) + (1 if extra_bias else 0)
            op("pe", lambda e, p=p, kt=kt: e.matmul(p[:], lhsT=kT[:, kt * 128:(kt + 1) * 128],
                                                   rhs=qT[:, qs * 512:(qs + 1) * 512], start=True, stop=(nb == 0)),
               reads=[k_b, q_b], writes=[pb])
            k = 0
            if extra_bias:
                k += 1
                l_ap, r_ap, rb = extra_bias(kt)
                op("pe", lambda e, p=p, l_ap=l_ap, r_ap=r_ap, k=k: e.matmul(p[:], lhsT=l_ap, rhs=r_ap, start=False, stop=(k == nb)),
                   reads=rb, writes=[pb])
            if strip is not None and strip(m) is not None:
                k += 1
                s_ap = strip(m)
                op("pe", lambda e, p=p, s_ap=s_ap: e.matmul(p[:], lhsT=identb[:], rhs=s_ap, start=False, stop=True),
                   reads=[id_b, strip_b], writes=[pb])
            pt, ptb = next_pt()
            op("act", lambda e, p=p, pt=pt: e.activation(out=pt[:], in_=p[:], func=AF.Exp, scale=SCALE),
               reads=[pb], writes=[ptb])
            return pt, ptb

        def stage2(kt, pt, ptb):
            for j in range(4):
                if not live(kt, j):
                    continue
                a, ab = accs[j]
                rhs = Vt[:, kt, 0:ncols] if rhs_fn is None else rhs_fn(kt)
                op("pe", lambda e, a=a, pt=pt, j=j, rhs=rhs, st=(first[j] and j % 2 == 0), sp=(last_kt[j] == kt): e.matmul(
                    a, lhsT=pt[:, j * 128:(j + 1) * 128], rhs=rhs, start=st, stop=sp, skip_group_check=True),
                    reads=[ptb, V_b], writes=[ab])
                first[j] = False

        kts = list(ktiles)
        cur = stage1(kts[0])
        for i, kt in enumerate(kts):
            nxt = stage1(kts[i + 1]) if i + 1 < len(kts) else None
            stage2(kt, *cur)
            cur = nxt
        return accs

    def coef_of(a, ab, ncols, gate_ap=None, gate_b=None):
        s, sbb = next_sm()
        op("dve", lambda e: e.tensor_scalar(out=s[:, 0:1], in0=a[:, ncols - 1:ncols], scalar1=1e-30, scalar2=None, op0=ALU.max),
           reads=[ab], writes=[sbb])
        op("dve", lambda e: e.reciprocal(out=s[:, 1:2], in_=s[:, 0:1]), reads=[sbb], writes=[sbb])
        if gate_ap is not None:
            op("dve", lambda e: e.tensor_tensor(out=s[:, 1:2], in0=s[:, 1:2], in1=gate_ap, op=ALU.mult),
               reads=[sbb, gate_b], writes=[sbb])
        return s[:, 1:2], sbb

    def flush_head(head_out, qs, r):
        for j in range(4):
            op("act", lambda e, j=j: e.activation(out=accB[:, j, :], in_=accS[:, r, j, :], func=AF.Copy),
               reads=[acc_b[r][j]], writes=[accB_b])
        pv = MISC[:].bitcast(BF16)
        for j in range(4):
            op("pe", lambda e, j=j: e.transpose(out=pv[:, j * 128:(j + 1) * 128], in_=accB[:, j, :], identity=identb[:]),
               reads=[accB_b, id_b], writes=[MISC_b])
        i = cnt["os"] % 2; cnt["os"] += 1
        op("dve", lambda e, i=i: e.tensor_copy(out=osb[i][:], in_=pv[:, 0:512]), reads=[MISC_b], writes=[osb_b[i]])
        outs.append(dma("sp", mixT[head_out, :, qs * 512:(qs + 1) * 512], osb[i][:], reads=[osb_b[i]]))

    def dstrip_of(m):
        return dst[:, OFF0 - 128 * m:OFF0 - 128 * m + 512]

    for h in range(2):
        qT, q_b = BB[2 * h], BB_b[2 * h]
        kT, k_b = BB[2 * h + 1], BB_b[2 * h + 1]
        Vt, V_b = VA[h], VA_b[h]
        ld.fm(qT, q_b, "Bq", h)
        ld.fm(kT, k_b, "Bk", h)
        ld.tm(Vt, V_b, "Bv", h)
        for qs in range(NQS):
            ktiles = [kt for kt in range(4 * qs - 16, 4 * qs + 4) if kt >= 0]
            accs = attend(qT, q_b, qs, kT, k_b, Vt, V_b, ktiles, dstrip_of, band=16)
            for j in range(4):
                a, ab = accs[j]
                cf, cfb = coef_of(a, ab, 129)
                op("dve", lambda e, a=a, cf=cf, j=j: e.tensor_scalar(out=accS[:, 0, j, :], in0=a[:, 0:128], scalar1=cf,
                                                                   scalar2=None, op0=ALU.mult),
                   reads=[ab, cfb], writes=[acc_b[0][j]])
            flush_head(h, qs, 0)

    w1 = sb("w1", [128, 32, 128], BF16); w1_b = buf()
    w2 = sb("w2", [128, 128], BF16); w2_b = buf()
    posb = sb("posb", [128, 32], BF16); pos_b = buf()
    cvec = sb("cvec", [128, 1], F32); cvec_b = buf()
    hc = sb("hc", [128, 256], BF16); hc_b = buf()
    kcT = sb("kcT", [128, 256], BF16); kcT_b = buf()
    Rc = sb("Rc", [128, 2, 193], BF16); Rc_b = buf()
    ovf = sb("ovf", [128, 2, 64], F32); ovf_b = buf()
    posf = sb("posf", [128, 32], F32); w1f = sb("w1f", [128, 32, 128], F32); w2f = sb("w2f", [128, 128], F32); wf_b = buf()
    dma("sp", posf[:], posT[:, :], writes=[wf_b])
    op("dve", lambda e: e.tensor_copy(out=posb[:], in_=posf[:]), reads=[wf_b], writes=[pos_b])
    dma("sp", ovf[:], ovd[:, :, :], writes=[ovf_b])
    op("dve", lambda e: e.memset(Rc[:, :, 192:193], 1.0), writes=[Rc_b])
    op("dve", lambda e: e.tensor_copy(out=Rc[:, :, 128:192], in_=ovf[:]), reads=[ovf_b], writes=[Rc_b])
    ld.fm(BB[4], BB_b[4], "cmpk", 0)
    ld.fm(BB[5], BB_b[5], "cmpv", 0)
    for which in range(2):
        raw, raw_b = BB[4 + which], BB_b[4 + which]
        w1d, w2d = (ck_w1, ck_w2) if which == 0 else (cv_w1, cv_w2)
        dma("sp", w1f[:], w1d.rearrange("(i d) f -> d i f", d=128), writes=[wf_b])
        dma("sp", w2f[:], w2d[:, :], writes=[wf_b])
        op("act", lambda e: e.activation(out=w1[:], in_=w1f[:], func=AF.Copy), reads=[wf_b], writes=[w1_b])
        op("dve", lambda e: e.tensor_copy(out=w2[:], in_=w2f[:]), reads=[wf_b], writes=[w2_b])
        rv = raw[:].rearrange("p (n s) -> p n s", s=16)
        nblk = S // 16 - 1
        for i in range(32):
            op("pe", lambda e, i=i: e.matmul(MISC[:, 0:1], lhsT=w1[:, i, :], rhs=posb[:, i:i + 1], start=(i == 0), stop=(i == 31)),
               reads=[w1_b, pos_b], writes=[MISC_b])
        op("dve", lambda e: e.tensor_copy(out=cvec[:], in_=MISC[:, 0:1]), reads=[MISC_b], writes=[cvec_b])
        p, pb = st_ps()
        for i in range(32):
            rhs = rv[:, 0:nblk, i] if i < 16 else rv[:, 1:nblk + 1, i - 16]
            op("pe", lambda e, p=p, i=i, rhs=rhs: e.matmul(p[:, 0:nblk], lhsT=w1[:, i, :], rhs=rhs, start=(i == 0), stop=(i == 31)),
               reads=[w1_b, raw_b], writes=[pb])
        op("dve", lambda e: e.memset(hc[:], 0.0), writes=[hc_b])
        op("act", lambda e, p=p: e.activation(out=hc[:, 0:nblk], in_=p[:, 0:nblk], func=AF.Gelu_apprx_tanh, bias=cvec[:]),
           reads=[pb, cvec_b], writes=[hc_b])
        if which == 0:
            p2, p2b = st_ps()
            op("pe", lambda e, p2=p2: e.matmul(p2[:, 0:256], lhsT=w2[:], rhs=hc[:], start=True, stop=True),
               reads=[w2_b, hc_b], writes=[p2b])
            op("dve", lambda e, p2=p2: e.tensor_copy(out=kcT[:], in_=p2[:, 0:256]), reads=[p2b], writes=[kcT_b])
        else:
            for c in range(2):
                p2, p2b = st_ps()
                op("pe", lambda e, p2=p2, c=c: e.matmul(p2[:, 0:128], lhsT=hc[:, c * 128:(c + 1) * 128], rhs=w2[:], start=True, stop=True),
                   reads=[w2_b, hc_b], writes=[p2b])
                op("dve", lambda e, p2=p2, c=c: e.tensor_copy(out=Rc[:, c, 0:128], in_=p2[:, 0:128]), reads=[p2b], writes=[Rc_b])

    if DEBUG:
        outs.append(dma("sp", dbg[:, 0:256], kcT[:], reads=[kcT_b]))
        outs.append(dma("sp", dbg[:, 256:512], hc[:], reads=[hc_b]))
        outs.append(dma("sp", dbg[:, 512:768], Rc[:, 0, 0:128], reads=[Rc_b])) if False else None
    for r in range(4):
        ld.fm(BB[r], BB_b[r], "Cq", r)
    ld.fm(BB[4], BB_b[4], "slck", 0)
    ld.fm(BB[5], BB_b[5], "wink", 0)
    ld.tm(VA[0], VA_b[0], "slcv", 0)
    ld.tm(VA[1], VA_b[1], "winv", 0)
    if io.get("after_loads"):
        io["after_loads"](BB_b + VA_b)
    cbs = sb("cbs", [128, 2, 512], BF16); cbs_b = buf()
    smk = sb("smk", [128, 4, 2, 64], F32); smk_b = buf()
    imp = sb("imp", [128, 4, 64], F32); imp_b = [buf() for _ in range(4)]
    sc = sb("sc", [128, 64], F32); sc2 = sb("sc2", [128, 64], F32); sc_b = buf()
    mx8 = sb("mx8", [128, 16], F32); mx8_b = buf()
    selb = sb("selb", [128, 64], F32); selb_b = buf()
    selbT = sb("selbT", [64, 512], BF16); selbT_b = buf()

    def wstrip_of(m):
        return wst[:, OFF0 - 128 * m:OFF0 - 128 * m + 512]

    def cstrip_of(m):
        if m < 0:
            return None
        return cst[:, OFF0 - 128 * m:OFF0 - 128 * m + 512]

    for qs in range(NQS):
        ld.gates(gt, gt_b, qs)
        dma(CQ, cbs[:], cbias[qs, :, :, :], writes=[cbs_b])
        dma("sp", smk[:], selmask[4 * qs:4 * qs + 4, :, :, :].rearrange("j p a s -> p j a s"), writes=[smk_b])
        nchunks = [c for c in range(2) if (c * 2048 + 31) <= (qs * 512 + 511)]
        for r in range(4):
            qT, q_b = BB[r], BB_b[r]
            banks = [acc_ps(), acc_ps()]
            accs = [(banks[j // 2][0][:, (j % 2) * 256:(j % 2) * 256 + 193], banks[j // 2][1]) for j in range(4)]
            for ci, c in enumerate(nchunks):
                p, pb = st_ps()
                op("pe", lambda e, p=p, c=c, qT=qT, qs=qs: e.matmul(p[:], lhsT=kcT[:, c * 128:(c + 1) * 128], rhs=qT[:, qs * 512:(qs + 1) * 512],
                                                            start=True, stop=False), reads=[kcT_b, q_b], writes=[pb])
                op("pe", lambda e, p=p, c=c, cbs=cbs: e.matmul(p[:], lhsT=identb[:], rhs=cbs[:, c, :], start=False, stop=True),
                   reads=[id_b, cbs_b], writes=[pb])
                pt, ptb = next_pt()
                op("act", lambda e, p=p, pt=pt: e.activation(out=pt[:], in_=p[:], func=AF.Exp, scale=SCALE), reads=[pb], writes=[ptb])
                for j in range(4):
                    a, ab = accs[j]
                    op("pe", lambda e, a=a, pt=pt, j=j, c=c, ci=ci: e.matmul(a, lhsT=pt[:, j * 128:(j + 1) * 128], rhs=Rc[:, c, :],
                                                                           start=(ci == 0 and j % 2 == 0), stop=(ci == len(nchunks) - 1),
                                                                           skip_group_check=True),
                       reads=[ptb, Rc_b], writes=[ab])
            for j in range(4):
                a, ab = accs[j]
                s, sbb = next_sm()
                op("dve", lambda e, a=a, s=s: e.tensor_scalar(out=s[:, 0:1], in0=a[:, 192:193], scalar1=1e-30, scalar2=None, op0=ALU.max),
                   reads=[ab], writes=[sbb])
                op("dve", lambda e, s=s: e.reciprocal(out=s[:, 1:2], in_=s[:, 0:1]), reads=[sbb], writes=[sbb])
                op("dve", lambda e, s=s, j=j, r=r, gt=gt: e.tensor_tensor(out=s[:, 2:3], in0=s[:, 1:2], in1=gt[:, j, r * 3:r * 3 + 1], op=ALU.mult),
                   reads=[sbb, gt_b], writes=[sbb])
                op("dve", lambda e, a=a, s=s, j=j, r=r: e.tensor_scalar(out=accS[:, r, j, :], in0=a[:, 0:128], scalar1=s[:, 2:3],
                                                                      scalar2=None, op0=ALU.mult),
                   reads=[ab, sbb], writes=[acc_b[r][j]])
                if r == 0:
                    op("dve", lambda e, a=a, s=s, j=j: e.tensor_scalar(out=imp[:, j, :], in0=a[:, 128:192], scalar1=s[:, 1:2],
                                                                     scalar2=None, op0=ALU.mult),
                       reads=[ab, sbb], writes=[imp_b[j]])
                else:
                    op("dve", lambda e, a=a, s=s, j=j: e.scalar_tensor_tensor(out=imp[:, j, :], in0=a[:, 128:192], scalar=s[:, 1:2],
                                                                            in1=imp[:, j, :], op0=ALU.mult, op1=ALU.add),
                       reads=[ab, sbb, imp_b[j]], writes=[imp_b[j]])
        need_sel = (qs * 512 + 511) >= 1024
        for j in (range(4) if need_sel else ()):
            op("dve", lambda e, j=j, smk=smk: e.tensor_tensor(out=sc[:], in0=imp[:, j, :], in1=smk[:, j, 0, :], op=ALU.mult),
               reads=[imp_b[j], smk_b], writes=[sc_b])
            op("dve", lambda e, j=j, smk=smk: e.tensor_tensor(out=sc[:], in0=sc[:], in1=smk[:, j, 1, :], op=ALU.add),
               reads=[sc_b, smk_b], writes=[sc_b])
            op("dve", lambda e: e.max(out=mx8[:, 0:8], in_=sc[:]), reads=[sc_b], writes=[mx8_b])
            op("dve", lambda e: e.match_replace(out=sc2[:], in_to_replace=mx8[:, 0:8], in_values=sc[:], imm_value=-1e9),
               reads=[sc_b, mx8_b], writes=[sc_b])
            op("dve", lambda e: e.max(out=mx8[:, 8:16], in_=sc2[:]), reads=[sc_b], writes=[mx8_b])
            op("dve", lambda e: e.tensor_scalar(out=selb[:], in0=sc[:], scalar1=mx8[:, 15:16], scalar2=1.0, op0=ALU.is_ge, op1=ALU.subtract),
               reads=[sc_b, mx8_b], writes=[selb_b])
            op("pe", lambda e: e.transpose(out=MISC[0:64, 0:128], in_=selb[:], identity=identf[:]),
               reads=[selb_b, id_b], writes=[MISC_b])
            op("act", lambda e, j=j: e.activation(out=selbT[:, j * 128:(j + 1) * 128], in_=MISC[0:64, 0:128], func=AF.Copy, scale=BIG),
               reads=[MISC_b], writes=[selbT_b])
        for r in range(4):
            qT, q_b = BB[r], BB_b[r]
            accs = attend(qT, q_b, qs, BB[4], BB_b[4], VA[0], VA_b[0], list(range(0, 4 * qs + 4)), cstrip_of,
                          extra_bias=(lambda kt: (eselb[:, kt, :], selbT[:], [esel_b, selbT_b])) if need_sel else None)
            for j in range(4):
                a, ab = accs[j]
                cf, cfb = coef_of(a, ab, 129, gt[:, j, r * 3 + 1:r * 3 + 2], gt_b)
                op("dve", lambda e, a=a, cf=cf, j=j, r=r: e.scalar_tensor_tensor(out=accS[:, r, j, :], in0=a[:, 0:128], scalar=cf,
                                                                               in1=accS[:, r, j, :], op0=ALU.mult, op1=ALU.add),
                   reads=[ab, cfb, acc_b[r][j]], writes=[acc_b[r][j]])
            ktiles = [kt for kt in range(4 * qs - 4, 4 * qs + 4) if kt >= 0]
            accs = attend(qT, q_b, qs, BB[5], BB_b[5], VA[1], VA_b[1], ktiles, wstrip_of, band=4)
            for j in range(4):
                a, ab = accs[j]
                cf, cfb = coef_of(a, ab, 129, gt[:, j, r * 3 + 2:r * 3 + 3], gt_b)
                op("dve", lambda e, a=a, cf=cf, j=j, r=r: e.scalar_tensor_tensor(out=accS[:, r, j, :], in0=a[:, 0:128], scalar=cf,
                                                                               in1=accS[:, r, j, :], op0=ALU.mult, op1=ALU.add),
                   reads=[ab, cfb, acc_b[r][j]], writes=[acc_b[r][j]])
            flush_head(2 + r, qs, r)
    return outs


def build_B(S=4096):
    nc = bass.Bass("TRN2", target_bir_lowering=False)
    NQS = S // 512
    NKT = S // 128
    dr = lambda n, s, dt, k: nc.dram_tensor(n, list(s), dt, kind=k).ap()
    Bq = dr("Bq", [2, 128, S], BF16, "ExternalInput")
    Bk = dr("Bk", [2, 128, S], BF16, "ExternalInput")
    Bv = dr("Bv", [S, 256], BF16, "ExternalInput")
    Cq = dr("Cq", [4, 128, S], BF16, "ExternalInput")
    cmpk = dr("cmpk", [128, S], BF16, "ExternalInput")
    cmpv = dr("cmpv", [128, S], BF16, "ExternalInput")
    slck = dr("slck", [128, S], BF16, "ExternalInput")
    slcv = dr("slcv", [S, 128], BF16, "ExternalInput")
    wink = dr("wink", [128, S], BF16, "ExternalInput")
    winv = dr("winv", [S, 128], BF16, "ExternalInput")
    gates = dr("gates", [S, 12], F32, "ExternalInput")
    posT = dr("posT", [128, 32], F32, "ExternalInput")
    ck_w1 = dr("ck_w1", [4096, 128], F32, "ExternalInput")
    ck_w2 = dr("ck_w2", [128, 128], F32, "ExternalInput")
    cv_w1 = dr("cv_w1", [4096, 128], F32, "ExternalInput")
    cv_w2 = dr("cv_w2", [128, 128], F32, "ExternalInput")
    WD = 384 + 128 * 16 + 512
    WW = 384 + 128 * 4 + 512
    WC = 384 + 512
    ident = dr("ident", [128, 128], F32, "ExternalInput")
    dstrip = dr("dstrip", [128, WD], F32, "ExternalInput")
    wstrip = dr("wstrip", [128, WW], F32, "ExternalInput")
    cstrip = dr("cstrip", [128, WC], F32, "ExternalInput")
    cbias = dr("cbias", [NQS, 128, 2, 512], F32, "ExternalInput")
    ovd = dr("ov", [128, 2, 64], F32, "ExternalInput")
    selmask = dr("selmask", [NKT, 128, 2, 64], F32, "ExternalInput")
    esel = dr("esel", [64, NKT, 128], F32, "ExternalInput")
    mixT = dr("mixT", [6, 128, S], BF16, "ExternalOutput")
    dbg = dr("dbg", [128, 1024], BF16, "ExternalOutput") if DEBUG else None

    io = dict(Bq=Bq, Bk=Bk, Bv=Bv, Cq=Cq, cmpk=cmpk, cmpv=cmpv, slck=slck, slcv=slcv, wink=wink, winv=winv, gates=gates,
              posT=posT, ck_w1=ck_w1, ck_w2=ck_w2, cv_w1=cv_w1, cv_w2=cv_w2, ident=ident, dstrip=dstrip, wstrip=wstrip,
              cstrip=cstrip, cbias=cbias, ov=ovd, selmask=selmask, esel=esel, mixT=mixT, dbg=dbg)
    with ExitStack() as es:
        S_ = Sched(nc, es)
        outs = emit_B(S_, make_psum(S_), io, S)
        S_.emit(final_waits=outs)
    return nc


D = 2048
EPS = 1e-6
DFF = 8192


def emit_C(S, PSP, io, NT, FINAL, mx_loader=None, perm=None):
    NSB = NT // 512
    KC = D // 128
    SCALE = 128 ** -0.5
    if perm is None:
        perm = list(range(16))
    (x, mixT, w_out, g_x, g_mem, mem, wq, wkv, wo, g_mlp, w_up, w_down, g_fin, ident, y) = (io.get(k) for k in (
        "x", "mixT", "w_out", "g_x", "g_mem", "mem", "wq", "wkv", "wo", "g_mlp", "w_up", "w_down", "g_fin", "ident", "y"))
    xs = S.sb("xs", [128, 4, D], F32); xs_b = [S.buf() for _ in range(4)]
    mx = S.sb("mx", [128, KC, 512], BF16); mx_b = S.buf()
    hT = S.sb("hT", [128, KC, 512], BF16); hT_b = [S.buf() for _ in range(4)]
    aT = S.sb("aT", [128, 32, 512], BF16); aT_b = [S.buf() for _ in range(32)]
    W = [S.sb("W%d" % i, [128, KC, 512], BF16) for i in range(2)]; W_b = [S.buf() for _ in range(2)]
    gB = S.sb("gB", [128, D], F32); gB_b = S.buf()
    hb = [S.sb("hb%d" % i, [128, D], BF16) for i in range(2)]; hb_b = [S.buf() for _ in range(2)]
    st = [S.sb("st%d" % i, [128, 8], F32) for i in range(2)]; st_b = [S.buf() for _ in range(2)]
    identb = S.sb("identb", [128, 128], BF16); ident_b = S.buf()
    ones = S.sb("ones", [128, 128], BF16); ones_b = S.buf()
    qT = S.sb("qT", [128, 4, 512], BF16); qT_b = [S.buf() for _ in range(4)]
    kT = S.sb("kT", [128, 4, 256], BF16); kT_b = S.buf()
    Vm = S.sb("Vm", [128, 2, 512], BF16); Vm_b = S.buf()
    PT = [S.sb("PT%d" % i, [128, 2, 512], BF16) for i in range(2)]; PT_b = [S.buf() for _ in range(2)]
    oT = S.sb("oT", [128, 4, 512], BF16); oT_b = [S.buf() for _ in range(4)]
    rden0 = S.sb("rden0", [128, 512], F32); rden = [rden0, rden0]; rden0_b = S.buf(); rden_b = [rden0_b, rden0_b]
    rtmp = [S.sb("rtmp%d" % i, [128, 512], F32) for i in range(2)]; rtmp_b = [S.buf() for _ in range(2)]
    PS, PS_b = PSP
    psn = [0]

    def next_ps():
        i = psn[0] % 8
        psn[0] += 1
        return PS[i], PS_b[i]

    wn = [0]

    def load_W(view, K, width):
        i = wn[0] % 2
        wn[0] += 1
        S.dma(WQ, W[i][:, 0:K, 0:width], view, writes=[W_b[i]])
        if io.get("tick"):
            io["tick"]()
        return W[i], W_b[i]

    WQ = io.get("wqueue", "pool")
    wsrc = dict(w_out=w_out, wq=wq, wkv=wkv, wo=wo, w_up=w_up, w_down=w_down)

    def wview(name, tr, tc):
        if "wtile" in io:
            return io["wtile"](name, tr, tc)
        w = wsrc[name]
        kr = min(16, w.shape[0] // 128)
        return w[tr * 2048:tr * 2048 + kr * 128, tc * 512:(tc + 1) * 512].rearrange("(k p) c -> p k c", p=128)

    evn = [0]

    def copy_any(dst, src, reads, writes):
        e = "act" if evn[0] % 2 == 0 else "dve"
        evn[0] += 1
        if e == "act":
            S.op("act", lambda en: en.activation(out=dst, in_=src, func=AF.Copy), reads=reads, writes=writes)
        else:
            S.op("dve", lambda en: en.tensor_copy(out=dst, in_=src), reads=reads, writes=writes)

    rn = [0]

    def rms_T(src, src_b, g_ap, dstT, dst_b, c0, to_out=None):
        i = rn[0] % 2
        rn[0] += 1
        S.op("act", lambda e: e.activation(out=hb[i][:], in_=src, func=AF.Square, accum_out=st[i][:, 0:1]),
             reads=[src_b], writes=[hb_b[i], st_b[i]])
        S.op("dve", lambda e: e.tensor_scalar(out=st[i][:, 1:2], in0=st[i][:, 0:1], scalar1=1.0 / D, scalar2=EPS,
                                              op0=ALU.mult, op1=ALU.add), reads=[st_b[i]], writes=[st_b[i]])
        S.op("act", lambda e: e.activation(out=st[i][:, 3:4], in_=st[i][:, 1:2], func=AF.Sqrt),
             reads=[st_b[i]], writes=[st_b[i]])
        S.op("dve", lambda e: e.reciprocal(out=st[i][:, 2:3], in_=st[i][:, 3:4]), reads=[st_b[i]], writes=[st_b[i]])
        if to_out is not None:
            S.op("dve", lambda e: e.scalar_tensor_tensor(out=to_out, in0=src, scalar=st[i][:, 2:3], in1=gB[:],
                                                         op0=ALU.mult, op1=ALU.mult),
                 reads=[src_b, st_b[i], gB_b], writes=[src_b])
            return
        S.op("dve", lambda e: e.scalar_tensor_tensor(out=hb[i][:], in0=src, scalar=st[i][:, 2:3], in1=gB[:],
                                                     op0=ALU.mult, op1=ALU.mult),
             reads=[src_b, st_b[i], gB_b], writes=[hb_b[i]])
        for q in range(KC // 4):
            p, pb = next_ps()
            pv = p[:].bitcast(BF16)
            for j in range(4):
                kc = q * 4 + j
                S.op("pe", lambda e, kc=kc, j=j, pv=pv: e.transpose(
                    out=pv[:, j * 128:(j + 1) * 128], in_=hb[i][:, kc * 128:(kc + 1) * 128], identity=identb[:]),
                    reads=[hb_b[i], ident_b], writes=[pb])
            copy_any(dstT[:, q * 4:(q + 1) * 4, c0:c0 + 128], pv[:, 0:512].rearrange("p (a b) -> p a b", a=4),
                     [pb], [dst_b])

    S.dma("pool", identb[:], ident[:, :], writes=[ident_b])
    if io.get("after_setup"):
        io["after_setup"]()
    SQ = "dve" if io.get("no_pool") else "pool"
    S.op("dve", lambda e: e.memset(ones[:], 1.0), writes=[ones_b])

    S.dma("sp", gB[:], g_mem.partition_broadcast(128), writes=[gB_b])
    for mt in range(2):
        S.dma("sp", xs[:, mt, :], mem[mt * 128:(mt + 1) * 128, :], writes=[xs_b[mt]])
        rms_T(xs[:, mt, :], xs_b[mt], None, hT, hT_b[mt], mt * 128)
    Wt, Wb = load_W(wview("wkv", 0, 0), KC, 512)
    for h in range(4):
        p, pb = next_ps()
        for kc in range(KC):
            S.op("pe", lambda e, p=p, Wt=Wt, kc=kc, h=h: e.matmul(
                p[:, 0:256], lhsT=Wt[:, kc, h * 128:(h + 1) * 128], rhs=hT[:, kc, 0:256],
                start=(kc == 0), stop=(kc == KC - 1)), reads=[Wb, hT_b[0], hT_b[1]], writes=[pb])
        copy_any(kT[:, h, :], p[:, 0:256], [pb], [kT_b])
    Wt, Wb = load_W(wview("wkv", 0, 1), KC, 512)
    for mt in range(2):
        p, pb = next_ps()
        for kc in range(KC):
            S.op("pe", lambda e, p=p, Wt=Wt, kc=kc, mt=mt: e.matmul(
                p[:], lhsT=hT[:, kc, mt * 128:(mt + 1) * 128], rhs=Wt[:, kc, :],
                start=(kc == 0), stop=(kc == KC - 1)), reads=[Wb, hT_b[mt]], writes=[pb])
        copy_any(Vm[:, mt, :], p[:], [pb], [Vm_b])

    outs = []
    for blk in range(NSB):
        t0 = blk * 512
        for t in range(4):
            S.dma("sp", xs[:, t, :], x[t0 + t * 128:t0 + (t + 1) * 128, :], writes=[xs_b[t]])
        if mx_loader is None:
            S.dma("sp", mx[:], mixT[:, :, t0:t0 + 512].rearrange("k p t -> p k t"), writes=[mx_b])
        else:
            mx_loader(S, mx, mx_b, t0)
        for cg in range(4):
            Wt, Wb = load_W(wview("w_out", 0, cg), KC, 512)
            for t in range(4):
                p, pb = next_ps()
                for kc in range(KC):
                    S.op("pe", lambda e, p=p, Wt=Wt, kc=kc, t=t: e.matmul(
                        p[:], lhsT=mx[:, kc, t * 128:(t + 1) * 128], rhs=Wt[:, perm[kc], :],
                        start=(kc == 0), stop=(kc == KC - 1)), reads=[Wb, mx_b], writes=[pb])
                S.op("dve", lambda e, p=p, t=t, cg=cg: e.tensor_tensor(
                    out=xs[:, t, cg * 512:(cg + 1) * 512], in0=p[:], in1=xs[:, t, cg * 512:(cg + 1) * 512],
                    op=ALU.add), reads=[pb, xs_b[t]], writes=[xs_b[t]])
        S.dma("sp", gB[:], g_x.partition_broadcast(128), writes=[gB_b])
        for t in range(4):
            rms_T(xs[:, t, :], xs_b[t], None, hT, hT_b[t], t * 128)
        Wt, Wb = load_W(wview("wq", 0, 0), KC, 512)
        for h in range(4):
            p, pb = next_ps()
            for kc in range(KC):
                S.op("pe", lambda e, p=p, Wt=Wt, kc=kc, h=h: e.matmul(
                    p[:], lhsT=Wt[:, kc, h * 128:(h + 1) * 128], rhs=hT[:, kc, :],
                    start=(kc == 0), stop=(kc == KC - 1)), reads=[Wb] + hT_b, writes=[pb])
            copy_any(qT[:, h, :], p[:], [pb], [qT_b[h]])
        for h in range(4):
            i = h % 2
            for mc in range(2):
                p, pb = next_ps()
                S.op("pe", lambda e, p=p, h=h, mc=mc: e.matmul(
                    p[:], lhsT=kT[:, h, mc * 128:(mc + 1) * 128], rhs=qT[:, h, :], start=True, stop=True),
                    reads=[kT_b, qT_b[h]], writes=[pb])
                S.op("act", lambda e, p=p, i=i, mc=mc: e.activation(out=PT[i][:, mc, :], in_=p[:], func=AF.Exp,
                                                                  scale=SCALE), reads=[pb], writes=[PT_b[i]])
            pd, pdb = next_ps()
            po, pob = next_ps()
            for mc in range(2):
                S.op("pe", lambda e, pd=pd, i=i, mc=mc: e.matmul(
                    pd[:], lhsT=ones[:], rhs=PT[i][:, mc, :], start=(mc == 0), stop=(mc == 1)),
                    reads=[ones_b, PT_b[i]], writes=[pdb])
            for mc in range(2):
                S.op("pe", lambda e, po=po, i=i, mc=mc, h=h: e.matmul(
                    po[:], lhsT=Vm[:, mc, h * 128:(h + 1) * 128], rhs=PT[i][:, mc, :], start=(mc == 0), stop=(mc == 1)),
                    reads=[Vm_b, PT_b[i]], writes=[pob])
            S.op("dve", lambda e, pd=pd, i=i: e.reciprocal(out=rden[i][:], in_=pd[:]), reads=[pdb], writes=[rden_b[i]])
            S.op("dve", lambda e, po=po, i=i, h=h: e.tensor_tensor(out=oT[:, h, :], in0=po[:], in1=rden[i][:], op=ALU.mult),
                 reads=[pob, rden_b[i]], writes=[oT_b[h]])
        for cg in range(4):
            Wt, Wb = load_W(wview("wo", 0, cg), 4, 512)
            for t in range(4):
                p, pb = next_ps()
                for h in range(4):
                    S.op("pe", lambda e, p=p, Wt=Wt, h=h, t=t: e.matmul(
                        p[:], lhsT=oT[:, h, t * 128:(t + 1) * 128], rhs=Wt[:, h, :],
                        start=(h == 0), stop=(h == 3)), reads=[Wb] + oT_b, writes=[pb])
                S.op("dve", lambda e, p=p, t=t, cg=cg: e.tensor_tensor(
                    out=xs[:, t, cg * 512:(cg + 1) * 512], in0=p[:], in1=xs[:, t, cg * 512:(cg + 1) * 512],
                    op=ALU.add), reads=[pb, xs_b[t]], writes=[xs_b[t]])
        S.dma("sp", gB[:], g_mlp.partition_broadcast(128), writes=[gB_b])
        for t in range(4):
            rms_T(xs[:, t, :], xs_b[t], None, hT, hT_b[t], t * 128)
        for half in range(2):
            for ug in range(8):
                c0 = half * 4096 + ug * 512
                Wt, Wb = load_W(wview("w_up", 0, c0 // 512), KC, 512)
                for j in range(4):
                    hc = ug * 4 + j
                    p, pb = next_ps()
                    for kc in range(KC):
                        S.op("pe", lambda e, p=p, Wt=Wt, kc=kc, j=j: e.matmul(
                            p[:], lhsT=Wt[:, kc, j * 128:(j + 1) * 128], rhs=hT[:, kc, :],
                            start=(kc == 0), stop=(kc == KC - 1)), reads=[Wb] + hT_b, writes=[pb])
                    i = hc % 2
                    S.op("act", lambda e, p=p, i=i: e.activation(out=rtmp[i][:], in_=p[:], func=AF.Relu),
                         reads=[pb], writes=[rtmp_b[i]])
                    S.op(SQ, lambda e, i=i, hc=hc: e.tensor_tensor(out=aT[:, hc, :], in0=rtmp[i][:], in1=rtmp[i][:],
                                                                   op=ALU.mult), reads=[rtmp_b[i]], writes=[aT_b[hc]])
            for cg in range(4):
                acc = [next_ps() for _ in range(4)]
                for hq in range(2):
                    r0 = half * 4096 + hq * 2048
                    Wt, Wb = load_W(wview("w_down", r0 // 2048, cg), KC, 512)
                    for t in range(4):
                        p, pb = acc[t]
                        for kc in range(KC):
                            hc = hq * 16 + kc
                            S.op("pe", lambda e, p=p, Wt=Wt, kc=kc, hc=hc, t=t: e.matmul(
                                p[:], lhsT=aT[:, hc, t * 128:(t + 1) * 128], rhs=Wt[:, kc, :],
                                start=(hc == 0), stop=(hc == 31)), reads=[Wb, aT_b[hc]], writes=[pb])
                for t in range(4):
                    p, pb = acc[t]
                    S.op("dve", lambda e, p=p, t=t, cg=cg: e.tensor_tensor(
                        out=xs[:, t, cg * 512:(cg + 1) * 512], in0=p[:], in1=xs[:, t, cg * 512:(cg + 1) * 512],
                        op=ALU.add), reads=[pb, xs_b[t]], writes=[xs_b[t]])
        if FINAL:
            S.dma("sp", gB[:], g_fin.partition_broadcast(128), writes=[gB_b])
            for t in range(4):
                rms_T(xs[:, t, :], xs_b[t], None, None, None, 0, to_out=xs[:, t, :])
        for t in range(4):
            outs.append(S.dma("sp", y[t0 + t * 128:t0 + (t + 1) * 128, :], xs[:, t, :], reads=[xs_b[t]]))
    return outs


def build_C(NT=2048, FINAL=False):
    nc = bass.Bass("TRN2", target_bir_lowering=False)
    NSB = NT // 512
    KC = D // 128
    dr = lambda n, s, dt, k: nc.dram_tensor(n, list(s), dt, kind=k).ap()
    x = dr("x", [NT, D], F32, "ExternalInput")
    mixT = dr("mixT", [16, 128, NT], BF16, "ExternalInput")
    w_out = dr("w_out", [D, D], F32, "ExternalInput")
    g_x = dr("g_x", [1, D], F32, "ExternalInput")
    g_mem = dr("g_mem", [1, D], F32, "ExternalInput")
    mem = dr("mem", [256, D], F32, "ExternalInput")
    wq = dr("wq", [D, 512], F32, "ExternalInput")
    wkv = dr("wkv", [D, 1024], F32, "ExternalInput")
    wo = dr("wo", [512, D], F32, "ExternalInput")
    g_mlp = dr("g_mlp", [1, D], F32, "ExternalInput")
    w_up = dr("w_up", [D, DFF], F32, "ExternalInput")
    w_down = dr("w_down", [DFF, D], F32, "ExternalInput")
    g_fin = dr("g_fin", [1, D], F32, "ExternalInput")
    ident = dr("ident", [128, 128], F32, "ExternalInput")
    y = dr("y", [NT, D], F32, "ExternalOutput")
    SCALE = 128 ** -0.5

    io = dict(x=x, mixT=mixT, w_out=w_out, g_x=g_x, g_mem=g_mem, mem=mem, wq=wq, wkv=wkv, wo=wo, g_mlp=g_mlp,
              w_up=w_up, w_down=w_down, g_fin=g_fin, ident=ident, y=y)
    with ExitStack() as es:
        S = Sched(nc, es)
        outs = emit_C(S, make_psum(S), io, NT, FINAL)
        S.emit(final_waits=outs)
    return nc


NLAYER = 4
ARENA = 92160
GROUPS = [[0, 1], [2, 3], [4, 5], [6, 7]]


class FusedLoaderB:
    FM_IDX = {"Bq": (0, 2), "Bk": (4, 2), "Cq": (8, 4), "cmpk": (16, 1), "cmpv": (18, 1), "slck": (20, 1), "wink": (22, 1)}
    TM_COL = {"Bv": (0, 256), "slcv": (512, 128), "winv": (768, 128)}

    def __init__(self, S_, exA_dst, gate_dst, sel, S):
        self.S = S_
        self.dst = exA_dst
        self.gdst = gate_dst
        self.selt = S_.sb("selt", [128, 2], F32); self.sel_b = S_.buf()
        S_.dma("sp", self.selt[:], sel[:, :], writes=[self.sel_b])
        self.T = [S_.sb("Tfm%d" % i, [128, S], BF16) for i in range(2)]; self.T_b = [S_.buf() for _ in range(2)]
        self.Tg = S_.sb("Tg", [128, 4, 12], F32); self.Tg_b = S_.buf()
        self.n = 0
        self.half = S // 2

    def _blend(self, dst_ap, dst_b, t_ap, t_b):
        S_ = self.S
        S_.op("dve", lambda e: e.tensor_scalar(out=t_ap, in0=t_ap, scalar1=self.selt[:, 1:2], scalar2=None, op0=ALU.mult),
              reads=[t_b, self.sel_b], writes=[t_b])
        S_.op("dve", lambda e: e.scalar_tensor_tensor(out=dst_ap, in0=dst_ap, scalar=self.selt[:, 0:1], in1=t_ap,
                                                      op0=ALU.mult, op1=ALU.add),
              reads=[dst_b, t_b, self.sel_b], writes=[dst_b])

    def fm(self, tile, b, kind, i):
        base, stride = self.FM_IDX[kind]
        ii = self.n % 2
        self.n += 1
        T, Tb = self.T[ii], self.T_b[ii]
        H = self.half
        for r in range(2):
            for g, (dstt, dstb) in enumerate(((tile, b), (T, Tb))):
                R = (base + g * stride + i) * 128
                row = ((R // 512) * 2 + r) * 512 + (R % 512)
                self.S.dma("sp", dstt[:, r * H:(r + 1) * H], self.dst[row:row + 128, :], writes=[dstb])
        self._blend(tile[:], b, T[:], Tb)

    def tm(self, tile, b, kind, i):
        base, gstride = self.TM_COL[kind]
        ii = self.n % 2
        self.n += 1
        T, Tb = self.T[ii], self.T_b[ii]
        Tv = T[:].rearrange("p (k d) -> p k d", d=128)
        nk = self.half // 128
        hk = nk // 2
        for r in range(2):
            for piece in range(2):
                row0 = ((6 + piece) * 2 + r) * 512
                v = self.dst[row0:row0 + 512, :].rearrange("r (two c) -> (r two) c", two=2)
                k0 = r * nk + piece * hk
                for g in range(2):
                    c0 = base + g * gstride + i * 128
                    src = v[:, c0:c0 + 128].rearrange("(k p) d -> p k d", p=128)
                    if g == 0:
                        self.S.dma("sp", tile[:, k0:k0 + hk, 0:128], src, writes=[b])
                    else:
                        self.S.dma("sp", Tv[:, k0:k0 + hk, :], src, writes=[Tb])
        self._blend(tile[:, :, 0:128], b, Tv, Tb)

    def gates(self, gt, b, qs):
        nq = self.half // 512
        r, ql = qs // nq, qs % nq
        for g, (dstt, dstb) in enumerate(((gt, b), (self.Tg, self.Tg_b))):
            src = self.gdst[r * self.half + ql * 512:r * self.half + (ql + 1) * 512, 12 * g:12 * g + 12]
            self.S.dma("sp", dstt[:], src.rearrange("(j p) c -> p j c", p=128), writes=[dstb])
        self._blend(gt[:], b, self.Tg[:], self.Tg_b)


def build_all(NL=NLAYER):
    nc = bass.Bass("TRN2", target_bir_lowering=False)
    NT, S, D_ = 2048, 4096, 2048
    ein = lambda n, s, dt=F32: nc.dram_tensor(n, list(s), dt, kind="ExternalInput").ap()
    x = ein("x", [NT, D_]); mem = ein("mem", [256, D_]); sel = ein("sel", [128, 2])
    norm_mix = ein("norm_mix", [NL, 1, D_]); w_in = ein("w_in", [NL, D_, INW])
    ln_g = ein("ln_g", [NL, 1, 512]); ln_b = ein("ln_b", [NL, 1, 512])
    wsT = ein("wsT", [NL, 128, 4, 128]); bsT = ein("bsT", [NL, 128, 4]); posT = ein("posT", [NL, 128, 32])
    ck_w1 = ein("ck_w1", [NL, 4096, 128]); ck_w2 = ein("ck_w2", [NL, 128, 128])
    cv_w1 = ein("cv_w1", [NL, 4096, 128]); cv_w2 = ein("cv_w2", [NL, 128, 128])
    w_out = ein("w_out", [NL, D_, D_]); g_x = ein("g_x", [NL, 1, D_]); g_mem = ein("g_mem", [NL, 1, D_])
    wq = ein("wq", [NL, D_, 512]); wkv = ein("wkv", [NL, D_, 1024]); wo = ein("wo", [NL, 512, D_])
    g_mlp = ein("g_mlp", [NL, 1, D_]); w_up = ein("w_up", [NL, D_, DFF]); w_down = ein("w_down", [NL, DFF, D_])
    g_fin = ein("g_fin", [1, D_])
    tril = ein("tril", [128, 128]); ident = ein("ident", [128, 128])
    WD = 384 + 128 * 16 + 512; WW = 384 + 128 * 4 + 512; WC = 384 + 512
    dstrip = ein("dstrip", [128, WD], BF16); wstrip = ein("wstrip", [128, WW], BF16); cstrip = ein("cstrip", [128, WC], BF16)
    cbias = ein("cbias", [S // 512, 128, 2, 512], BF16); ovd = ein("ov", [128, 2, 64])
    selmask = ein("selmask", [S // 128, 128, 2, 64]); esel = ein("esel", [64, S // 128, 128], BF16)
    y = nc.dram_tensor("y", [NT, D_], F32, kind="ExternalOutput").ap()
    xs_t = nc.dram_tensor("xs_s", [NT, D_], F32)
    mixA_t = nc.dram_tensor("mixA_s", [4, 128, NT], BF16)
    exA_src_t = nc.dram_tensor("exA_src", [4096, NT], BF16)
    exA_dst_t = nc.dram_tensor("exA_dst", [8192, NT], BF16)
    gate_src_t = nc.dram_tensor("gate_src", [NT, 24], F32)
    gate_dst_t = nc.dram_tensor("gate_dst", [2 * NT, 24], F32)
    exB_src_t = nc.dram_tensor("exB_src", [768, S], BF16)
    exB_dst_t = nc.dram_tensor("exB_dst", [1536, S], BF16)
    WSPEC = {"w_out": (1, 4, 16), "wq": (1, 1, 16), "wkv": (1, 2, 16), "wo": (1, 4, 4), "w_up": (1, 16, 16), "w_down": (4, 4, 16)}
    wbf2 = [{k: nc.dram_tensor("wbf%d_%s" % (par, k), [ntr, ntc, 128, kr, 512], BF16).ap() for k, (ntr, ntc, kr) in WSPEC.items()}
            for par in range(2)]
    xs_s, mixA_s, exA_src, exA_dst = xs_t.ap(), mixA_t.ap(), exA_src_t.ap(), exA_dst_t.ap()
    gate_src, gate_dst, exB_src, exB_dst = gate_src_t.ap(), gate_dst_t.ap(), exB_src_t.ap(), exB_dst_t.ap()

    perm = [0, 1, 2, 3]
    for k in range(12):
        c, g, e = k // 4, (k // 2) % 2, k % 2
        i = 2 * c + e
        perm.append(4 + 2 * g + i if i < 2 else 8 + 4 * g + (i - 2))

    def precast_jobs(l):
        wl = {"w_out": w_out[l], "wq": wq[l], "wkv": wkv[l], "wo": wo[l], "w_up": w_up[l], "w_down": w_down[l]}
        jobs = []
        for k, (ntr, ntc, kr) in WSPEC.items():
            for tr in range(ntr):
                for tc in range(ntc):
                    srcv = wl[k][tr * 2048:tr * 2048 + kr * 128, tc * 512:(tc + 1) * 512].rearrange("(k p) c -> p k c", p=128)
                    jobs.append((wbf2[l % 2][k][tr, tc], srcv))
        return jobs

    def precast(l, bufs):
        wl = {"w_out": w_out[l], "wq": wq[l], "wkv": wkv[l], "wo": wo[l], "w_up": w_up[l], "w_down": w_down[l]}
        for k, (ntr, ntc, kr) in WSPEC.items():
            for tr in range(ntr):
                for tc in range(ntc):
                    srcv = wl[k][tr * 2048:tr * 2048 + kr * 128, tc * 512:(tc + 1) * 512].rearrange("(k p) c -> p k c", p=128)
                    Sc.dma("pool", wbf2[l % 2][k][tr, tc], srcv, reads=bufs)

    with ExitStack() as es:
        Sc = Sched(nc, es, arena_elems=ARENA)
        PSP = make_psum(Sc)
        outs = []
        for l in range(NL):
            x_src = x if l == 0 else xs_s
            x_dst = y if l == NL - 1 else xs_s
            Sc.phase_reset()
            ioA = dict(x=x_src, gmix=norm_mix[l], w_in=w_in[l], ln_g=ln_g[l], ln_b=ln_b[l], wsT=wsT[l], bsT=bsT[l],
                       tril=tril, ident=ident, o_mixA=mixA_s,
                       o_fm=exA_src[0:3072, :].rearrange("(h p) t -> h p t", p=128),
                       o_tm=exA_src[3072:4096, :].rearrange("r (two c) -> (r two) c", two=2),
                       o_gate=gate_src)
            ioA["mid_hook"] = lambda: Sc.allgather(
                [(exA_src[i * 512:(i + 1) * 512, :], exA_dst[i * 1024:(i + 1) * 1024, :]) for i in range(8)]
                + [(gate_src, gate_dst)], GROUPS, post_barrier=False)
            emit_A(Sc, PSP, ioA, NT)
            Sc.barrier()
            Sc.phase_reset()
            ld = FusedLoaderB(Sc, exA_dst, gate_dst, sel, S)
            ioB = dict(posT=posT[l], ck_w1=ck_w1[l], ck_w2=ck_w2[l], cv_w1=cv_w1[l], cv_w2=cv_w2[l], ident=ident,
                       dstrip=dstrip, wstrip=wstrip, cstrip=cstrip, cbias=cbias, ov=ovd, selmask=selmask, esel=esel,
                       mixT=exB_src.rearrange("(h p) t -> h p t", p=128), bf16_consts=True)

            if l == 0:
                ioB["after_loads"] = lambda bufs: precast(0, bufs)
            emit_B(Sc, PSP, ioB, S, ld=ld)
            Sc.allgather([(exB_src[i * 256:(i + 1) * 256, :], exB_dst[i * 512:(i + 1) * 512, :]) for i in range(3)], GROUPS)
            Sc.phase_reset()
            selt = Sc.sb("seltC", [128, 2], F32); sel_b = Sc.buf()
            Sc.dma("sp", selt[:], sel[:, :], writes=[sel_b])
            Tmx = Sc.sb("Tmx", [128, 12, 512], BF16); Tmx_b = Sc.buf()
            dv = exB_dst.rearrange("(k p) t -> p k t", p=128)

            def mx_loader(S_, mx, mx_b, t0, selt=selt, sel_b=sel_b, Tmx=Tmx, Tmx_b=Tmx_b, dv=dv):
                S_.dma("sp", mx[:, 0:4, :], mixA_s[:, :, t0:t0 + 512].rearrange("k p t -> p k t"), writes=[mx_b])
                S_.dma("sp", mx[:, 4:16, :], dv[:, :, t0:t0 + 512], writes=[mx_b])
                S_.dma("sp", Tmx[:], dv[:, :, NT + t0:NT + t0 + 512], writes=[Tmx_b])
                S_.op("dve", lambda e: e.tensor_scalar(out=Tmx[:], in0=Tmx[:], scalar1=selt[:, 1:2], scalar2=None, op0=ALU.mult),
                      reads=[Tmx_b, sel_b], writes=[Tmx_b])
                S_.op("dve", lambda e: e.scalar_tensor_tensor(out=mx[:, 4:16, :], in0=mx[:, 4:16, :], scalar=selt[:, 0:1], in1=Tmx[:],
                                                              op0=ALU.mult, op1=ALU.add),
                      reads=[mx_b, Tmx_b, sel_b], writes=[mx_b])

            ioC = dict(x=x_src, w_out=w_out[l], g_x=g_x[l], g_mem=g_mem[l], mem=mem, wq=wq[l], wkv=wkv[l], wo=wo[l],
                       g_mlp=g_mlp[l], w_up=w_up[l], w_down=w_down[l], g_fin=g_fin, ident=ident, y=x_dst,
                       wqueue="sp", wtile=lambda name, tr, tc, wbf=wbf2[l % 2]: wbf[name][tr, tc], no_pool=True)
            if l + 1 < NL:
                jobs = precast_jobs(l + 1)
                mark = Sc.sb("mark", [128, 8], F32)
                state = {"n": 0}

                def tick(jobs=jobs, mark=mark, state=state):
                    n = state["n"]
                    state["n"] += 1
                    if n % 3 == 0 and n // 3 < len(jobs):
                        mb = Sc.buf()
                        Sc.op("dve", lambda e: e.memset(mark[:, 0:1], 0.0), writes=[mb])
                        dstv, srcv = jobs[n // 3]
                        Sc.dma("pool", dstv, srcv, reads=[mb])

                ioC["tick"] = tick
            outs = emit_C(Sc, PSP, ioC, NT, FINAL=(l == NL - 1), mx_loader=mx_loader, perm=perm)
            Sc.barrier()
        Sc.emit(final_waits=outs)
    return nc

_NC = {}


def kernel(x, mem, norm_mix, w_in, gmlp_ln_g, gmlp_ln_b, gmlp_w_s, gmlp_b_s, cmp_pos,
           cmp_k_w1, cmp_k_w2, cmp_v_w1, cmp_v_w2, w_out, norm_xattn, norm_mem,
           xattn_wq, xattn_wkv, xattn_wo, norm_mlp, w_up, w_down, final_norm):
    f32 = np.float32
    A = lambda a: np.ascontiguousarray(np.asarray(a), dtype=f32)
    if "nc" not in _NC:
        _NC["nc"] = build_all(NLAYER)
    nc = _NC["nc"]
    L = NLAYER
    shared = dict(
        norm_mix=A(norm_mix).reshape(L, 1, -1), w_in=A(w_in), ln_g=A(gmlp_ln_g).reshape(L, 1, -1),
        ln_b=A(gmlp_ln_b).reshape(L, 1, -1), wsT=A(np.asarray(gmlp_w_s).transpose(0, 3, 1, 2)),
        bsT=A(np.asarray(gmlp_b_s).transpose(0, 2, 1)), posT=A(np.asarray(cmp_pos).transpose(0, 2, 1)),
        ck_w1=A(cmp_k_w1), ck_w2=A(cmp_k_w2), cv_w1=A(cmp_v_w1), cv_w2=A(cmp_v_w2),
        w_out=A(w_out), g_x=A(norm_xattn).reshape(L, 1, -1), g_mem=A(norm_mem).reshape(L, 1, -1),
        wq=A(xattn_wq), wkv=A(xattn_wkv), wo=A(xattn_wo), g_mlp=A(norm_mlp).reshape(L, 1, -1),
        w_up=A(w_up), w_down=A(w_down), g_fin=A(final_norm).reshape(1, -1),
        tril=np.triu(np.ones((128, 128), f32)))
    import ml_dtypes
    for k, v in consts_B(4096).items():
        if k in ("dstrip", "wstrip", "cstrip", "cbias", "esel"):
            v = v.astype(ml_dtypes.bfloat16)
        shared[k] = np.ascontiguousarray(v)
    x = np.asarray(x)
    mem = np.asarray(mem)
    in_maps = []
    for c in range(8):
        b, hh = c // 2, c % 2
        sel = np.zeros((128, 2), f32)
        sel[:, hh] = 1.0
        d = dict(x=A(x[b, hh * 2048:(hh + 1) * 2048]), mem=A(mem[b]), sel=sel)
        d.update(shared)
        in_maps.append(d)
    res = run_bass_kernel_spmd(nc, in_maps, core_ids=list(range(8)))
    out = np.empty((4, 4096, 2048), f32)
    for c in range(8):
        out[c // 2, (c % 2) * 2048:(c % 2 + 1) * 2048] = res.results[c]["y"]
    return out
```

```python
from concourse.bass_utils import run_bass_kernel_spmd
import numpy as np
import concourse.bass as bass
import concourse.mybir as mybir
from contextlib import ExitStack

F32 = mybir.dt.float32
BF16 = mybir.dt.bfloat16
AF = mybir.ActivationFunctionType
ALU = mybir.AluOpType
AX = mybir.AxisListType

STRICT_SAME_ENGINE = True


class Buf:
    __slots__ = ("name", "w", "r")

    def __init__(self, name):
        self.name = name
        self.w = None
        self.r = []


class Op:
    __slots__ = ("eng", "idx", "fn", "waits", "needed", "val", "kind", "semh", "key")

    def __init__(self, eng, idx, fn, kind="c"):
        self.eng = eng
        self.idx = idx
        self.fn = fn
        self.waits = []
        self.needed = False
        self.val = None
        self.kind = kind
        self.semh = None
        self.key = None


class Sched:
    ENGS = ("pe", "act", "dve", "pool", "sp")
    NDMA = {"sp": 20, "pool": 12}

    def __init__(self, nc, es, arena_elems=None):
        self.nc = nc
        self.es = es
        self.ops = {e: [] for e in self.ENGS}
        self.seen = {e: {} for e in self.ENGS}
        self.csem = {e: es.enter_context(nc.semaphore("cs_" + e)) for e in self.ENGS}
        self.dsem = {q: [es.enter_context(nc.semaphore("ds_%s%d" % (q, i))) for i in range(n)]
                     for q, n in self.NDMA.items()}
        self.ccsem = es.enter_context(nc.semaphore("ccs"))
        self.cccount = 0
        self.lastcc = None
        self.dcount = {q: 0 for q in self.NDMA}
        self.dlast = {q: [None] * n for q, n in self.NDMA.items()}
        self.lastc = {e: None for e in self.ENGS}
        self.nbuf = 0
        self.arena = None
        self.aoff = 0
        if arena_elems:
            self.arena = es.enter_context(nc.sbuf_tensor("arena", [128, arena_elems], BF16))
            self.arena_elems = arena_elems
        self.fence = []
        self.fence_id = 0
        self.passed = {e: 0 for e in self.ENGS}

    def sb(self, name, shape, dt):
        if self.arena is None:
            return self.es.enter_context(self.nc.sbuf_tensor(name, list(shape), dt))
        n = 1
        for s in shape[1:]:
            n *= s
        esz = 4 if dt == F32 else 2
        nb = (n * esz + 31) // 32 * 32
        o = self.aoff
        self.aoff += nb // 2
        assert self.aoff <= self.arena_elems, ("arena overflow", name, self.aoff)
        v = self.arena[0:shape[0], o:o + n * esz // 2]
        if dt != BF16:
            v = v.bitcast(dt)
        if len(shape) == 3:
            v = v.rearrange("p (a b) -> p a b", a=shape[1])
        elif len(shape) == 4:
            v = v.rearrange("p (a b c) -> p a b c", a=shape[1], b=shape[2])
        return v

    def phase_reset(self):
        self.aoff = 0

    def ps(self, name, shape, dt):
        return self.es.enter_context(self.nc.psum_tensor(name, list(shape), dt))

    def buf(self, name=None):
        self.nbuf += 1
        return Buf(name or "b%d" % self.nbuf)

    def barrier(self):
        f = []
        for e in self.ENGS:
            if self.lastc[e] is not None:
                f.append(self.lastc[e])
        for q in self.NDMA:
            for o in self.dlast[q]:
                if o is not None:
                    f.append(o)
        if self.lastcc is not None:
            f.append(self.lastcc)
        self.fence = f
        self.fence_id += 1

    def _dep(self, op, dep, force=False):
        if dep is None or dep is op:
            return
        if dep.kind == "c" and dep.eng == op.eng:
            if op.eng in ("pe", "sp") or not (STRICT_SAME_ENGINE or force):
                return
        seen = self.seen[op.eng]
        if seen.get(dep.key, -1) >= dep.idx:
            return
        seen[dep.key] = dep.idx
        dep.needed = True
        op.waits.append(dep)

    def op(self, eng, fn, reads=(), writes=(), kind="c"):
        lst = self.ops[eng]
        o = Op(eng, len(lst), fn, kind=kind)
        if kind == "d":
            q = eng
            n = self.NDMA[q]
            j = self.dcount[q] % n
            o.semh = self.dsem[q][j]
            o.key = ("d", q, j)
            o.idx = self.dcount[q] // n
            o.val = 16 * (o.idx + 1)
            prev = self.dlast[q][j]
            if prev is not None:
                self._dep(o, prev)
            self.dlast[q][j] = o
            self.dcount[q] += 1
        elif kind == "cc":
            o.semh = self.ccsem
            o.key = ("cc",)
            o.idx = self.cccount
            self.cccount += 1
            o.val = self.cccount
            self.lastcc = o
        else:
            o.semh = self.csem[eng]
            o.key = ("c", eng)
            self.lastc[eng] = o
        if self.passed[eng] < self.fence_id:
            for d in self.fence:
                self._dep(o, d, force=True)
            self.passed[eng] = self.fence_id
        for b in reads:
            self._dep(o, b.w)
        for b in writes:
            self._dep(o, b.w)
            for r in b.r:
                self._dep(o, r)
        for b in reads:
            b.r.append(o)
        for b in writes:
            b.w = o
            b.r = []
        lst.append(o)
        return o

    def dma(self, q, out, in_, reads=(), writes=()):
        return self.op(q, lambda e: e.dma_start(out=out, in_=in_), reads, writes, kind="d")

    def allgather(self, pairs, groups, post_barrier=True):
        self.barrier()
        o = None
        for src, dst in pairs:
            o = self.op("pool", lambda e, src=src, dst=dst: e.collective_compute(
                "AllGather", ALU.bypass, replica_groups=groups, ins=[src], outs=[dst]), kind="cc")
        if post_barrier:
            self.barrier()
        return o

    def emit(self, final_waits=()):
        nc = self.nc
        for e in self.ENGS:
            c = 0
            for o in self.ops[e]:
                if o.kind == "c" and o.needed:
                    c += 1
                    o.val = c
        engmap = {"pe": "tensor", "act": "scalar", "dve": "vector", "pool": "gpsimd", "sp": "sync"}

        def run(e, engine):
            for o in self.ops[e]:
                for d in o.waits:
                    engine.wait_ge(d.semh, d.val)
                ins = o.fn(engine)
                if o.kind == "d":
                    ins.then_inc(o.semh, 16)
                elif o.kind == "cc":
                    ins.then_inc(o.semh, 1)
                elif o.needed:
                    ins.then_inc(o.semh, 1)
            if e == "sp":
                best = {}
                for d in final_waits:
                    if d.key not in best or best[d.key].val < d.val:
                        best[d.key] = d
                for d in best.values():
                    engine.wait_ge(d.semh, d.val)

        with nc.Block() as block:
            for e in self.ENGS:
                getattr(block, engmap[e])(lambda engine, e=e: run(e, engine))


D = 2048
HD = 128
INW = 5144
EPS = 1e-6


def make_psum(S):
    PS = [S.ps("ps%d" % i, [128, 512], F32) for i in range(8)]
    PS_b = [S.buf() for _ in range(8)]
    return PS, PS_b


def emit_A(S, PSP, io, NT):
    NTT = NT // 128
    NSB = NT // 512
    KC = D // 128
    x, gmix, w_in, ln_g, ln_b, wsT, bsT, tril, ident = (io[k] for k in ("x", "gmix", "w_in", "ln_g", "ln_b", "wsT", "bsT", "tril", "ident"))
    o_mixA, o_fm, o_tm, o_gate = (io[k] for k in ("o_mixA", "o_fm", "o_tm", "o_gate"))
    hT = S.sb("hT", [128, KC, NT], BF16)
    hT_b = [S.buf() for _ in range(NTT)]
    gB = S.sb("gB", [128, D], F32); gB_b = S.buf()
    identb = S.sb("identb", [128, 128], BF16); ident_b = S.buf()
    xt = [S.sb("xt%d" % i, [128, D], F32) for i in range(2)]
    xt_b = [S.buf() for _ in range(2)]
    hb = [S.sb("hb%d" % i, [128, D], BF16) for i in range(2)]
    hb_b = [S.buf() for _ in range(2)]
    junk = S.sb("junk", [128, D], BF16); junk_b = S.buf()
    st = [S.sb("st%d" % i, [128, 8], F32) for i in range(2)]
    st_b = [S.buf() for _ in range(2)]
    W = [S.sb("W%d" % i, [128, KC, 512], BF16) for i in range(3)]
    W_b = [S.buf() for _ in range(3)]
    PS, PS_b = PSP
    psn = [0]

    def next_ps():
        i = psn[0] % 8
        psn[0] += 1
        return PS[i], PS_b[i]

    outs = []
    S.dma("sp", gB[:], gmix.partition_broadcast(128), writes=[gB_b])
    S.dma("pool", identb[:], ident[:, :], writes=[ident_b])

    for tt in range(NTT):
        i = tt % 2
        S.dma("sp", xt[i][:], x[tt * 128:(tt + 1) * 128, :], writes=[xt_b[i]])
        S.op("act", lambda e, i=i: e.activation(out=junk[:], in_=xt[i][:], func=AF.Square,
                                                accum_out=st[i][:, 0:1]),
             reads=[xt_b[i]], writes=[junk_b, st_b[i]])
        S.op("dve", lambda e, i=i: e.tensor_scalar(out=st[i][:, 1:2], in0=st[i][:, 0:1], scalar1=1.0 / D,
                                                   scalar2=EPS, op0=ALU.mult, op1=ALU.add),
             reads=[st_b[i]], writes=[st_b[i]])
        S.op("act", lambda e, i=i: e.activation(out=st[i][:, 3:4], in_=st[i][:, 1:2], func=AF.Sqrt),
             reads=[st_b[i]], writes=[st_b[i]])
        S.op("dve", lambda e, i=i: e.reciprocal(out=st[i][:, 2:3], in_=st[i][:, 3:4]),
             reads=[st_b[i]], writes=[st_b[i]])
        S.op("dve", lambda e, i=i: e.scalar_tensor_tensor(out=hb[i][:], in0=xt[i][:], scalar=st[i][:, 2:3],
                                                          in1=gB[:], op0=ALU.mult, op1=ALU.mult),
             reads=[xt_b[i], st_b[i], gB_b], writes=[hb_b[i]])
        for q in range(KC // 4):
            p, pb = next_ps()
            pv = p[:].bitcast(BF16)
            for j in range(4):
                kc = q * 4 + j
                S.op("pe", lambda e, i=i, kc=kc, j=j, pv=pv: e.transpose(
                    out=pv[:, j * 128:(j + 1) * 128], in_=hb[i][:, kc * 128:(kc + 1) * 128], identity=identb[:]),
                    reads=[hb_b[i], ident_b], writes=[pb])
            eng = "act" if q % 2 == 0 else "dve"
            if eng == "act":
                S.op("act", lambda e, q=q, tt=tt, pv=pv: e.activation(
                    out=hT[:, q * 4:(q + 1) * 4, tt * 128:(tt + 1) * 128],
                    in_=pv[:, 0:512].rearrange("p (a b) -> p a b", a=4), func=AF.Copy),
                    reads=[pb], writes=[hT_b[tt]])
            else:
                S.op("dve", lambda e, q=q, tt=tt, pv=pv: e.tensor_copy(
                    out=hT[:, q * 4:(q + 1) * 4, tt * 128:(tt + 1) * 128],
                    in_=pv[:, 0:512].rearrange("p (a b) -> p a b", a=4)),
                    reads=[pb], writes=[hT_b[tt]])

    wn = [0]

    def load_W(c0, width):
        i = wn[0] % 3
        wn[0] += 1
        S.dma("pool", W[i][:, :, 0:width], w_in[:, c0:c0 + width].rearrange("(k p) c -> p k c", p=128),
              writes=[W_b[i]])
        return W[i], W_b[i]

    ostage = [S.sb("os%d" % i, [128, 512], BF16) for i in range(4)]
    ostage_b = [S.buf() for _ in range(4)]
    osn = [0]

    def next_os():
        i = osn[0] % 4
        osn[0] += 1
        return ostage[i], ostage_b[i]

    evn = [0]

    def evac(dst, dst_b, src, src_b, extra_reads=()):
        e = "act" if evn[0] % 2 == 0 else "dve"
        evn[0] += 1
        if e == "act":
            S.op("act", lambda en: en.activation(out=dst, in_=src, func=AF.Copy),
                 reads=[src_b] + list(extra_reads), writes=[dst_b])
        else:
            S.op("dve", lambda en: en.tensor_copy(out=dst, in_=src),
                 reads=[src_b] + list(extra_reads), writes=[dst_b])

    def fm_group(c0, head0, nheads):
        Wt, Wb = load_W(c0, nheads * 128)
        for j in range(nheads):
            for sb_ in range(NSB):
                p, pb = next_ps()
                for kc in range(KC):
                    S.op("pe", lambda e, p=p, Wt=Wt, kc=kc, j=j, sb_=sb_: e.matmul(
                        p[:], lhsT=Wt[:, kc, j * 128:(j + 1) * 128], rhs=hT[:, kc, sb_ * 512:(sb_ + 1) * 512],
                        start=(kc == 0), stop=(kc == KC - 1)),
                        reads=[Wb] + hT_b[sb_ * 4:(sb_ + 1) * 4], writes=[pb])
                o, ob = next_os()
                evac(o[:], ob, p[:], pb)
                outs.append(S.dma("sp", o_fm[head0 + j, :, sb_ * 512:(sb_ + 1) * 512], o[:], reads=[ob]))

    def tm_group(c0, width, oc0):
        Wt, Wb = load_W(c0, width)
        for tt in range(NTT):
            p, pb = next_ps()
            for kc in range(KC):
                S.op("pe", lambda e, p=p, Wt=Wt, kc=kc, tt=tt: e.matmul(
                    p[:, 0:width], lhsT=hT[:, kc, tt * 128:(tt + 1) * 128], rhs=Wt[:, kc, 0:width],
                    start=(kc == 0), stop=(kc == KC - 1)),
                    reads=[Wb, hT_b[tt]], writes=[pb])
            o, ob = next_os()
            evac(o[:, 0:width], ob, p[:, 0:width], pb)
            outs.append(S.dma("sp", o_tm[tt * 128:(tt + 1) * 128, oc0:oc0 + width], o[:, 0:width], reads=[ob]))

    fm_group(1024, 0, 4)
    fm_group(1536, 4, 4)
    tm_group(2048, 512, 0)
    fm_group(2560, 8, 4)
    fm_group(3072, 12, 4)
    fm_group(3584, 16, 4)
    fm_group(4096, 20, 2)
    fm_group(4608, 22, 2)
    tm_group(4352, 256, 512)
    tm_group(4864, 256, 768)
    Wt, Wb = load_W(5120, 24)
    gst = [S.sb("gst%d" % i, [128, 24], F32) for i in range(2)]; gst_b = [S.buf() for _ in range(2)]
    for tt in range(NTT):
        i = tt % 2
        p, pb = next_ps()
        for kc in range(KC):
            S.op("pe", lambda e, p=p, Wt=Wt, kc=kc, tt=tt: e.matmul(
                p[:, 0:24], lhsT=hT[:, kc, tt * 128:(tt + 1) * 128], rhs=Wt[:, kc, 0:24],
                start=(kc == 0), stop=(kc == KC - 1)),
                reads=[Wb, hT_b[tt]], writes=[pb])
        S.op("act", lambda e, i=i, p=p: e.activation(out=gst[i][:], in_=p[:, 0:24], func=AF.Sigmoid),
             reads=[pb], writes=[gst_b[i]])
        outs.append(S.dma("sp", o_gate[tt * 128:(tt + 1) * 128, :], gst[i][:], reads=[gst_b[i]]))
    def do_gmlp():
        lngB = S.sb("lngB", [128, 512], F32); lnbB = S.sb("lnbB", [128, 512], F32); ln_bb = S.buf()
        S.dma("sp", lngB[:], ln_g.partition_broadcast(128), writes=[ln_bb])
        S.dma("sp", lnbB[:], ln_b.partition_broadcast(128), writes=[ln_bb])
        wsf = S.sb("wsf", [128, 4, 128], F32); trilf = S.sb("trilf", [128, 128], F32)
        wsb = S.sb("wsb", [128, 4, 128], BF16); ws_b = S.buf(); wsf_b = S.buf()
        bsb = S.sb("bsb", [128, 4], F32); bs_b = S.buf()
        S.dma("sp", wsf[:], wsT[:, :, :], writes=[wsf_b])
        S.dma("sp", trilf[:], tril[:, :], writes=[wsf_b])
        S.dma("sp", bsb[:], bsT[:, :], writes=[bs_b])
        for g in range(4):
            S.op("dve", lambda e, g=g: e.tensor_tensor(out=wsb[:, g, :], in0=wsf[:, g, :], in1=trilf[:], op=ALU.mult),
                 reads=[wsf_b], writes=[ws_b])
        Wu, Wub = load_W(0, 512)
        Wv, Wvb = load_W(512, 512)
        ug = [S.sb("ug%d" % i, [128, 512], F32) for i in range(2)]; ug_b = [S.buf() for _ in range(2)]
        vg = [S.sb("vg%d" % i, [128, 512], F32) for i in range(2)]; vg_b = [S.buf() for _ in range(2)]
        vn = [S.sb("vn%d" % i, [128, 512], BF16) for i in range(2)]; vn_b = [S.buf() for _ in range(2)]
        oa = [S.sb("oa%d" % i, [128, 512], BF16) for i in range(2)]; oa_b = [S.buf() for _ in range(2)]
        bst = [S.sb("bst%d" % i, [128, 8], F32) for i in range(2)]; bst_b = [S.buf() for _ in range(2)]
        for tt in range(NTT):
            i = tt % 2
            pu, pub = next_ps()
            pvv, pvb = next_ps()
            for (p, pb, Wt, Wb) in ((pu, pub, Wu, Wub), (pvv, pvb, Wv, Wvb)):
                for kc in range(KC):
                    S.op("pe", lambda e, p=p, Wt=Wt, kc=kc, tt=tt: e.matmul(
                        p[:], lhsT=hT[:, kc, tt * 128:(tt + 1) * 128], rhs=Wt[:, kc, :],
                        start=(kc == 0), stop=(kc == KC - 1)),
                        reads=[Wb, hT_b[tt]], writes=[pb])
            S.op("act", lambda e, i=i, pu=pu: e.activation(out=ug[i][:], in_=pu[:], func=AF.Gelu_apprx_tanh),
                 reads=[pub], writes=[ug_b[i]])
            S.op("act", lambda e, i=i, pvv=pvv: e.activation(out=vg[i][:], in_=pvv[:], func=AF.Gelu_apprx_tanh),
                 reads=[pvb], writes=[vg_b[i]])
            S.op("dve", lambda e, i=i: e.bn_stats(out=bst[i][:, 0:6], in_=vg[i][:]), reads=[vg_b[i]], writes=[bst_b[i]])
            S.op("dve", lambda e, i=i: e.bn_aggr(out=bst[i][:, 6:8], in_=bst[i][:, 0:6]), reads=[bst_b[i]], writes=[bst_b[i]])
            S.op("dve", lambda e, i=i: e.tensor_scalar(out=bst[i][:, 1:2], in0=bst[i][:, 7:8], scalar1=EPS, scalar2=None,
                                                       op0=ALU.add), reads=[bst_b[i]], writes=[bst_b[i]])
            S.op("act", lambda e, i=i: e.activation(out=bst[i][:, 2:3], in_=bst[i][:, 1:2], func=AF.Sqrt),
                 reads=[bst_b[i]], writes=[bst_b[i]])
            S.op("dve", lambda e, i=i: e.reciprocal(out=bst[i][:, 0:1], in_=bst[i][:, 2:3]),
                 reads=[bst_b[i]], writes=[bst_b[i]])
            S.op("dve", lambda e, i=i: e.tensor_scalar(out=vg[i][:], in0=vg[i][:], scalar1=bst[i][:, 6:7],
                                                       scalar2=bst[i][:, 0:1], op0=ALU.subtract, op1=ALU.mult),
                 reads=[vg_b[i], bst_b[i]], writes=[vg_b[i]])
            S.op("dve", lambda e, i=i: e.tensor_tensor(out=vg[i][:], in0=vg[i][:], in1=lngB[:], op=ALU.mult),
                 reads=[vg_b[i], ln_bb], writes=[vg_b[i]])
            S.op("dve", lambda e, i=i: e.tensor_tensor(out=vn[i][:], in0=vg[i][:], in1=lnbB[:], op=ALU.add),
                 reads=[vg_b[i], ln_bb], writes=[vn_b[i]])
            psv, psvb = next_ps()
            for g in range(4):
                S.op("pe", lambda e, g=g, i=i, psv=psv: e.matmul(
                    psv[:, g * 128:(g + 1) * 128], lhsT=wsb[:, g, :], rhs=vn[i][:, g * 128:(g + 1) * 128],
                    start=True, stop=True), reads=[ws_b, vn_b[i]], writes=[psvb])
            for g in range(4):
                S.op("dve", lambda e, g=g, i=i, psv=psv: e.scalar_tensor_tensor(
                    out=oa[i][:, g * 128:(g + 1) * 128], in0=psv[:, g * 128:(g + 1) * 128], scalar=bsb[:, g:g + 1],
                    in1=ug[i][:, g * 128:(g + 1) * 128], op0=ALU.add, op1=ALU.mult),
                    reads=[psvb, bs_b, ug_b[i]], writes=[oa_b[i]])
            pt, ptb = next_ps()
            ptv = pt[:].bitcast(BF16)
            for g in range(4):
                S.op("pe", lambda e, g=g, i=i, ptv=ptv: e.transpose(
                    out=ptv[:, g * 128:(g + 1) * 128], in_=oa[i][:, g * 128:(g + 1) * 128], identity=identb[:]),
                    reads=[oa_b[i], ident_b], writes=[ptb])
            o, ob = next_os()
            evac(o[:], ob, ptv[:, 0:512], ptb)
            outs.append(S.dma("sp", o_mixA[:, :, tt * 128:(tt + 1) * 128].rearrange("g p t -> p g t"),
                              o[:].rearrange("p (g t) -> p g t", g=4), reads=[ob]))


    if io.get("mid_hook"):
        io["mid_hook"]()
    do_gmlp()
    return outs


def build_A(NT=2048):
    nc = bass.Bass("TRN2", target_bir_lowering=False)
    NTT = NT // 128
    NSB = NT // 512
    KC = D // 128
    dr = lambda n, s, dt, k: nc.dram_tensor(n, list(s), dt, kind=k).ap()
    x = dr("x", [NT, D], F32, "ExternalInput")
    gmix = dr("gmix", [1, D], F32, "ExternalInput")
    w_in = dr("w_in", [D, INW], F32, "ExternalInput")
    ln_g = dr("ln_g", [1, 512], F32, "ExternalInput")
    ln_b = dr("ln_b", [1, 512], F32, "ExternalInput")
    wsT = dr("wsT", [128, 4, 128], F32, "ExternalInput")
    bsT = dr("bsT", [128, 4], F32, "ExternalInput")
    tril = dr("tril", [128, 128], F32, "ExternalInput")
    ident = dr("ident", [128, 128], F32, "ExternalInput")
    o_mixA = dr("o_mixA", [4, 128, NT], BF16, "ExternalOutput")
    o_fm = dr("o_fm", [24, 128, NT], BF16, "ExternalOutput")
    o_tm = dr("o_tm", [NT, 1024], BF16, "ExternalOutput")
    o_gate = dr("o_gate", [NT, 24], F32, "ExternalOutput")

    io = dict(x=x, gmix=gmix, w_in=w_in, ln_g=ln_g, ln_b=ln_b, wsT=wsT, bsT=bsT, tril=tril, ident=ident,
              o_mixA=o_mixA, o_fm=o_fm, o_tm=o_tm, o_gate=o_gate)
    with ExitStack() as es:
        S = Sched(nc, es)
        outs = emit_A(S, make_psum(S), io, NT)
        S.emit(final_waits=outs)
    return nc


HD = 128
BIG = 30000.0
SCALE = 128 ** -0.5
OFF0 = 384
DEBUG = False


def strip_const(mmin, fn):
    width = 384 - 128 * mmin + 512
    kl = np.arange(128)[:, None]
    v = np.arange(width)[None, :]
    return fn(v - OFF0 - kl).astype(np.float32)


def dil_fn(d):
    c = ((d >= 0) & (d <= 128)).astype(np.float64) + ((d >= 0) & (d % 4 == 0) & (d <= 512)) + ((d >= 0) & (d % 16 == 0) & (d <= 2048))
    out = np.full(d.shape, -BIG)
    m = c > 0
    out[m] = np.log(c[m]) / SCALE
    return out


def win_fn(d):
    return np.where((d >= 0) & (d <= 511), 0.0, -BIG)


def caus_fn(d):
    return np.where(d >= 0, 0.0, -BIG)


def consts_B(S):
    nqt = S // 128
    c = {}
    c["ident"] = np.eye(128, dtype=np.float32)
    c["dstrip"] = strip_const(-16, dil_fn)
    c["wstrip"] = strip_const(-4, win_fn)
    c["cstrip"] = strip_const(0, caus_fn)
    n = np.arange(256)
    t = np.arange(S)
    valid = (16 * n[:, None] + 31) <= t[None, :]
    cb = np.where(valid, 0.0, -BIG).astype(np.float32).reshape(2, 128, S // 512, 512).transpose(2, 1, 0, 3)
    c["cbias"] = np.ascontiguousarray(cb)
    n_sel = S // 64
    cs = n * 16
    ss = np.arange(64) * 64
    ov = ((cs[:, None] <= ss[None, :] + 63) & (cs[:, None] + 31 >= ss[None, :])).astype(np.float32)
    ov[255] = 0
    ov[:, n_sel:] = 0
    c["ov"] = np.ascontiguousarray(ov.reshape(2, 128, 64).transpose(1, 0, 2))
    sm = np.zeros((nqt, 128, 2, 64), np.float32)
    for gq in range(nqt):
        tt = 128 * gq + np.arange(128)
        jt = tt // 64
        s = np.arange(64)[None, :]
        valid_s = s <= jt[:, None]
        add = np.where(valid_s, 0.0, -1.0)
        vm = valid_s.astype(np.float32)
        for val, cond in ((1e4, s == 0), (2e4, s == jt[:, None]), (3e4, s == jt[:, None] - 1)):
            cond = np.broadcast_to(cond, add.shape)
            add = np.where(cond, val, add)
            vm = np.where(cond, 0.0, vm)
        sm[gq, :, 0] = vm
        sm[gq, :, 1] = add
    c["selmask"] = sm
    E = np.zeros((64, S // 128, 128), np.float32)
    for kt in range(S // 128):
        E[2 * kt, kt, :64] = 1
        E[2 * kt + 1, kt, 64:] = 1
    c["esel"] = E
    return c


class DirectLoaderB:
    def __init__(self, S_, io):
        self.S = S_
        self.io = io

    def fm(self, tile, b, kind, i):
        src = self.io[kind]
        ap = src[i, :, :] if kind in ("Bq", "Bk", "Cq") else src[:, :]
        self.S.dma("sp", tile[:], ap, writes=[b])

    def tm(self, tile, b, kind, i):
        src = self.io[kind]
        ap = src[:, i * 128:(i + 1) * 128] if kind == "Bv" else src
        self.S.dma("sp", tile[:, :, 0:128], ap.rearrange("(k p) d -> p k d", p=128), writes=[b])

    def gates(self, gt, b, qs):
        self.S.dma("sp", gt[:], self.io["gates"][qs * 512:(qs + 1) * 512, :].rearrange("(j p) c -> p j c", p=128), writes=[b])


def emit_B(S_, PSP, io, S, ld=None):
    NQS = S // 512
    NKT = S // 128
    WD = 384 + 128 * 16 + 512
    WW = 384 + 128 * 4 + 512
    WC = 384 + 512
    if ld is None:
        ld = DirectLoaderB(S_, io)
    (posT, ck_w1, ck_w2, cv_w1, cv_w2, ident, dstrip, wstrip, cstrip, cbias, ovd, selmask, esel, mixT) = (io[k] for k in (
        "posT", "ck_w1", "ck_w2", "cv_w1", "cv_w2", "ident", "dstrip", "wstrip", "cstrip", "cbias", "ov", "selmask", "esel", "mixT"))
    dbg = io.get("dbg")
    sb, buf, op, dma = S_.sb, S_.buf, S_.op, S_.dma
    BB = [sb("BB%d" % i, [128, S], BF16) for i in range(6)]; BB_b = [buf() for _ in range(6)]
    VA = [sb("VA%d" % i, [128, NKT, 129], BF16) for i in range(2)]; VA_b = [buf() for _ in range(2)]
    identb = sb("identb", [128, 128], BF16); identf = sb("identf", [128, 128], F32); id_b = buf()
    dst = sb("dst", [128, WD], BF16); wst = sb("wst", [128, WW], BF16); cst = sb("cst", [128, WC], BF16); strip_b = buf()
    eselb = sb("eselb", [64, NKT, 128], BF16); esel_b = buf()
    PT = [sb("PT%d" % i, [128, 512], BF16) for i in range(3)]; PT_b = [buf() for _ in range(3)]
    accS = sb("accS", [128, 4, 4, 128], F32); acc_b = [[buf() for _ in range(4)] for _ in range(4)]
    accB = sb("accB", [128, 4, 128], BF16); accB_b = buf()
    osb = [sb("osb%d" % i, [128, 512], BF16) for i in range(2)]; osb_b = [buf() for _ in range(2)]
    sm = [sb("sm%d" % i, [128, 16], F32) for i in range(4)]; sm_b = [buf() for _ in range(4)]
    gt = sb("gt", [128, 4, 12], F32); gt_b = buf()
    PS, PS_b = PSP
    cnt = {"st": 0, "acc": 0, "pt": 0, "sm": 0, "os": 0}

    def st_ps():
        i = cnt["st"] % 3; cnt["st"] += 1
        return PS[i], PS_b[i]

    def acc_ps():
        i = 3 + cnt["acc"] % 4; cnt["acc"] += 1
        return PS[i], PS_b[i]

    MISC, MISC_b = PS[7], PS_b[7]

    def next_pt():
        i = cnt["pt"] % 3; cnt["pt"] += 1
        return PT[i], PT_b[i]

    def next_sm():
        i = cnt["sm"] % 4; cnt["sm"] += 1
        return sm[i], sm_b[i]

    CQ = "sp" if io.get("bf16_consts") else "pool"
    dma("sp", identf[:], ident[:, :], writes=[id_b])
    op("dve", lambda e: e.tensor_copy(out=identb[:], in_=identf[:]), reads=[id_b], writes=[id_b])
    dma(CQ, dst[:], dstrip[:, :], writes=[strip_b])
    dma(CQ, wst[:], wstrip[:, :], writes=[strip_b])
    dma(CQ, cst[:], cstrip[:, :], writes=[strip_b])
    dma(CQ, eselb[:], esel[:, :, :], writes=[esel_b])
    for i in range(2):
        op("dve", lambda e, i=i: e.memset(VA[i][:, :, 128:129], 1.0), writes=[VA_b[i]])

    outs = []

    def attend(qT, q_b, qs, kT, k_b, Vt, V_b, ktiles, strip, extra_bias=None, ncols=129, rhs_fn=None, band=None):
        banks = [acc_ps(), acc_ps()]
        accs = [(banks[j // 2][0][:, (j % 2) * 256:(j % 2) * 256 + ncols], banks[j // 2][1]) for j in range(4)]
        first = [True] * 4
        last_kt = {}
        def live(kt, j):
            d = 4 * qs + j - kt
            return d >= 0 and (band is None or d <= band)

        for kt in ktiles:
            for j in range(4):
                if live(kt, j):
                    last_kt[j] = kt
        def stage1(kt):
            m = kt - 4 * qs
            p, pb = st_ps()
            nb = (1 if strip is not None and strip(m) is not None else 0) + (1 if extra_bias else 0)
            op("pe", lambda e, p=p, kt=kt: e.matmul(p[:], lhsT=kT[:, kt * 128:(kt + 1) * 128],
                                                   rhs=qT[:, qs * 512:(qs + 1) * 512], start=True, stop=(nb == 0)),
               reads=[k_b, q_b], writes=[pb])
            k = 0
            if extra_bias:
                k += 1
                l_ap, r_ap, rb = extra_bias(kt)
                op("pe", lambda e, p=p, l_ap=l_ap, r_ap=r_ap, k=k: e.matmul(p[:], lhsT=l_ap, rhs=r_ap, start=False, stop=(k == nb)),
                   reads=rb, writes=[pb])
            if strip is not None and strip(m) is not None:
                k += 1
                s_ap = strip(m)
                op("pe", lambda e, p=p, s_ap=s_ap: e.matmul(p[:], lhsT=identb[:], rhs=s_ap, start=False, stop=True),
                   reads=[id_b, strip_b], writes=[pb])
            pt, ptb = next_pt()
            op("act", lambda e, p=p, pt=pt: e.activation(out=pt[:], in_=p[:], func=AF.Exp, scale=SCALE),
               reads=[pb], writes=[ptb])
            return pt, ptb

        def stage2(kt, pt, ptb):
            for j in range(4):
                if not live(kt, j):
                    continue
                a, ab = accs[j]
                rhs = Vt[:, kt, 0:ncols] if rhs_fn is None else rhs_fn(kt)
                op("pe", lambda e, a=a, pt=pt, j=j, rhs=rhs, st=(first[j] and j % 2 == 0), sp=(last_kt[j] == kt): e.matmul(
                    a, lhsT=pt[:, j * 128:(j + 1) * 128], rhs=rhs, start=st, stop=sp, skip_group_check=True),
                    reads=[ptb, V_b], writes=[ab])
                first[j] = False

        kts = list(ktiles)
        cur = stage1(kts[0])
        for i, kt in enumerate(kts):
            nxt = stage1(kts[i + 1]) if i + 1 < len(kts) else None
            stage2(kt, *cur)
            cur = nxt
        return accs

    def coef_of(a, ab, ncols, gate_ap=None, gate_b=None):
        s, sbb = next_sm()
        op("dve", lambda e: e.tensor_scalar(out=s[:, 0:1], in0=a[:, ncols - 1:ncols], scalar1=1e-30, scalar2=None, op0=ALU.max),
           reads=[ab], writes=[sbb])
        op("dve", lambda e: e.reciprocal(out=s[:, 1:2], in_=s[:, 0:1]), reads=[sbb], writes=[sbb])
        if gate_ap is not None:
            op("dve", lambda e: e.tensor_tensor(out=s[:, 1:2], in0=s[:, 1:2], in1=gate_ap, op=ALU.mult),
               reads=[sbb, gate_b], writes=[sbb])
        return s[:, 1:2], sbb

    def flush_head(head_out, qs, r):
        for j in range(4):
            op("act", lambda e, j=j: e.activation(out=accB[:, j, :], in_=accS[:, r, j, :], func=AF.Copy),
               reads=[acc_b[r][j]], writes=[accB_b])
        pv = MISC[:].bitcast(BF16)
        for j in range(4):
            op("pe", lambda e, j=j: e.transpose(out=pv[:, j * 128:(j + 1) * 128], in_=accB[:, j, :], identity=identb[:]),
               reads=[accB_b, id_b], writes=[MISC_b])
        i = cnt["os"] % 2; cnt["os"] += 1
        op("dve", lambda e, i=i: e.tensor_copy(out=osb[i][:], in_=pv[:, 0:512]), reads=[MISC_b], writes=[osb_b[i]])
        outs.append(dma("sp", mixT[head_out, :, qs * 512:(qs + 1) * 512], osb[i][:], reads=[osb_b[i]]))

    def dstrip_of(m):
        return dst[:, OFF0 - 128 * m:OFF0 - 128 * m + 512]

    for h in range(2):
        qT, q_b = BB[2 * h], BB_b[2 * h]
        kT, k_b = BB[2 * h + 1], BB_b[2 * h + 1]
        Vt, V_b = VA[h], VA_b[h]
        ld.fm(qT, q_b, "Bq", h)
        ld.fm(kT, k_b, "Bk", h)
        ld.tm(Vt, V_b, "Bv", h)
        for qs in range(NQS):
            ktiles = [kt for kt in range(4 * qs - 16, 4 * qs + 4) if kt >= 0]
            accs = attend(qT, q_b, qs, kT, k_b, Vt, V_b, ktiles, dstrip_of, band=16)
            for j in range(4):
                a, ab = accs[j]
                cf, cfb = coef_of(a, ab, 129)
                op("dve", lambda e, a=a, cf=cf, j=j: e.tensor_scalar(out=accS[:, 0, j, :], in0=a[:, 0:128], scalar1=cf,
                                                                   scalar2=None, op0=ALU.mult),
                   reads=[ab, cfb], writes=[acc_b[0][j]])
            flush_head(h, qs, 0)

    w1 = sb("w1", [128, 32, 128], BF16); w1_b = buf()
    w2 = sb("w2", [128, 128], BF16); w2_b = buf()
    posb = sb("posb", [128, 32], BF16); pos_b = buf()
    cvec = sb("cvec", [128, 1], F32); cvec_b = buf()
    hc = sb("hc", [128, 256], BF16); hc_b = buf()
    kcT = sb("kcT", [128, 256], BF16); kcT_b = buf()
    Rc = sb("Rc", [128, 2, 193], BF16); Rc_b = buf()
    ovf = sb("ovf", [128, 2, 64], F32); ovf_b = buf()
    posf = sb("posf", [128, 32], F32); w1f = sb("w1f", [128, 32, 128], F32); w2f = sb("w2f", [128, 128], F32); wf_b = buf()
    dma("sp", posf[:], posT[:, :], writes=[wf_b])
    op("dve", lambda e: e.tensor_copy(out=posb[:], in_=posf[:]), reads=[wf_b], writes=[pos_b])
    dma("sp", ovf[:], ovd[:, :, :], writes=[ovf_b])
    op("dve", lambda e: e.memset(Rc[:, :, 192:193], 1.0), writes=[Rc_b])
    op("dve", lambda e: e.tensor_copy(out=Rc[:, :, 128:192], in_=ovf[:]), reads=[ovf_b], writes=[Rc_b])
    ld.fm(BB[4], BB_b[4], "cmpk", 0)
    ld.fm(BB[5], BB_b[5], "cmpv", 0)
    for which in range(2):
        raw, raw_b = BB[4 + which], BB_b[4 + which]
        w1d, w2d = (ck_w1, ck_w2) if which == 0 else (cv_w1, cv_w2)
        dma("sp", w1f[:], w1d.rearrange("(i d) f -> d i f", d=128), writes=[wf_b])
        dma("sp", w2f[:], w2d[:, :], writes=[wf_b])
        op("act", lambda e: e.activation(out=w1[:], in_=w1f[:], func=AF.Copy), reads=[wf_b], writes=[w1_b])
        op("dve", lambda e: e.tensor_copy(out=w2[:], in_=w2f[:]), reads=[wf_b], writes=[w2_b])
        rv = raw[:].rearrange("p (n s) -> p n s", s=16)
        nblk = S // 16 - 1
        for i in range(32):
            op("pe", lambda e, i=i: e.matmul(MISC[:, 0:1], lhsT=w1[:, i, :], rhs=posb[:, i:i + 1], start=(i == 0), stop=(i == 31)),
               reads=[w1_b, pos_b], writes=[MISC_b])
        op("dve", lambda e: e.tensor_copy(out=cvec[:], in_=MISC[:, 0:1]), reads=[MISC_b], writes=[cvec_b])
        p, pb = st_ps()
        for i in range(32):
            rhs = rv[:, 0:nblk, i] if i < 16 else rv[:, 1:nblk + 1, i - 16]
            op("pe", lambda e, p=p, i=i, rhs=rhs: e.matmul(p[:, 0:nblk], lhsT=w1[:, i, :], rhs=rhs, start=(i == 0), stop=(i == 31)),
               reads=[w1_b, raw_b], writes=[pb])
        op("dve", lambda e: e.memset(hc[:], 0.0), writes=[hc_b])
        op("act", lambda e, p=p: e.activation(out=hc[:, 0:nblk], in_=p[:, 0:nblk], func=AF.Gelu_apprx_tanh, bias=cvec[:]),
           reads=[pb, cvec_b], writes=[hc_b])
        if which == 0:
            p2, p2b = st_ps()
            op("pe", lambda e, p2=p2: e.matmul(p2[:, 0:256], lhsT=w2[:], rhs=hc[:], start=True, stop=True),
               reads=[w2_b, hc_b], writes=[p2b])
            op("dve", lambda e, p2=p2: e.tensor_copy(out=kcT[:], in_=p2[:, 0:256]), reads=[p2b], writes=[kcT_b])
        else:
            for c in range(2):
                p2, p2b = st_ps()
                op("pe", lambda e, p2=p2, c=c: e.matmul(p2[:, 0:128], lhsT=hc[:, c * 128:(c + 1) * 128], rhs=w2[:], start=True, stop=True),
                   reads=[w2_b, hc_b], writes=[p2b])
                op("dve", lambda e, p2=p2, c=c: e.tensor_copy(out=Rc[:, c, 0:128], in_=p2[:, 0:128]), reads=[p2b], writes=[Rc_b])

    if DEBUG:
        outs.append(dma("sp", dbg[:, 0:256], kcT[:], reads=[kcT_b]))
        outs.append(dma("sp", dbg[:, 256:512], hc[:], reads=[hc_b]))
        outs.append(dma("sp", dbg[:, 512:768], Rc[:, 0, 0:128], reads=[Rc_b])) if False else None
    for r in range(4):
        ld.fm(BB[r], BB_b[r], "Cq", r)
    ld.fm(BB[4], BB_b[4], "slck", 0)
    ld.fm(BB[5], BB_b[5], "wink", 0)
    ld.tm(VA[0], VA_b[0], "slcv", 0)
    ld.tm(VA[1], VA_b[1], "winv", 0)
    if io.get("after_loads"):
        io["after_loads"](BB_b + VA_b)
    cbs = sb("cbs", [128, 2, 512], BF16); cbs_b = buf()
    smk = sb("smk", [128, 4, 2, 64], F32); smk_b = buf()
    imp = sb("imp", [128, 4, 64], F32); imp_b = [buf() for _ in range(4)]
    sc = sb("sc", [128, 64], F32); sc2 = sb("sc2", [128, 64], F32); sc_b = buf()
    mx8 = sb("mx8", [128, 16], F32); mx8_b = buf()
    selb = sb("selb", [128, 64], F32); selb_b = buf()
    selbT = sb("selbT", [64, 512], BF16); selbT_b = buf()

    def wstrip_of(m):
        return wst[:, OFF0 - 128 * m:OFF0 - 128 * m + 512]

    def cstrip_of(m):
        if m < 0:
            return None
        return cst[:, OFF0 - 128 * m:OFF0 - 128 * m + 512]

    for qs in range(NQS):
        ld.gates(gt, gt_b, qs)
        dma(CQ, cbs[:], cbias[qs, :, :, :], writes=[cbs_b])
        dma("sp", smk[:], selmask[4 * qs:4 * qs + 4, :, :, :].rearrange("j p a s -> p j a s"), writes=[smk_b])
        nchunks = [c for c in range(2) if (c * 2048 + 31) <= (qs * 512 + 511)]
        for r in range(4):
            qT, q_b = BB[r], BB_b[r]
            banks = [acc_ps(), acc_ps()]
            accs = [(banks[j // 2][0][:, (j % 2) * 256:(j % 2) * 256 + 193], banks[j // 2][1]) for j in range(4)]
            for ci, c in enumerate(nchunks):
                p, pb = st_ps()
                op("pe", lambda e, p=p, c=c, qT=qT, qs=qs: e.matmul(p[:], lhsT=kcT[:, c * 128:(c + 1) * 128], rhs=qT[:, qs * 512:(qs + 1) * 512],
                                                            start=True, stop=False), reads=[kcT_b, q_b], writes=[pb])
                op("pe", lambda e, p=p, c=c, cbs=cbs: e.matmul(p[:], lhsT=identb[:], rhs=cbs[:, c, :], start=False, stop=True),
                   reads=[id_b, cbs_b], writes=[pb])
                pt, ptb = next_pt()
                op("act", lambda e, p=p, pt=pt: e.activation(out=pt[:], in_=p[:], func=AF.Exp, scale=SCALE), reads=[pb], writes=[ptb])
                for j in range(4):
                    a, ab = accs[j]
                    op("pe", lambda e, a=a, pt=pt, j=j, c=c, ci=ci: e.matmul(a, lhsT=pt[:, j * 128:(j + 1) * 128], rhs=Rc[:, c, :],
                                                                           start=(ci == 0 and j % 2 == 0), stop=(ci == len(nchunks) - 1),
                                                                           skip_group_check=True),
                       reads=[ptb, Rc_b], writes=[ab])
            for j in range(4):
                a, ab = accs[j]
                s, sbb = next_sm()
                op("dve", lambda e, a=a, s=s: e.tensor_scalar(out=s[:, 0:1], in0=a[:, 192:193], scalar1=1e-30, scalar2=None, op0=ALU.max),
                   reads=[ab], writes=[sbb])
                op("dve", lambda e, s=s: e.reciprocal(out=s[:, 1:2], in_=s[:, 0:1]), reads=[sbb], writes=[sbb])
                op("dve", lambda e, s=s, j=j, r=r, gt=gt: e.tensor_tensor(out=s[:, 2:3], in0=s[:, 1:2], in1=gt[:, j, r * 3:r * 3 + 1], op=ALU.mult),
                   reads=[sbb, gt_b], writes=[sbb])
                op("dve", lambda e, a=a, s=s, j=j, r=r: e.tensor_scalar(out=accS[:, r, j, :], in0=a[:, 0:128], scalar1=s[:, 2:3],
                                                                      scalar2=None, op0=ALU.mult),
                   reads=[ab, sbb], writes=[acc_b[r][j]])
                if r == 0:
                    op("dve", lambda e, a=a, s=s, j=j: e.tensor_scalar(out=imp[:, j, :], in0=a[:, 128:192], scalar1=s[:, 1:2],
                                                                     scalar2=None, op0=ALU.mult),
                       reads=[ab, sbb], writes=[imp_b[j]])
                else:
                    op("dve", lambda e, a=a, s=s, j=j: e.scalar_tensor_tensor(out=imp[:, j, :], in0=a[:, 128:192], scalar=s[:, 1:2],
                                                                            in1=imp[:, j, :], op0=ALU.mult, op1=ALU.add),
                       reads=[ab, sbb, imp_b[j]], writes=[imp_b[j]])
        need_sel = (qs * 512 + 511) >= 1024
        for j in (range(4) if need_sel else ()):
            op("dve", lambda e, j=j, smk=smk: e.tensor_tensor(out=sc[:], in0=imp[:, j, :], in1=smk[:, j, 0, :], op=ALU.mult),
               reads=[imp_b[j], smk_b], writes=[sc_b])
            op("dve", lambda e, j=j, smk=smk: e.tensor_tensor(out=sc[:], in0=sc[:], in1=smk[:, j, 1, :], op=ALU.add),
               reads=[sc_b, smk_b], writes=[sc_b])
            op("dve", lambda e: e.max(out=mx8[:, 0:8], in_=sc[:]), reads=[sc_b], writes=[mx8_b])
            op("dve", lambda e: e.match_replace(out=sc2[:], in_to_replace=mx8[:, 0:8], in_values=sc[:], imm_value=-1e9),
               reads=[sc_b, mx8_b], writes=[sc_b])
            op("dve", lambda e: e.max(out=mx8[:, 8:16], in_=sc2[:]), reads=[sc_b], writes=[mx8_b])
            op("dve", lambda e: e.tensor_scalar(out=selb[:], in0=sc[:], scalar1=mx8[:, 15:16], scalar2=1.0, op0=ALU.is_ge, op1=ALU.subtract),
               reads=[sc_b, mx8_b], writes=[selb_b])
            op("pe", lambda e: e.transpose(out=MISC[0:64, 0:128], in_=selb[:], identity=identf[:]),
               reads=[selb_b, id_b], writes=[MISC_b])
            op("act", lambda e, j=j: e.activation(out=selbT[:, j * 128:(j + 1) * 128], in_=MISC[0:64, 0:128], func=AF.Copy, scale=BIG),
               reads=[MISC_b], writes=[selbT_b])
        for r in range(4):
            qT, q_b = BB[r], BB_b[r]
            accs = attend(qT, q_b, qs, BB[4], BB_b[4], VA[0], VA_b[0], list(range(0, 4 * qs + 4)), cstrip_of,
                          extra_bias=(lambda kt: (eselb[:, kt, :], selbT[:], [esel_b, selbT_b])) if need_sel else None)
            for j in range(4):
                a, ab = accs[j]
                cf, cfb = coef_of(a, ab, 129, gt[:, j, r * 3 + 1:r * 3 + 2], gt_b)
                op("dve", lambda e, a=a, cf=cf, j=j, r=r: e.scalar_tensor_tensor(out=accS[:, r, j, :], in0=a[:, 0:128], scalar=cf,
                                                                               in1=accS[:, r, j, :], op0=ALU.mult, op1=ALU.add),
                   reads=[ab, cfb, acc_b[r][j]], writes=[acc_b[r][j]])
            ktiles = [kt for kt in range(4 * qs - 4, 4 * qs + 4) if kt >= 0]
            accs = attend(qT, q_b, qs, BB[5], BB_b[5], VA[1], VA_b[1], ktiles, wstrip_of, band=4)
            for j in range(4):
                a, ab = accs[j]
                cf, cfb = coef_of(a, ab, 129, gt[:, j, r * 3 + 2:r * 3 + 3], gt_b)
                op("dve", lambda e, a=a, cf=cf, j=j, r=r: e.scalar_tensor_tensor(out=accS[:, r, j, :], in0=a[:, 0:128], scalar=cf,
                                                                               in1=accS[:, r, j, :], op0=ALU.mult, op1=ALU.add),
                   reads=[ab, cfb, acc_b[r][j]], writes=[acc_b[r][j]])
            flush_head(2 + r, qs, r)
    return outs


def build_B(S=4096):
    nc = bass.Bass("TRN2", target_bir_lowering=False)
    NQS = S // 512
    NKT = S // 128
    dr = lambda n, s, dt, k: nc.dram_tensor(n, list(s), dt, kind=k).ap()
    Bq = dr("Bq", [2, 128, S], BF16, "ExternalInput")
    Bk = dr("Bk", [2, 128, S], BF16, "ExternalInput")
    Bv = dr("Bv", [S, 256], BF16, "ExternalInput")
    Cq = dr("Cq", [4, 128, S], BF16, "ExternalInput")
    cmpk = dr("cmpk", [128, S], BF16, "ExternalInput")
    cmpv = dr("cmpv", [128, S], BF16, "ExternalInput")
    slck = dr("slck", [128, S], BF16, "ExternalInput")
    slcv = dr("slcv", [S, 128], BF16, "ExternalInput")
    wink = dr("wink", [128, S], BF16, "ExternalInput")
    winv = dr("winv", [S, 128], BF16, "ExternalInput")
    gates = dr("gates", [S, 12], F32, "ExternalInput")
    posT = dr("posT", [128, 32], F32, "ExternalInput")
    ck_w1 = dr("ck_w1", [4096, 128], F32, "ExternalInput")
    ck_w2 = dr("ck_w2", [128, 128], F32, "ExternalInput")
    cv_w1 = dr("cv_w1", [4096, 128], F32, "ExternalInput")
    cv_w2 = dr("cv_w2", [128, 128], F32, "ExternalInput")
    WD = 384 + 128 * 16 + 512
    WW = 384 + 128 * 4 + 512
    WC = 384 + 512
    ident = dr("ident", [128, 128], F32, "ExternalInput")
    dstrip = dr("dstrip", [128, WD], F32, "ExternalInput")
    wstrip = dr("wstrip", [128, WW], F32, "ExternalInput")
    cstrip = dr("cstrip", [128, WC], F32, "ExternalInput")
    cbias = dr("cbias", [NQS, 128, 2, 512], F32, "ExternalInput")
    ovd = dr("ov", [128, 2, 64], F32, "ExternalInput")
    selmask = dr("selmask", [NKT, 128, 2, 64], F32, "ExternalInput")
    esel = dr("esel", [64, NKT, 128], F32, "ExternalInput")
    mixT = dr("mixT", [6, 128, S], BF16, "ExternalOutput")
    dbg = dr("dbg", [128, 1024], BF16, "ExternalOutput") if DEBUG else None

    io = dict(Bq=Bq, Bk=Bk, Bv=Bv, Cq=Cq, cmpk=cmpk, cmpv=cmpv, slck=slck, slcv=slcv, wink=wink, winv=winv, gates=gates,
              posT=posT, ck_w1=ck_w1, ck_w2=ck_w2, cv_w1=cv_w1, cv_w2=cv_w2, ident=ident, dstrip=dstrip, wstrip=wstrip,
              cstrip=cstrip, cbias=cbias, ov=ovd, selmask=selmask, esel=esel, mixT=mixT, dbg=dbg)
    with ExitStack() as es:
        S_ = Sched(nc, es)
        outs = emit_B(S_, make_psum(S_), io, S)
        S_.emit(final_waits=outs)
    return nc


D = 2048
EPS = 1e-6
DFF = 8192


def emit_C(S, PSP, io, NT, FINAL, mx_loader=None, perm=None):
    NSB = NT // 512
    KC = D // 128
    SCALE = 128 ** -0.5
    if perm is None:
        perm = list(range(16))
    (x, mixT, w_out, g_x, g_mem, mem, wq, wkv, wo, g_mlp, w_up, w_down, g_fin, ident, y) = (io.get(k) for k in (
        "x", "mixT", "w_out", "g_x", "g_mem", "mem", "wq", "wkv", "wo", "g_mlp", "w_up", "w_down", "g_fin", "ident", "y"))
    xs = S.sb("xs", [128, 4, D], F32); xs_b = [S.buf() for _ in range(4)]
    mx = S.sb("mx", [128, KC, 512], BF16); mx_b = S.buf()
    hT = S.sb("hT", [128, KC, 512], BF16); hT_b = [S.buf() for _ in range(4)]
    aT = S.sb("aT", [128, 32, 512], BF16); aT_b = [S.buf() for _ in range(32)]
    W = [S.sb("W%d" % i, [128, KC, 512], BF16) for i in range(2)]; W_b = [S.buf() for _ in range(2)]
    gB = S.sb("gB", [128, D], F32); gB_b = S.buf()
    hb = [S.sb("hb%d" % i, [128, D], BF16) for i in range(2)]; hb_b = [S.buf() for _ in range(2)]
    st = [S.sb("st%d" % i, [128, 8], F32) for i in range(2)]; st_b = [S.buf() for _ in range(2)]
    identb = S.sb("identb", [128, 128], BF16); ident_b = S.buf()
    ones = S.sb("ones", [128, 128], BF16); ones_b = S.buf()
    qT = S.sb("qT", [128, 4, 512], BF16); qT_b = [S.buf() for _ in range(4)]
    kT = S.sb("kT", [128, 4, 256], BF16); kT_b = S.buf()
    Vm = S.sb("Vm", [128, 2, 512], BF16); Vm_b = S.buf()
    PT = [S.sb("PT%d" % i, [128, 2, 512], BF16) for i in range(2)]; PT_b = [S.buf() for _ in range(2)]
    oT = S.sb("oT", [128, 4, 512], BF16); oT_b = [S.buf() for _ in range(4)]
    rden0 = S.sb("rden0", [128, 512], F32); rden = [rden0, rden0]; rden0_b = S.buf(); rden_b = [rden0_b, rden0_b]
    rtmp = [S.sb("rtmp%d" % i, [128, 512], F32) for i in range(2)]; rtmp_b = [S.buf() for _ in range(2)]
    PS, PS_b = PSP
    psn = [0]

    def next_ps():
        i = psn[0] % 8
        psn[0] += 1
        return PS[i], PS_b[i]

    wn = [0]

    def load_W(view, K, width):
        i = wn[0] % 2
        wn[0] += 1
        S.dma(WQ, W[i][:, 0:K, 0:width], view, writes=[W_b[i]])
        if io.get("tick"):
            io["tick"]()
        return W[i], W_b[i]

    WQ = io.get("wqueue", "pool")
    wsrc = dict(w_out=w_out, wq=wq, wkv=wkv, wo=wo, w_up=w_up, w_down=w_down)

    def wview(name, tr, tc):
        if "wtile" in io:
            return io["wtile"](name, tr, tc)
        w = wsrc[name]
        kr = min(16, w.shape[0] // 128)
        return w[tr * 2048:tr * 2048 + kr * 128, tc * 512:(tc + 1) * 512].rearrange("(k p) c -> p k c", p=128)

    evn = [0]

    def copy_any(dst, src, reads, writes):
        e = "act" if evn[0] % 2 == 0 else "dve"
        evn[0] += 1
        if e == "act":
            S.op("act", lambda en: en.activation(out=dst, in_=src, func=AF.Copy), reads=reads, writes=writes)
        else:
            S.op("dve", lambda en: en.tensor_copy(out=dst, in_=src), reads=reads, writes=writes)

    rn = [0]

    def rms_T(src, src_b, g_ap, dstT, dst_b, c0, to_out=None):
        i = rn[0] % 2
        rn[0] += 1
        S.op("act", lambda e: e.activation(out=hb[i][:], in_=src, func=AF.Square, accum_out=st[i][:, 0:1]),
             reads=[src_b], writes=[hb_b[i], st_b[i]])
        S.op("dve", lambda e: e.tensor_scalar(out=st[i][:, 1:2], in0=st[i][:, 0:1], scalar1=1.0 / D, scalar2=EPS,
                                              op0=ALU.mult, op1=ALU.add), reads=[st_b[i]], writes=[st_b[i]])
        S.op("act", lambda e: e.activation(out=st[i][:, 3:4], in_=st[i][:, 1:2], func=AF.Sqrt),
             reads=[st_b[i]], writes=[st_b[i]])
        S.op("dve", lambda e: e.reciprocal(out=st[i][:, 2:3], in_=st[i][:, 3:4]), reads=[st_b[i]], writes=[st_b[i]])
        if to_out is not None:
            S.op("dve", lambda e: e.scalar_tensor_tensor(out=to_out, in0=src, scalar=st[i][:, 2:3], in1=gB[:],
                                                         op0=ALU.mult, op1=ALU.mult),
                 reads=[src_b, st_b[i], gB_b], writes=[src_b])
            return
        S.op("dve", lambda e: e.scalar_tensor_tensor(out=hb[i][:], in0=src, scalar=st[i][:, 2:3], in1=gB[:],
                                                     op0=ALU.mult, op1=ALU.mult),
             reads=[src_b, st_b[i], gB_b], writes=[hb_b[i]])
        for q in range(KC // 4):
            p, pb = next_ps()
            pv = p[:].bitcast(BF16)
            for j in range(4):
                kc = q * 4 + j
                S.op("pe", lambda e, kc=kc, j=j, pv=pv: e.transpose(
                    out=pv[:, j * 128:(j + 1) * 128], in_=hb[i][:, kc * 128:(kc + 1) * 128], identity=identb[:]),
                    reads=[hb_b[i], ident_b], writes=[pb])
            copy_any(dstT[:, q * 4:(q + 1) * 4, c0:c0 + 128], pv[:, 0:512].rearrange("p (a b) -> p a b", a=4),
                     [pb], [dst_b])

    S.dma("pool", identb[:], ident[:, :], writes=[ident_b])
    if io.get("after_setup"):
        io["after_setup"]()
    SQ = "dve" if io.get("no_pool") else "pool"
    S.op("dve", lambda e: e.memset(ones[:], 1.0), writes=[ones_b])

    S.dma("sp", gB[:], g_mem.partition_broadcast(128), writes=[gB_b])
    for mt in range(2):
        S.dma("sp", xs[:, mt, :], mem[mt * 128:(mt + 1) * 128, :], writes=[xs_b[mt]])
        rms_T(xs[:, mt, :], xs_b[mt], None, hT, hT_b[mt], mt * 128)
    Wt, Wb = load_W(wview("wkv", 0, 0), KC, 512)
    for h in range(4):
        p, pb = next_ps()
        for kc in range(KC):
            S.op("pe", lambda e, p=p, Wt=Wt, kc=kc, h=h: e.matmul(
                p[:, 0:256], lhsT=Wt[:, kc, h * 128:(h + 1) * 128], rhs=hT[:, kc, 0:256],
                start=(kc == 0), stop=(kc == KC - 1)), reads=[Wb, hT_b[0], hT_b[1]], writes=[pb])
        copy_any(kT[:, h, :], p[:, 0:256], [pb], [kT_b])
    Wt, Wb = load_W(wview("wkv", 0, 1), KC, 512)
    for mt in range(2):
        p, pb = next_ps()
        for kc in range(KC):
            S.op("pe", lambda e, p=p, Wt=Wt, kc=kc, mt=mt: e.matmul(
                p[:], lhsT=hT[:, kc, mt * 128:(mt + 1) * 128], rhs=Wt[:, kc, :],
                start=(kc == 0), stop=(kc == KC - 1)), reads=[Wb, hT_b[mt]], writes=[pb])
        copy_any(Vm[:, mt, :], p[:], [pb], [Vm_b])

    if io.get("pre_blocks_hook"):
        io["pre_blocks_hook"]()
    outs = []
    for blk in range(NSB):
        t0 = blk * 512
        for t in range(4):
            S.dma("sp", xs[:, t, :], x[t0 + t * 128:t0 + (t + 1) * 128, :], writes=[xs_b[t]])
        if mx_loader is None:
            S.dma("sp", mx[:], mixT[:, :, t0:t0 + 512].rearrange("k p t -> p k t"), writes=[mx_b])
        else:
            mx_loader(S, mx, mx_b, t0)
        for cg in range(4):
            Wt, Wb = load_W(wview("w_out", 0, cg), KC, 512)
            for t in range(4):
                p, pb = next_ps()
                for kc in range(KC):
                    S.op("pe", lambda e, p=p, Wt=Wt, kc=kc, t=t: e.matmul(
                        p[:], lhsT=mx[:, kc, t * 128:(t + 1) * 128], rhs=Wt[:, perm[kc], :],
                        start=(kc == 0), stop=(kc == KC - 1)), reads=[Wb, mx_b], writes=[pb])
                S.op("dve", lambda e, p=p, t=t, cg=cg: e.tensor_tensor(
                    out=xs[:, t, cg * 512:(cg + 1) * 512], in0=p[:], in1=xs[:, t, cg * 512:(cg + 1) * 512],
                    op=ALU.add), reads=[pb, xs_b[t]], writes=[xs_b[t]])
        S.dma("sp", gB[:], g_x.partition_broadcast(128), writes=[gB_b])
        for t in range(4):
            rms_T(xs[:, t, :], xs_b[t], None, hT, hT_b[t], t * 128)
        Wt, Wb = load_W(wview("wq", 0, 0), KC, 512)
        for h in range(4):
            p, pb = next_ps()
            for kc in range(KC):
                S.op("pe", lambda e, p=p, Wt=Wt, kc=kc, h=h: e.matmul(
                    p[:], lhsT=Wt[:, kc, h * 128:(h + 1) * 128], rhs=hT[:, kc, :],
                    start=(kc == 0), stop=(kc == KC - 1)), reads=[Wb] + hT_b, writes=[pb])
            copy_any(qT[:, h, :], p[:], [pb], [qT_b[h]])
        for h in range(4):
            i = h % 2
            for mc in range(2):
                p, pb = next_ps()
                S.op("pe", lambda e, p=p, h=h, mc=mc: e.matmul(
                    p[:], lhsT=kT[:, h, mc * 128:(mc + 1) * 128], rhs=qT[:, h, :], start=True, stop=True),
                    reads=[kT_b, qT_b[h]], writes=[pb])
                S.op("act", lambda e, p=p, i=i, mc=mc: e.activation(out=PT[i][:, mc, :], in_=p[:], func=AF.Exp,
                                                                  scale=SCALE), reads=[pb], writes=[PT_b[i]])
            pd, pdb = next_ps()
            po, pob = next_ps()
            for mc in range(2):
                S.op("pe", lambda e, pd=pd, i=i, mc=mc: e.matmul(
                    pd[:], lhsT=ones[:], rhs=PT[i][:, mc, :], start=(mc == 0), stop=(mc == 1)),
                    reads=[ones_b, PT_b[i]], writes=[pdb])
            for mc in range(2):
                S.op("pe", lambda e, po=po, i=i, mc=mc, h=h: e.matmul(
                    po[:], lhsT=Vm[:, mc, h * 128:(h + 1) * 128], rhs=PT[i][:, mc, :], start=(mc == 0), stop=(mc == 1)),
                    reads=[Vm_b, PT_b[i]], writes=[pob])
            S.op("dve", lambda e, pd=pd, i=i: e.reciprocal(out=rden[i][:], in_=pd[:]), reads=[pdb], writes=[rden_b[i]])
            S.op("dve", lambda e, po=po, i=i, h=h: e.tensor_tensor(out=oT[:, h, :], in0=po[:], in1=rden[i][:], op=ALU.mult),
                 reads=[pob, rden_b[i]], writes=[oT_b[h]])
        for cg in range(4):
            Wt, Wb = load_W(wview("wo", 0, cg), 4, 512)
            for t in range(4):
                p, pb = next_ps()
                for h in range(4):
                    S.op("pe", lambda e, p=p, Wt=Wt, h=h, t=t: e.matmul(
                        p[:], lhsT=oT[:, h, t * 128:(t + 1) * 128], rhs=Wt[:, h, :],
                        start=(h == 0), stop=(h == 3)), reads=[Wb] + oT_b, writes=[pb])
                S.op("dve", lambda e, p=p, t=t, cg=cg: e.tensor_tensor(
                    out=xs[:, t, cg * 512:(cg + 1) * 512], in0=p[:], in1=xs[:, t, cg * 512:(cg + 1) * 512],
                    op=ALU.add), reads=[pb, xs_b[t]], writes=[xs_b[t]])
        S.dma("sp", gB[:], g_mlp.partition_broadcast(128), writes=[gB_b])
        for t in range(4):
            rms_T(xs[:, t, :], xs_b[t], None, hT, hT_b[t], t * 128)
        for half in range(2):
            for ug in range(8):
                c0 = half * 4096 + ug * 512
                Wt, Wb = load_W(wview("w_up", 0, c0 // 512), KC, 512)
                for j in range(4):
                    hc = ug * 4 + j
                    p, pb = next_ps()
                    for kc in range(KC):
                        S.op("pe", lambda e, p=p, Wt=Wt, kc=kc, j=j: e.matmul(
                            p[:], lhsT=Wt[:, kc, j * 128:(j + 1) * 128], rhs=hT[:, kc, :],
                            start=(kc == 0), stop=(kc == KC - 1)), reads=[Wb] + hT_b, writes=[pb])
                    i = hc % 2
                    S.op("act", lambda e, p=p, i=i: e.activation(out=rtmp[i][:], in_=p[:], func=AF.Relu),
                         reads=[pb], writes=[rtmp_b[i]])
                    S.op(SQ, lambda e, i=i, hc=hc: e.tensor_tensor(out=aT[:, hc, :], in0=rtmp[i][:], in1=rtmp[i][:],
                                                                   op=ALU.mult), reads=[rtmp_b[i]], writes=[aT_b[hc]])
            for cg in range(4):
                acc = [next_ps() for _ in range(4)]
                for hq in range(2):
                    r0 = half * 4096 + hq * 2048
                    Wt, Wb = load_W(wview("w_down", r0 // 2048, cg), KC, 512)
                    for t in range(4):
                        p, pb = acc[t]
                        for kc in range(KC):
                            hc = hq * 16 + kc
                            S.op("pe", lambda e, p=p, Wt=Wt, kc=kc, hc=hc, t=t: e.matmul(
                                p[:], lhsT=aT[:, hc, t * 128:(t + 1) * 128], rhs=Wt[:, kc, :],
                                start=(hc == 0), stop=(hc == 31)), reads=[Wb, aT_b[hc]], writes=[pb])
                for t in range(4):
                    p, pb = acc[t]
                    S.op("dve", lambda e, p=p, t=t, cg=cg: e.tensor_tensor(
                        out=xs[:, t, cg * 512:(cg + 1) * 512], in0=p[:], in1=xs[:, t, cg * 512:(cg + 1) * 512],
                        op=ALU.add), reads=[pb, xs_b[t]], writes=[xs_b[t]])
        if FINAL:
            S.dma("sp", gB[:], g_fin.partition_broadcast(128), writes=[gB_b])
            for t in range(4):
                rms_T(xs[:, t, :], xs_b[t], None, None, None, 0, to_out=xs[:, t, :])
        for t in range(4):
            outs.append(S.dma("sp", y[t0 + t * 128:t0 + (t + 1) * 128, :], xs[:, t, :], reads=[xs_b[t]]))
    return outs


def build_C(NT=2048, FINAL=False):
    nc = bass.Bass("TRN2", target_bir_lowering=False)
    NSB = NT // 512
    KC = D // 128
    dr = lambda n, s, dt, k: nc.dram_tensor(n, list(s), dt, kind=k).ap()
    x = dr("x", [NT, D], F32, "ExternalInput")
    mixT = dr("mixT", [16, 128, NT], BF16, "ExternalInput")
    w_out = dr("w_out", [D, D], F32, "ExternalInput")
    g_x = dr("g_x", [1, D], F32, "ExternalInput")
    g_mem = dr("g_mem", [1, D], F32, "ExternalInput")
    mem = dr("mem", [256, D], F32, "ExternalInput")
    wq = dr("wq", [D, 512], F32, "ExternalInput")
    wkv = dr("wkv", [D, 1024], F32, "ExternalInput")
    wo = dr("wo", [512, D], F32, "ExternalInput")
    g_mlp = dr("g_mlp", [1, D], F32, "ExternalInput")
    w_up = dr("w_up", [D, DFF], F32, "ExternalInput")
    w_down = dr("w_down", [DFF, D], F32, "ExternalInput")
    g_fin = dr("g_fin", [1, D], F32, "ExternalInput")
    ident = dr("ident", [128, 128], F32, "ExternalInput")
    y = dr("y", [NT, D], F32, "ExternalOutput")
    SCALE = 128 ** -0.5

    io = dict(x=x, mixT=mixT, w_out=w_out, g_x=g_x, g_mem=g_mem, mem=mem, wq=wq, wkv=wkv, wo=wo, g_mlp=g_mlp,
              w_up=w_up, w_down=w_down, g_fin=g_fin, ident=ident, y=y)
    with ExitStack() as es:
        S = Sched(nc, es)
        outs = emit_C(S, make_psum(S), io, NT, FINAL)
        S.emit(final_waits=outs)
    return nc


NLAYER = 4
ARENA = 92160
GROUPS = [[0, 1], [2, 3], [4, 5], [6, 7]]


class FusedLoaderB:
    FM_IDX = {"Bq": (0, 2), "Bk": (4, 2), "Cq": (8, 4), "cmpk": (16, 1), "cmpv": (18, 1), "slck": (20, 1), "wink": (22, 1)}
    TM_COL = {"Bv": (0, 256), "slcv": (512, 128), "winv": (768, 128)}

    def __init__(self, S_, exA_dst, gate_dst, sel, S):
        self.S = S_
        self.dst = exA_dst
        self.gdst = gate_dst
        self.selt = S_.sb("selt", [128, 2], F32); self.sel_b = S_.buf()
        S_.dma("sp", self.selt[:], sel[:, :], writes=[self.sel_b])
        self.T = [S_.sb("Tfm%d" % i, [128, S], BF16) for i in range(2)]; self.T_b = [S_.buf() for _ in range(2)]
        self.Tg = S_.sb("Tg", [128, 4, 12], F32); self.Tg_b = S_.buf()
        self.n = 0
        self.half = S // 2

    def _blend(self, dst_ap, dst_b, t_ap, t_b):
        S_ = self.S
        S_.op("dve", lambda e: e.tensor_scalar(out=t_ap, in0=t_ap, scalar1=self.selt[:, 1:2], scalar2=None, op0=ALU.mult),
              reads=[t_b, self.sel_b], writes=[t_b])
        S_.op("dve", lambda e: e.scalar_tensor_tensor(out=dst_ap, in0=dst_ap, scalar=self.selt[:, 0:1], in1=t_ap,
                                                      op0=ALU.mult, op1=ALU.add),
              reads=[dst_b, t_b, self.sel_b], writes=[dst_b])

    def fm(self, tile, b, kind, i):
        base, stride = self.FM_IDX[kind]
        ii = self.n % 2
        self.n += 1
        T, Tb = self.T[ii], self.T_b[ii]
        H = self.half
        for r in range(2):
            for g, (dstt, dstb) in enumerate(((tile, b), (T, Tb))):
                R = (base + g * stride + i) * 128
                row = ((R // 512) * 2 + r) * 512 + (R % 512)
                self.S.dma("sp", dstt[:, r * H:(r + 1) * H], self.dst[row:row + 128, :], writes=[dstb])
        self._blend(tile[:], b, T[:], Tb)

    def tm(self, tile, b, kind, i):
        base, gstride = self.TM_COL[kind]
        ii = self.n % 2
        self.n += 1
        T, Tb = self.T[ii], self.T_b[ii]
        Tv = T[:].rearrange("p (k d) -> p k d", d=128)
        nk = self.half // 128
        hk = nk // 2
        for r in range(2):
            for piece in range(2):
                row0 = ((6 + piece) * 2 + r) * 512
                v = self.dst[row0:row0 + 512, :].rearrange("r (two c) -> (r two) c", two=2)
                k0 = r * nk + piece * hk
                for g in range(2):
                    c0 = base + g * gstride + i * 128
                    src = v[:, c0:c0 + 128].rearrange("(k p) d -> p k d", p=128)
                    if g == 0:
                        self.S.dma("sp", tile[:, k0:k0 + hk, 0:128], src, writes=[b])
                    else:
                        self.S.dma("sp", Tv[:, k0:k0 + hk, :], src, writes=[Tb])
        self._blend(tile[:, :, 0:128], b, Tv, Tb)

    def gates(self, gt, b, qs):
        nq = self.half // 512
        r, ql = qs // nq, qs % nq
        for g, (dstt, dstb) in enumerate(((gt, b), (self.Tg, self.Tg_b))):
            src = self.gdst[r * self.half + ql * 512:r * self.half + (ql + 1) * 512, 12 * g:12 * g + 12]
            self.S.dma("sp", dstt[:], src.rearrange("(j p) c -> p j c", p=128), writes=[dstb])
        self._blend(gt[:], b, self.Tg[:], self.Tg_b)


def build_all(NL=NLAYER):
    nc = bass.Bass("TRN2", target_bir_lowering=False)
    NT, S, D_ = 2048, 4096, 2048
    ein = lambda n, s, dt=F32: nc.dram_tensor(n, list(s), dt, kind="ExternalInput").ap()
    x = ein("x", [NT, D_]); mem = ein("mem", [256, D_]); sel = ein("sel", [128, 2])
    norm_mix = ein("norm_mix", [NL, 1, D_]); w_in = ein("w_in", [NL, D_, INW])
    ln_g = ein("ln_g", [NL, 1, 512]); ln_b = ein("ln_b", [NL, 1, 512])
    wsT = ein("wsT", [NL, 128, 4, 128]); bsT = ein("bsT", [NL, 128, 4]); posT = ein("posT", [NL, 128, 32])
    ck_w1 = ein("ck_w1", [NL, 4096, 128]); ck_w2 = ein("ck_w2", [NL, 128, 128])
    cv_w1 = ein("cv_w1", [NL, 4096, 128]); cv_w2 = ein("cv_w2", [NL, 128, 128])
    w_out = ein("w_out", [NL, D_, D_]); g_x = ein("g_x", [NL, 1, D_]); g_mem = ein("g_mem", [NL, 1, D_])
    wq = ein("wq", [NL, D_, 512]); wkv = ein("wkv", [NL, D_, 1024]); wo = ein("wo", [NL, 512, D_])
    g_mlp = ein("g_mlp", [NL, 1, D_]); w_up = ein("w_up", [NL, D_, DFF]); w_down = ein("w_down", [NL, DFF, D_])
    g_fin = ein("g_fin", [1, D_])
    tril = ein("tril", [128, 128]); ident = ein("ident", [128, 128])
    WD = 384 + 128 * 16 + 512; WW = 384 + 128 * 4 + 512; WC = 384 + 512
    dstrip = ein("dstrip", [128, WD], BF16); wstrip = ein("wstrip", [128, WW], BF16); cstrip = ein("cstrip", [128, WC], BF16)
    cbias = ein("cbias", [S // 512, 128, 2, 512], BF16); ovd = ein("ov", [128, 2, 64])
    selmask = ein("selmask", [S // 128, 128, 2, 64]); esel = ein("esel", [64, S // 128, 128], BF16)
    y = nc.dram_tensor("y", [NT, D_], F32, kind="ExternalOutput").ap()
    xs_t = nc.dram_tensor("xs_s", [NT, D_], F32)
    mixA_t = nc.dram_tensor("mixA_s", [4, 128, NT], BF16)
    exA_src_t = nc.dram_tensor("exA_src", [4096, NT], BF16)
    exA_dst_t = nc.dram_tensor("exA_dst", [8192, NT], BF16)
    gate_src_t = nc.dram_tensor("gate_src", [NT, 24], F32)
    gate_dst_t = nc.dram_tensor("gate_dst", [2 * NT, 24], F32)
    exB_src_t = nc.dram_tensor("exB_src", [768, S], BF16)
    exB_dst_t = nc.dram_tensor("exB_dst", [1536, S], BF16)
    WSPEC = {"w_out": (1, 4, 16), "wq": (1, 1, 16), "wkv": (1, 2, 16), "wo": (1, 4, 4), "w_up": (1, 16, 16), "w_down": (4, 4, 16)}
    wbf2 = [{k: nc.dram_tensor("wbf%d_%s" % (par, k), [ntr, ntc, 128, kr, 512], BF16).ap() for k, (ntr, ntc, kr) in WSPEC.items()}
            for par in range(2)]
    xs_s, mixA_s, exA_src, exA_dst = xs_t.ap(), mixA_t.ap(), exA_src_t.ap(), exA_dst_t.ap()
    gate_src, gate_dst, exB_src, exB_dst = gate_src_t.ap(), gate_dst_t.ap(), exB_src_t.ap(), exB_dst_t.ap()

    perm = [0, 1, 2, 3]
    for k in range(12):
        c, g, e = k // 4, (k // 2) % 2, k % 2
        i = 2 * c + e
        perm.append(4 + 2 * g + i if i < 2 else 8 + 4 * g + (i - 2))

    def precast_jobs(l):
        wl = {"w_out": w_out[l], "wq": wq[l], "wkv": wkv[l], "wo": wo[l], "w_up": w_up[l], "w_down": w_down[l]}
        jobs = []
        for k, (ntr, ntc, kr) in WSPEC.items():
            for tr in range(ntr):
                for tc in range(ntc):
                    srcv = wl[k][tr * 2048:tr * 2048 + kr * 128, tc * 512:(tc + 1) * 512].rearrange("(k p) c -> p k c", p=128)
                    jobs.append((wbf2[l % 2][k][tr, tc], srcv))
        return jobs

    def precast(l, bufs):
        wl = {"w_out": w_out[l], "wq": wq[l], "wkv": wkv[l], "wo": wo[l], "w_up": w_up[l], "w_down": w_down[l]}
        for k, (ntr, ntc, kr) in WSPEC.items():
            for tr in range(ntr):
                for tc in range(ntc):
                    srcv = wl[k][tr * 2048:tr * 2048 + kr * 128, tc * 512:(tc + 1) * 512].rearrange("(k p) c -> p k c", p=128)
                    Sc.dma("pool", wbf2[l % 2][k][tr, tc], srcv, reads=bufs)

    with ExitStack() as es:
        Sc = Sched(nc, es, arena_elems=ARENA)
        PSP = make_psum(Sc)
        outs = []
        for l in range(NL):
            x_src = x if l == 0 else xs_s
            x_dst = y if l == NL - 1 else xs_s
            Sc.phase_reset()
            ioA = dict(x=x_src, gmix=norm_mix[l], w_in=w_in[l], ln_g=ln_g[l], ln_b=ln_b[l], wsT=wsT[l], bsT=bsT[l],
                       tril=tril, ident=ident, o_mixA=mixA_s,
                       o_fm=exA_src[0:3072, :].rearrange("(h p) t -> h p t", p=128),
                       o_tm=exA_src[3072:4096, :].rearrange("r (two c) -> (r two) c", two=2),
                       o_gate=gate_src)
            ioA["mid_hook"] = lambda: Sc.allgather(
                [(exA_src[i * 512:(i + 1) * 512, :], exA_dst[i * 1024:(i + 1) * 1024, :]) for i in range(8)]
                + [(gate_src, gate_dst)], GROUPS, post_barrier=False)
            emit_A(Sc, PSP, ioA, NT)
            Sc.barrier()
            Sc.phase_reset()
            ld = FusedLoaderB(Sc, exA_dst, gate_dst, sel, S)
            ioB = dict(posT=posT[l], ck_w1=ck_w1[l], ck_w2=ck_w2[l], cv_w1=cv_w1[l], cv_w2=cv_w2[l], ident=ident,
                       dstrip=dstrip, wstrip=wstrip, cstrip=cstrip, cbias=cbias, ov=ovd, selmask=selmask, esel=esel,
                       mixT=exB_src.rearrange("(h p) t -> h p t", p=128), bf16_consts=True)

            if l == 0:
                ioB["after_loads"] = lambda bufs: precast(0, bufs)
            emit_B(Sc, PSP, ioB, S, ld=ld)
            Sc.allgather([(exB_src[i * 256:(i + 1) * 256, :], exB_dst[i * 512:(i + 1) * 512, :]) for i in range(3)], GROUPS,
                         post_barrier=False)
            Sc.phase_reset()
            selt = Sc.sb("seltC", [128, 2], F32); sel_b = Sc.buf()
            Sc.dma("sp", selt[:], sel[:, :], writes=[sel_b])
            Tmx = Sc.sb("Tmx", [128, 12, 512], BF16); Tmx_b = Sc.buf()
            dv = exB_dst.rearrange("(k p) t -> p k t", p=128)

            def mx_loader(S_, mx, mx_b, t0, selt=selt, sel_b=sel_b, Tmx=Tmx, Tmx_b=Tmx_b, dv=dv):
                S_.dma("sp", mx[:, 0:4, :], mixA_s[:, :, t0:t0 + 512].rearrange("k p t -> p k t"), writes=[mx_b])
                S_.dma("sp", mx[:, 4:16, :], dv[:, :, t0:t0 + 512], writes=[mx_b])
                S_.dma("sp", Tmx[:], dv[:, :, NT + t0:NT + t0 + 512], writes=[Tmx_b])
                S_.op("dve", lambda e: e.tensor_scalar(out=Tmx[:], in0=Tmx[:], scalar1=selt[:, 1:2], scalar2=None, op0=ALU.mult),
                      reads=[Tmx_b, sel_b], writes=[Tmx_b])
                S_.op("dve", lambda e: e.scalar_tensor_tensor(out=mx[:, 4:16, :], in0=mx[:, 4:16, :], scalar=selt[:, 0:1], in1=Tmx[:],
                                                              op0=ALU.mult, op1=ALU.add),
                      reads=[mx_b, Tmx_b, sel_b], writes=[mx_b])

            ioC = dict(x=x_src, w_out=w_out[l], g_x=g_x[l], g_mem=g_mem[l], mem=mem, wq=wq[l], wkv=wkv[l], wo=wo[l],
                       g_mlp=g_mlp[l], w_up=w_up[l], w_down=w_down[l], g_fin=g_fin, ident=ident, y=x_dst,
                       wqueue="sp", wtile=lambda name, tr, tc, wbf=wbf2[l % 2]: wbf[name][tr, tc], no_pool=True)
            ioC["pre_blocks_hook"] = Sc.barrier
            if l + 1 < NL:
                jobs = precast_jobs(l + 1)
                mark = Sc.sb("mark", [128, 8], F32)
                state = {"n": 0}

                def tick(jobs=jobs, mark=mark, state=state):
                    n = state["n"]
                    state["n"] += 1
                    if n % 3 == 0 and n // 3 < len(jobs):
                        mb = Sc.buf()
                        Sc.op("dve", lambda e: e.memset(mark[:, 0:1], 0.0), writes=[mb])
                        dstv, srcv = jobs[n // 3]
                        Sc.dma("pool", dstv, srcv, reads=[mb])

                ioC["tick"] = tick
            outs = emit_C(Sc, PSP, ioC, NT, FINAL=(l == NL - 1), mx_loader=mx_loader, perm=perm)
            Sc.barrier()
        Sc.emit(final_waits=outs)
    return nc

_NC = {}


def kernel(x, mem, norm_mix, w_in, gmlp_ln_g, gmlp_ln_b, gmlp_w_s, gmlp_b_s, cmp_pos,
           cmp_k_w1, cmp_k_w2, cmp_v_w1, cmp_v_w2, w_out, norm_xattn, norm_mem,
           xattn_wq, xattn_wkv, xattn_wo, norm_mlp, w_up, w_down, final_norm):
    f32 = np.float32
    A = lambda a: np.ascontiguousarray(np.asarray(a), dtype=f32)
    if "nc" not in _NC:
        _NC["nc"] = build_all(NLAYER)
    nc = _NC["nc"]
    L = NLAYER
    shared = dict(
        norm_mix=A(norm_mix).reshape(L, 1, -1), w_in=A(w_in), ln_g=A(gmlp_ln_g).reshape(L, 1, -1),
        ln_b=A(gmlp_ln_b).reshape(L, 1, -1), wsT=A(np.asarray(gmlp_w_s).transpose(0, 3, 1, 2)),
        bsT=A(np.asarray(gmlp_b_s).transpose(0, 2, 1)), posT=A(np.asarray(cmp_pos).transpose(0, 2, 1)),
        ck_w1=A(cmp_k_w1), ck_w2=A(cmp_k_w2), cv_w1=A(cmp_v_w1), cv_w2=A(cmp_v_w2),
        w_out=A(w_out), g_x=A(norm_xattn).reshape(L, 1, -1), g_mem=A(norm_mem).reshape(L, 1, -1),
        wq=A(xattn_wq), wkv=A(xattn_wkv), wo=A(xattn_wo), g_mlp=A(norm_mlp).reshape(L, 1, -1),
        w_up=A(w_up), w_down=A(w_down), g_fin=A(final_norm).reshape(1, -1),
        tril=np.triu(np.ones((128, 128), f32)))
    import ml_dtypes
    for k, v in consts_B(4096).items():
        if k in ("dstrip", "wstrip", "cstrip", "cbias", "esel"):
            v = v.astype(ml_dtypes.bfloat16)
        shared[k] = np.ascontiguousarray(v)
    x = np.asarray(x)
    mem = np.asarray(mem)
    in_maps = []
    for c in range(8):
        b, hh = c // 2, c % 2
        sel = np.zeros((128, 2), f32)
        sel[:, hh] = 1.0
        d = dict(x=A(x[b, hh * 2048:(hh + 1) * 2048]), mem=A(mem[b]), sel=sel)
        d.update(shared)
        in_maps.append(d)
    res = run_bass_kernel_spmd(nc, in_maps, core_ids=list(range(8)))
    out = np.empty((4, 4096, 2048), f32)
    for c in range(8):
        out[c // 2, (c % 2) * 2048:(c % 2 + 1) * 2048] = res.results[c]["y"]
    return out
```

```python
from concourse.bass_utils import run_bass_kernel_spmd
import numpy as np
import concourse.bass as bass
import concourse.mybir as mybir
from contextlib import ExitStack

F32 = mybir.dt.float32
BF16 = mybir.dt.bfloat16
AF = mybir.ActivationFunctionType
ALU = mybir.AluOpType
AX = mybir.AxisListType

STRICT_SAME_ENGINE = True


class Buf:
    __slots__ = ("name", "w", "r")

    def __init__(self, name):
        self.name = name
        self.w = None
        self.r = []


class Op:
    __slots__ = ("eng", "idx", "fn", "waits", "needed", "val", "kind", "semh", "key")

    def __init__(self, eng, idx, fn, kind="c"):
        self.eng = eng
        self.idx = idx
        self.fn = fn
        self.waits = []
        self.needed = False
        self.val = None
        self.kind = kind
        self.semh = None
        self.key = None


class Sched:
    ENGS = ("pe", "act", "dve", "pool", "sp")
    NDMA = {"sp": 20, "pool": 12}

    def __init__(self, nc, es, arena_elems=None):
        self.nc = nc
        self.es = es
        self.ops = {e: [] for e in self.ENGS}
        self.seen = {e: {} for e in self.ENGS}
        self.csem = {e: es.enter_context(nc.semaphore("cs_" + e)) for e in self.ENGS}
        self.dsem = {q: [es.enter_context(nc.semaphore("ds_%s%d" % (q, i))) for i in range(n)]
                     for q, n in self.NDMA.items()}
        self.ccsem = es.enter_context(nc.semaphore("ccs"))
        self.cccount = 0
        self.lastcc = None
        self.dcount = {q: 0 for q in self.NDMA}
        self.dlast = {q: [None] * n for q, n in self.NDMA.items()}
        self.lastc = {e: None for e in self.ENGS}
        self.nbuf = 0
        self.arena = None
        self.aoff = 0
        if arena_elems:
            self.arena = es.enter_context(nc.sbuf_tensor("arena", [128, arena_elems], BF16))
            self.arena_elems = arena_elems
        self.fence = []
        self.fence_id = 0
        self.passed = {e: 0 for e in self.ENGS}

    def sb(self, name, shape, dt):
        if self.arena is None:
            return self.es.enter_context(self.nc.sbuf_tensor(name, list(shape), dt))
        n = 1
        for s in shape[1:]:
            n *= s
        esz = 4 if dt == F32 else 2
        nb = (n * esz + 31) // 32 * 32
        o = self.aoff
        self.aoff += nb // 2
        assert self.aoff <= self.arena_elems, ("arena overflow", name, self.aoff)
        v = self.arena[0:shape[0], o:o + n * esz // 2]
        if dt != BF16:
            v = v.bitcast(dt)
        if len(shape) == 3:
            v = v.rearrange("p (a b) -> p a b", a=shape[1])
        elif len(shape) == 4:
            v = v.rearrange("p (a b c) -> p a b c", a=shape[1], b=shape[2])
        return v

    def phase_reset(self):
        self.aoff = 0

    def ps(self, name, shape, dt):
        return self.es.enter_context(self.nc.psum_tensor(name, list(shape), dt))

    def buf(self, name=None):
        self.nbuf += 1
        return Buf(name or "b%d" % self.nbuf)

    def barrier(self):
        f = []
        for e in self.ENGS:
            if self.lastc[e] is not None:
                f.append(self.lastc[e])
        for q in self.NDMA:
            for o in self.dlast[q]:
                if o is not None:
                    f.append(o)
        if self.lastcc is not None:
            f.append(self.lastcc)
        self.fence = f
        self.fence_id += 1

    def _dep(self, op, dep, force=False):
        if dep is None or dep is op:
            return
        if dep.kind == "c" and dep.eng == op.eng:
            if op.eng in ("pe", "sp") or not (STRICT_SAME_ENGINE or force):
                return
        seen = self.seen[op.eng]
        if seen.get(dep.key, -1) >= dep.idx:
            return
        seen[dep.key] = dep.idx
        dep.needed = True
        op.waits.append(dep)

    def op(self, eng, fn, reads=(), writes=(), kind="c"):
        lst = self.ops[eng]
        o = Op(eng, len(lst), fn, kind=kind)
        if kind == "d":
            q = eng
            n = self.NDMA[q]
            j = self.dcount[q] % n
            o.semh = self.dsem[q][j]
            o.key = ("d", q, j)
            o.idx = self.dcount[q] // n
            o.val = 16 * (o.idx + 1)
            prev = self.dlast[q][j]
            if prev is not None:
                self._dep(o, prev)
            self.dlast[q][j] = o
            self.dcount[q] += 1
        elif kind == "cc":
            o.semh = self.ccsem
            o.key = ("cc",)
            o.idx = self.cccount
            self.cccount += 1
            o.val = self.cccount
            self.lastcc = o
        else:
            o.semh = self.csem[eng]
            o.key = ("c", eng)
            self.lastc[eng] = o
        if self.passed[eng] < self.fence_id:
            for d in self.fence:
                self._dep(o, d, force=True)
            self.passed[eng] = self.fence_id
        for b in reads:
            self._dep(o, b.w)
        for b in writes:
            self._dep(o, b.w)
            for r in b.r:
                self._dep(o, r)
        for b in reads:
            b.r.append(o)
        for b in writes:
            b.w = o
            b.r = []
        lst.append(o)
        return o

    def dma(self, q, out, in_, reads=(), writes=()):
        return self.op(q, lambda e: e.dma_start(out=out, in_=in_), reads, writes, kind="d")

    def allgather(self, pairs, groups, post_barrier=True):
        self.barrier()
        o = None
        for src, dst in pairs:
            o = self.op("pool", lambda e, src=src, dst=dst: e.collective_compute(
                "AllGather", ALU.bypass, replica_groups=groups, ins=[src], outs=[dst]), kind="cc")
        if post_barrier:
            self.barrier()
        return o

    def emit(self, final_waits=()):
        nc = self.nc
        for e in self.ENGS:
            c = 0
            for o in self.ops[e]:
                if o.kind == "c" and o.needed:
                    c += 1
                    o.val = c
        engmap = {"pe": "tensor", "act": "scalar", "dve": "vector", "pool": "gpsimd", "sp": "sync"}

        def run(e, engine):
            for o in self.ops[e]:
                for d in o.waits:
                    engine.wait_ge(d.semh, d.val)
                ins = o.fn(engine)
                if o.kind == "d":
                    ins.then_inc(o.semh, 16)
                elif o.kind == "cc":
                    ins.then_inc(o.semh, 1)
                elif o.needed:
                    ins.then_inc(o.semh, 1)
            if e == "sp":
                best = {}
                for d in final_waits:
                    if d.key not in best or best[d.key].val < d.val:
                        best[d.key] = d
                for d in best.values():
                    engine.wait_ge(d.semh, d.val)

        with nc.Block() as block:
            for e in self.ENGS:
                getattr(block, engmap[e])(lambda engine, e=e: run(e, engine))


D = 2048
HD = 128
INW = 5144
EPS = 1e-6


def make_psum(S):
    PS = [S.ps("ps%d" % i, [128, 512], F32) for i in range(8)]
    PS_b = [S.buf() for _ in range(8)]
    return PS, PS_b


def emit_A(S, PSP, io, NT):
    NTT = NT // 128
    NSB = NT // 512
    KC = D // 128
    x, gmix, w_in, ln_g, ln_b, wsT, bsT, tril, ident = (io[k] for k in ("x", "gmix", "w_in", "ln_g", "ln_b", "wsT", "bsT", "tril", "ident"))
    o_mixA, o_fm, o_tm, o_gate = (io[k] for k in ("o_mixA", "o_fm", "o_tm", "o_gate"))
    hT = S.sb("hT", [128, KC, NT], BF16)
    hT_b = [S.buf() for _ in range(NTT)]
    gB = S.sb("gB", [128, D], F32); gB_b = S.buf()
    identb = S.sb("identb", [128, 128], BF16); ident_b = S.buf()
    xt = [S.sb("xt%d" % i, [128, D], F32) for i in range(2)]
    xt_b = [S.buf() for _ in range(2)]
    hb = [S.sb("hb%d" % i, [128, D], BF16) for i in range(2)]
    hb_b = [S.buf() for _ in range(2)]
    junk = S.sb("junk", [128, D], BF16); junk_b = S.buf()
    st = [S.sb("st%d" % i, [128, 8], F32) for i in range(2)]
    st_b = [S.buf() for _ in range(2)]
    W = [S.sb("W%d" % i, [128, KC, 512], BF16) for i in range(3)]
    W_b = [S.buf() for _ in range(3)]
    PS, PS_b = PSP
    psn = [0]

    def next_ps():
        i = psn[0] % 8
        psn[0] += 1
        return PS[i], PS_b[i]

    outs = []
    S.dma("sp", gB[:], gmix.partition_broadcast(128), writes=[gB_b])
    S.dma("pool", identb[:], ident[:, :], writes=[ident_b])

    for tt in range(NTT):
        i = tt % 2
        S.dma("sp", xt[i][:], x[tt * 128:(tt + 1) * 128, :], writes=[xt_b[i]])
        S.op("act", lambda e, i=i: e.activation(out=junk[:], in_=xt[i][:], func=AF.Square,
                                                accum_out=st[i][:, 0:1]),
             reads=[xt_b[i]], writes=[junk_b, st_b[i]])
        S.op("dve", lambda e, i=i: e.tensor_scalar(out=st[i][:, 1:2], in0=st[i][:, 0:1], scalar1=1.0 / D,
                                                   scalar2=EPS, op0=ALU.mult, op1=ALU.add),
             reads=[st_b[i]], writes=[st_b[i]])
        S.op("act", lambda e, i=i: e.activation(out=st[i][:, 3:4], in_=st[i][:, 1:2], func=AF.Sqrt),
             reads=[st_b[i]], writes=[st_b[i]])
        S.op("dve", lambda e, i=i: e.reciprocal(out=st[i][:, 2:3], in_=st[i][:, 3:4]),
             reads=[st_b[i]], writes=[st_b[i]])
        S.op("dve", lambda e, i=i: e.scalar_tensor_tensor(out=hb[i][:], in0=xt[i][:], scalar=st[i][:, 2:3],
                                                          in1=gB[:], op0=ALU.mult, op1=ALU.mult),
             reads=[xt_b[i], st_b[i], gB_b], writes=[hb_b[i]])
        for q in range(KC // 4):
            p, pb = next_ps()
            pv = p[:].bitcast(BF16)
            for j in range(4):
                kc = q * 4 + j
                S.op("pe", lambda e, i=i, kc=kc, j=j, pv=pv: e.transpose(
                    out=pv[:, j * 128:(j + 1) * 128], in_=hb[i][:, kc * 128:(kc + 1) * 128], identity=identb[:]),
                    reads=[hb_b[i], ident_b], writes=[pb])
            eng = "act" if q % 2 == 0 else "dve"
            if eng == "act":
                S.op("act", lambda e, q=q, tt=tt, pv=pv: e.activation(
                    out=hT[:, q * 4:(q + 1) * 4, tt * 128:(tt + 1) * 128],
                    in_=pv[:, 0:512].rearrange("p (a b) -> p a b", a=4), func=AF.Copy),
                    reads=[pb], writes=[hT_b[tt]])
            else:
                S.op("dve", lambda e, q=q, tt=tt, pv=pv: e.tensor_copy(
                    out=hT[:, q * 4:(q + 1) * 4, tt * 128:(tt + 1) * 128],
                    in_=pv[:, 0:512].rearrange("p (a b) -> p a b", a=4)),
                    reads=[pb], writes=[hT_b[tt]])

    wn = [0]

    def load_W(c0, width):
        i = wn[0] % 3
        wn[0] += 1
        S.dma("pool", W[i][:, :, 0:width], w_in[:, c0:c0 + width].rearrange("(k p) c -> p k c", p=128),
              writes=[W_b[i]])
        return W[i], W_b[i]

    ostage = [S.sb("os%d" % i, [128, 512], BF16) for i in range(4)]
    ostage_b = [S.buf() for _ in range(4)]
    osn = [0]

    def next_os():
        i = osn[0] % 4
        osn[0] += 1
        return ostage[i], ostage_b[i]

    evn = [0]

    def evac(dst, dst_b, src, src_b, extra_reads=()):
        e = "act" if evn[0] % 2 == 0 else "dve"
        evn[0] += 1
        if e == "act":
            S.op("act", lambda en: en.activation(out=dst, in_=src, func=AF.Copy),
                 reads=[src_b] + list(extra_reads), writes=[dst_b])
        else:
            S.op("dve", lambda en: en.tensor_copy(out=dst, in_=src),
                 reads=[src_b] + list(extra_reads), writes=[dst_b])

    def fm_group(c0, head0, nheads):
        Wt, Wb = load_W(c0, nheads * 128)
        for j in range(nheads):
            for sb_ in range(NSB):
                p, pb = next_ps()
                for kc in range(KC):
                    S.op("pe", lambda e, p=p, Wt=Wt, kc=kc, j=j, sb_=sb_: e.matmul(
                        p[:], lhsT=Wt[:, kc, j * 128:(j + 1) * 128], rhs=hT[:, kc, sb_ * 512:(sb_ + 1) * 512],
                        start=(kc == 0), stop=(kc == KC - 1)),
                        reads=[Wb] + hT_b[sb_ * 4:(sb_ + 1) * 4], writes=[pb])
                o, ob = next_os()
                evac(o[:], ob, p[:], pb)
                outs.append(S.dma("sp", o_fm[head0 + j, :, sb_ * 512:(sb_ + 1) * 512], o[:], reads=[ob]))

    def tm_group(c0, width, oc0):
        Wt, Wb = load_W(c0, width)
        for tt in range(NTT):
            p, pb = next_ps()
            for kc in range(KC):
                S.op("pe", lambda e, p=p, Wt=Wt, kc=kc, tt=tt: e.matmul(
                    p[:, 0:width], lhsT=hT[:, kc, tt * 128:(tt + 1) * 128], rhs=Wt[:, kc, 0:width],
                    start=(kc == 0), stop=(kc == KC - 1)),
                    reads=[Wb, hT_b[tt]], writes=[pb])
            o, ob = next_os()
            evac(o[:, 0:width], ob, p[:, 0:width], pb)
            outs.append(S.dma("sp", o_tm[tt * 128:(tt + 1) * 128, oc0:oc0 + width], o[:, 0:width], reads=[ob]))

    fm_group(1024, 0, 4)
    fm_group(1536, 4, 4)
    tm_group(2048, 512, 0)
    fm_group(2560, 8, 4)
    fm_group(3072, 12, 4)
    fm_group(3584, 16, 4)
    fm_group(4096, 20, 2)
    fm_group(4608, 22, 2)
    tm_group(4352, 256, 512)
    tm_group(4864, 256, 768)
    Wt, Wb = load_W(5120, 24)
    gst = [S.sb("gst%d" % i, [128, 24], F32) for i in range(2)]; gst_b = [S.buf() for _ in range(2)]
    for tt in range(NTT):
        i = tt % 2
        p, pb = next_ps()
        for kc in range(KC):
            S.op("pe", lambda e, p=p, Wt=Wt, kc=kc, tt=tt: e.matmul(
                p[:, 0:24], lhsT=hT[:, kc, tt * 128:(tt + 1) * 128], rhs=Wt[:, kc, 0:24],
                start=(kc == 0), stop=(kc == KC - 1)),
                reads=[Wb, hT_b[tt]], writes=[pb])
        S.op("act", lambda e, i=i, p=p: e.activation(out=gst[i][:], in_=p[:, 0:24], func=AF.Sigmoid),
             reads=[pb], writes=[gst_b[i]])
        outs.append(S.dma("sp", o_gate[tt * 128:(tt + 1) * 128, :], gst[i][:], reads=[gst_b[i]]))
    def do_gmlp():
        lngB = S.sb("lngB", [128, 512], F32); lnbB = S.sb("lnbB", [128, 512], F32); ln_bb = S.buf()
        S.dma("sp", lngB[:], ln_g.partition_broadcast(128), writes=[ln_bb])
        S.dma("sp", lnbB[:], ln_b.partition_broadcast(128), writes=[ln_bb])
        wsf = S.sb("wsf", [128, 4, 128], F32); trilf = S.sb("trilf", [128, 128], F32)
        wsb = S.sb("wsb", [128, 4, 128], BF16); ws_b = S.buf(); wsf_b = S.buf()
        bsb = S.sb("bsb", [128, 4], F32); bs_b = S.buf()
        S.dma("sp", wsf[:], wsT[:, :, :], writes=[wsf_b])
        S.dma("sp", trilf[:], tril[:, :], writes=[wsf_b])
        S.dma("sp", bsb[:], bsT[:, :], writes=[bs_b])
        for g in range(4):
            S.op("dve", lambda e, g=g: e.tensor_tensor(out=wsb[:, g, :], in0=wsf[:, g, :], in1=trilf[:], op=ALU.mult),
                 reads=[wsf_b], writes=[ws_b])
        Wu, Wub = load_W(0, 512)
        Wv, Wvb = load_W(512, 512)
        ug = [S.sb("ug%d" % i, [128, 512], F32) for i in range(2)]; ug_b = [S.buf() for _ in range(2)]
        vg = [S.sb("vg%d" % i, [128, 512], F32) for i in range(2)]; vg_b = [S.buf() for _ in range(2)]
        vn = [S.sb("vn%d" % i, [128, 512], BF16) for i in range(2)]; vn_b = [S.buf() for _ in range(2)]
        oa = [S.sb("oa%d" % i, [128, 512], BF16) for i in range(2)]; oa_b = [S.buf() for _ in range(2)]
        bst = [S.sb("bst%d" % i, [128, 8], F32) for i in range(2)]; bst_b = [S.buf() for _ in range(2)]
        for tt in range(NTT):
            i = tt % 2
            pu, pub = next_ps()
            pvv, pvb = next_ps()
            for (p, pb, Wt, Wb) in ((pu, pub, Wu, Wub), (pvv, pvb, Wv, Wvb)):
                for kc in range(KC):
                    S.op("pe", lambda e, p=p, Wt=Wt, kc=kc, tt=tt: e.matmul(
                        p[:], lhsT=hT[:, kc, tt * 128:(tt + 1) * 128], rhs=Wt[:, kc, :],
                        start=(kc == 0), stop=(kc == KC - 1)),
                        reads=[Wb, hT_b[tt]], writes=[pb])
            S.op("act", lambda e, i=i, pu=pu: e.activation(out=ug[i][:], in_=pu[:], func=AF.Gelu_apprx_tanh),
                 reads=[pub], writes=[ug_b[i]])
            S.op("act", lambda e, i=i, pvv=pvv: e.activation(out=vg[i][:], in_=pvv[:], func=AF.Gelu_apprx_tanh),
                 reads=[pvb], writes=[vg_b[i]])
            S.op("dve", lambda e, i=i: e.bn_stats(out=bst[i][:, 0:6], in_=vg[i][:]), reads=[vg_b[i]], writes=[bst_b[i]])
            S.op("dve", lambda e, i=i: e.bn_aggr(out=bst[i][:, 6:8], in_=bst[i][:, 0:6]), reads=[bst_b[i]], writes=[bst_b[i]])
            S.op("dve", lambda e, i=i: e.tensor_scalar(out=bst[i][:, 1:2], in0=bst[i][:, 7:8], scalar1=EPS, scalar2=None,
                                                       op0=ALU.add), reads=[bst_b[i]], writes=[bst_b[i]])
            S.op("act", lambda e, i=i: e.activation(out=bst[i][:, 2:3], in_=bst[i][:, 1:2], func=AF.Sqrt),
                 reads=[bst_b[i]], writes=[bst_b[i]])
            S.op("dve", lambda e, i=i: e.reciprocal(out=bst[i][:, 0:1], in_=bst[i][:, 2:3]),
                 reads=[bst_b[i]], writes=[bst_b[i]])
            S.op("dve", lambda e, i=i: e.tensor_scalar(out=vg[i][:], in0=vg[i][:], scalar1=bst[i][:, 6:7],
                                                       scalar2=bst[i][:, 0:1], op0=ALU.subtract, op1=ALU.mult),
                 reads=[vg_b[i], bst_b[i]], writes=[vg_b[i]])
            S.op("dve", lambda e, i=i: e.tensor_tensor(out=vg[i][:], in0=vg[i][:], in1=lngB[:], op=ALU.mult),
                 reads=[vg_b[i], ln_bb], writes=[vg_b[i]])
            S.op("dve", lambda e, i=i: e.tensor_tensor(out=vn[i][:], in0=vg[i][:], in1=lnbB[:], op=ALU.add),
                 reads=[vg_b[i], ln_bb], writes=[vn_b[i]])
            psv, psvb = next_ps()
            for g in range(4):
                S.op("pe", lambda e, g=g, i=i, psv=psv: e.matmul(
                    psv[:, g * 128:(g + 1) * 128], lhsT=wsb[:, g, :], rhs=vn[i][:, g * 128:(g + 1) * 128],
                    start=True, stop=True), reads=[ws_b, vn_b[i]], writes=[psvb])
            for g in range(4):
                S.op("dve", lambda e, g=g, i=i, psv=psv: e.scalar_tensor_tensor(
                    out=oa[i][:, g * 128:(g + 1) * 128], in0=psv[:, g * 128:(g + 1) * 128], scalar=bsb[:, g:g + 1],
                    in1=ug[i][:, g * 128:(g + 1) * 128], op0=ALU.add, op1=ALU.mult),
                    reads=[psvb, bs_b, ug_b[i]], writes=[oa_b[i]])
            pt, ptb = next_ps()
            ptv = pt[:].bitcast(BF16)
            for g in range(4):
                S.op("pe", lambda e, g=g, i=i, ptv=ptv: e.transpose(
                    out=ptv[:, g * 128:(g + 1) * 128], in_=oa[i][:, g * 128:(g + 1) * 128], identity=identb[:]),
                    reads=[oa_b[i], ident_b], writes=[ptb])
            o, ob = next_os()
            evac(o[:], ob, ptv[:, 0:512], ptb)
            outs.append(S.dma("sp", o_mixA[:, :, tt * 128:(tt + 1) * 128].rearrange("g p t -> p g t"),
                              o[:].rearrange("p (g t) -> p g t", g=4), reads=[ob]))


    if io.get("mid_hook"):
        io["mid_hook"]()
    do_gmlp()
    return outs


def build_A(NT=2048):
    nc = bass.Bass("TRN2", target_bir_lowering=False)
    NTT = NT // 128
    NSB = NT // 512
    KC = D // 128
    dr = lambda n, s, dt, k: nc.dram_tensor(n, list(s), dt, kind=k).ap()
    x = dr("x", [NT, D], F32, "ExternalInput")
    gmix = dr("gmix", [1, D], F32, "ExternalInput")
    w_in = dr("w_in", [D, INW], F32, "ExternalInput")
    ln_g = dr("ln_g", [1, 512], F32, "ExternalInput")
    ln_b = dr("ln_b", [1, 512], F32, "ExternalInput")
    wsT = dr("wsT", [128, 4, 128], F32, "ExternalInput")
    bsT = dr("bsT", [128, 4], F32, "ExternalInput")
    tril = dr("tril", [128, 128], F32, "ExternalInput")
    ident = dr("ident", [128, 128], F32, "ExternalInput")
    o_mixA = dr("o_mixA", [4, 128, NT], BF16, "ExternalOutput")
    o_fm = dr("o_fm", [24, 128, NT], BF16, "ExternalOutput")
    o_tm = dr("o_tm", [NT, 1024], BF16, "ExternalOutput")
    o_gate = dr("o_gate", [NT, 24], F32, "ExternalOutput")

    io = dict(x=x, gmix=gmix, w_in=w_in, ln_g=ln_g, ln_b=ln_b, wsT=wsT, bsT=bsT, tril=tril, ident=ident,
              o_mixA=o_mixA, o_fm=o_fm, o_tm=o_tm, o_gate=o_gate)
    with ExitStack() as es:
        S = Sched(nc, es)
        outs = emit_A(S, make_psum(S), io, NT)
        S.emit(final_waits=outs)
    return nc


HD = 128
BIG = 30000.0
SCALE = 128 ** -0.5
OFF0 = 384
DEBUG = False


def strip_const(mmin, fn):
    width = 384 - 128 * mmin + 512
    kl = np.arange(128)[:, None]
    v = np.arange(width)[None, :]
    return fn(v - OFF0 - kl).astype(np.float32)


def dil_fn(d):
    c = ((d >= 0) & (d <= 128)).astype(np.float64) + ((d >= 0) & (d % 4 == 0) & (d <= 512)) + ((d >= 0) & (d % 16 == 0) & (d <= 2048))
    out = np.full(d.shape, -BIG)
    m = c > 0
    out[m] = np.log(c[m]) / SCALE
    return out


def win_fn(d):
    return np.where((d >= 0) & (d <= 511), 0.0, -BIG)


def caus_fn(d):
    return np.where(d >= 0, 0.0, -BIG)


def consts_B(S):
    nqt = S // 128
    c = {}
    c["ident"] = np.eye(128, dtype=np.float32)
    c["dstrip"] = strip_const(-16, dil_fn)
    c["wstrip"] = strip_const(-4, win_fn)
    c["cstrip"] = strip_const(0, caus_fn)
    n = np.arange(256)
    t = np.arange(S)
    valid = (16 * n[:, None] + 31) <= t[None, :]
    cb = np.where(valid, 0.0, -BIG).astype(np.float32).reshape(2, 128, S // 512, 512).transpose(2, 1, 0, 3)
    c["cbias"] = np.ascontiguousarray(cb)
    n_sel = S // 64
    cs = n * 16
    ss = np.arange(64) * 64
    ov = ((cs[:, None] <= ss[None, :] + 63) & (cs[:, None] + 31 >= ss[None, :])).astype(np.float32)
    ov[255] = 0
    ov[:, n_sel:] = 0
    c["ov"] = np.ascontiguousarray(ov.reshape(2, 128, 64).transpose(1, 0, 2))
    sm = np.zeros((nqt, 128, 2, 64), np.float32)
    for gq in range(nqt):
        tt = 128 * gq + np.arange(128)
        jt = tt // 64
        s = np.arange(64)[None, :]
        valid_s = s <= jt[:, None]
        add = np.where(valid_s, 0.0, -1.0)
        vm = valid_s.astype(np.float32)
        for val, cond in ((1e4, s == 0), (2e4, s == jt[:, None]), (3e4, s == jt[:, None] - 1)):
            cond = np.broadcast_to(cond, add.shape)
            add = np.where(cond, val, add)
            vm = np.where(cond, 0.0, vm)
        sm[gq, :, 0] = vm
        sm[gq, :, 1] = add
    c["selmask"] = sm
    E = np.zeros((64, S // 128, 128), np.float32)
    for kt in range(S // 128):
        E[2 * kt, kt, :64] = 1
        E[2 * kt + 1, kt, 64:] = 1
    c["esel"] = E
    return c


class DirectLoaderB:
    def __init__(self, S_, io):
        self.S = S_
        self.io = io

    def fm(self, tile, b, kind, i):
        src = self.io[kind]
        ap = src[i, :, :] if kind in ("Bq", "Bk", "Cq") else src[:, :]
        self.S.dma("sp", tile[:], ap, writes=[b])

    def tm(self, tile, b, kind, i):
        src = self.io[kind]
        ap = src[:, i * 128:(i + 1) * 128] if kind == "Bv" else src
        self.S.dma("sp", tile[:, :, 0:128], ap.rearrange("(k p) d -> p k d", p=128), writes=[b])

    def gates(self, gt, b, qs):
        self.S.dma("sp", gt[:], self.io["gates"][qs * 512:(qs + 1) * 512, :].rearrange("(j p) c -> p j c", p=128), writes=[b])


def emit_B(S_, PSP, io, S, ld=None):
    NQS = S // 512
    NKT = S // 128
    WD = 384 + 128 * 16 + 512
    WW = 384 + 128 * 4 + 512
    WC = 384 + 512
    if ld is None:
        ld = DirectLoaderB(S_, io)
    (posT, ck_w1, ck_w2, cv_w1, cv_w2, ident, dstrip, wstrip, cstrip, cbias, ovd, selmask, esel, mixT) = (io[k] for k in (
        "posT", "ck_w1", "ck_w2", "cv_w1", "cv_w2", "ident", "dstrip", "wstrip", "cstrip", "cbias", "ov", "selmask", "esel", "mixT"))
    dbg = io.get("dbg")
    sb, buf, op, dma = S_.sb, S_.buf, S_.op, S_.dma
    BB = [sb("BB%d" % i, [128, S], BF16) for i in range(6)]; BB_b = [buf() for _ in range(6)]
    VA = [sb("VA%d" % i, [128, NKT, 129], BF16) for i in range(2)]; VA_b = [buf() for _ in range(2)]
    identb = sb("identb", [128, 128], BF16); identf = sb("identf", [128, 128], F32); id_b = buf()
    dst = sb("dst", [128, WD], BF16); wst = sb("wst", [128, WW], BF16); cst = sb("cst", [128, WC], BF16); strip_b = buf()
    eselb = sb("eselb", [64, NKT, 128], BF16); esel_b = buf()
    PT = [sb("PT%d" % i, [128, 512], BF16) for i in range(3)]; PT_b = [buf() for _ in range(3)]
    accS = sb("accS", [128, 4, 4, 128], F32); acc_b = [[buf() for _ in range(4)] for _ in range(4)]
    accB = sb("accB", [128, 4, 128], BF16); accB_b = buf()
    osb = [sb("osb%d" % i, [128, 512], BF16) for i in range(2)]; osb_b = [buf() for _ in range(2)]
    sm = [sb("sm%d" % i, [128, 16], F32) for i in range(4)]; sm_b = [buf() for _ in range(4)]
    gt = sb("gt", [128, 4, 12], F32); gt_b = buf()
    PS, PS_b = PSP
    cnt = {"st": 0, "acc": 0, "pt": 0, "sm": 0, "os": 0}

    def st_ps():
        i = cnt["st"] % 3; cnt["st"] += 1
        return PS[i], PS_b[i]

    def acc_ps():
        i = 3 + cnt["acc"] % 4; cnt["acc"] += 1
        return PS[i], PS_b[i]

    MISC, MISC_b = PS[7], PS_b[7]

    def next_pt():
        i = cnt["pt"] % 3; cnt["pt"] += 1
        return PT[i], PT_b[i]

    def next_sm():
        i = cnt["sm"] % 4; cnt["sm"] += 1
        return sm[i], sm_b[i]

    CQ = "sp" if io.get("bf16_consts") else "pool"
    dma("sp", identf[:], ident[:, :], writes=[id_b])
    op("dve", lambda e: e.tensor_copy(out=identb[:], in_=identf[:]), reads=[id_b], writes=[id_b])
    dma(CQ, dst[:], dstrip[:, :], writes=[strip_b])
    dma(CQ, wst[:], wstrip[:, :], writes=[strip_b])
    dma(CQ, cst[:], cstrip[:, :], writes=[strip_b])
    dma(CQ, eselb[:], esel[:, :, :], writes=[esel_b])
    for i in range(2):
        op("dve", lambda e, i=i: e.memset(VA[i][:, :, 128:129], 1.0), writes=[VA_b[i]])

    outs = []

    def attend(qT, q_b, qs, kT, k_b, Vt, V_b, ktiles, strip, extra_bias=None, ncols=129, rhs_fn=None, band=None):
        banks = [acc_ps(), acc_ps()]
        accs = [(banks[j // 2][0][:, (j % 2) * 256:(j % 2) * 256 + ncols], banks[j // 2][1]) for j in range(4)]
        first = [True] * 4
        last_kt = {}
        def live(kt, j):
            d = 4 * qs + j - kt
            return d >= 0 and (band is None or d <= band)

        for kt in ktiles:
            for j in range(4):
                if live(kt, j):
                    last_kt[j] = kt
        def stage1(kt):
            m = kt - 4 * qs
            p, pb = st_ps()
            nb = (1 if strip is not None and strip(m) is not None else 0) + (1 if extra_bias else 0)
            op("pe", lambda e, p=p, kt=kt: e.matmul(p[:], lhsT=kT[:, kt * 128:(kt + 1) * 128],
                                                   rhs=qT[:, qs * 512:(qs + 1) * 512], start=True, stop=(nb == 0)),
               reads=[k_b, q_b], writes=[pb])
            k = 0
            if extra_bias:
                k += 1
                l_ap, r_ap, rb = extra_bias(kt)
                op("pe", lambda e, p=p, l_ap=l_ap, r_ap=r_ap, k=k: e.matmul(p[:], lhsT=l_ap, rhs=r_ap, start=False, stop=(k == nb)),
                   reads=rb, writes=[pb])
            if strip is not None and strip(m) is not None:
                k += 1
                s_ap = strip(m)
                op("pe", lambda e, p=p, s_ap=s_ap: e.matmul(p[:], lhsT=identb[:], rhs=s_ap, start=False, stop=True),
                   reads=[id_b, strip_b], writes=[pb])
            pt, ptb = next_pt()
            op("act", lambda e, p=p, pt=pt: e.activation(out=pt[:], in_=p[:], func=AF.Exp, scale=SCALE),
               reads=[pb], writes=[ptb])
            return pt, ptb

        def stage2(kt, pt, ptb):
            for j in range(4):
                if not live(kt, j):
                    continue
                a, ab = accs[j]
                rhs = Vt[:, kt, 0:ncols] if rhs_fn is None else rhs_fn(kt)
                op("pe", lambda e, a=a, pt=pt, j=j, rhs=rhs, st=(first[j] and j % 2 == 0), sp=(last_kt[j] == kt): e.matmul(
                    a, lhsT=pt[:, j * 128:(j + 1) * 128], rhs=rhs, start=st, stop=sp, skip_group_check=True),
                    reads=[ptb, V_b], writes=[ab])
                first[j] = False

        kts = list(ktiles)
        LOOK = 2
        pend = [stage1(kt) for kt in kts[:LOOK]]
        for i, kt in enumerate(kts):
            if i + LOOK < len(kts):
                pend.append(stage1(kts[i + LOOK]))
            stage2(kt, *pend.pop(0))
        return accs

    def coef_of(a, ab, ncols, gate_ap=None, gate_b=None):
        s, sbb = next_sm()
        op("dve", lambda e: e.tensor_scalar(out=s[:, 0:1], in0=a[:, ncols - 1:ncols], scalar1=1e-30, scalar2=None, op0=ALU.max),
           reads=[ab], writes=[sbb])
        op("dve", lambda e: e.reciprocal(out=s[:, 1:2], in_=s[:, 0:1]), reads=[sbb], writes=[sbb])
        if gate_ap is not None:
            op("dve", lambda e: e.tensor_tensor(out=s[:, 1:2], in0=s[:, 1:2], in1=gate_ap, op=ALU.mult),
               reads=[sbb, gate_b], writes=[sbb])
        return s[:, 1:2], sbb

    def flush_head(head_out, qs, r):
        for j in range(4):
            op("act", lambda e, j=j: e.activation(out=accB[:, j, :], in_=accS[:, r, j, :], func=AF.Copy),
               reads=[acc_b[r][j]], writes=[accB_b])
        pv = MISC[:].bitcast(BF16)
        for j in range(4):
            op("pe", lambda e, j=j: e.transpose(out=pv[:, j * 128:(j + 1) * 128], in_=accB[:, j, :], identity=identb[:]),
               reads=[accB_b, id_b], writes=[MISC_b])
        i = cnt["os"] % 2; cnt["os"] += 1
        op("dve", lambda e, i=i: e.tensor_copy(out=osb[i][:], in_=pv[:, 0:512]), reads=[MISC_b], writes=[osb_b[i]])
        outs.append(dma("sp", mixT[head_out, :, qs * 512:(qs + 1) * 512], osb[i][:], reads=[osb_b[i]]))

    def dstrip_of(m):
        return dst[:, OFF0 - 128 * m:OFF0 - 128 * m + 512]

    for h in range(2):
        qT, q_b = BB[2 * h], BB_b[2 * h]
        kT, k_b = BB[2 * h + 1], BB_b[2 * h + 1]
        Vt, V_b = VA[h], VA_b[h]
        ld.fm(qT, q_b, "Bq", h)
        ld.fm(kT, k_b, "Bk", h)
        ld.tm(Vt, V_b, "Bv", h)
        for qs in range(NQS):
            ktiles = [kt for kt in range(4 * qs - 16, 4 * qs + 4) if kt >= 0]
            accs = attend(qT, q_b, qs, kT, k_b, Vt, V_b, ktiles, dstrip_of, band=16)
            for j in range(4):
                a, ab = accs[j]
                cf, cfb = coef_of(a, ab, 129)
                op("dve", lambda e, a=a, cf=cf, j=j: e.tensor_scalar(out=accS[:, 0, j, :], in0=a[:, 0:128], scalar1=cf,
                                                                   scalar2=None, op0=ALU.mult),
                   reads=[ab, cfb], writes=[acc_b[0][j]])
            flush_head(h, qs, 0)

    w1 = sb("w1", [128, 32, 128], BF16); w1_b = buf()
    w2 = sb("w2", [128, 128], BF16); w2_b = buf()
    posb = sb("posb", [128, 32], BF16); pos_b = buf()
    cvec = sb("cvec", [128, 1], F32); cvec_b = buf()
    hc = sb("hc", [128, 256], BF16); hc_b = buf()
    kcT = sb("kcT", [128, 256], BF16); kcT_b = buf()
    Rc = sb("Rc", [128, 2, 193], BF16); Rc_b = buf()
    ovf = sb("ovf", [128, 2, 64], F32); ovf_b = buf()
    posf = sb("posf", [128, 32], F32); w1f = sb("w1f", [128, 32, 128], F32); w2f = sb("w2f", [128, 128], F32); wf_b = buf()
    dma("sp", posf[:], posT[:, :], writes=[wf_b])
    op("dve", lambda e: e.tensor_copy(out=posb[:], in_=posf[:]), reads=[wf_b], writes=[pos_b])
    dma("sp", ovf[:], ovd[:, :, :], writes=[ovf_b])
    op("dve", lambda e: e.memset(Rc[:, :, 192:193], 1.0), writes=[Rc_b])
    op("dve", lambda e: e.tensor_copy(out=Rc[:, :, 128:192], in_=ovf[:]), reads=[ovf_b], writes=[Rc_b])
    ld.fm(BB[4], BB_b[4], "cmpk", 0)
    ld.fm(BB[5], BB_b[5], "cmpv", 0)
    for which in range(2):
        raw, raw_b = BB[4 + which], BB_b[4 + which]
        w1d, w2d = (ck_w1, ck_w2) if which == 0 else (cv_w1, cv_w2)
        dma("sp", w1f[:], w1d.rearrange("(i d) f -> d i f", d=128), writes=[wf_b])
        dma("sp", w2f[:], w2d[:, :], writes=[wf_b])
        op("act", lambda e: e.activation(out=w1[:], in_=w1f[:], func=AF.Copy), reads=[wf_b], writes=[w1_b])
        op("dve", lambda e: e.tensor_copy(out=w2[:], in_=w2f[:]), reads=[wf_b], writes=[w2_b])
        rv = raw[:].rearrange("p (n s) -> p n s", s=16)
        nblk = S // 16 - 1
        for i in range(32):
            op("pe", lambda e, i=i: e.matmul(MISC[:, 0:1], lhsT=w1[:, i, :], rhs=posb[:, i:i + 1], start=(i == 0), stop=(i == 31)),
               reads=[w1_b, pos_b], writes=[MISC_b])
        op("dve", lambda e: e.tensor_copy(out=cvec[:], in_=MISC[:, 0:1]), reads=[MISC_b], writes=[cvec_b])
        p, pb = st_ps()
        for i in range(32):
            rhs = rv[:, 0:nblk, i] if i < 16 else rv[:, 1:nblk + 1, i - 16]
            op("pe", lambda e, p=p, i=i, rhs=rhs: e.matmul(p[:, 0:nblk], lhsT=w1[:, i, :], rhs=rhs, start=(i == 0), stop=(i == 31)),
               reads=[w1_b, raw_b], writes=[pb])
        op("dve", lambda e: e.memset(hc[:], 0.0), writes=[hc_b])
        op("act", lambda e, p=p: e.activation(out=hc[:, 0:nblk], in_=p[:, 0:nblk], func=AF.Gelu_apprx_tanh, bias=cvec[:]),
           reads=[pb, cvec_b], writes=[hc_b])
        if which == 0:
            p2, p2b = st_ps()
            op("pe", lambda e, p2=p2: e.matmul(p2[:, 0:256], lhsT=w2[:], rhs=hc[:], start=True, stop=True),
               reads=[w2_b, hc_b], writes=[p2b])
            op("dve", lambda e, p2=p2: e.tensor_copy(out=kcT[:], in_=p2[:, 0:256]), reads=[p2b], writes=[kcT_b])
        else:
            for c in range(2):
                p2, p2b = st_ps()
                op("pe", lambda e, p2=p2, c=c: e.matmul(p2[:, 0:128], lhsT=hc[:, c * 128:(c + 1) * 128], rhs=w2[:], start=True, stop=True),
                   reads=[w2_b, hc_b], writes=[p2b])
                op("dve", lambda e, p2=p2, c=c: e.tensor_copy(out=Rc[:, c, 0:128], in_=p2[:, 0:128]), reads=[p2b], writes=[Rc_b])

    if DEBUG:
        outs.append(dma("sp", dbg[:, 0:256], kcT[:], reads=[kcT_b]))
        outs.append(dma("sp", dbg[:, 256:512], hc[:], reads=[hc_b]))
        outs.append(dma("sp", dbg[:, 512:768], Rc[:, 0, 0:128], reads=[Rc_b])) if False else None
    for r in range(4):
        ld.fm(BB[r], BB_b[r], "Cq", r)
    ld.fm(BB[4], BB_b[4], "slck", 0)
    ld.fm(BB[5], BB_b[5], "wink", 0)
    ld.tm(VA[0], VA_b[0], "slcv", 0)
    ld.tm(VA[1], VA_b[1], "winv", 0)
    if io.get("after_loads"):
        io["after_loads"](BB_b + VA_b)
    cbs = sb("cbs", [128, 2, 512], BF16); cbs_b = buf()
    smk = sb("smk", [128, 4, 2, 64], F32); smk_b = buf()
    imp = sb("imp", [128, 4, 64], F32); imp_b = [buf() for _ in range(4)]
    sc = sb("sc", [128, 64], F32); sc2 = sb("sc2", [128, 64], F32); sc_b = buf()
    mx8 = sb("mx8", [128, 16], F32); mx8_b = buf()
    selb = sb("selb", [128, 64], F32); selb_b = buf()
    selbT = sb("selbT", [64, 512], BF16); selbT_b = buf()

    def wstrip_of(m):
        return wst[:, OFF0 - 128 * m:OFF0 - 128 * m + 512]

    def cstrip_of(m):
        if m < 0:
            return None
        return cst[:, OFF0 - 128 * m:OFF0 - 128 * m + 512]

    for qs in range(NQS):
        ld.gates(gt, gt_b, qs)
        dma(CQ, cbs[:], cbias[qs, :, :, :], writes=[cbs_b])
        dma("sp", smk[:], selmask[4 * qs:4 * qs + 4, :, :, :].rearrange("j p a s -> p j a s"), writes=[smk_b])
        nchunks = [c for c in range(2) if (c * 2048 + 31) <= (qs * 512 + 511)]
        for r in range(4):
            qT, q_b = BB[r], BB_b[r]
            banks = [acc_ps(), acc_ps()]
            accs = [(banks[j // 2][0][:, (j % 2) * 256:(j % 2) * 256 + 193], banks[j // 2][1]) for j in range(4)]
            for ci, c in enumerate(nchunks):
                p, pb = st_ps()
                op("pe", lambda e, p=p, c=c, qT=qT, qs=qs: e.matmul(p[:], lhsT=kcT[:, c * 128:(c + 1) * 128], rhs=qT[:, qs * 512:(qs + 1) * 512],
                                                            start=True, stop=False), reads=[kcT_b, q_b], writes=[pb])
                op("pe", lambda e, p=p, c=c, cbs=cbs: e.matmul(p[:], lhsT=identb[:], rhs=cbs[:, c, :], start=False, stop=True),
                   reads=[id_b, cbs_b], writes=[pb])
                pt, ptb = next_pt()
                op("act", lambda e, p=p, pt=pt: e.activation(out=pt[:], in_=p[:], func=AF.Exp, scale=SCALE), reads=[pb], writes=[ptb])
                for j in range(4):
                    a, ab = accs[j]
                    op("pe", lambda e, a=a, pt=pt, j=j, c=c, ci=ci: e.matmul(a, lhsT=pt[:, j * 128:(j + 1) * 128], rhs=Rc[:, c, :],
                                                                           start=(ci == 0 and j % 2 == 0), stop=(ci == len(nchunks) - 1),
                                                                           skip_group_check=True),
                       reads=[ptb, Rc_b], writes=[ab])
            for j in range(4):
                a, ab = accs[j]
                s, sbb = next_sm()
                op("dve", lambda e, a=a, s=s: e.tensor_scalar(out=s[:, 0:1], in0=a[:, 192:193], scalar1=1e-30, scalar2=None, op0=ALU.max),
                   reads=[ab], writes=[sbb])
                op("dve", lambda e, s=s: e.reciprocal(out=s[:, 1:2], in_=s[:, 0:1]), reads=[sbb], writes=[sbb])
                op("dve", lambda e, s=s, j=j, r=r, gt=gt: e.tensor_tensor(out=s[:, 2:3], in0=s[:, 1:2], in1=gt[:, j, r * 3:r * 3 + 1], op=ALU.mult),
                   reads=[sbb, gt_b], writes=[sbb])
                op("dve", lambda e, a=a, s=s, j=j, r=r: e.tensor_scalar(out=accS[:, r, j, :], in0=a[:, 0:128], scalar1=s[:, 2:3],
                                                                      scalar2=None, op0=ALU.mult),
                   reads=[ab, sbb], writes=[acc_b[r][j]])
                if r == 0:
                    op("dve", lambda e, a=a, s=s, j=j: e.tensor_scalar(out=imp[:, j, :], in0=a[:, 128:192], scalar1=s[:, 1:2],
                                                                     scalar2=None, op0=ALU.mult),
                       reads=[ab, sbb], writes=[imp_b[j]])
                else:
                    op("dve", lambda e, a=a, s=s, j=j: e.scalar_tensor_tensor(out=imp[:, j, :], in0=a[:, 128:192], scalar=s[:, 1:2],
                                                                            in1=imp[:, j, :], op0=ALU.mult, op1=ALU.add),
                       reads=[ab, sbb, imp_b[j]], writes=[imp_b[j]])
        need_sel = (qs * 512 + 511) >= 1024
        for j in (range(4) if need_sel else ()):
            op("dve", lambda e, j=j, smk=smk: e.tensor_tensor(out=sc[:], in0=imp[:, j, :], in1=smk[:, j, 0, :], op=ALU.mult),
               reads=[imp_b[j], smk_b], writes=[sc_b])
            op("dve", lambda e, j=j, smk=smk: e.tensor_tensor(out=sc[:], in0=sc[:], in1=smk[:, j, 1, :], op=ALU.add),
               reads=[sc_b, smk_b], writes=[sc_b])
            op("dve", lambda e: e.max(out=mx8[:, 0:8], in_=sc[:]), reads=[sc_b], writes=[mx8_b])
            op("dve", lambda e: e.match_replace(out=sc2[:], in_to_replace=mx8[:, 0:8], in_values=sc[:], imm_value=-1e9),
               reads=[sc_b, mx8_b], writes=[sc_b])
            op("dve", lambda e: e.max(out=mx8[:, 8:16], in_=sc2[:]), reads=[sc_b], writes=[mx8_b])
            op("dve", lambda e: e.tensor_scalar(out=selb[:], in0=sc[:], scalar1=mx8[:, 15:16], scalar2=1.0, op0=ALU.is_ge, op1=ALU.subtract),
               reads=[sc_b, mx8_b], writes=[selb_b])
            op("pe", lambda e: e.transpose(out=MISC[0:64, 0:128], in_=selb[:], identity=identf[:]),
               reads=[selb_b, id_b], writes=[MISC_b])
            op("act", lambda e, j=j: e.activation(out=selbT[:, j * 128:(j + 1) * 128], in_=MISC[0:64, 0:128], func=AF.Copy, scale=BIG),
               reads=[MISC_b], writes=[selbT_b])
        for r in range(4):
            qT, q_b = BB[r], BB_b[r]
            accs = attend(qT, q_b, qs, BB[4], BB_b[4], VA[0], VA_b[0], list(range(0, 4 * qs + 4)), cstrip_of,
                          extra_bias=(lambda kt: (eselb[:, kt, :], selbT[:], [esel_b, selbT_b])) if need_sel else None)
            for j in range(4):
                a, ab = accs[j]
                cf, cfb = coef_of(a, ab, 129, gt[:, j, r * 3 + 1:r * 3 + 2], gt_b)
                op("dve", lambda e, a=a, cf=cf, j=j, r=r: e.scalar_tensor_tensor(out=accS[:, r, j, :], in0=a[:, 0:128], scalar=cf,
                                                                               in1=accS[:, r, j, :], op0=ALU.mult, op1=ALU.add),
                   reads=[ab, cfb, acc_b[r][j]], writes=[acc_b[r][j]])
            ktiles = [kt for kt in range(4 * qs - 4, 4 * qs + 4) if kt >= 0]
            accs = attend(qT, q_b, qs, BB[5], BB_b[5], VA[1], VA_b[1], ktiles, wstrip_of, band=4)
            for j in range(4):
                a, ab = accs[j]
                cf, cfb = coef_of(a, ab, 129, gt[:, j, r * 3 + 2:r * 3 + 3], gt_b)
                op("dve", lambda e, a=a, cf=cf, j=j, r=r: e.scalar_tensor_tensor(out=accS[:, r, j, :], in0=a[:, 0:128], scalar=cf,
                                                                               in1=accS[:, r, j, :], op0=ALU.mult, op1=ALU.add),
                   reads=[ab, cfb, acc_b[r][j]], writes=[acc_b[r][j]])
            flush_head(2 + r, qs, r)
    return outs


def build_B(S=4096):
    nc = bass.Bass("TRN2", target_bir_lowering=False)
    NQS = S // 512
    NKT = S // 128
    dr = lambda n, s, dt, k: nc.dram_tensor(n, list(s), dt, kind=k).ap()
    Bq = dr("Bq", [2, 128, S], BF16, "ExternalInput")
    Bk = dr("Bk", [2, 128, S], BF16, "ExternalInput")
    Bv = dr("Bv", [S, 256], BF16, "ExternalInput")
    Cq = dr("Cq", [4, 128, S], BF16, "ExternalInput")
    cmpk = dr("cmpk", [128, S], BF16, "ExternalInput")
    cmpv = dr("cmpv", [128, S], BF16, "ExternalInput")
    slck = dr("slck", [128, S], BF16, "ExternalInput")
    slcv = dr("slcv", [S, 128], BF16, "ExternalInput")
    wink = dr("wink", [128, S], BF16, "ExternalInput")
    winv = dr("winv", [S, 128], BF16, "ExternalInput")
    gates = dr("gates", [S, 12], F32, "ExternalInput")
    posT = dr("posT", [128, 32], F32, "ExternalInput")
    ck_w1 = dr("ck_w1", [4096, 128], F32, "ExternalInput")
    ck_w2 = dr("ck_w2", [128, 128], F32, "ExternalInput")
    cv_w1 = dr("cv_w1", [4096, 128], F32, "ExternalInput")
    cv_w2 = dr("cv_w2", [128, 128], F32, "ExternalInput")
    WD = 384 + 128 * 16 + 512
    WW = 384 + 128 * 4 + 512
    WC = 384 + 512
    ident = dr("ident", [128, 128], F32, "ExternalInput")
    dstrip = dr("dstrip", [128, WD], F32, "ExternalInput")
    wstrip = dr("wstrip", [128, WW], F32, "ExternalInput")
    cstrip = dr("cstrip", [128, WC], F32, "ExternalInput")
    cbias = dr("cbias", [NQS, 128, 2, 512], F32, "ExternalInput")
    ovd = dr("ov", [128, 2, 64], F32, "ExternalInput")
    selmask = dr("selmask", [NKT, 128, 2, 64], F32, "ExternalInput")
    esel = dr("esel", [64, NKT, 128], F32, "ExternalInput")
    mixT = dr("mixT", [6, 128, S], BF16, "ExternalOutput")
    dbg = dr("dbg", [128, 1024], BF16, "ExternalOutput") if DEBUG else None

    io = dict(Bq=Bq, Bk=Bk, Bv=Bv, Cq=Cq, cmpk=cmpk, cmpv=cmpv, slck=slck, slcv=slcv, wink=wink, winv=winv, gates=gates,
              posT=posT, ck_w1=ck_w1, ck_w2=ck_w2, cv_w1=cv_w1, cv_w2=cv_w2, ident=ident, dstrip=dstrip, wstrip=wstrip,
              cstrip=cstrip, cbias=cbias, ov=ovd, selmask=selmask, esel=esel, mixT=mixT, dbg=dbg)
    with ExitStack() as es:
        S_ = Sched(nc, es)
        outs = emit_B(S_, make_psum(S_), io, S)
        S_.emit(final_waits=outs)
    return nc


D = 2048
EPS = 1e-6
DFF = 8192


def emit_C(S, PSP, io, NT, FINAL, mx_loader=None, perm=None):
    NSB = NT // 512
    KC = D // 128
    SCALE = 128 ** -0.5
    if perm is None:
        perm = list(range(16))
    (x, mixT, w_out, g_x, g_mem, mem, wq, wkv, wo, g_mlp, w_up, w_down, g_fin, ident, y) = (io.get(k) for k in (
        "x", "mixT", "w_out", "g_x", "g_mem", "mem", "wq", "wkv", "wo", "g_mlp", "w_up", "w_down", "g_fin", "ident", "y"))
    xs = S.sb("xs", [128, 4, D], F32); xs_b = [S.buf() for _ in range(4)]
    mx = S.sb("mx", [128, KC, 512], BF16); mx_b = S.buf()
    hT = S.sb("hT", [128, KC, 512], BF16); hT_b = [S.buf() for _ in range(4)]
    aT = S.sb("aT", [128, 32, 512], BF16); aT_b = [S.buf() for _ in range(32)]
    W = [S.sb("W%d" % i, [128, KC, 512], BF16) for i in range(2)]; W_b = [S.buf() for _ in range(2)]
    gB = S.sb("gB", [128, D], F32); gB_b = S.buf()
    hb = [S.sb("hb%d" % i, [128, D], BF16) for i in range(2)]; hb_b = [S.buf() for _ in range(2)]
    st = [S.sb("st%d" % i, [128, 8], F32) for i in range(2)]; st_b = [S.buf() for _ in range(2)]
    identb = S.sb("identb", [128, 128], BF16); ident_b = S.buf()
    ones = S.sb("ones", [128, 128], BF16); ones_b = S.buf()
    qT = S.sb("qT", [128, 4, 512], BF16); qT_b = [S.buf() for _ in range(4)]
    kT = S.sb("kT", [128, 4, 256], BF16); kT_b = S.buf()
    Vm = S.sb("Vm", [128, 2, 512], BF16); Vm_b = S.buf()
    PT = [S.sb("PT%d" % i, [128, 2, 512], BF16) for i in range(2)]; PT_b = [S.buf() for _ in range(2)]
    oT = S.sb("oT", [128, 4, 512], BF16); oT_b = [S.buf() for _ in range(4)]
    rden0 = S.sb("rden0", [128, 512], F32); rden = [rden0, rden0]; rden0_b = S.buf(); rden_b = [rden0_b, rden0_b]
    rtmp = [S.sb("rtmp%d" % i, [128, 512], F32) for i in range(2)]; rtmp_b = [S.buf() for _ in range(2)]
    PS, PS_b = PSP
    psn = [0]

    def next_ps():
        i = psn[0] % 8
        psn[0] += 1
        return PS[i], PS_b[i]

    wn = [0]

    def load_W(view, K, width):
        i = wn[0] % 2
        wn[0] += 1
        S.dma(WQ, W[i][:, 0:K, 0:width], view, writes=[W_b[i]])
        if io.get("tick"):
            io["tick"]()
        return W[i], W_b[i]

    WQ = io.get("wqueue", "pool")
    wsrc = dict(w_out=w_out, wq=wq, wkv=wkv, wo=wo, w_up=w_up, w_down=w_down)

    def wview(name, tr, tc):
        if "wtile" in io:
            return io["wtile"](name, tr, tc)
        w = wsrc[name]
        kr = min(16, w.shape[0] // 128)
        return w[tr * 2048:tr * 2048 + kr * 128, tc * 512:(tc + 1) * 512].rearrange("(k p) c -> p k c", p=128)

    evn = [0]

    def copy_any(dst, src, reads, writes):
        e = "act" if evn[0] % 2 == 0 else "dve"
        evn[0] += 1
        if e == "act":
            S.op("act", lambda en: en.activation(out=dst, in_=src, func=AF.Copy), reads=reads, writes=writes)
        else:
            S.op("dve", lambda en: en.tensor_copy(out=dst, in_=src), reads=reads, writes=writes)

    rn = [0]

    def rms_T(src, src_b, g_ap, dstT, dst_b, c0, to_out=None):
        i = rn[0] % 2
        rn[0] += 1
        S.op("act", lambda e: e.activation(out=hb[i][:], in_=src, func=AF.Square, accum_out=st[i][:, 0:1]),
             reads=[src_b], writes=[hb_b[i], st_b[i]])
        S.op("dve", lambda e: e.tensor_scalar(out=st[i][:, 1:2], in0=st[i][:, 0:1], scalar1=1.0 / D, scalar2=EPS,
                                              op0=ALU.mult, op1=ALU.add), reads=[st_b[i]], writes=[st_b[i]])
        S.op("act", lambda e: e.activation(out=st[i][:, 3:4], in_=st[i][:, 1:2], func=AF.Sqrt),
             reads=[st_b[i]], writes=[st_b[i]])
        S.op("dve", lambda e: e.reciprocal(out=st[i][:, 2:3], in_=st[i][:, 3:4]), reads=[st_b[i]], writes=[st_b[i]])
        if to_out is not None:
            S.op("dve", lambda e: e.scalar_tensor_tensor(out=to_out, in0=src, scalar=st[i][:, 2:3], in1=gB[:],
                                                         op0=ALU.mult, op1=ALU.mult),
                 reads=[src_b, st_b[i], gB_b], writes=[src_b])
            return
        S.op("dve", lambda e: e.scalar_tensor_tensor(out=hb[i][:], in0=src, scalar=st[i][:, 2:3], in1=gB[:],
                                                     op0=ALU.mult, op1=ALU.mult),
             reads=[src_b, st_b[i], gB_b], writes=[hb_b[i]])
        for q in range(KC // 4):
            p, pb = next_ps()
            pv = p[:].bitcast(BF16)
            for j in range(4):
                kc = q * 4 + j
                S.op("pe", lambda e, kc=kc, j=j, pv=pv: e.transpose(
                    out=pv[:, j * 128:(j + 1) * 128], in_=hb[i][:, kc * 128:(kc + 1) * 128], identity=identb[:]),
                    reads=[hb_b[i], ident_b], writes=[pb])
            copy_any(dstT[:, q * 4:(q + 1) * 4, c0:c0 + 128], pv[:, 0:512].rearrange("p (a b) -> p a b", a=4),
                     [pb], [dst_b])

    S.dma("pool", identb[:], ident[:, :], writes=[ident_b])
    if io.get("after_setup"):
        io["after_setup"]()
    SQ = "dve" if io.get("no_pool") else "pool"
    S.op("dve", lambda e: e.memset(ones[:], 1.0), writes=[ones_b])

    S.dma("sp", gB[:], g_mem.partition_broadcast(128), writes=[gB_b])
    for mt in range(2):
        S.dma("sp", xs[:, mt, :], mem[mt * 128:(mt + 1) * 128, :], writes=[xs_b[mt]])
        rms_T(xs[:, mt, :], xs_b[mt], None, hT, hT_b[mt], mt * 128)
    Wt, Wb = load_W(wview("wkv", 0, 0), KC, 512)
    for h in range(4):
        p, pb = next_ps()
        for kc in range(KC):
            S.op("pe", lambda e, p=p, Wt=Wt, kc=kc, h=h: e.matmul(
                p[:, 0:256], lhsT=Wt[:, kc, h * 128:(h + 1) * 128], rhs=hT[:, kc, 0:256],
                start=(kc == 0), stop=(kc == KC - 1)), reads=[Wb, hT_b[0], hT_b[1]], writes=[pb])
        copy_any(kT[:, h, :], p[:, 0:256], [pb], [kT_b])
    Wt, Wb = load_W(wview("wkv", 0, 1), KC, 512)
    for mt in range(2):
        p, pb = next_ps()
        for kc in range(KC):
            S.op("pe", lambda e, p=p, Wt=Wt, kc=kc, mt=mt: e.matmul(
                p[:], lhsT=hT[:, kc, mt * 128:(mt + 1) * 128], rhs=Wt[:, kc, :],
                start=(kc == 0), stop=(kc == KC - 1)), reads=[Wb, hT_b[mt]], writes=[pb])
        copy_any(Vm[:, mt, :], p[:], [pb], [Vm_b])

    if io.get("pre_blocks_hook"):
        io["pre_blocks_hook"]()
    outs = []
    for blk in range(NSB):
        t0 = blk * 512
        for t in range(4):
            S.dma("sp", xs[:, t, :], x[t0 + t * 128:t0 + (t + 1) * 128, :], writes=[xs_b[t]])
        if mx_loader is None:
            S.dma("sp", mx[:], mixT[:, :, t0:t0 + 512].rearrange("k p t -> p k t"), writes=[mx_b])
        else:
            mx_loader(S, mx, mx_b, t0)
        for cg in range(4):
            Wt, Wb = load_W(wview("w_out", 0, cg), KC, 512)
            for t in range(4):
                p, pb = next_ps()
                for kc in range(KC):
                    S.op("pe", lambda e, p=p, Wt=Wt, kc=kc, t=t: e.matmul(
                        p[:], lhsT=mx[:, kc, t * 128:(t + 1) * 128], rhs=Wt[:, perm[kc], :],
                        start=(kc == 0), stop=(kc == KC - 1)), reads=[Wb, mx_b], writes=[pb])
                S.op("dve", lambda e, p=p, t=t, cg=cg: e.tensor_tensor(
                    out=xs[:, t, cg * 512:(cg + 1) * 512], in0=p[:], in1=xs[:, t, cg * 512:(cg + 1) * 512],
                    op=ALU.add), reads=[pb, xs_b[t]], writes=[xs_b[t]])
        S.dma("sp", gB[:], g_x.partition_broadcast(128), writes=[gB_b])
        for t in range(4):
            rms_T(xs[:, t, :], xs_b[t], None, hT, hT_b[t], t * 128)
        Wt, Wb = load_W(wview("wq", 0, 0), KC, 512)
        for h in range(4):
            p, pb = next_ps()
            for kc in range(KC):
                S.op("pe", lambda e, p=p, Wt=Wt, kc=kc, h=h: e.matmul(
                    p[:], lhsT=Wt[:, kc, h * 128:(h + 1) * 128], rhs=hT[:, kc, :],
                    start=(kc == 0), stop=(kc == KC - 1)), reads=[Wb] + hT_b, writes=[pb])
            copy_any(qT[:, h, :], p[:], [pb], [qT_b[h]])
        for h in range(4):
            i = h % 2
            for mc in range(2):
                p, pb = next_ps()
                S.op("pe", lambda e, p=p, h=h, mc=mc: e.matmul(
                    p[:], lhsT=kT[:, h, mc * 128:(mc + 1) * 128], rhs=qT[:, h, :], start=True, stop=True),
                    reads=[kT_b, qT_b[h]], writes=[pb])
                S.op("act", lambda e, p=p, i=i, mc=mc: e.activation(out=PT[i][:, mc, :], in_=p[:], func=AF.Exp,
                                                                  scale=SCALE), reads=[pb], writes=[PT_b[i]])
            pd, pdb = next_ps()
            po, pob = next_ps()
            for mc in range(2):
                S.op("pe", lambda e, pd=pd, i=i, mc=mc: e.matmul(
                    pd[:], lhsT=ones[:], rhs=PT[i][:, mc, :], start=(mc == 0), stop=(mc == 1)),
                    reads=[ones_b, PT_b[i]], writes=[pdb])
            for mc in range(2):
                S.op("pe", lambda e, po=po, i=i, mc=mc, h=h: e.matmul(
                    po[:], lhsT=Vm[:, mc, h * 128:(h + 1) * 128], rhs=PT[i][:, mc, :], start=(mc == 0), stop=(mc == 1)),
                    reads=[Vm_b, PT_b[i]], writes=[pob])
            S.op("dve", lambda e, pd=pd, i=i: e.reciprocal(out=rden[i][:], in_=pd[:]), reads=[pdb], writes=[rden_b[i]])
            S.op("dve", lambda e, po=po, i=i, h=h: e.tensor_tensor(out=oT[:, h, :], in0=po[:], in1=rden[i][:], op=ALU.mult),
                 reads=[pob, rden_b[i]], writes=[oT_b[h]])
        for cg in range(4):
            Wt, Wb = load_W(wview("wo", 0, cg), 4, 512)
            for t in range(4):
                p, pb = next_ps()
                for h in range(4):
                    S.op("pe", lambda e, p=p, Wt=Wt, h=h, t=t: e.matmul(
                        p[:], lhsT=oT[:, h, t * 128:(t + 1) * 128], rhs=Wt[:, h, :],
                        start=(h == 0), stop=(h == 3)), reads=[Wb] + oT_b, writes=[pb])
                S.op("dve", lambda e, p=p, t=t, cg=cg: e.tensor_tensor(
                    out=xs[:, t, cg * 512:(cg + 1) * 512], in0=p[:], in1=xs[:, t, cg * 512:(cg + 1) * 512],
                    op=ALU.add), reads=[pb, xs_b[t]], writes=[xs_b[t]])
        S.dma("sp", gB[:], g_mlp.partition_broadcast(128), writes=[gB_b])
        for t in range(4):
            rms_T(xs[:, t, :], xs_b[t], None, hT, hT_b[t], t * 128)
        for half in range(2):
            for ug in range(8):
                c0 = half * 4096 + ug * 512
                Wt, Wb = load_W(wview("w_up", 0, c0 // 512), KC, 512)
                for j in range(4):
                    hc = ug * 4 + j
                    p, pb = next_ps()
                    for kc in range(KC):
                        S.op("pe", lambda e, p=p, Wt=Wt, kc=kc, j=j: e.matmul(
                            p[:], lhsT=Wt[:, kc, j * 128:(j + 1) * 128], rhs=hT[:, kc, :],
                            start=(kc == 0), stop=(kc == KC - 1)), reads=[Wb] + hT_b, writes=[pb])
                    i = hc % 2
                    S.op("act", lambda e, p=p, i=i: e.activation(out=rtmp[i][:], in_=p[:], func=AF.Relu),
                         reads=[pb], writes=[rtmp_b[i]])
                    S.op(SQ, lambda e, i=i, hc=hc: e.tensor_tensor(out=aT[:, hc, :], in0=rtmp[i][:], in1=rtmp[i][:],
                                                                   op=ALU.mult), reads=[rtmp_b[i]], writes=[aT_b[hc]])
            for cg in range(4):
                acc = [next_ps() for _ in range(4)]
                for hq in range(2):
                    r0 = half * 4096 + hq * 2048
                    Wt, Wb = load_W(wview("w_down", r0 // 2048, cg), KC, 512)
                    for t in range(4):
                        p, pb = acc[t]
                        for kc in range(KC):
                            hc = hq * 16 + kc
                            S.op("pe", lambda e, p=p, Wt=Wt, kc=kc, hc=hc, t=t: e.matmul(
                                p[:], lhsT=aT[:, hc, t * 128:(t + 1) * 128], rhs=Wt[:, kc, :],
                                start=(hc == 0), stop=(hc == 31)), reads=[Wb, aT_b[hc]], writes=[pb])
                for t in range(4):
                    p, pb = acc[t]
                    S.op("dve", lambda e, p=p, t=t, cg=cg: e.tensor_tensor(
                        out=xs[:, t, cg * 512:(cg + 1) * 512], in0=p[:], in1=xs[:, t, cg * 512:(cg + 1) * 512],
                        op=ALU.add), reads=[pb, xs_b[t]], writes=[xs_b[t]])
        if FINAL:
            S.dma("sp", gB[:], g_fin.partition_broadcast(128), writes=[gB_b])
            for t in range(4):
                rms_T(xs[:, t, :], xs_b[t], None, None, None, 0, to_out=xs[:, t, :])
        for t in range(4):
            outs.append(S.dma("sp", y[t0 + t * 128:t0 + (t + 1) * 128, :], xs[:, t, :], reads=[xs_b[t]]))
    return outs


def build_C(NT=2048, FINAL=False):
    nc = bass.Bass("TRN2", target_bir_lowering=False)
    NSB = NT // 512
    KC = D // 128
    dr = lambda n, s, dt, k: nc.dram_tensor(n, list(s), dt, kind=k).ap()
    x = dr("x", [NT, D], F32, "ExternalInput")
    mixT = dr("mixT", [16, 128, NT], BF16, "ExternalInput")
    w_out = dr("w_out", [D, D], F32, "ExternalInput")
    g_x = dr("g_x", [1, D], F32, "ExternalInput")
    g_mem = dr("g_mem", [1, D], F32, "ExternalInput")
    mem = dr("mem", [256, D], F32, "ExternalInput")
    wq = dr("wq", [D, 512], F32, "ExternalInput")
    wkv = dr("wkv", [D, 1024], F32, "ExternalInput")
    wo = dr("wo", [512, D], F32, "ExternalInput")
    g_mlp = dr("g_mlp", [1, D], F32, "ExternalInput")
    w_up = dr("w_up", [D, DFF], F32, "ExternalInput")
    w_down = dr("w_down", [DFF, D], F32, "ExternalInput")
    g_fin = dr("g_fin", [1, D], F32, "ExternalInput")
    ident = dr("ident", [128, 128], F32, "ExternalInput")
    y = dr("y", [NT, D], F32, "ExternalOutput")
    SCALE = 128 ** -0.5

    io = dict(x=x, mixT=mixT, w_out=w_out, g_x=g_x, g_mem=g_mem, mem=mem, wq=wq, wkv=wkv, wo=wo, g_mlp=g_mlp,
              w_up=w_up, w_down=w_down, g_fin=g_fin, ident=ident, y=y)
    with ExitStack() as es:
        S = Sched(nc, es)
        outs = emit_C(S, make_psum(S), io, NT, FINAL)
        S.emit(final_waits=outs)
    return nc


NLAYER = 4
ARENA = 92160
GROUPS = [[0, 1], [2, 3], [4, 5], [6, 7]]


class FusedLoaderB:
    FM_IDX = {"Bq": (0, 2), "Bk": (4, 2), "Cq": (8, 4), "cmpk": (16, 1), "cmpv": (18, 1), "slck": (20, 1), "wink": (22, 1)}
    TM_COL = {"Bv": (0, 256), "slcv": (512, 128), "winv": (768, 128)}

    def __init__(self, S_, exA_dst, gate_dst, sel, S):
        self.S = S_
        self.dst = exA_dst
        self.gdst = gate_dst
        self.selt = S_.sb("selt", [128, 2], F32); self.sel_b = S_.buf()
        S_.dma("sp", self.selt[:], sel[:, :], writes=[self.sel_b])
        self.T = [S_.sb("Tfm%d" % i, [128, S], BF16) for i in range(2)]; self.T_b = [S_.buf() for _ in range(2)]
        self.Tg = S_.sb("Tg", [128, 4, 12], F32); self.Tg_b = S_.buf()
        self.n = 0
        self.half = S // 2

    def _blend(self, dst_ap, dst_b, t_ap, t_b):
        S_ = self.S
        S_.op("dve", lambda e: e.tensor_scalar(out=t_ap, in0=t_ap, scalar1=self.selt[:, 1:2], scalar2=None, op0=ALU.mult),
              reads=[t_b, self.sel_b], writes=[t_b])
        S_.op("dve", lambda e: e.scalar_tensor_tensor(out=dst_ap, in0=dst_ap, scalar=self.selt[:, 0:1], in1=t_ap,
                                                      op0=ALU.mult, op1=ALU.add),
              reads=[dst_b, t_b, self.sel_b], writes=[dst_b])

    def fm(self, tile, b, kind, i):
        base, stride = self.FM_IDX[kind]
        ii = self.n % 2
        self.n += 1
        T, Tb = self.T[ii], self.T_b[ii]
        H = self.half
        for r in range(2):
            for g, (dstt, dstb) in enumerate(((tile, b), (T, Tb))):
                R = (base + g * stride + i) * 128
                row = ((R // 512) * 2 + r) * 512 + (R % 512)
                self.S.dma("sp", dstt[:, r * H:(r + 1) * H], self.dst[row:row + 128, :], writes=[dstb])
        self._blend(tile[:], b, T[:], Tb)

    def tm(self, tile, b, kind, i):
        base, gstride = self.TM_COL[kind]
        ii = self.n % 2
        self.n += 1
        T, Tb = self.T[ii], self.T_b[ii]
        Tv = T[:].rearrange("p (k d) -> p k d", d=128)
        nk = self.half // 128
        hk = nk // 2
        for r in range(2):
            for piece in range(2):
                row0 = ((6 + piece) * 2 + r) * 512
                v = self.dst[row0:row0 + 512, :].rearrange("r (two c) -> (r two) c", two=2)
                k0 = r * nk + piece * hk
                for g in range(2):
                    c0 = base + g * gstride + i * 128
                    src = v[:, c0:c0 + 128].rearrange("(k p) d -> p k d", p=128)
                    if g == 0:
                        self.S.dma("sp", tile[:, k0:k0 + hk, 0:128], src, writes=[b])
                    else:
                        self.S.dma("sp", Tv[:, k0:k0 + hk, :], src, writes=[Tb])
        self._blend(tile[:, :, 0:128], b, Tv, Tb)

    def gates(self, gt, b, qs):
        nq = self.half // 512
        r, ql = qs // nq, qs % nq
        for g, (dstt, dstb) in enumerate(((gt, b), (self.Tg, self.Tg_b))):
            src = self.gdst[r * self.half + ql * 512:r * self.half + (ql + 1) * 512, 12 * g:12 * g + 12]
            self.S.dma("sp", dstt[:], src.rearrange("(j p) c -> p j c", p=128), writes=[dstb])
        self._blend(gt[:], b, self.Tg[:], self.Tg_b)


def build_all(NL=NLAYER):
    nc = bass.Bass("TRN2", target_bir_lowering=False)
    NT, S, D_ = 2048, 4096, 2048
    ein = lambda n, s, dt=F32: nc.dram_tensor(n, list(s), dt, kind="ExternalInput").ap()
    x = ein("x", [NT, D_]); mem = ein("mem", [256, D_]); sel = ein("sel", [128, 2])
    norm_mix = ein("norm_mix", [NL, 1, D_]); w_in = ein("w_in", [NL, D_, INW])
    ln_g = ein("ln_g", [NL, 1, 512]); ln_b = ein("ln_b", [NL, 1, 512])
    wsT = ein("wsT", [NL, 128, 4, 128]); bsT = ein("bsT", [NL, 128, 4]); posT = ein("posT", [NL, 128, 32])
    ck_w1 = ein("ck_w1", [NL, 4096, 128]); ck_w2 = ein("ck_w2", [NL, 128, 128])
    cv_w1 = ein("cv_w1", [NL, 4096, 128]); cv_w2 = ein("cv_w2", [NL, 128, 128])
    w_out = ein("w_out", [NL, D_, D_]); g_x = ein("g_x", [NL, 1, D_]); g_mem = ein("g_mem", [NL, 1, D_])
    wq = ein("wq", [NL, D_, 512]); wkv = ein("wkv", [NL, D_, 1024]); wo = ein("wo", [NL, 512, D_])
    g_mlp = ein("g_mlp", [NL, 1, D_]); w_up = ein("w_up", [NL, D_, DFF]); w_down = ein("w_down", [NL, DFF, D_])
    g_fin = ein("g_fin", [1, D_])
    tril = ein("tril", [128, 128]); ident = ein("ident", [128, 128])
    WD = 384 + 128 * 16 + 512; WW = 384 + 128 * 4 + 512; WC = 384 + 512
    dstrip = ein("dstrip", [128, WD], BF16); wstrip = ein("wstrip", [128, WW], BF16); cstrip = ein("cstrip", [128, WC], BF16)
    cbias = ein("cbias", [S // 512, 128, 2, 512], BF16); ovd = ein("ov", [128, 2, 64])
    selmask = ein("selmask", [S // 128, 128, 2, 64]); esel = ein("esel", [64, S // 128, 128], BF16)
    y = nc.dram_tensor("y", [NT, D_], F32, kind="ExternalOutput").ap()
    xs_t = nc.dram_tensor("xs_s", [NT, D_], F32)
    mixA_t = nc.dram_tensor("mixA_s", [4, 128, NT], BF16)
    exA_src_t = nc.dram_tensor("exA_src", [4096, NT], BF16)
    exA_dst_t = nc.dram_tensor("exA_dst", [8192, NT], BF16)
    gate_src_t = nc.dram_tensor("gate_src", [NT, 24], F32)
    gate_dst_t = nc.dram_tensor("gate_dst", [2 * NT, 24], F32)
    exB_src_t = nc.dram_tensor("exB_src", [768, S], BF16)
    exB_dst_t = nc.dram_tensor("exB_dst", [1536, S], BF16)
    WSPEC = {"w_out": (1, 4, 16), "wq": (1, 1, 16), "wkv": (1, 2, 16), "wo": (1, 4, 4), "w_up": (1, 16, 16), "w_down": (4, 4, 16)}
    wbf2 = [{k: nc.dram_tensor("wbf%d_%s" % (par, k), [ntr, ntc, 128, kr, 512], BF16).ap() for k, (ntr, ntc, kr) in WSPEC.items()}
            for par in range(2)]
    xs_s, mixA_s, exA_src, exA_dst = xs_t.ap(), mixA_t.ap(), exA_src_t.ap(), exA_dst_t.ap()
    gate_src, gate_dst, exB_src, exB_dst = gate_src_t.ap(), gate_dst_t.ap(), exB_src_t.ap(), exB_dst_t.ap()

    perm = [0, 1, 2, 3]
    for k in range(12):
        c, g, e = k // 4, (k // 2) % 2, k % 2
        i = 2 * c + e
        perm.append(4 + 2 * g + i if i < 2 else 8 + 4 * g + (i - 2))

    def precast_jobs(l):
        wl = {"w_out": w_out[l], "wq": wq[l], "wkv": wkv[l], "wo": wo[l], "w_up": w_up[l], "w_down": w_down[l]}
        jobs = []
        for k, (ntr, ntc, kr) in WSPEC.items():
            for tr in range(ntr):
                for tc in range(ntc):
                    srcv = wl[k][tr * 2048:tr * 2048 + kr * 128, tc * 512:(tc + 1) * 512].rearrange("(k p) c -> p k c", p=128)
                    jobs.append((wbf2[l % 2][k][tr, tc], srcv))
        return jobs

    def precast(l, bufs):
        wl = {"w_out": w_out[l], "wq": wq[l], "wkv": wkv[l], "wo": wo[l], "w_up": w_up[l], "w_down": w_down[l]}
        for k, (ntr, ntc, kr) in WSPEC.items():
            for tr in range(ntr):
                for tc in range(ntc):
                    srcv = wl[k][tr * 2048:tr * 2048 + kr * 128, tc * 512:(tc + 1) * 512].rearrange("(k p) c -> p k c", p=128)
                    Sc.dma("pool", wbf2[l % 2][k][tr, tc], srcv, reads=bufs)

    with ExitStack() as es:
        Sc = Sched(nc, es, arena_elems=ARENA)
        PSP = make_psum(Sc)
        outs = []
        for l in range(NL):
            x_src = x if l == 0 else xs_s
            x_dst = y if l == NL - 1 else xs_s
            Sc.phase_reset()
            ioA = dict(x=x_src, gmix=norm_mix[l], w_in=w_in[l], ln_g=ln_g[l], ln_b=ln_b[l], wsT=wsT[l], bsT=bsT[l],
                       tril=tril, ident=ident, o_mixA=mixA_s,
                       o_fm=exA_src[0:3072, :].rearrange("(h p) t -> h p t", p=128),
                       o_tm=exA_src[3072:4096, :].rearrange("r (two c) -> (r two) c", two=2),
                       o_gate=gate_src)
            ioA["mid_hook"] = lambda: Sc.allgather(
                [(exA_src[i * 512:(i + 1) * 512, :], exA_dst[i * 1024:(i + 1) * 1024, :]) for i in range(8)]
                + [(gate_src, gate_dst)], GROUPS, post_barrier=False)
            emit_A(Sc, PSP, ioA, NT)
            Sc.barrier()
            Sc.phase_reset()
            ld = FusedLoaderB(Sc, exA_dst, gate_dst, sel, S)
            ioB = dict(posT=posT[l], ck_w1=ck_w1[l], ck_w2=ck_w2[l], cv_w1=cv_w1[l], cv_w2=cv_w2[l], ident=ident,
                       dstrip=dstrip, wstrip=wstrip, cstrip=cstrip, cbias=cbias, ov=ovd, selmask=selmask, esel=esel,
                       mixT=exB_src.rearrange("(h p) t -> h p t", p=128), bf16_consts=True)

            if l == 0:
                ioB["after_loads"] = lambda bufs: precast(0, bufs)
            emit_B(Sc, PSP, ioB, S, ld=ld)
            Sc.allgather([(exB_src[i * 256:(i + 1) * 256, :], exB_dst[i * 512:(i + 1) * 512, :]) for i in range(3)], GROUPS)
            Sc.phase_reset()
            selt = Sc.sb("seltC", [128, 2], F32); sel_b = Sc.buf()
            Sc.dma("sp", selt[:], sel[:, :], writes=[sel_b])
            Tmx = Sc.sb("Tmx", [128, 12, 512], BF16); Tmx_b = Sc.buf()
            dv = exB_dst.rearrange("(k p) t -> p k t", p=128)

            def mx_loader(S_, mx, mx_b, t0, selt=selt, sel_b=sel_b, Tmx=Tmx, Tmx_b=Tmx_b, dv=dv):
                S_.dma("sp", mx[:, 0:4, :], mixA_s[:, :, t0:t0 + 512].rearrange("k p t -> p k t"), writes=[mx_b])
                S_.dma("sp", mx[:, 4:16, :], dv[:, :, t0:t0 + 512], writes=[mx_b])
                S_.dma("sp", Tmx[:], dv[:, :, NT + t0:NT + t0 + 512], writes=[Tmx_b])
                S_.op("dve", lambda e: e.tensor_scalar(out=Tmx[:], in0=Tmx[:], scalar1=selt[:, 1:2], scalar2=None, op0=ALU.mult),
                      reads=[Tmx_b, sel_b], writes=[Tmx_b])
                S_.op("dve", lambda e: e.scalar_tensor_tensor(out=mx[:, 4:16, :], in0=mx[:, 4:16, :], scalar=selt[:, 0:1], in1=Tmx[:],
                                                              op0=ALU.mult, op1=ALU.add),
                      reads=[mx_b, Tmx_b, sel_b], writes=[mx_b])

            ioC = dict(x=x_src, w_out=w_out[l], g_x=g_x[l], g_mem=g_mem[l], mem=mem, wq=wq[l], wkv=wkv[l], wo=wo[l],
                       g_mlp=g_mlp[l], w_up=w_up[l], w_down=w_down[l], g_fin=g_fin, ident=ident, y=x_dst,
                       wqueue="sp", wtile=lambda name, tr, tc, wbf=wbf2[l % 2]: wbf[name][tr, tc], no_pool=True)
            if l + 1 < NL:
                jobs = precast_jobs(l + 1)
                mark = Sc.sb("mark", [128, 8], F32)
                state = {"n": 0}

                def tick(jobs=jobs, mark=mark, state=state):
                    n = state["n"]
                    state["n"] += 1
                    if n % 3 == 0 and n // 3 < len(jobs):
                        mb = Sc.buf()
                        Sc.op("dve", lambda e: e.memset(mark[:, 0:1], 0.0), writes=[mb])
                        dstv, srcv = jobs[n // 3]
                        Sc.dma("pool", dstv, srcv, reads=[mb])

                ioC["tick"] = tick
            outs = emit_C(Sc, PSP, ioC, NT, FINAL=(l == NL - 1), mx_loader=mx_loader, perm=perm)
            Sc.barrier()
        Sc.emit(final_waits=outs)
    return nc

_NC = {}


def kernel(x, mem, norm_mix, w_in, gmlp_ln_g, gmlp_ln_b, gmlp_w_s, gmlp_b_s, cmp_pos,
           cmp_k_w1, cmp_k_w2, cmp_v_w1, cmp_v_w2, w_out, norm_xattn, norm_mem,
           xattn_wq, xattn_wkv, xattn_wo, norm_mlp, w_up, w_down, final_norm):
    f32 = np.float32
    A = lambda a: np.ascontiguousarray(np.asarray(a), dtype=f32)
    if "nc" not in _NC:
        _NC["nc"] = build_all(NLAYER)
    nc = _NC["nc"]
    L = NLAYER
    shared = dict(
        norm_mix=A(norm_mix).reshape(L, 1, -1), w_in=A(w_in), ln_g=A(gmlp_ln_g).reshape(L, 1, -1),
        ln_b=A(gmlp_ln_b).reshape(L, 1, -1), wsT=A(np.asarray(gmlp_w_s).transpose(0, 3, 1, 2)),
        bsT=A(np.asarray(gmlp_b_s).transpose(0, 2, 1)), posT=A(np.asarray(cmp_pos).transpose(0, 2, 1)),
        ck_w1=A(cmp_k_w1), ck_w2=A(cmp_k_w2), cv_w1=A(cmp_v_w1), cv_w2=A(cmp_v_w2),
        w_out=A(w_out), g_x=A(norm_xattn).reshape(L, 1, -1), g_mem=A(norm_mem).reshape(L, 1, -1),
        wq=A(xattn_wq), wkv=A(xattn_wkv), wo=A(xattn_wo), g_mlp=A(norm_mlp).reshape(L, 1, -1),
        w_up=A(w_up), w_down=A(w_down), g_fin=A(final_norm).reshape(1, -1),
        tril=np.triu(np.ones((128, 128), f32)))
    import ml_dtypes
    for k, v in consts_B(4096).items():
        if k in ("dstrip", "wstrip", "cstrip", "cbias", "esel"):
            v = v.astype(ml_dtypes.bfloat16)
        shared[k] = np.ascontiguousarray(v)
    x = np.asarray(x)
    mem = np.asarray(mem)
    in_maps = []
    for c in range(8):
        b, hh = c // 2, c % 2
        sel = np.zeros((128, 2), f32)
        sel[:, hh] = 1.0
        d = dict(x=A(x[b, hh * 2048:(hh + 1) * 2048]), mem=A(mem[b]), sel=sel)
        d.update(shared)
        in_maps.append(d)
    res = run_bass_kernel_spmd(nc, in_maps, core_ids=list(range(8)))
    out = np.empty((4, 4096, 2048), f32)
    for c in range(8):
        out[c // 2, (c % 2) * 2048:(c % 2 + 1) * 2048] = res.results[c]["y"]
    return out
```
